# Optimizing a Trainium2 kernel written in Bass

```python
import math
import jax, jax.numpy as jnp
from jax import lax
import numpy as np

D_MODEL = 1024
BATCH = 8
SEQ = 4096
DEPTH = 1
DEC_BATCH = 8
DEC_SEQ = 8192
PAST_LEN = 128

MLA_HEADS = 8
MLA_NOPE = 64
MLA_ROPE = 32
MLA_QK = MLA_NOPE + MLA_ROPE
MLA_V = 64
Q_LORA = 384
KV_LORA = 256
ROPE_THETA = 10000.0
Q_BLOCK = 128
GDN_HEADS = 4
GDN_DK = 128
GDN_DV = 128
CONV_K = 5
CHUNK = 64
D_FF = 2816
EPS = 1e-6

MLA_OUT = MLA_HEADS * MLA_V
GDN_QK = GDN_HEADS * GDN_DK
GDN_OUT = GDN_HEADS * GDN_DV
GDN_CONV_CH = 2 * GDN_QK + GDN_OUT
MIX_WIDTH = MLA_OUT + GDN_OUT
IN_SPLITS = (Q_LORA,
             Q_LORA + KV_LORA,
             Q_LORA + KV_LORA + MLA_ROPE,
             Q_LORA + KV_LORA + MLA_ROPE + GDN_CONV_CH,
             Q_LORA + KV_LORA + MLA_ROPE + GDN_CONV_CH + GDN_OUT,
             Q_LORA + KV_LORA + MLA_ROPE + GDN_CONV_CH + GDN_OUT + 2 * GDN_HEADS)
IN_COLS = Q_LORA + KV_LORA + MLA_ROPE + GDN_CONV_CH + GDN_OUT + 4 * GDN_HEADS

kernel_name = "hybrid_mla_gdn_macaron_encoder"


def rmsnorm(x, g):
    xf = x.astype(jnp.float32)
    y = xf * lax.rsqrt(jnp.mean(xf * xf, axis=-1, keepdims=True) + EPS)
    return (y * g.astype(jnp.float32)).astype(x.dtype)


def l2norm(t):
    return t * lax.rsqrt(jnp.sum(t * t, axis=-1, keepdims=True) + EPS)


def swiglu(x, w_gate, w_up, w_down):
    return (jax.nn.silu(x @ w_gate) * (x @ w_up)) @ w_down


def rope_tables(L):
    inv = ROPE_THETA ** (-jnp.arange(0, MLA_ROPE, 2, dtype=jnp.float32) / MLA_ROPE)
    ang = jnp.arange(L, dtype=jnp.float32)[:, None] * inv[None, :]
    return jnp.cos(ang), jnp.sin(ang)


def apply_rope(x, cos, sin):
    xf = x.astype(jnp.float32)
    x1, x2 = jnp.split(xf, 2, axis=-1)
    c = cos[None, :, None, :]
    s = sin[None, :, None, :]
    return jnp.concatenate([x1 * c - x2 * s, x1 * s + x2 * c], axis=-1).astype(x.dtype)


def mla_mixer(c_q, c_kv, k_rope, q_norm_g, w_uq, kv_norm_g, w_ukv, out_norm_g):
    B, L, _ = c_q.shape
    q = (rmsnorm(c_q, q_norm_g) @ w_uq).reshape(B, L, MLA_HEADS, MLA_QK)
    q_nope, q_pe = jnp.split(q, [MLA_NOPE], axis=-1)
    kv = (rmsnorm(c_kv, kv_norm_g) @ w_ukv).reshape(B, L, MLA_HEADS, MLA_NOPE + MLA_V)
    k_nope, v = jnp.split(kv, [MLA_NOPE], axis=-1)
    cos, sin = rope_tables(L)
    q_pe = apply_rope(q_pe, cos, sin)
    k_pe = apply_rope(k_rope[:, :, None, :], cos, sin)
    q = jnp.concatenate([q_nope, q_pe], axis=-1)
    k = jnp.concatenate([k_nope, jnp.broadcast_to(k_pe, (B, L, MLA_HEADS, MLA_ROPE))], axis=-1)
    scale = MLA_QK ** -0.5
    nb = L // Q_BLOCK
    qb = q.reshape(B, nb, Q_BLOCK, MLA_HEADS, MLA_QK).swapaxes(0, 1)

    def attend(qi):
        s = jnp.einsum('bqhd,bkhd->bhqk', qi, k, preferred_element_type=jnp.float32) * scale
        p = jax.nn.softmax(s, axis=-1).astype(v.dtype)
        return jnp.einsum('bhqk,bkhd->bqhd', p, v)

    o = lax.map(attend, qb)
    o = o.swapaxes(0, 1).reshape(B, L, MLA_OUT)
    return rmsnorm(o, out_norm_g)


def gdn_chunked(q, k, v, g, beta):
    B, L, H, dk = q.shape
    dv = v.shape[-1]
    N = L // CHUNK

    def blocks(t):
        return t.reshape(B, N, CHUNK, H, t.shape[-1]).transpose(1, 0, 3, 2, 4)

    q, k, v = blocks(q), blocks(k), blocks(v)
    g = g.reshape(B, N, CHUNK, H).transpose(1, 0, 3, 2)
    beta = beta.reshape(B, N, CHUNK, H).transpose(1, 0, 3, 2)
    gc = jnp.cumsum(g, axis=-1)
    idx = jnp.arange(CHUNK)
    incl = idx[:, None] >= idx[None, :]
    strict = idx[:, None] > idx[None, :]
    diff = gc[..., :, None] - gc[..., None, :]
    decay = jnp.where(incl, jnp.exp(jnp.where(incl, diff, 0.0)), 0.0)
    kb = k * beta[..., None]
    A = jnp.where(strict, jnp.einsum('nbhid,nbhjd->nbhij', kb, k) * decay, 0.0)
    eye = jnp.eye(CHUNK, dtype=jnp.float32)
    T = lax.linalg.triangular_solve(eye + A, jnp.broadcast_to(eye, A.shape),
                                    left_side=True, lower=True, unit_diagonal=True)
    w = jnp.einsum('nbhij,nbhjd->nbhid', T, kb * jnp.exp(gc)[..., None])
    u = jnp.einsum('nbhij,nbhjd->nbhid', T, v * beta[..., None])
    qk = jnp.where(incl, jnp.einsum('nbhid,nbhjd->nbhij', q, k) * decay, 0.0)
    g_last = gc[..., -1]
    q_g = q * jnp.exp(gc)[..., None]
    k_g = k * jnp.exp(g_last[..., None] - gc)[..., None]
    d_last = jnp.exp(g_last)

    def step(S, xs):
        u_n, w_n, q_n, qk_n, k_n, dl = xs
        v_new = u_n - jnp.einsum('bhcd,bhde->bhce', w_n, S)
        o = jnp.einsum('bhcd,bhde->bhce', q_n, S) + jnp.einsum('bhij,bhje->bhie', qk_n, v_new)
        S = S * dl[..., None, None] + jnp.einsum('bhcd,bhce->bhde', k_n, v_new)
        return S, o

    S0 = jnp.zeros((B, H, dk, dv), jnp.float32)
    _, o = lax.scan(step, S0, (u, w, q_g, qk, k_g, d_last))
    return o.transpose(1, 0, 3, 2, 4).reshape(B, L, H, dv)


def gdn_mixer(qkv, z, a, b, conv_w, a_log, dt_bias, out_norm_g):
    B, L, _ = qkv.shape
    dtype = qkv.dtype
    qkv = lax.conv_general_dilated(qkv, conv_w[:, None, :].astype(dtype), window_strides=(1,),
                                   padding=[(CONV_K // 2, CONV_K // 2)],
                                   dimension_numbers=('NWC', 'WIO', 'NWC'),
                                   feature_group_count=GDN_CONV_CH)
    qkv = jax.nn.silu(qkv)
    q, k, v = jnp.split(qkv, [GDN_QK, 2 * GDN_QK], axis=-1)
    q = l2norm(q.reshape(B, L, GDN_HEADS, GDN_DK).astype(jnp.float32)) * (GDN_DK ** -0.5)
    k = l2norm(k.reshape(B, L, GDN_HEADS, GDN_DK).astype(jnp.float32))
    v = v.reshape(B, L, GDN_HEADS, GDN_DV).astype(jnp.float32)
    a = a.astype(jnp.float32).reshape(B, L, 2, GDN_HEADS)
    b = b.astype(jnp.float32).reshape(B, L, 2, GDN_HEADS)
    g = -jnp.exp(a_log.astype(jnp.float32)) * jax.nn.softplus(a + dt_bias.astype(jnp.float32))
    beta = jax.nn.sigmoid(b)
    o_f = gdn_chunked(q, k, v, g[:, :, 0], beta[:, :, 0])
    flip = lambda t: jnp.flip(t, axis=1)
    o_b = flip(gdn_chunked(flip(q), flip(k), flip(v), flip(g[:, :, 1]), flip(beta[:, :, 1])))
    o = rmsnorm(o_f + o_b, out_norm_g) * jax.nn.silu(z.reshape(B, L, GDN_HEADS, GDN_DV).astype(jnp.float32))
    return o.reshape(B, L, GDN_OUT).astype(dtype)


def encoder_layer(h, ffn1_pre_g, ffn1_w_gate, ffn1_w_up, ffn1_w_down, ffn1_post_g,
                  mix_pre_g, w_in, mla_q_norm_g, mla_w_uq, mla_kv_norm_g, mla_w_ukv, mla_out_norm_g,
                  gdn_conv_w, gdn_a_log, gdn_dt_bias, gdn_out_norm_g, w_out, mix_post_g,
                  ffn2_pre_g, ffn2_w_gate, ffn2_w_up, ffn2_w_down, ffn2_post_g, final_norm_g):
    h = h + 0.5 * rmsnorm(swiglu(rmsnorm(h, ffn1_pre_g), ffn1_w_gate, ffn1_w_up, ffn1_w_down), ffn1_post_g)
    proj = rmsnorm(h, mix_pre_g) @ w_in
    c_q, c_kv, k_rope, qkv, z, a, b = jnp.split(proj, IN_SPLITS, axis=-1)
    y_a = mla_mixer(c_q, c_kv, k_rope, mla_q_norm_g, mla_w_uq, mla_kv_norm_g, mla_w_ukv, mla_out_norm_g)
    y_b = gdn_mixer(qkv, z, a, b, gdn_conv_w, gdn_a_log, gdn_dt_bias, gdn_out_norm_g)
    mix = jnp.concatenate([y_a, y_b], axis=-1) @ w_out
    h = h + rmsnorm(mix, mix_post_g)
    h = h + 0.5 * rmsnorm(swiglu(rmsnorm(h, ffn2_pre_g), ffn2_w_gate, ffn2_w_up, ffn2_w_down), ffn2_post_g)
    return rmsnorm(h, final_norm_g)


def setup_inputs(seed: int = 0) -> dict:
    key = jax.random.key(seed)
    ks = jax.random.split(key, 32)
    f32 = jnp.float32

    def normal(k, shape, scale):
        return jax.random.normal(k, shape, f32) * scale

    def gain(k, n):
        return 1.0 + 0.1 * jax.random.normal(k, (DEPTH, n), f32)

    dt = jnp.exp(jax.random.uniform(ks[20], (DEPTH, 2, GDN_HEADS), f32, math.log(1e-3), math.log(1e-1)))
    return {
        "x_prompt": normal(ks[0], (BATCH, SEQ, D_MODEL), 1.0),
        "x_sample": normal(ks[1], (DEC_BATCH, DEC_SEQ, D_MODEL), 1.0),
        "ffn1_pre_g": gain(ks[2], D_MODEL),
        "ffn1_w_gate": normal(ks[3], (DEPTH, D_MODEL, D_FF), D_MODEL ** -0.5),
        "ffn1_w_up": normal(ks[4], (DEPTH, D_MODEL, D_FF), D_MODEL ** -0.5),
        "ffn1_w_down": normal(ks[5], (DEPTH, D_FF, D_MODEL), D_FF ** -0.5),
        "ffn1_post_g": gain(ks[6], D_MODEL),
        "mix_pre_g": gain(ks[7], D_MODEL),
        "w_in": normal(ks[8], (DEPTH, D_MODEL, IN_COLS), D_MODEL ** -0.5),
        "mla_q_norm_g": gain(ks[9], Q_LORA),
        "mla_w_uq": normal(ks[10], (DEPTH, Q_LORA, MLA_HEADS * MLA_QK), Q_LORA ** -0.5),
        "mla_kv_norm_g": gain(ks[11], KV_LORA),
        "mla_w_ukv": normal(ks[12], (DEPTH, KV_LORA, MLA_HEADS * (MLA_NOPE + MLA_V)), KV_LORA ** -0.5),
        "mla_out_norm_g": gain(ks[13], MLA_OUT),
        "gdn_conv_w": normal(ks[14], (DEPTH, CONV_K, GDN_CONV_CH), CONV_K ** -0.5),
        "gdn_a_log": jnp.log(jax.random.uniform(ks[15], (DEPTH, 2, GDN_HEADS), f32, 1.0, 16.0)),
        "gdn_dt_bias": dt + jnp.log(-jnp.expm1(-dt)),
        "gdn_out_norm_g": gain(ks[16], GDN_DV),
        "w_out": normal(ks[17], (DEPTH, MIX_WIDTH, D_MODEL), MIX_WIDTH ** -0.5),
        "mix_post_g": gain(ks[18], D_MODEL),
        "ffn2_pre_g": gain(ks[19], D_MODEL),
        "ffn2_w_gate": normal(ks[21], (DEPTH, D_MODEL, D_FF), D_MODEL ** -0.5),
        "ffn2_w_up": normal(ks[22], (DEPTH, D_MODEL, D_FF), D_MODEL ** -0.5),
        "ffn2_w_down": normal(ks[23], (DEPTH, D_FF, D_MODEL), D_FF ** -0.5),
        "ffn2_post_g": gain(ks[24], D_MODEL),
        "final_norm_g": gain(ks[25], D_MODEL),
    }


def reference(x_prompt, x_sample, ffn1_pre_g, ffn1_w_gate, ffn1_w_up, ffn1_w_down, ffn1_post_g,
              mix_pre_g, w_in, mla_q_norm_g, mla_w_uq, mla_kv_norm_g, mla_w_ukv, mla_out_norm_g,
              gdn_conv_w, gdn_a_log, gdn_dt_bias, gdn_out_norm_g, w_out, mix_post_g,
              ffn2_pre_g, ffn2_w_gate, ffn2_w_up, ffn2_w_down, ffn2_post_g, final_norm_g):
    weights = (ffn1_pre_g, ffn1_w_gate, ffn1_w_up, ffn1_w_down, ffn1_post_g,
               mix_pre_g, w_in, mla_q_norm_g, mla_w_uq, mla_kv_norm_g, mla_w_ukv, mla_out_norm_g,
               gdn_conv_w, gdn_a_log, gdn_dt_bias, gdn_out_norm_g, w_out, mix_post_g,
               ffn2_pre_g, ffn2_w_gate, ffn2_w_up, ffn2_w_down, ffn2_post_g, final_norm_g)

    def trunk(x):
        h = x
        for l in range(DEPTH):
            h = encoder_layer(h, *[w[l] for w in weights])
        return h

    y_prompt = trunk(x_prompt)
    y_sample = trunk(x_sample)
    return (y_prompt, y_sample)
```

```python
from contextlib import ExitStack
import numpy as np
import concourse.bass as bass
import concourse.mybir as mybir
from concourse.bass_utils import run_bass_kernel_spmd

F32 = mybir.dt.float32
BF16 = mybir.dt.bfloat16
ALU = mybir.AluOpType
AF = mybir.ActivationFunctionType
AX = mybir.AxisListType

D = 1024
DFF = 2816
NF = DFF // 128
INC = 2736
EPS = 1e-6
NEG = -30000.0
ENG = ("pe", "act", "dve", "pool", "sp")


class Buf:
    __slots__ = ("name", "ap", "lw", "rd", "sem", "semcnt")

    def __init__(self, name, ap):
        self.name = name
        self.ap = ap
        self.lw = None
        self.rd = {}
        self.sem = None
        self.semcnt = 0

    def __getitem__(self, k):
        return self.ap[k]


class Prog:
    def __init__(self, nc, stack):
        self.nc = nc
        self.stack = stack
        self.q = {e: [] for e in ENG}
        self.cnt = {e: 0 for e in ENG}
        self.known = {e: {} for e in ENG}
        self.hist = {}
        self.esem = {e: stack.enter_context(nc.semaphore("s_" + e)) for e in ENG}
        self.semobj = {e: self.esem[e] for e in ENG}
        self.dmabufs = []
        self.nwaits = 0
        self.nins = 0
        self.E = {"pe": nc.tensor, "act": nc.scalar, "dve": nc.vector, "pool": nc.gpsimd, "sp": nc.sync}

    def _need(self, deps, tok):
        if tok is None:
            return
        k, v = tok
        if deps.get(k, 0) < v:
            deps[k] = v

    def _collect(self, reads, writes):
        deps = {}
        for b in reads:
            self._need(deps, b.lw)
        for b in writes:
            self._need(deps, b.lw)
            for k, v in b.rd.items():
                self._need(deps, (k, v))
        return deps

    def _emit_waits(self, eng, deps):
        kn = self.known[eng]
        new = None
        for k, v in deps.items():
            if k == eng:
                if eng == "pe" or eng == "sp":
                    continue
                if self.cnt[eng] - v > 1:
                    continue
            cur = new if new is not None else kn
            if cur.get(k, 0) >= v:
                continue
            sem = self.semobj[k]
            self.E[eng].wait_ge(sem, v)
            self.nwaits += 1
            if new is None:
                new = dict(kn)
            new[k] = v
            h = self.hist.get((k, v))
            if h:
                for k2, v2 in h.items():
                    if k2 != eng and new.get(k2, 0) < v2:
                        new[k2] = v2
        if new is not None:
            self.known[eng] = new

    def op(self, eng, fn, reads=(), writes=()):
        deps = self._collect(reads, writes)
        self._emit_waits(eng, deps)
        sem = self.esem[eng]
        self.cnt[eng] += 1
        v = self.cnt[eng]
        fn(self.E[eng]).then_inc(sem, 1)
        self.nins += 1
        tok = (eng, v)
        self.hist[tok] = self.known[eng]
        for b in writes:
            b.lw = tok
            b.rd = {}
        for b in reads:
            if b.rd.get(eng, 0) < v:
                b.rd[eng] = v
        return tok

    def dma(self, out_ap, in_ap, reads=(), writes=(), sb=None, queue="sp", **kw):
        deps = self._collect(reads, writes)
        self._emit_waits(queue, deps)
        if sb.sem is None:
            sb.sem = self.stack.enter_context(self.nc.semaphore("d%d_%s" % (len(self.dmabufs), sb.name)))
            self.semobj[id(sb)] = sb.sem
            self.dmabufs.append(sb)
        sb.semcnt += 16
        v = sb.semcnt
        sem = sb.sem
        self.E[queue].dma_start(out=out_ap, in_=in_ap, **kw).then_inc(sem, 16)
        self.nins += 1
        tok = (id(sb), v)
        self.hist[tok] = self.known[queue]
        for b in writes:
            b.lw = tok
            b.rd = {}
        for b in reads:
            if b.rd.get(tok[0], 0) < v:
                b.rd[tok[0]] = v
        return tok

    def barrier(self):
        deps = {}
        for e in ("pe", "act", "dve", "pool"):
            if self.cnt[e]:
                deps[e] = self.cnt[e]
        for b in self.dmabufs:
            deps[id(b)] = b.semcnt
        for e in ENG:
            d = {k: v for k, v in deps.items() if k != e}
            self._emit_waits(e, d)

    def emit(self):
        pass


WNAMES = ["ffn1_pre_g", "ffn1_w_gate", "ffn1_w_up", "ffn1_w_down", "ffn1_post_g", "mix_pre_g", "w_in",
          "mla_q_norm_g", "mla_w_uq", "mla_kv_norm_g", "mla_w_ukv", "mla_out_norm_g", "gdn_conv_w",
          "gdn_a_log", "gdn_dt_bias", "gdn_out_norm_g", "w_out", "mix_post_g", "ffn2_pre_g",
          "ffn2_w_gate", "ffn2_w_up", "ffn2_w_down", "ffn2_post_g", "final_norm_g"]
WSHAPES = {"ffn1_pre_g": [1, D], "ffn1_w_gate": [D, DFF], "ffn1_w_up": [D, DFF], "ffn1_w_down": [DFF, D],
           "ffn1_post_g": [1, D], "mix_pre_g": [1, D], "w_in": [D, INC], "mla_q_norm_g": [1, 384],
           "mla_w_uq": [384, 768], "mla_kv_norm_g": [1, 256], "mla_w_ukv": [256, 1024],
           "mla_out_norm_g": [1, 512], "gdn_conv_w": [5, 1536], "gdn_a_log": [1, 8], "gdn_dt_bias": [1, 8],
           "gdn_out_norm_g": [1, 128], "w_out": [D, D], "mix_post_g": [1, D], "ffn2_pre_g": [1, D],
           "ffn2_w_gate": [D, DFF], "ffn2_w_up": [D, DFF], "ffn2_w_down": [DFF, D], "ffn2_post_g": [1, D],
           "final_norm_g": [1, D]}


def build(Lp, Ls, debug=False, phases="S A M B G H C"):
    phases = phases.split()
    nc = bass.Bass("TRN2", target_bir_lowering=False)
    Lmax = max(Lp, Ls)
    SEQ = (("p", Lp), ("s", Ls))

    def din(name, shape, dt=F32):
        return nc.dram_tensor(name, list(shape), dt, kind="ExternalInput").ap()

    def dscr(name, shape, dt=F32):
        return nc.dram_tensor(name, list(shape), dt, kind="ExternalOutput" if debug else "Internal").ap()

    X = {"p": din("x_p", [Lp, D]), "s": din("x_s", [Ls, D])}
    W = {n: din(n, WSHAPES[n]) for n in WNAMES}
    ROPE = din("rope_tab", [Lmax, 64])
    GMASK = din("gdn_masks", [64, 5, 8, 64])
    Y = {"p": nc.dram_tensor("y_p", [Lp, D], F32, kind="ExternalOutput").ap(),
         "s": nc.dram_tensor("y_s", [Ls, D], F32, kind="ExternalOutput").ap()}
    WB = {n: dscr("bf_" + n, WSHAPES[n], BF16) for n in
          ["ffn1_w_gate", "ffn1_w_up", "ffn1_w_down", "w_in", "ffn2_w_gate", "ffn2_w_up", "ffn2_w_down"]}
    H1 = {s: dscr("h1_" + s, [L, D]) for s, L in SEQ}
    LAT = {s: dscr("lat_" + s, [L, 672]) for s, L in SEQ}
    QKVT = {s: dscr("qkvT_" + s, [1536, L]) for s, L in SEQ}
    ZS = {s: dscr("z_" + s, [L, 512]) for s, L in SEQ}
    GBL = {s: dscr("gbl_" + s, [2, L, 12]) for s, L in SEQ}
    QT = {s: dscr("QT_" + s, [8, 96, L], BF16) for s, L in SEQ}
    KT = {s: dscr("KT_" + s, [8, 96, L], BF16) for s, L in SEQ}
    VV = {s: dscr("V_" + s, [8, 128, L // 128, 64], BF16) for s, L in SEQ}
    YA = {s: dscr("ya_" + s, [L, 512]) for s, L in SEQ}
    TOK = {s: dscr("tok_" + s, [L, 1536], BF16) for s, L in SEQ}
    OF = {s: dscr("of_" + s, [2, L, 512]) for s, L in SEQ}

    with ExitStack() as st:
        P = Prog(nc, st)
        ARENA = 52800
        arena = st.enter_context(nc.sbuf_tensor("arena", [128, ARENA], F32))
        psum = st.enter_context(nc.psum_tensor("psum", [128, 4096], F32))
        psum_bf = psum.bitcast(BF16)
        BK = [Buf("bank%d" % i, psum[:, i * 512:(i + 1) * 512]) for i in range(8)]

        def pf(i, a=0, b=512):
            return psum[:, i * 512 + a:i * 512 + b]

        def pb(i, a=0, b=1024):
            return psum_bf[:, i * 1024 + a:i * 1024 + b]

        off = [0]
        perm_end = [0]

        def alloc(name, ncols, dt=F32, parts=128):
            n32 = ncols if dt == F32 else (ncols + 1) // 2
            assert off[0] + n32 <= ARENA, ("SBUF arena overflow", name, off[0] + n32)
            a = arena[0:parts, off[0]:off[0] + n32]
            off[0] += n32
            if dt != F32:
                a = a.bitcast(dt)[:, 0:ncols]
            return Buf(name, a)

        def phase_reset():
            P.barrier()
            off[0] = perm_end[0]

        op = P.op
        dma = P.dma

        identf = alloc("identf", 128)
        identb = alloc("identb", 128, BF16)
        onesf = alloc("onesf", 128)
        gT = {n: alloc("gT_" + n, 8) for n in ("ffn1_pre_g", "mix_pre_g", "ffn2_pre_g")}
        gB = {n: alloc("gB_" + n, D) for n in ("ffn1_post_g", "mix_post_g", "ffn2_post_g", "final_norm_g")}
        small = alloc("small", 64)
        perm_end[0] = off[0]

        op("pool", lambda E: E.memset(identf[:], 0.0), writes=[identf])
        op("pool", lambda E: E.affine_select(out=identf[:], in_=identf[:], pattern=[[-1, 128]], compare_op=ALU.not_equal,
                                             fill=1.0, base=0, channel_multiplier=1), reads=[identf], writes=[identf])
        op("dve", lambda E: E.tensor_copy(out=identb[:], in_=identf[:]), reads=[identf], writes=[identb])
        op("pool", lambda E: E.memset(onesf[:], 1.0), writes=[onesf])
        for n, b in gT.items():
            dma(b[:], W[n].rearrange("o (c p) -> p (o c)", p=128), writes=[b], sb=b, allow_slow_non_contiguous=True)
        for n, b in gB.items():
            dma(b[:], W[n].broadcast_to([128, D]), writes=[b], sb=b)
        for n in ("ffn1_post_g", "ffn2_post_g"):
            b = gB[n]
            op("pool", lambda E, b=b: E.tensor_scalar(out=b[:], in0=b[:], scalar1=0.5, scalar2=None, op0=ALU.mult),
               reads=[b], writes=[b])
        sm_q = Buf("sm_q", small.ap)
        dma(small[:, 0:3], W["mla_q_norm_g"].rearrange("o (c p) -> p (o c)", p=128), writes=[small], sb=small,
            allow_slow_non_contiguous=True)
        dma(small[:, 3:5], W["mla_kv_norm_g"].rearrange("o (c p) -> p (o c)", p=128), reads=[small], sb=small,
            allow_slow_non_contiguous=True)
        dma(small[:, 5:9], W["mla_out_norm_g"].rearrange("o (c p) -> p (o c)", p=128), reads=[small], sb=small,
            allow_slow_non_contiguous=True)
        dma(small[:, 9:10], W["gdn_out_norm_g"].rearrange("o (c p) -> p (o c)", p=128), reads=[small], sb=small,
            allow_slow_non_contiguous=True)
        dma(small[:, 16:24], W["gdn_dt_bias"].broadcast_to([128, 8]), reads=[small], sb=small)
        dma(small[:, 24:32], W["gdn_a_log"].broadcast_to([128, 8]), reads=[small], sb=small)
        small.lw = (id(small), small.semcnt)
        op("act", lambda E: E.activation(out=small[:, 24:32], in_=small[:, 24:32], func=AF.Exp), reads=[small], writes=[small])
        op("dve", lambda E: E.tensor_scalar(out=small[:, 24:32], in0=small[:, 24:32], scalar1=-1.0, scalar2=None, op0=ALU.mult),
           reads=[small], writes=[small])

        def rstd_chain(ss, k, n, eps=EPS, post=None):
            op("dve", lambda E: E.tensor_scalar(out=ss[:, 0:k], in0=ss[:, 0:k], scalar1=1.0 / n, scalar2=eps, op0=ALU.mult, op1=ALU.add),
               reads=[ss], writes=[ss])
            op("act", lambda E: E.activation(out=ss[:, 0:k], in_=ss[:, 0:k], func=AF.Ln), reads=[ss], writes=[ss])
            op("act", lambda E: E.activation(out=ss[:, 0:k], in_=ss[:, 0:k], func=AF.Exp, scale=-0.5), reads=[ss], writes=[ss])

        cast_rr = [0]

        def cast(out_ap, in_ap, reads, writes):
            e = ("dve", "pool", "act")[cast_rr[0] % 3]
            cast_rr[0] += 1
            if e == "act":
                op("act", lambda E: E.activation(out=out_ap, in_=in_ap, func=AF.Copy), reads=reads, writes=writes)
            else:
                op(e, lambda E: E.tensor_copy(out=out_ap, in_=in_ap), reads=reads, writes=writes)

        if "S" in phases:
            stf = [alloc("stf%d" % i, DFF) for i in range(2)]
            stb = [alloc("stb%d" % i, DFF, BF16) for i in range(2)]
            it = 0
            for n in WB:
                K_, N_ = WSHAPES[n]
                for kc in range(K_ // 128):
                    f, b = stf[it % 2], stb[it % 2]
                    dma(f[:, 0:N_], W[n][kc * 128:(kc + 1) * 128, :], writes=[f], sb=f)
                    cast(b[:, 0:N_], f[:, 0:N_], [f], [b])
                    dma(WB[n][kc * 128:(kc + 1) * 128, :], b[:, 0:N_], reads=[b], sb=b, queue="pool")
                    it += 1
        phase_reset()

        def norm_T(srcs, src_bufs, gTb, xT, ss, xnb, junk):
            for j in range(4):
                op("act", lambda E, j=j: E.activation(out=junk[:], in_=srcs[j], func=AF.Square, accum_out=ss[:, j:j + 1]),
                   reads=[src_bufs[j]], writes=[ss])
            rstd_chain(ss, 4, D)
            for j in range(4):
                xb = xnb[j % 2]
                if j % 2 == 0:
                    op("dve", lambda E, j=j, xb=xb: E.tensor_scalar(out=xb[:], in0=srcs[j], scalar1=ss[:, j:j + 1], scalar2=None, op0=ALU.mult),
                       reads=[src_bufs[j], ss], writes=[xb])
                else:
                    op("act", lambda E, j=j, xb=xb: E.activation(out=xb[:], in_=srcs[j], func=AF.Copy, scale=ss[:, j:j + 1]),
                       reads=[src_bufs[j], ss], writes=[xb])
                bk = j % 2
                for dc in range(8):
                    op("pe", lambda E, dc=dc, bk=bk, xb=xb: E.transpose(out=pb(bk, dc * 128, (dc + 1) * 128), in_=xb[:, dc * 128:(dc + 1) * 128], identity=identb[:]),
                       reads=[xb, identb], writes=[BK[bk]])
                op("dve", lambda E, j=j, bk=bk: E.tensor_tensor(
                    out=xT.ap[:, :, j * 128:(j + 1) * 128], in0=pb(bk).rearrange("p (c t) -> p c t", c=8),
                    in1=gTb[:, 0:8].unsqueeze(2).to_broadcast([128, 8, 128]), op=ALU.mult),
                    reads=[gTb], writes=[xT, BK[bk]])

        SLABS = [(0, 512), (512, 512), (1024, 512), (1536, 512), (2048, 512), (2560, 256)]

        def ffn(hbufs, hsrcs, pre_g, wg, wu, wd_res, post_bc, T):
            norm_T(hsrcs, hbufs, gT[pre_g], T["xT"], T["ss"], T["xnb"], T["junk"])
            xT, act = T["xT"], T["act"]
            for si, (c0, w) in enumerate(SLABS):
                sg = T["slab"][T["slabi"] % 3]
                su = T["slab"][(T["slabi"] + 1) % 3]
                T["slabi"] += 2
                dma(sg.ap[:, :, 0:w], wg[:, c0:c0 + w].rearrange("(dc p) f -> p dc f", p=128), writes=[sg], sb=sg)
                dma(su.ap[:, :, 0:w], wu[:, c0:c0 + w].rearrange("(dc p) f -> p dc f", p=128), writes=[su], sb=su)
                for fl in range(w // 128):
                    f = c0 // 128 + fl
                    g_, u_ = 2 + f % 2, 4 + f % 2
                    for dc in range(8):
                        op("pe", lambda E, dc=dc, fl=fl, g_=g_, sg=sg: E.matmul(pf(g_), lhsT=sg.ap[:, dc, fl * 128:(fl + 1) * 128], rhs=xT.ap[:, dc, :],
                                                                                  start=(dc == 0), stop=(dc == 7)), reads=[sg, xT], writes=[BK[g_]])
                    for dc in range(8):
                        op("pe", lambda E, dc=dc, fl=fl, u_=u_, su=su: E.matmul(pf(u_), lhsT=su.ap[:, dc, fl * 128:(fl + 1) * 128], rhs=xT.ap[:, dc, :],
                                                                                  start=(dc == 0), stop=(dc == 7)), reads=[su, xT], writes=[BK[u_]])
                    et = T["etmp"][f % 2]
                    op("act", lambda E, et=et, g_=g_: E.activation(out=et[:], in_=pf(g_), func=AF.Exp, scale=-1.0), writes=[et, BK[g_]])
                    op("act", lambda E, et=et: E.activation(out=et[:], in_=et[:], func=AF.Ln, bias=1.0), reads=[et], writes=[et])
                    op("act", lambda E, et=et: E.activation(out=et[:], in_=et[:], func=AF.Exp, scale=-1.0), reads=[et], writes=[et])
                    op("dve", lambda E, et=et, g_=g_: E.tensor_tensor(out=et[:], in0=et[:], in1=pf(g_), op=ALU.mult), reads=[et], writes=[et, BK[g_]])
                    op("dve", lambda E, et=et, u_=u_, f=f: E.tensor_tensor(out=act.ap[:, f, :], in0=et[:], in1=pf(u_), op=ALU.mult),
                       reads=[et], writes=[T["actc"][f], BK[u_]])
            ss2 = T["ss2"]
            for j in range(4):
                b0 = 2 if j % 2 == 0 else 4
                for hf in range(2):
                    for f in range(NF):
                        op("pe", lambda E, f=f, hf=hf, b0=b0, j=j: E.matmul(pf(b0 + hf), lhsT=act.ap[:, f, j * 128:(j + 1) * 128],
                                                                             rhs=wd_res.ap[:, f, hf * 512:(hf + 1) * 512], start=(f == 0), stop=(f == NF - 1)),
                           reads=[T["actc"][f], wd_res], writes=[BK[b0 + hf]])
                yps = psum[:, b0 * 512:(b0 + 2) * 512]
                op("act", lambda E, j=j, yps=yps: E.activation(out=T["junk"][:], in_=yps, func=AF.Square, accum_out=ss2[:, j:j + 1]),
                   writes=[ss2, BK[b0], BK[b0 + 1]])
                op("dve", lambda E, j=j: E.tensor_scalar(out=ss2[:, 4 + j:5 + j], in0=ss2[:, j:j + 1], scalar1=1.0 / D, scalar2=EPS, op0=ALU.mult, op1=ALU.add),
                   reads=[ss2], writes=[ss2])
                op("act", lambda E, j=j: E.activation(out=ss2[:, 4 + j:5 + j], in_=ss2[:, 4 + j:5 + j], func=AF.Ln), reads=[ss2], writes=[ss2])
                op("act", lambda E, j=j: E.activation(out=ss2[:, 4 + j:5 + j], in_=ss2[:, 4 + j:5 + j], func=AF.Exp, scale=-0.5), reads=[ss2], writes=[ss2])
                tmp = T["tmp"][j % 2]
                op("dve", lambda E, j=j, yps=yps, tmp=tmp: E.scalar_tensor_tensor(out=tmp[:], in0=yps, scalar=ss2[:, 4 + j:5 + j], in1=post_bc[:],
                                                                                    op0=ALU.mult, op1=ALU.mult),
                   reads=[ss2, post_bc], writes=[tmp, BK[b0], BK[b0 + 1]])
                op("pool", lambda E, j=j, tmp=tmp: E.tensor_tensor(out=hsrcs[j], in0=hsrcs[j], in1=tmp[:], op=ALU.add),
                   reads=[tmp], writes=[hbufs[j]])

        def ffn_bufs():
            T = {}
            T["wd"] = alloc("wd_res", NF * D, BF16)
            T["wd"].ap = T["wd"].ap.rearrange("p (f d) -> p f d", f=NF)
            T["xT"] = alloc("xT", 8 * 512, BF16)
            T["xT"].ap = T["xT"].ap.rearrange("p (c t) -> p c t", c=8)
            T["act"] = alloc("act", NF * 512, BF16)
            T["act"].ap = T["act"].ap.rearrange("p (f t) -> p f t", f=NF)
            T["actc"] = [Buf("act%d" % f, T["act"].ap[:, f, :]) for f in range(NF)]
            T["slab"] = []
            for i in range(3):
                b = alloc("slab%d" % i, 8 * 512, BF16)
                b.ap = b.ap.rearrange("p (c f) -> p c f", c=8)
                T["slab"].append(b)
            T["slabi"] = 0
            T["etmp"] = [alloc("etmp%d" % i, 512) for i in range(2)]
            T["tmp"] = [alloc("tmp%d" % i, D) for i in range(2)]
            T["xnb"] = [alloc("xnb%d" % i, D, BF16) for i in range(2)]
            T["junk"] = alloc("junk", D, BF16)
            T["ss"] = alloc("ss", 8)
            T["ss2"] = alloc("ss2", 8)
            T["h"] = alloc("h", 4 * D)
            T["h"].ap = T["h"].ap.rearrange("p (j d) -> p j d", j=4)
            T["hb"] = [Buf("h%d" % j, T["h"].ap[:, j, :]) for j in range(4)]
            T["hs"] = [T["h"].ap[:, j, :] for j in range(4)]
            return T

        def load_wd(T, name):
            wd = T["wd"]
            for f0 in range(0, NF, 2):
                dma(wd.ap[:, f0:f0 + 2, :], WB[name][f0 * 128:(f0 + 2) * 128, :].rearrange("(f p) d -> p f d", p=128),
                    writes=[wd] if f0 == 0 else [], reads=[] if f0 == 0 else [], sb=wd)
            wd.lw = (id(wd), wd.semcnt)

        if "A" in phases:
            T = ffn_bufs()
            load_wd(T, "ffn1_w_down")
            pst = [alloc("pst%d" % i, 1200) for i in range(2)]
            qst = [alloc("qst%d" % i, 512) for i in range(3)]
            gst = [alloc("gst%d" % i, 64) for i in range(2)]
            pi = 0
            qi = 0
            for s, L in SEQ:
                for ti in range(L // 512):
                    r0 = ti * 512
                    h = T["h"]
                    dma(h.ap, X[s][r0:r0 + 512, :].rearrange("(j p) d -> p j d", p=128), writes=T["hb"], sb=h)
                    for hb in T["hb"]:
                        hb.lw = (id(h), h.semcnt)
                    ffn(T["hb"], T["hs"], "ffn1_pre_g", WB["ffn1_w_gate"], WB["ffn1_w_up"], T["wd"], gB["ffn1_post_g"], T)
                    dma(H1[s][r0:r0 + 512, :].rearrange("(j p) d -> p j d", p=128), h.ap, reads=T["hb"], sb=h, queue="pool")
                    norm_T(T["hs"], T["hb"], gT["mix_pre_g"], T["xT"], T["ss"], T["xnb"], T["junk"])
                    xT = T["xT"]
                    for q3 in range(3):
                        sl = T["slab"][T["slabi"] % 3]
                        T["slabi"] += 1
                        c0 = 672 + q3 * 512
                        dma(sl.ap[:, :, 0:512], WB["w_in"][:, c0:c0 + 512].rearrange("(dc p) f -> p dc f", p=128), writes=[sl], sb=sl)
                        for fl in range(4):
                            ch = q3 * 4 + fl
                            bk = 2 + ch % 2
                            for dc in range(8):
                                op("pe", lambda E, dc=dc, fl=fl, bk=bk, sl=sl: E.matmul(pf(bk), lhsT=sl.ap[:, dc, fl * 128:(fl + 1) * 128], rhs=xT.ap[:, dc, :],
                                                                                          start=(dc == 0), stop=(dc == 7)), reads=[sl, xT], writes=[BK[bk]])
                            qs = qst[qi % 3]
                            qi += 1
                            if ch % 2 == 0:
                                op("act", lambda E, qs=qs, bk=bk: E.activation(out=qs[:], in_=pf(bk), func=AF.Copy), writes=[qs, BK[bk]])
                            else:
                                op("dve", lambda E, qs=qs, bk=bk: E.tensor_copy(out=qs[:], in_=pf(bk)), writes=[qs, BK[bk]])
                            dma(QKVT[s][ch * 128:(ch + 1) * 128, r0:r0 + 512], qs[:], reads=[qs], sb=qs, queue="pool")
                    sA = T["slab"][T["slabi"] % 3]
                    sB = T["slab"][(T["slabi"] + 1) % 3]
                    sC = T["slab"][(T["slabi"] + 2) % 3]
                    T["slabi"] += 3
                    dma(sA.ap[:, :, 0:512], WB["w_in"][:, 0:512].rearrange("(dc p) f -> p dc f", p=128), writes=[sA], sb=sA)
                    dma(sB.ap[:, :, 0:512], WB["w_in"][:, 2208:2720].rearrange("(dc p) f -> p dc f", p=128), writes=[sB], sb=sB)
                    dma(sC.ap[:, :, 0:160], WB["w_in"][:, 512:672].rearrange("(dc p) f -> p dc f", p=128), writes=[sC], sb=sC)
                    dma(sC.ap[:, :, 160:176], WB["w_in"][:, 2720:2736].rearrange("(dc p) f -> p dc f", p=128), reads=[sC], sb=sC)
                    sC.lw = (id(sC), sC.semcnt)
                    for j in range(4):
                        for dc in range(8):
                            lhs = xT.ap[:, dc, j * 128:(j + 1) * 128]
                            op("pe", lambda E, dc=dc, lhs=lhs: E.matmul(pf(6), lhsT=lhs, rhs=sA.ap[:, dc, 0:512], start=(dc == 0), stop=(dc == 7)),
                               reads=[sA, xT], writes=[BK[6]])
                        for dc in range(8):
                            lhs = xT.ap[:, dc, j * 128:(j + 1) * 128]
                            op("pe", lambda E, dc=dc, lhs=lhs: E.matmul(pf(7), lhsT=lhs, rhs=sB.ap[:, dc, 0:512], start=(dc == 0), stop=(dc == 7)),
                               reads=[sB, xT], writes=[BK[7]])
                        for dc in range(8):
                            lhs = xT.ap[:, dc, j * 128:(j + 1) * 128]
                            op("pe", lambda E, dc=dc, lhs=lhs: E.matmul(pf(0, 0, 176), lhsT=lhs, rhs=sC.ap[:, dc, 0:176], start=(dc == 0), stop=(dc == 7)),
                               reads=[sC, xT], writes=[BK[0]])
                        ps_ = pst[pi % 2]
                        gs_ = gst[pi % 2]
                        pi += 1
                        op("act", lambda E, ps_=ps_: E.activation(out=ps_[:, 0:512], in_=pf(6), func=AF.Copy), writes=[ps_, BK[6]])
                        op("dve", lambda E, ps_=ps_: E.tensor_copy(out=ps_[:, 672:1184], in_=pf(7)), reads=[ps_], writes=[ps_, BK[7]])
                        op("dve", lambda E, ps_=ps_: E.tensor_copy(out=ps_[:, 512:672], in_=pf(0, 0, 160)), reads=[ps_], writes=[ps_, BK[0]])
                        op("dve", lambda E, gs_=gs_: E.tensor_tensor(out=gs_[:, 0:8], in0=pf(0, 160, 168), in1=small[:, 16:24], op=ALU.add),
                           reads=[small], writes=[gs_, BK[0]])
                        op("act", lambda E, gs_=gs_: E.activation(out=gs_[:, 0:8], in_=gs_[:, 0:8], func=AF.Exp), reads=[gs_], writes=[gs_])
                        op("act", lambda E, gs_=gs_: E.activation(out=gs_[:, 8:16], in_=pf(0, 168, 176), func=AF.Exp, scale=-1.0), reads=[gs_], writes=[gs_, BK[0]])
                        op("act", lambda E, gs_=gs_: E.activation(out=gs_[:, 0:16], in_=gs_[:, 0:16], func=AF.Ln, bias=1.0), reads=[gs_], writes=[gs_])
                        gv = gs_.ap[:, 32:56].rearrange("p (d k) -> p d k", d=2)
                        op("dve", lambda E, gs_=gs_, gv=gv: E.tensor_tensor(out=gv[:, :, 0:4], in0=gs_.ap[:, 0:8].rearrange("p (d k) -> p d k", d=2),
                                                                            in1=small.ap[:, 24:32].rearrange("p (d k) -> p d k", d=2), op=ALU.mult),
                           reads=[gs_, small], writes=[gs_])
                        op("dve", lambda E, gs_=gs_, gv=gv: E.tensor_scalar(out=gv[:, :, 4:8], in0=gs_.ap[:, 8:16].rearrange("p (d k) -> p d k", d=2),
                                                                            scalar1=-1.0, scalar2=None, op0=ALU.mult), reads=[gs_], writes=[gs_])
                        op("act", lambda E, gs_=gs_, gv=gv: E.activation(out=gv[:, :, 8:12], in_=gs_.ap[:, 8:16].rearrange("p (d k) -> p d k", d=2),
                                                                         func=AF.Exp, scale=-1.0), reads=[gs_], writes=[gs_])
                        rr = slice(r0 + j * 128, r0 + (j + 1) * 128)
                        dma(LAT[s][rr, :], ps_[:, 0:672], reads=[ps_], sb=ps_, queue="pool")
                        dma(ZS[s][rr, :], ps_[:, 672:1184], reads=[ps_], sb=ps_, queue="pool")
                        dma(GBL[s][:, rr, :].rearrange("d p k -> p d k"), gv, reads=[gs_], sb=gs_, queue="pool")
            phase_reset()

        if "M" in phases:
            wst = alloc("wst", 1024)
            wuq = alloc("wuq", 3 * 768, BF16)
            wuq.ap = wuq.ap.rearrange("p (c f) -> p c f", c=3)
            wk = alloc("wk", 2 * 512, BF16)
            wk.ap = wk.ap.rearrange("p (c f) -> p c f", c=2)
            wv = alloc("wv", 2 * 512, BF16)
            wv.ap = wv.ap.rearrange("p (c f) -> p c f", c=2)
            for c in range(3):
                dma(wst[:, 0:768], W["mla_w_uq"][c * 128:(c + 1) * 128, :], writes=[wst], sb=wst)
                op("dve", lambda E, c=c: E.tensor_scalar(out=wuq.ap[:, c, :], in0=wst[:, 0:768], scalar1=small[:, c:c + 1], scalar2=None, op0=ALU.mult),
                   reads=[wst, small], writes=[wuq])
            for c in range(2):
                dma(wst[:, 0:1024], W["mla_w_ukv"][c * 128:(c + 1) * 128, :], writes=[wst], sb=wst)
                wv4 = wst.ap[:, 0:1024].rearrange("p (h e) -> p h e", h=8)
                op("dve", lambda E, c=c, wv4=wv4: E.tensor_scalar(out=wk.ap[:, c, :].rearrange("p (h e) -> p h e", h=8), in0=wv4[:, :, 0:64],
                                                                  scalar1=small[:, 3 + c:4 + c], scalar2=None, op0=ALU.mult), reads=[wst, small], writes=[wk])
                op("dve", lambda E, c=c, wv4=wv4: E.tensor_scalar(out=wv.ap[:, c, :].rearrange("p (h e) -> p h e", h=8), in0=wv4[:, :, 64:128],
                                                                  scalar1=small[:, 3 + c:4 + c], scalar2=None, op0=ALU.mult), reads=[wst, small], writes=[wv])
            lat = [alloc("lat%d" % i, 672) for i in range(2)]
            rp = [alloc("rp%d" % i, 64) for i in range(2)]
            lnb = [alloc("lnb%d" % i, 736, BF16) for i in range(2)]
            for b in lnb:
                op("pool", lambda E, b=b: E.memset(b[:, 640:736], 0.0), writes=[b])
            ssm = [alloc("ssm%d" % i, 4) for i in range(2)]
            junkm = alloc("junkm", 384, BF16)
            cT = [alloc("cT%d" % i, 3 * 128, BF16) for i in range(2)]
            ckvT = alloc("ckvT", 2 * 512, BF16)
            ckvT.ap = ckvT.ap.rearrange("p (c t) -> p c t", c=2)
            ckvc = [Buf("ckv%d" % j, ckvT.ap[:, :, j * 128:(j + 1) * 128]) for j in range(4)]
            kpst = alloc("kpst", 512, BF16)
            kpc = [Buf("kp%d" % j, kpst.ap[:, j * 128:(j + 1) * 128]) for j in range(4)]
            qr = [alloc("qr%d" % i, 768, BF16) for i in range(2)]
            rtmp = [alloc("rtmp%d" % i, 512) for i in range(2)]
            qtst = alloc("qtst", 8 * 512, BF16)
            qtst.ap = qtst.ap.rearrange("p (h t) -> p h t", h=8)
            qtc = [Buf("qt%d" % j, qtst.ap[:, :, j * 128:(j + 1) * 128]) for j in range(4)]
            ktst = alloc("ktst", 4 * 512, BF16)
            ktst.ap = ktst.ap.rearrange("p (a t) -> p a t", a=4)
            vst = alloc("vst", 4 * 512, BF16)
            vst.ap = vst.ap.rearrange("p (j f) -> p j f", j=4)
            vsc = [Buf("vs%d" % j, vst.ap[:, j, :]) for j in range(4)]
            li = 0
            for s, L in SEQ:
                for ti in range(L // 512):
                    r0 = ti * 512
                    for j in range(4):
                        rr = slice(r0 + j * 128, r0 + (j + 1) * 128)
                        lt, rpt, lb, sm_, ct = lat[li % 2], rp[li % 2], lnb[li % 2], ssm[li % 2], cT[li % 2]
                        qrt, rt = qr[li % 2], rtmp[li % 2]
                        li += 1
                        dma(lt[:], LAT[s][rr, :], writes=[lt], sb=lt)
                        dma(rpt[:], ROPE[rr, :], writes=[rpt], sb=rpt)
                        op("act", lambda E, lt=lt, sm_=sm_: E.activation(out=junkm[:, 0:384], in_=lt[:, 0:384], func=AF.Square, accum_out=sm_[:, 0:1]),
                           reads=[lt], writes=[sm_])
                        op("act", lambda E, lt=lt, sm_=sm_: E.activation(out=junkm[:, 0:256], in_=lt[:, 384:640], func=AF.Square, accum_out=sm_[:, 1:2]),
                           reads=[lt], writes=[sm_])
                        op("dve", lambda E, sm_=sm_: E.tensor_scalar(out=sm_[:, 0:1], in0=sm_[:, 0:1], scalar1=1.0 / 384, scalar2=EPS, op0=ALU.mult, op1=ALU.add),
                           reads=[sm_], writes=[sm_])
                        op("dve", lambda E, sm_=sm_: E.tensor_scalar(out=sm_[:, 1:2], in0=sm_[:, 1:2], scalar1=1.0 / 256, scalar2=EPS, op0=ALU.mult, op1=ALU.add),
                           reads=[sm_], writes=[sm_])
                        op("act", lambda E, sm_=sm_: E.activation(out=sm_[:, 0:2], in_=sm_[:, 0:2], func=AF.Ln), reads=[sm_], writes=[sm_])
                        op("act", lambda E, sm_=sm_: E.activation(out=sm_[:, 0:2], in_=sm_[:, 0:2], func=AF.Exp, scale=-0.5), reads=[sm_], writes=[sm_])
                        op("dve", lambda E, lt=lt, lb=lb, sm_=sm_: E.tensor_scalar(out=lb[:, 0:384], in0=lt[:, 0:384], scalar1=sm_[:, 0:1], scalar2=None, op0=ALU.mult),
                           reads=[lt, sm_], writes=[lb])
                        op("act", lambda E, lt=lt, lb=lb, sm_=sm_: E.activation(out=lb[:, 384:640], in_=lt[:, 384:640], func=AF.Copy, scale=sm_[:, 1:2]),
                           reads=[lt, sm_, lb], writes=[lb])
                        op("pool", lambda E, lt=lt, rt=rt, rpt=rpt: E.tensor_tensor(out=rt[:, 0:32], in0=lt[:, 640:672], in1=rpt[:, 0:32], op=ALU.mult),
                           reads=[lt, rpt], writes=[rt])
                        op("pool", lambda E, lt=lt, rt=rt, rpt=rpt: E.tensor_tensor(out=rt[:, 32:48], in0=lt[:, 656:672], in1=rpt[:, 32:48], op=ALU.mult),
                           reads=[lt, rpt, rt], writes=[rt])
                        op("pool", lambda E, lt=lt, rt=rt, rpt=rpt: E.tensor_tensor(out=rt[:, 48:64], in0=lt[:, 640:656], in1=rpt[:, 48:64], op=ALU.mult),
                           reads=[lt, rpt, rt], writes=[rt])
                        op("pool", lambda E, rt=rt, lb=lb: E.tensor_tensor(out=lb[:, 704:736], in0=rt[:, 0:32], in1=rt[:, 32:64], op=ALU.add),
                           reads=[rt, lb], writes=[lb])
                        for c in range(5):
                            op("pe", lambda E, c=c, lb=lb: E.transpose(out=pb(0, c * 128, (c + 1) * 128), in_=lb[:, c * 128:(c + 1) * 128], identity=identb[:]),
                               reads=[lb, identb], writes=[BK[0]])
                        op("pe", lambda E, lb=lb: E.transpose(out=psum_bf[0:96, 640:768], in_=lb[:, 640:736], identity=identb[:]),
                           reads=[lb, identb], writes=[BK[0]])
                        op("dve", lambda E, ct=ct: E.tensor_copy(out=ct[:], in_=pb(0, 0, 384)), writes=[ct, BK[0]])
                        op("act", lambda E, j=j: E.activation(out=ckvT.ap[:, :, j * 128:(j + 1) * 128], in_=pb(0, 384, 640).rearrange("p (c t) -> p c t", c=2), func=AF.Copy),
                           writes=[ckvc[j], BK[0]])
                        op("dve", lambda E, j=j: E.tensor_copy(out=kpst.ap[64:96, j * 128:(j + 1) * 128], in_=psum_bf[64:96, 640:768]), writes=[kpc[j], BK[0]])
                        ctv = ct.ap.rearrange("p (c t) -> p c t", c=3)
                        for c in range(3):
                            op("pe", lambda E, c=c, ctv=ctv: E.matmul(pf(1, 0, 480), lhsT=ctv[:, c, :], rhs=wuq.ap[:, c, 0:480], start=(c == 0), stop=(c == 2)),
                               reads=[ct, wuq], writes=[BK[1]])
                        for c in range(3):
                            op("pe", lambda E, c=c, ctv=ctv: E.matmul(pf(2, 0, 288), lhsT=ctv[:, c, :], rhs=wuq.ap[:, c, 480:768], start=(c == 0), stop=(c == 2)),
                               reads=[ct, wuq], writes=[BK[2]])
                        for (bk, h0, nh) in ((1, 0, 5), (2, 5, 3)):
                            pv = pf(bk, 0, nh * 96).rearrange("p (h e) -> p h e", h=nh)
                            qv = qrt.ap[:, h0 * 96:(h0 + nh) * 96].rearrange("p (h e) -> p h e", h=nh)
                            tv = rt.ap[:, 64:64 + nh * 64].rearrange("p (h e) -> p h e", h=nh)
                            cs = rpt.ap[:, 0:32].unsqueeze(1).to_broadcast([128, nh, 32])
                            sn1 = rpt.ap[:, 32:48].unsqueeze(1).to_broadcast([128, nh, 16])
                            sn2 = rpt.ap[:, 48:64].unsqueeze(1).to_broadcast([128, nh, 16])
                            op("act", lambda E, pv=pv, qv=qv: E.activation(out=qv[:, :, 0:64], in_=pv[:, :, 0:64], func=AF.Copy), reads=[], writes=[qrt, BK[bk]])
                            op("dve", lambda E, pv=pv, tv=tv, cs=cs: E.tensor_tensor(out=tv[:, :, 0:32], in0=pv[:, :, 64:96], in1=cs, op=ALU.mult),
                               reads=[rpt], writes=[rt, BK[bk]])
                            op("dve", lambda E, pv=pv, tv=tv, sn1=sn1: E.tensor_tensor(out=tv[:, :, 32:48], in0=pv[:, :, 80:96], in1=sn1, op=ALU.mult),
                               reads=[rpt], writes=[rt, BK[bk]])
                            op("dve", lambda E, pv=pv, tv=tv, sn2=sn2: E.tensor_tensor(out=tv[:, :, 48:64], in0=pv[:, :, 64:80], in1=sn2, op=ALU.mult),
                               reads=[rpt], writes=[rt, BK[bk]])
                            op("pool", lambda E, tv=tv, qv=qv: E.tensor_tensor(out=qv[:, :, 64:96], in0=tv[:, :, 0:32], in1=tv[:, :, 32:64], op=ALU.add),
                               reads=[rt], writes=[qrt])
                        for hh in range(8):
                            op("pe", lambda E, hh=hh, qrt=qrt: E.transpose(out=psum_bf[0:96, 3 * 1024 + hh * 128:3 * 1024 + (hh + 1) * 128], in_=qrt[:, hh * 96:(hh + 1) * 96],
                                                                           identity=identb[:]), reads=[qrt, identb], writes=[BK[3]])
                        op("act", lambda E, j=j: E.activation(out=qtst.ap[0:96, :, j * 128:(j + 1) * 128],
                                                              in_=psum_bf[0:96, 3 * 1024:4 * 1024].rearrange("p (h t) -> p h t", h=8), func=AF.Copy),
                           writes=[qtc[j], BK[3]])
                        for c in range(2):
                            op("pe", lambda E, c=c, j=j: E.matmul(pf(4), lhsT=ckvT.ap[:, c, j * 128:(j + 1) * 128], rhs=wv.ap[:, c, :], start=(c == 0), stop=(c == 1)),
                               reads=[ckvc[j], wv], writes=[BK[4]])
                        op("dve", lambda E, j=j: E.tensor_copy(out=vst.ap[:, j, :], in_=pf(4)), writes=[vsc[j], BK[4]])
                    for pr in range(4):
                        bk = 5 + pr % 2
                        for c in range(2):
                            op("pe", lambda E, c=c, pr=pr, bk=bk: E.matmul(pf(bk), lhsT=wk.ap[:, c, pr * 128:(pr + 1) * 128], rhs=ckvT.ap[:, c, :], start=(c == 0), stop=(c == 1)),
                               reads=ckvc + [wk], writes=[BK[bk]])
                        if pr % 2 == 0:
                            op("act", lambda E, pr=pr, bk=bk: E.activation(out=ktst.ap[:, pr, :], in_=pf(bk), func=AF.Copy), writes=[ktst, BK[bk]])
                        else:
                            op("dve", lambda E, pr=pr, bk=bk: E.tensor_copy(out=ktst.ap[:, pr, :], in_=pf(bk)), reads=[ktst], writes=[ktst, BK[bk]])
                    cc = slice(r0, r0 + 512)
                    for hh in range(8):
                        pr, hi = hh // 2, hh % 2
                        dma(KT[s][hh, 0:64, cc], ktst.ap[hi * 64:(hi + 1) * 64, pr, :], reads=[ktst], sb=ktst, queue="pool")
                        dma(KT[s][hh, 64:96, cc], kpst.ap[64:96, :], reads=kpc, sb=kpst, queue="pool")
                    dma(QT[s][:, :, cc].rearrange("h r t -> r h t"), qtst.ap[0:96, :, :], reads=qtc, sb=qtst, queue="pool")
                    for j in range(4):
                        dma(VV[s][:, :, ti * 4 + j, :].rearrange("h p e -> p h e"),
                            vst.ap[:, j, :].rearrange("p (h e) -> p h e", h=8), reads=vsc, sb=vst, queue="pool")
            phase_reset()

        if "B" in phases:
            SC = 96 ** -0.5
            for s, L in SEQ:
                nkb = L // 128
                off_seq = off[0]
                ktb = []
                vtb = []
                for i in range(2):
                    b = alloc("ktb%d" % i, L, BF16)
                    ktb.append(b)
                    v = alloc("vtb%d" % i, nkb * 65, BF16)
                    v.ap = v.ap.rearrange("p (k e) -> p k e", e=65)
                    op("pool", lambda E, v=v: E.memset(v.ap[:, :, 64:65], 1.0), writes=[v])
                    vtb.append(v)
                qtb = [alloc("qtb%d" % i, 512, BF16) for i in range(3)]
                ptb = [alloc("ptb%d" % i, 512, BF16) for i in range(4)]
                osb = [alloc("osb%d" % i, 512) for i in range(2)]
                yst = [alloc("yst%d" % i, 4 * 64) for i in range(2)]
                rdn = [alloc("rdn%d" % i, 4) for i in range(2)]
                qi = 0
                pi = 0
                for hh in range(8):
                    kt, vt = ktb[hh % 2], vtb[hh % 2]
                    dma(kt[0:96, :], KT[s][hh, :, :], writes=[kt], sb=kt)
                    dma(vt.ap[:, :, 0:64], VV[s][hh, :, :, :], writes=[vt], sb=vt)
                    for qt in range(L // 512):
                        qb = qtb[qi % 3]
                        ob, ys, rd = osb[qi % 2], yst[qi % 2], rdn[qi % 2]
                        obk = 4 + qi % 2
                        qi += 1
                        dma(qb[0:96, :], QT[s][hh, :, qt * 512:(qt + 1) * 512], writes=[qb], sb=qb)

                        def smm(kb, qb=qb, kt=kt):
                            bk = kb % 4
                            op("pe", lambda E: E.matmul(pf(bk), lhsT=kt[0:96, kb * 128:(kb + 1) * 128], rhs=qb[0:96, :], start=True, stop=True),
                               reads=[kt, qb], writes=[BK[bk]])
                        smm(0)
                        if nkb > 1:
                            smm(1)
                        for kb in range(nkb):
                            if kb + 2 < nkb:
                                smm(kb + 2)
                            pt = ptb[pi % 4]
                            pi += 1
                            bk = kb % 4
                            op("act", lambda E, pt=pt, bk=bk: E.activation(out=pt[:], in_=pf(bk), func=AF.Exp, scale=SC), writes=[pt, BK[bk]])
                            op("pe", lambda E, pt=pt, kb=kb, vt=vt, obk=obk: E.matmul(psum[0:65, obk * 512:(obk + 1) * 512], lhsT=vt.ap[:, kb, 0:65], rhs=pt[:],
                                                                                         start=(kb == 0), stop=(kb == nkb - 1)), reads=[vt, pt], writes=[BK[obk]])
                        op("dve", lambda E, ob=ob, obk=obk: E.tensor_copy(out=ob[0:65, :], in_=psum[0:65, obk * 512:(obk + 1) * 512]), writes=[ob, BK[obk]])
                        for j in range(4):
                            op("pe", lambda E, j=j, ob=ob: E.matmul(pf(6, j * 65, (j + 1) * 65), lhsT=ob[0:65, j * 128:(j + 1) * 128], rhs=identf[0:65, 0:65], start=True, stop=True),
                               reads=[ob, identf], writes=[BK[6]])
                        p6 = pf(6, 0, 260).rearrange("p (j e) -> p j e", j=4)
                        op("dve", lambda E, rd=rd, p6=p6: E.reciprocal(out=rd.ap[:, 0:4].unsqueeze(2), in_=p6[:, :, 64:65]), writes=[rd, BK[6]])
                        op("dve", lambda E, rd=rd, ys=ys, p6=p6: E.tensor_tensor(out=ys.ap.rearrange("p (j e) -> p j e", j=4), in0=p6[:, :, 0:64],
                                                                                in1=rd.ap[:, 0:4].unsqueeze(2).to_broadcast([128, 4, 64]), op=ALU.mult),
                           reads=[rd], writes=[ys, BK[6]])
                        dma(YA[s][qt * 512:(qt + 1) * 512, hh * 64:(hh + 1) * 64].rearrange("(j p) e -> p j e", p=128),
                            ys.ap.rearrange("p (j e) -> p j e", j=4), reads=[ys], sb=ys, queue="pool")
                P.barrier()
                off[0] = off_seq
            phase_reset()

        if "G" in phases:
            cw = alloc("cw", 12 * 5)
            for c in range(12):
                dma(cw[:, c * 5:(c + 1) * 5], W["gdn_conv_w"][:, c * 128:(c + 1) * 128].rearrange("k p -> p k"), writes=[cw] if c == 0 else [],
                    reads=[] if c == 0 else [cw], sb=cw, allow_slow_non_contiguous=True)
            cw.lw = (id(cw), cw.semcnt)
            xin = [alloc("gxin%d" % i, 516) for i in range(3)]
            acc = [alloc("gacc%d" % i, 512) for i in range(2)]
            ex = [alloc("gex%d" % i, 512) for i in range(2)]
            sb_ = [alloc("gsb%d" % i, 512, BF16) for i in range(2)]
            tokst = alloc("tokst", 4 * 1536, BF16)
            tokst.ap = tokst.ap.rearrange("p (j c) -> p j c", j=4)
            tokc = [Buf("tokc%d" % c, tokst.ap[:, :, c * 128:(c + 1) * 128]) for c in range(12)]
            sq = alloc("gsq", 1024)
            ssg = alloc("ssg", 32)
            ci = 0
            for s, L in SEQ:
                for ti in range(L // 512):
                    r0 = ti * 512
                    for c in range(12):
                        xi, ac, e_, sbb = xin[ci % 3], acc[ci % 2], ex[ci % 2], sb_[ci % 2]
                        ci += 1
                        lo, hi = max(r0 - 2, 0), min(r0 + 514, L)
                        if r0 == 0:
                            op("pool", lambda E, xi=xi: E.memset(xi[:, 0:2], 0.0), writes=[xi])
                        if r0 + 512 == L:
                            op("pool", lambda E, xi=xi: E.memset(xi[:, 514:516], 0.0), writes=[xi])
                        dma(xi[:, lo - (r0 - 2):hi - (r0 - 2)], QKVT[s][c * 128:(c + 1) * 128, lo:hi], writes=[xi], sb=xi)
                        e1 = "dve"
                        op(e1, lambda E, xi=xi, ac=ac, c=c: E.tensor_scalar(out=ac[:], in0=xi[:, 0:512], scalar1=cw[:, c * 5:c * 5 + 1], scalar2=None, op0=ALU.mult),
                           reads=[xi, cw], writes=[ac])
                        for k in range(1, 5):
                            op(e1, lambda E, xi=xi, ac=ac, c=c, k=k: E.scalar_tensor_tensor(out=ac[:], in0=xi[:, k:k + 512], scalar=cw[:, c * 5 + k:c * 5 + k + 1], in1=ac[:],
                                                                                           op0=ALU.mult, op1=ALU.add), reads=[xi, cw, ac], writes=[ac])
                        op("act", lambda E, ac=ac, e_=e_: E.activation(out=e_[:], in_=ac[:], func=AF.Exp, scale=-1.0), reads=[ac], writes=[e_])
                        op("act", lambda E, e_=e_: E.activation(out=e_[:], in_=e_[:], func=AF.Ln, bias=1.0), reads=[e_], writes=[e_])
                        op("act", lambda E, e_=e_: E.activation(out=e_[:], in_=e_[:], func=AF.Exp, scale=-1.0), reads=[e_], writes=[e_])
                        op("dve", lambda E, ac=ac, e_=e_, sbb=sbb: E.tensor_tensor(out=sbb[:], in0=ac[:], in1=e_[:], op=ALU.mult), reads=[ac, e_], writes=[sbb])
                        bk = c % 2
                        for j in range(4):
                            op("pe", lambda E, j=j, sbb=sbb, bk=bk: E.transpose(out=pb(bk, j * 128, (j + 1) * 128), in_=sbb[:, j * 128:(j + 1) * 128], identity=identb[:]),
                               reads=[sbb, identb], writes=[BK[bk]])
                        if c % 2 == 0:
                            op("act", lambda E, c=c, bk=bk: E.activation(out=tokst.ap[:, :, c * 128:(c + 1) * 128], in_=pb(bk, 0, 512).rearrange("p (j d) -> p j d", j=4), func=AF.Copy),
                               writes=[tokc[c], BK[bk]])
                        else:
                            op("dve", lambda E, c=c, bk=bk: E.tensor_copy(out=tokst.ap[:, :, c * 128:(c + 1) * 128], in_=pb(bk, 0, 512).rearrange("p (j d) -> p j d", j=4)),
                               writes=[tokc[c], BK[bk]])
                    for j in range(4):
                        tv = tokst.ap[:, j, 0:1024]
                        op("dve", lambda E, tv=tv: E.tensor_tensor(out=sq[:], in0=tv, in1=tv, op=ALU.mult), reads=tokc[0:8], writes=[sq])
                        op("dve", lambda E, j=j: E.tensor_reduce(out=ssg[:, j * 8:(j + 1) * 8], in_=sq.ap.rearrange("p (h d) -> p h d", h=8), axis=AX.X, op=ALU.add),
                           reads=[sq], writes=[ssg])
                    op("dve", lambda E: E.tensor_scalar(out=ssg[:, 0:32], in0=ssg[:, 0:32], scalar1=EPS, scalar2=None, op0=ALU.add), reads=[ssg], writes=[ssg])
                    op("act", lambda E: E.activation(out=ssg[:, 0:32], in_=ssg[:, 0:32], func=AF.Ln), reads=[ssg], writes=[ssg])
                    op("act", lambda E: E.activation(out=ssg[:, 0:32], in_=ssg[:, 0:32], func=AF.Exp, scale=-0.5), reads=[ssg], writes=[ssg])
                    sv = ssg.ap[:, 0:32].rearrange("p (j h) -> p j h", j=4)
                    op("dve", lambda E, sv=sv: E.tensor_scalar(out=sv[:, :, 0:4], in0=sv[:, :, 0:4], scalar1=128 ** -0.5, scalar2=None, op0=ALU.mult), reads=[ssg], writes=[ssg])
                    for j in range(4):
                        tv = tokst.ap[:, j, 0:1024].rearrange("p (h d) -> p h d", h=8)
                        e1 = "dve" if j % 2 == 0 else "pool"
                        op(e1, lambda E, tv=tv, j=j: E.tensor_tensor(out=tv, in0=tv, in1=ssg.ap[:, j * 8:(j + 1) * 8].unsqueeze(2).to_broadcast([128, 8, 128]), op=ALU.mult),
                           reads=[ssg] + tokc[0:8], writes=tokc[0:8])
                    dma(TOK[s][r0:r0 + 512, :].rearrange("(j p) c -> p j c", p=128), tokst.ap, reads=tokc, sb=tokst, queue="pool")
            phase_reset()

        if "H" in phases:
            gm = alloc("gm", 5 * 512)
            gm.ap = gm.ap.rearrange("p (m h j) -> p m h j", m=5, h=8)
            dma(gm.ap[0:64], GMASK[:, :, :, :], writes=[gm], sb=gm)
            NA, NAT, NQK = gm.ap[0:64, 0], gm.ap[0:64, 1], gm.ap[0:64, 2]
            trif, trib = gm.ap[0:64, 3, 0, :], gm.ap[0:64, 3, 1, :]
            idb8 = identb.ap[0:64, 0:64].unsqueeze(1).to_broadcast([64, 8, 64])
            idf8 = identf.ap[0:64, 0:64].unsqueeze(1).to_broadcast([64, 8, 64])

            def A3(name, n, dt=F32, parts=128):
                b = alloc(name, 8 * n, dt)
                b.ap = b.ap.rearrange("p (h x) -> p h x", h=8)
                return b
            tok = [alloc("tk%d" % i, 2 * 1536, BF16) for i in range(2)]
            gsel = [alloc("gsel%d" % i, 24) for i in range(2)]
            sm8 = [alloc("sm8_%d" % i, 64) for i in range(2)]
            Dg = A3("Dg", 64)
            Dc = A3("Dc", 64)
            De = A3("De", 64, BF16)
            kqT = alloc("kqT", 16 * 64, BF16)
            kqT.ap = kqT.ap.rearrange("p (h x) -> p h x", h=16)
            dA = A3("dA", 64)
            dAT = A3("dAT", 64)
            dQK = A3("dQK", 64)
            Xb = [A3("Xb%d" % i, 64, BF16) for i in range(2)]
            Yb = [A3("Yb%d" % i, 64, BF16) for i in range(2)]
            Zb = [A3("Zb%d" % i, 64, BF16) for i in range(2)]
            qkT = A3("qkT", 64, BF16)
            kbg = A3("kbg", 128, BF16)
            vb = A3("vb", 128, BF16)
            kg = A3("kg", 128, BF16)
            wT = A3("wT", 64, BF16)
            qgT = A3("qgT", 64, BF16)
            uu = A3("uu", 128)
            vnew = A3("vnew", 128, BF16)
            ost = [alloc("ost%d" % i, 1024) for i in range(2)]
            S = A3("S", 128)
            Sb = A3("Sb", 128, BF16)
            for s, L in SEQ:
                N = L // 64
                op("pool", lambda E: E.memset(S.ap, 0.0), writes=[S])
                op("pool", lambda E: E.memset(Sb.ap, 0.0), writes=[Sb])
                for n in range(N):
                    cf, cb = n, N - 1 - n
                    tk, gs, m8, os_ = tok[n % 2], gsel[n % 2], sm8[n % 2], ost[n % 2]
                    tkv = tk.ap.rearrange("p (d c) -> p d c", d=2)
                    for d, ch in ((0, cf), (1, cb)):
                        dma(tkv[0:64, d, :], TOK[s][ch * 64:(ch + 1) * 64, :], writes=[tk] if d == 0 else [], reads=[] if d == 0 else [tk], sb=tk)
                        dma(gs.ap[0:64, d * 12:(d + 1) * 12], GBL[s][d, ch * 64:(ch + 1) * 64, :], writes=[gs] if d == 0 else [], reads=[] if d == 0 else [gs], sb=gs)
                    tk.lw = (id(tk), tk.semcnt)
                    gs.lw = (id(gs), gs.semcnt)
                    gv = gs.ap[0:64, :].rearrange("p (d k) -> p d k", d=2)
                    g8, lnb8, beta8 = gv[:, :, 0:4], gv[:, :, 4:8], gv[:, :, 8:12]
                    m = m8.ap[0:64, :]

                    def v8(a, b):
                        return m8.ap[0:64, a:b].rearrange("p (d k) -> p d k", d=2)
                    op("pe", lambda E, gs=gs: E.matmul(psum[0:64, 0:4], lhsT=trif, rhs=gs.ap[0:64, 0:4], start=True, stop=True), reads=[gm, gs], writes=[BK[0]])
                    op("pe", lambda E, gs=gs: E.matmul(psum[0:64, 4:8], lhsT=trib, rhs=gs.ap[0:64, 12:16], start=True, stop=True), reads=[gm, gs], writes=[BK[0]])
                    op("pe", lambda E, g8=g8: E.matmul(psum[:, 8:16].rearrange("p (d k) -> p d k", d=2), lhsT=onesf[0:64, :], rhs=g8, start=True, stop=True),
                       reads=[onesf, gs], writes=[BK[0]])
                    op("dve", lambda E, m8=m8: E.tensor_copy(out=m8[0:64, 0:8], in_=psum[0:64, 0:8]), writes=[m8, BK[0]])
                    op("dve", lambda E, m8=m8, lnb8=lnb8, v8=v8: E.tensor_tensor(out=v8(8, 16), in0=v8(0, 8), in1=lnb8, op=ALU.add), reads=[m8, gs], writes=[m8])
                    op("act", lambda E, m8=m8: E.activation(out=m8[0:64, 16:24], in_=m8[0:64, 0:8], func=AF.Exp), reads=[m8], writes=[m8])
                    op("dve", lambda E, m8=m8, beta8=beta8, v8=v8: E.tensor_tensor(out=v8(24, 32), in0=v8(16, 24), in1=beta8, op=ALU.mult), reads=[m8, gs], writes=[m8])
                    op("dve", lambda E, m8=m8: E.tensor_tensor(out=m8[0:64, 40:48], in0=psum[0:64, 8:16], in1=m8[0:64, 0:8], op=ALU.subtract), reads=[m8], writes=[m8, BK[0]])
                    op("act", lambda E, m8=m8: E.activation(out=m8[0:64, 32:40], in_=m8[0:64, 40:48], func=AF.Exp), reads=[m8], writes=[m8])
                    op("act", lambda E, m8=m8: E.activation(out=m8[:, 48:56], in_=psum[:, 8:16], func=AF.Exp), reads=[m8], writes=[m8, BK[0]])
                    op("pool", lambda E, m8=m8: E.tensor_tensor(out=Dg.ap[0:64], in0=idf8, in1=m8.ap[0:64, 0:8].unsqueeze(2).to_broadcast([64, 8, 64]), op=ALU.mult),
                       reads=[identf, m8], writes=[Dg])
                    op("pool", lambda E, m8=m8: E.tensor_tensor(out=Dc.ap[0:64], in0=idf8, in1=m8.ap[0:64, 8:16].unsqueeze(2).to_broadcast([64, 8, 64]), op=ALU.mult),
                       reads=[identf, m8], writes=[Dc])
                    op("pool", lambda E, m8=m8: E.tensor_tensor(out=De.ap[0:64], in0=idf8, in1=m8.ap[0:64, 16:24].unsqueeze(2).to_broadcast([64, 8, 64]), op=ALU.mult),
                       reads=[identf, m8], writes=[De])
                    op("pe", lambda E: E.matmul(psum[0:64, 512:1024], lhsT=onesf[0:64, 0:64], rhs=Dg.ap[0:64].rearrange("p h x -> p (h x)"), start=True, stop=True),
                       reads=[onesf, Dg], writes=[BK[1]])
                    op("pe", lambda E: E.matmul(psum[0:64, 1024:1536], lhsT=onesf[0:64, 0:64], rhs=Dc.ap[0:64].rearrange("p h x -> p (h x)"), start=True, stop=True),
                       reads=[onesf, Dc], writes=[BK[2]])
                    for d in range(2):
                        for hh in range(4):
                            hd = d * 4 + hh
                            op("pe", lambda E, d=d, hh=hh, hd=hd, tkv=tkv: E.transpose(out=psum_bf[:, 3 * 1024 + hd * 64:3 * 1024 + (hd + 1) * 64],
                                                                                     in_=tkv[0:64, d, 512 + hh * 128:512 + (hh + 1) * 128], identity=identb[0:64, 0:64]),
                               reads=[tk, identb], writes=[BK[3]])
                            op("pe", lambda E, d=d, hh=hh, hd=hd, tkv=tkv: E.transpose(out=psum_bf[:, 3 * 1024 + 512 + hd * 64:3 * 1024 + 512 + (hd + 1) * 64],
                                                                                     in_=tkv[0:64, d, hh * 128:(hh + 1) * 128], identity=identb[0:64, 0:64]),
                               reads=[tk, identb], writes=[BK[3]])
                    op("act", lambda E: E.activation(out=kqT.ap, in_=pb(3).rearrange("p (h x) -> p h x", h=16), func=AF.Copy), writes=[kqT, BK[3]])
                    for hd in range(8):
                        op("pe", lambda E, hd=hd: E.matmul(psum[0:64, 4 * 512 + hd * 64:4 * 512 + (hd + 1) * 64], lhsT=kqT.ap[:, hd, :], rhs=kqT.ap[:, hd, :], start=True, stop=True),
                           reads=[kqT], writes=[BK[4]])
                    for hd in range(8):
                        op("pe", lambda E, hd=hd: E.matmul(psum[0:64, 5 * 512 + hd * 64:5 * 512 + (hd + 1) * 64], lhsT=kqT.ap[:, hd, :], rhs=kqT.ap[:, 8 + hd, :], start=True, stop=True),
                           reads=[kqT], writes=[BK[5]])
                    P1 = psum[0:64, 512:1024].rearrange("p (h x) -> p h x", h=8)
                    P2 = psum[0:64, 1024:1536].rearrange("p (h x) -> p h x", h=8)
                    PG = psum[0:64, 4 * 512:5 * 512].rearrange("p (h x) -> p h x", h=8)
                    PQ = psum[0:64, 5 * 512:6 * 512].rearrange("p (h x) -> p h x", h=8)

                    def bc8(a):
                        return m8.ap[0:64, a:a + 8].unsqueeze(2).to_broadcast([64, 8, 64])
                    op("dve", lambda E: E.scalar_tensor_tensor(out=dA.ap[0:64], in0=P1, scalar=-1.0, in1=NA, op0=ALU.mult, op1=ALU.add), reads=[gm], writes=[dA, BK[1]])
                    op("pool", lambda E, bc8=bc8: E.tensor_tensor(out=dA.ap[0:64], in0=dA.ap[0:64], in1=bc8(8), op=ALU.add), reads=[m8, dA], writes=[dA])
                    op("act", lambda E: E.activation(out=dA.ap[0:64], in_=dA.ap[0:64], func=AF.Exp), reads=[dA], writes=[dA])
                    op("dve", lambda E: E.tensor_tensor(out=dAT.ap[0:64], in0=P2, in1=NAT, op=ALU.add), reads=[gm], writes=[dAT, BK[2]])
                    op("pool", lambda E, bc8=bc8: E.tensor_tensor(out=dAT.ap[0:64], in0=dAT.ap[0:64], in1=bc8(0), op=ALU.subtract), reads=[m8, dAT], writes=[dAT])
                    op("act", lambda E: E.activation(out=dAT.ap[0:64], in_=dAT.ap[0:64], func=AF.Exp), reads=[dAT], writes=[dAT])
                    op("dve", lambda E: E.tensor_tensor(out=dQK.ap[0:64], in0=P1, in1=NQK, op=ALU.add), reads=[gm], writes=[dQK, BK[1]])
                    op("pool", lambda E, bc8=bc8: E.tensor_tensor(out=dQK.ap[0:64], in0=dQK.ap[0:64], in1=bc8(0), op=ALU.subtract), reads=[m8, dQK], writes=[dQK])
                    op("act", lambda E: E.activation(out=dQK.ap[0:64], in_=dQK.ap[0:64], func=AF.Exp), reads=[dQK], writes=[dQK])
                    X0, Y0, Z0 = Xb[0], Yb[0], Zb[0]
                    op("dve", lambda E, X0=X0: E.scalar_tensor_tensor(out=X0.ap[0:64], in0=dA.ap[0:64], scalar=-1.0, in1=PG, op0=ALU.mult, op1=ALU.mult), reads=[dA], writes=[X0, BK[4]])
                    op("dve", lambda E, Y0=Y0: E.scalar_tensor_tensor(out=Y0.ap[0:64], in0=dAT.ap[0:64], scalar=-1.0, in1=PG, op0=ALU.mult, op1=ALU.mult), reads=[dAT], writes=[Y0, BK[4]])
                    op("dve", lambda E: E.tensor_tensor(out=qkT.ap[0:64], in0=dQK.ap[0:64], in1=PQ, op=ALU.mult), reads=[dQK], writes=[qkT, BK[5]])
                    op("pool", lambda E, Y0=Y0, Z0=Z0: E.tensor_tensor(out=Z0.ap[0:64], in0=Y0.ap[0:64], in1=idb8, op=ALU.add), reads=[Y0, identb], writes=[Z0])
                    for k in range(1, 6):
                        Xp, Yp, Zp = Xb[(k - 1) % 2], Yb[(k - 1) % 2], Zb[(k - 1) % 2]
                        Xn, Yn, Zn = Xb[k % 2], Yb[k % 2], Zb[k % 2]
                        for hd in range(8):
                            op("pe", lambda E, hd=hd, Xp=Xp, Yp=Yp: E.matmul(psum[0:64, 1 * 512 + hd * 64:1 * 512 + (hd + 1) * 64], lhsT=Yp.ap[0:64, hd, :], rhs=Xp.ap[0:64, hd, :],
                                                                             start=True, stop=True), reads=[Xp, Yp], writes=[BK[1]])
                        if k < 5:
                            for hd in range(8):
                                op("pe", lambda E, hd=hd, Xp=Xp, Yp=Yp: E.matmul(psum[0:64, 2 * 512 + hd * 64:2 * 512 + (hd + 1) * 64], lhsT=Xp.ap[0:64, hd, :], rhs=Yp.ap[0:64, hd, :],
                                                                                 start=True, stop=True), reads=[Xp, Yp], writes=[BK[2]])
                        op("act", lambda E, Xn=Xn: E.activation(out=Xn.ap[0:64], in_=P1, func=AF.Copy), writes=[Xn, BK[1]])
                        if k < 5:
                            op("dve", lambda E, Yn=Yn: E.tensor_copy(out=Yn.ap[0:64], in_=P2), writes=[Yn, BK[2]])
                        for hd in range(8):
                            op("pe", lambda E, hd=hd, Xn=Xn, Zp=Zp: E.matmul(psum[0:64, 3 * 512 + hd * 64:3 * 512 + (hd + 1) * 64], lhsT=Xn.ap[0:64, hd, :], rhs=Zp.ap[0:64, hd, :],
                                                                             start=True, stop=True), reads=[Xn, Zp], writes=[BK[3]])
                        op("dve", lambda E, Zn=Zn, Zp=Zp: E.tensor_tensor(out=Zn.ap[0:64], in0=psum[0:64, 3 * 512:4 * 512].rearrange("p (h x) -> p h x", h=8), in1=Zp.ap[0:64], op=ALU.add),
                           reads=[Zp], writes=[Zn, BK[3]])
                    Z = Zb[5 % 2]
                    kv4 = tkv[0:64, :, 512:1024].rearrange("p d (h x) -> p d h x", h=4)
                    vv4 = tkv[0:64, :, 1024:1536].rearrange("p d (h x) -> p d h x", h=4)

                    def b4(a):
                        return m8.ap[0:64, a:a + 8].rearrange("p (d h) -> p d h", d=2).unsqueeze(3).to_broadcast([64, 2, 4, 128])
                    op("pool", lambda E, kv4=kv4, b4=b4: E.tensor_tensor(out=kbg.ap[0:64].rearrange("p (d h) x -> p d h x", d=2), in0=kv4, in1=b4(24), op=ALU.mult),
                       reads=[tk, m8], writes=[kbg])
                    op("dve", lambda E, vv4=vv4, gs=gs: E.tensor_tensor(out=vb.ap[0:64].rearrange("p (d h) x -> p d h x", d=2), in0=vv4,
                                                                        in1=gs.ap[0:64, :].rearrange("p (d k) -> p d k", d=2)[:, :, 8:12].unsqueeze(3).to_broadcast([64, 2, 4, 128]), op=ALU.mult),
                       reads=[tk, gs], writes=[vb])
                    op("pool", lambda E, kv4=kv4, b4=b4: E.tensor_tensor(out=kg.ap[0:64].rearrange("p (d h) x -> p d h x", d=2), in0=kv4, in1=b4(32), op=ALU.mult),
                       reads=[tk, m8], writes=[kg])
                    for hd in range(8):
                        op("pe", lambda E, hd=hd, Z=Z: E.matmul(psum[:, 0 * 512 + hd * 64:0 * 512 + (hd + 1) * 64], lhsT=kbg.ap[0:64, hd, :], rhs=Z.ap[0:64, hd, :], start=True, stop=True),
                           reads=[kbg, Z], writes=[BK[0]])
                    op("act", lambda E: E.activation(out=wT.ap, in_=pf(0).rearrange("p (h x) -> p h x", h=8), func=AF.Copy), writes=[wT, BK[0]])
                    for hd in range(8):
                        bk = 6 + hd // 4
                        op("pe", lambda E, hd=hd, Z=Z: E.matmul(psum[0:64, 6 * 512 + hd * 128:6 * 512 + (hd + 1) * 128], lhsT=Z.ap[0:64, hd, :], rhs=vb.ap[0:64, hd, :], start=True, stop=True),
                           reads=[vb, Z], writes=[BK[bk]])
                    op("act", lambda E: E.activation(out=uu.ap[0:64], in_=psum[0:64, 6 * 512:8 * 512].rearrange("p (h x) -> p h x", h=8), func=AF.Copy), writes=[uu, BK[6], BK[7]])
                    for d in range(2):
                        for hh in range(4):
                            hd = d * 4 + hh
                            op("pe", lambda E, d=d, hh=hh, hd=hd, tkv=tkv: E.matmul(psum[:, 1 * 512 + hd * 64:1 * 512 + (hd + 1) * 64], lhsT=tkv[0:64, d, hh * 128:(hh + 1) * 128],
                                                                                    rhs=De.ap[0:64, hd, :], start=True, stop=True), reads=[tk, De], writes=[BK[1]])
                    op("dve", lambda E: E.tensor_copy(out=qgT.ap, in_=pf(1).rearrange("p (h x) -> p h x", h=8)), writes=[qgT, BK[1]])
                    for hd in range(8):
                        bk = 6 + hd // 4
                        op("pe", lambda E, hd=hd: E.matmul(psum[0:64, 6 * 512 + hd * 128:6 * 512 + (hd + 1) * 128], lhsT=wT.ap[:, hd, :], rhs=Sb.ap[:, hd, :], start=True, stop=True),
                           reads=[wT, Sb], writes=[BK[bk]])
                    op("dve", lambda E: E.tensor_tensor(out=vnew.ap[0:64], in0=uu.ap[0:64], in1=psum[0:64, 6 * 512:8 * 512].rearrange("p (h x) -> p h x", h=8), op=ALU.subtract),
                       reads=[uu], writes=[vnew, BK[6], BK[7]])
                    for hd in range(8):
                        bk = 4 + hd // 4
                        op("pe", lambda E, hd=hd: E.matmul(psum[0:64, 4 * 512 + hd * 128:4 * 512 + (hd + 1) * 128], lhsT=qgT.ap[:, hd, :], rhs=Sb.ap[:, hd, :], start=True, stop=False),
                           reads=[qgT, Sb], writes=[BK[bk]])
                        op("pe", lambda E, hd=hd: E.matmul(psum[0:64, 4 * 512 + hd * 128:4 * 512 + (hd + 1) * 128], lhsT=qkT.ap[0:64, hd, :], rhs=vnew.ap[0:64, hd, :], start=False, stop=True),
                           reads=[qkT, vnew], writes=[BK[bk]])
                    op("act", lambda E, os_=os_: E.activation(out=os_[0:64, :], in_=psum[0:64, 4 * 512:6 * 512], func=AF.Copy), writes=[os_, BK[4], BK[5]])
                    dma(OF[s][0, cf * 64:(cf + 1) * 64, :], os_[0:64, 0:512], reads=[os_], sb=os_, queue="pool")
                    dma(OF[s][1, cb * 64:(cb + 1) * 64, :], os_[0:64, 512:1024], reads=[os_], sb=os_, queue="pool")
                    for hd in range(8):
                        bk = 6 + hd // 4
                        op("pe", lambda E, hd=hd: E.matmul(psum[:, 6 * 512 + hd * 128:6 * 512 + (hd + 1) * 128], lhsT=kg.ap[0:64, hd, :], rhs=vnew.ap[0:64, hd, :], start=True, stop=True),
                           reads=[kg, vnew], writes=[BK[bk]])
                    op("pool", lambda E, m8=m8: E.tensor_tensor(out=S.ap, in0=S.ap, in1=m8.ap[:, 48:56].unsqueeze(2).to_broadcast([128, 8, 128]), op=ALU.mult),
                       reads=[m8, S], writes=[S])
                    op("dve", lambda E: E.tensor_tensor(out=S.ap, in0=S.ap, in1=psum[:, 6 * 512:8 * 512].rearrange("p (h x) -> p h x", h=8), op=ALU.add),
                       reads=[S], writes=[S, BK[6], BK[7]])
                    op("act", lambda E: E.activation(out=Sb.ap, in_=S.ap, func=AF.Copy), reads=[S], writes=[Sb])
            phase_reset()

        if "C" in phases:
            T = ffn_bufs()
            load_wd(T, "ffn2_w_down")
            wst = alloc("wstc", 1024)
            wo = alloc("wo", 8 * 1024, BF16)
            wo.ap = wo.ap.rearrange("p (c f) -> p c f", c=8)
            for c in range(8):
                dma(wst[:], W["w_out"][c * 128:(c + 1) * 128, :], writes=[wst], sb=wst)
                sc = small[:, 5 + c:6 + c] if c < 4 else small[:, 9:10]
                op("dve", lambda E, c=c, sc=sc: E.tensor_scalar(out=wo.ap[:, c, :], in0=wst[:], scalar1=sc, scalar2=None, op0=ALU.mult),
                   reads=[wst, small], writes=[wo])
            ya = alloc("ya", 4 * 512)
            ya.ap = ya.ap.rearrange("p (j e) -> p j e", j=4)
            ofb = [alloc("ofb%d" % i, 4 * 512) for i in range(2)]
            for b in ofb:
                b.ap = b.ap.rearrange("p (j e) -> p j e", j=4)
            zz = alloc("zz", 4 * 512)
            zz.ap = zz.ap.rearrange("p (j e) -> p j e", j=4)
            mixb = [alloc("mixb%d" % i, 1024, BF16) for i in range(2)]
            sq = T["tmp"][1]
            ssc = alloc("ssc", 32)
            for s, L in SEQ:
                for ti in range(L // 512):
                    r0 = ti * 512
                    rows = slice(r0, r0 + 512)
                    h = T["h"]
                    dma(h.ap, H1[s][rows, :].rearrange("(j p) d -> p j d", p=128), writes=T["hb"], sb=h)
                    for hb in T["hb"]:
                        hb.lw = (id(h), h.semcnt)
                    dma(ya.ap, YA[s][rows, :].rearrange("(j p) e -> p j e", p=128), writes=[ya], sb=ya)
                    dma(ofb[0].ap, OF[s][0, rows, :].rearrange("(j p) e -> p j e", p=128), writes=[ofb[0]], sb=ofb[0])
                    dma(ofb[1].ap, OF[s][1, rows, :].rearrange("(j p) e -> p j e", p=128), writes=[ofb[1]], sb=ofb[1])
                    dma(zz.ap, ZS[s][rows, :].rearrange("(j p) e -> p j e", p=128), writes=[zz], sb=zz)
                    o = ofb[0]
                    op("pool", lambda E: E.tensor_tensor(out=o.ap, in0=o.ap, in1=ofb[1].ap, op=ALU.add), reads=[ofb[1], o], writes=[o])
                    e2 = ofb[1]
                    op("act", lambda E: E.activation(out=e2.ap, in_=zz.ap, func=AF.Exp, scale=-1.0), reads=[zz], writes=[e2])
                    op("act", lambda E: E.activation(out=e2.ap, in_=e2.ap, func=AF.Ln, bias=1.0), reads=[e2], writes=[e2])
                    op("act", lambda E: E.activation(out=e2.ap, in_=e2.ap, func=AF.Exp, scale=-1.0), reads=[e2], writes=[e2])
                    op("pool", lambda E: E.tensor_tensor(out=zz.ap, in0=zz.ap, in1=e2.ap, op=ALU.mult), reads=[e2, zz], writes=[zz])
                    for j in range(4):
                        op("dve", lambda E, j=j: E.tensor_tensor(out=sq[:, 0:512], in0=o.ap[:, j, :], in1=o.ap[:, j, :], op=ALU.mult), reads=[o], writes=[sq])
                        op("dve", lambda E, j=j: E.tensor_reduce(out=ssc[:, j * 8:j * 8 + 4], in_=sq.ap[:, 0:512].rearrange("p (h d) -> p h d", h=4), axis=AX.X, op=ALU.add),
                           reads=[sq], writes=[ssc])
                        op("act", lambda E, j=j: E.activation(out=T["junk"][:, 0:512], in_=ya.ap[:, j, :], func=AF.Square, accum_out=ssc[:, j * 8 + 4:j * 8 + 5]),
                           reads=[ya], writes=[ssc])
                    sv = ssc.ap[:, 0:32].rearrange("p (j k) -> p j k", j=4)
                    op("dve", lambda E, sv=sv: E.tensor_scalar(out=sv[:, :, 0:4], in0=sv[:, :, 0:4], scalar1=1.0 / 128, scalar2=EPS, op0=ALU.mult, op1=ALU.add), reads=[ssc], writes=[ssc])
                    op("dve", lambda E, sv=sv: E.tensor_scalar(out=sv[:, :, 4:5], in0=sv[:, :, 4:5], scalar1=1.0 / 512, scalar2=EPS, op0=ALU.mult, op1=ALU.add), reads=[ssc], writes=[ssc])
                    op("act", lambda E, sv=sv: E.activation(out=sv[:, :, 0:5], in_=sv[:, :, 0:5], func=AF.Ln), reads=[ssc], writes=[ssc])
                    op("act", lambda E, sv=sv: E.activation(out=sv[:, :, 0:5], in_=sv[:, :, 0:5], func=AF.Exp, scale=-0.5), reads=[ssc], writes=[ssc])
                    xT = T["xT"]
                    for j in range(4):
                        mb = mixb[j % 2]
                        op("act", lambda E, j=j, mb=mb: E.activation(out=mb[:, 0:512], in_=ya.ap[:, j, :], func=AF.Copy, scale=ssc[:, j * 8 + 4:j * 8 + 5]),
                           reads=[ya, ssc], writes=[mb])
                        op("dve", lambda E, j=j: E.tensor_tensor(out=o.ap[:, j, :].rearrange("p (h d) -> p h d", h=4), in0=o.ap[:, j, :].rearrange("p (h d) -> p h d", h=4),
                                                                 in1=ssc.ap[:, j * 8:j * 8 + 4].unsqueeze(2).to_broadcast([128, 4, 128]), op=ALU.mult), reads=[ssc, o], writes=[o])
                        op("pool", lambda E, j=j, mb=mb: E.tensor_tensor(out=mb[:, 512:1024], in0=o.ap[:, j, :], in1=zz.ap[:, j, :], op=ALU.mult), reads=[o, zz, mb], writes=[mb])
                        bk = j % 2
                        for dc in range(8):
                            op("pe", lambda E, dc=dc, bk=bk, mb=mb: E.transpose(out=pb(bk, dc * 128, (dc + 1) * 128), in_=mb[:, dc * 128:(dc + 1) * 128], identity=identb[:]),
                               reads=[mb, identb], writes=[BK[bk]])
                        op("dve", lambda E, j=j, bk=bk: E.tensor_copy(out=xT.ap[:, :, j * 128:(j + 1) * 128], in_=pb(bk).rearrange("p (c t) -> p c t", c=8)),
                           writes=[xT, BK[bk]])
                    ss2 = T["ss2"]
                    for j in range(4):
                        b0 = 2 if j % 2 == 0 else 4
                        for hf in range(2):
                            for c in range(8):
                                op("pe", lambda E, c=c, hf=hf, b0=b0, j=j: E.matmul(pf(b0 + hf), lhsT=xT.ap[:, c, j * 128:(j + 1) * 128], rhs=wo.ap[:, c, hf * 512:(hf + 1) * 512],
                                                                                     start=(c == 0), stop=(c == 7)), reads=[xT, wo], writes=[BK[b0 + hf]])
                        yps = psum[:, b0 * 512:(b0 + 2) * 512]
                        op("act", lambda E, j=j, yps=yps: E.activation(out=T["junk"][:], in_=yps, func=AF.Square, accum_out=ss2[:, j:j + 1]), writes=[ss2, BK[b0], BK[b0 + 1]])
                        op("dve", lambda E, j=j: E.tensor_scalar(out=ss2[:, 4 + j:5 + j], in0=ss2[:, j:j + 1], scalar1=1.0 / D, scalar2=EPS, op0=ALU.mult, op1=ALU.add), reads=[ss2], writes=[ss2])
                        op("act", lambda E, j=j: E.activation(out=ss2[:, 4 + j:5 + j], in_=ss2[:, 4 + j:5 + j], func=AF.Ln), reads=[ss2], writes=[ss2])
                        op("act", lambda E, j=j: E.activation(out=ss2[:, 4 + j:5 + j], in_=ss2[:, 4 + j:5 + j], func=AF.Exp, scale=-0.5), reads=[ss2], writes=[ss2])
                        tmp = T["tmp"][j % 2]
                        op("dve", lambda E, j=j, yps=yps, tmp=tmp: E.scalar_tensor_tensor(out=tmp[:], in0=yps, scalar=ss2[:, 4 + j:5 + j], in1=gB["mix_post_g"][:], op0=ALU.mult, op1=ALU.mult),
                           reads=[ss2, gB["mix_post_g"]], writes=[tmp, BK[b0], BK[b0 + 1]])
                        op("pool", lambda E, j=j, tmp=tmp: E.tensor_tensor(out=T["hs"][j], in0=T["hs"][j], in1=tmp[:], op=ALU.add), reads=[tmp], writes=[T["hb"][j]])
                    ffn(T["hb"], T["hs"], "ffn2_pre_g", WB["ffn2_w_gate"], WB["ffn2_w_up"], T["wd"], gB["ffn2_post_g"], T)
                    ss = T["ss"]
                    for j in range(4):
                        op("act", lambda E, j=j: E.activation(out=T["junk"][:], in_=T["hs"][j], func=AF.Square, accum_out=ss[:, j:j + 1]), reads=[T["hb"][j]], writes=[ss])
                    rstd_chain(ss, 4, D)
                    for j in range(4):
                        e1 = "dve"
                        op(e1, lambda E, j=j: E.scalar_tensor_tensor(out=T["hs"][j], in0=T["hs"][j], scalar=ss[:, j:j + 1], in1=gB["final_norm_g"][:], op0=ALU.mult, op1=ALU.mult),
                           reads=[ss, gB["final_norm_g"]], writes=[T["hb"][j]])
                    dma(Y[s][rows, :].rearrange("(j p) d -> p j d", p=128), h.ap, reads=T["hb"], sb=h, queue="pool")
            phase_reset()
        P.barrier()
        P.emit()
        stats = dict(nins=P.nins, nwaits=P.nwaits, nsem=len(P.dmabufs) + 5)
    return nc, stats


def rope_table(L):
    inv = 10000.0 ** (-np.arange(0, 32, 2, dtype=np.float32) / 32)
    ang = np.arange(L, dtype=np.float32)[:, None] * inv[None, :].astype(np.float32)
    c, s = np.cos(ang).astype(np.float32), np.sin(ang).astype(np.float32)
    return np.ascontiguousarray(np.concatenate([c, c, -s, s], axis=1).astype(np.float32))


def gdn_masks():
    i = np.arange(64)
    m = np.zeros((64, 5, 8, 64), np.float32)
    for hd in range(8):
        fwd = hd < 4
        al = (i[:, None] > i[None, :]) if fwd else (i[:, None] < i[None, :])
        m[:, 0, hd, :] = np.where(al, 0.0, NEG)
        al = (i[None, :] > i[:, None]) if fwd else (i[None, :] < i[:, None])
        m[:, 1, hd, :] = np.where(al, 0.0, NEG)
        al = (i[None, :] >= i[:, None]) if fwd else (i[None, :] <= i[:, None])
        m[:, 2, hd, :] = np.where(al, 0.0, NEG)
    m[:, 3, 0, :] = (i[:, None] <= i[None, :]).astype(np.float32)
    m[:, 3, 1, :] = (i[:, None] >= i[None, :]).astype(np.float32)
    return m


_CACHE = {}


def kernel(**inputs):
    xp = np.asarray(inputs["x_prompt"], np.float32)
    xs = np.asarray(inputs["x_sample"], np.float32)
    B, Lp, _ = xp.shape
    Ls = xs.shape[1]
    assert B == 8 and xs.shape[0] == 8
    key = (Lp, Ls)
    if key not in _CACHE:
        _CACHE[key] = build(Lp, Ls)[0]
    nc = _CACHE[key]
    shared = {n: np.ascontiguousarray(np.asarray(inputs[n], np.float32).reshape(WSHAPES[n])) for n in WNAMES}
    shared["rope_tab"] = rope_table(max(Lp, Ls))
    shared["gdn_masks"] = gdn_masks()
    in_maps = []
    for c in range(8):
        m = dict(shared)
        m["x_p"] = np.ascontiguousarray(xp[c])
        m["x_s"] = np.ascontiguousarray(xs[c])
        in_maps.append(m)
    res = run_bass_kernel_spmd(nc, in_maps, core_ids=list(range(8)))
    yp = np.stack([np.asarray(r["y_p"], np.float32) for r in res.results], 0)
    ys = np.stack([np.asarray(r["y_s"], np.float32) for r in res.results], 0)
    return (yp, ys)
```

```python
from contextlib import ExitStack
import numpy as np
import concourse.bass as bass
import concourse.mybir as mybir
from concourse.bass_utils import run_bass_kernel_spmd

F32 = mybir.dt.float32
BF16 = mybir.dt.bfloat16
ALU = mybir.AluOpType
AF = mybir.ActivationFunctionType
AX = mybir.AxisListType

D = 1024
DFF = 2816
NF = DFF // 128
INC = 2736
EPS = 1e-6
NEG = -30000.0
ENG = ("pe", "act", "dve", "pool", "sp")


class Buf:
    __slots__ = ("name", "ap", "lw", "rd", "sem", "semcnt")

    def __init__(self, name, ap):
        self.name = name
        self.ap = ap
        self.lw = None
        self.rd = {}
        self.sem = None
        self.semcnt = 0

    def __getitem__(self, k):
        return self.ap[k]


class Prog:
    def __init__(self, nc, stack):
        self.nc = nc
        self.stack = stack
        self.q = {e: [] for e in ENG}
        self.cnt = {e: 0 for e in ENG}
        self.known = {e: {} for e in ENG}
        self.hist = {}
        self.esem = {e: stack.enter_context(nc.semaphore("s_" + e)) for e in ENG}
        self.semobj = {e: self.esem[e] for e in ENG}
        self.dmabufs = []
        self.nwaits = 0
        self.nins = 0
        self.E = {"pe": nc.tensor, "act": nc.scalar, "dve": nc.vector, "pool": nc.gpsimd, "sp": nc.sync}

    def _need(self, deps, tok):
        if tok is None:
            return
        k, v = tok
        if deps.get(k, 0) < v:
            deps[k] = v

    def _collect(self, reads, writes):
        deps = {}
        for b in reads:
            self._need(deps, b.lw)
        for b in writes:
            self._need(deps, b.lw)
            for k, v in b.rd.items():
                self._need(deps, (k, v))
        return deps

    def _emit_waits(self, eng, deps):
        kn = self.known[eng]
        new = None
        for k, v in deps.items():
            if k == eng:
                if eng == "pe" or eng == "sp":
                    continue
                if self.cnt[eng] - v > 1:
                    continue
            cur = new if new is not None else kn
            if cur.get(k, 0) >= v:
                continue
            sem = self.semobj[k]
            self.E[eng].wait_ge(sem, v)
            self.nwaits += 1
            if new is None:
                new = dict(kn)
            new[k] = v
            h = self.hist.get((k, v))
            if h:
                for k2, v2 in h.items():
                    if k2 != eng and new.get(k2, 0) < v2:
                        new[k2] = v2
        if new is not None:
            self.known[eng] = new

    def op(self, eng, fn, reads=(), writes=()):
        deps = self._collect(reads, writes)
        self._emit_waits(eng, deps)
        sem = self.esem[eng]
        self.cnt[eng] += 1
        v = self.cnt[eng]
        fn(self.E[eng]).then_inc(sem, 1)
        self.nins += 1
        tok = (eng, v)
        self.hist[tok] = self.known[eng]
        for b in writes:
            b.lw = tok
            b.rd = {}
        for b in reads:
            if b.rd.get(eng, 0) < v:
                b.rd[eng] = v
        return tok

    def dma(self, out_ap, in_ap, reads=(), writes=(), sb=None, queue="sp", **kw):
        deps = self._collect(reads, writes)
        self._emit_waits(queue, deps)
        if sb.sem is None:
            sb.sem = self.stack.enter_context(self.nc.semaphore("d%d_%s" % (len(self.dmabufs), sb.name)))
            self.semobj[id(sb)] = sb.sem
            self.dmabufs.append(sb)
        sb.semcnt += 16
        v = sb.semcnt
        sem = sb.sem
        self.E[queue].dma_start(out=out_ap, in_=in_ap, **kw).then_inc(sem, 16)
        self.nins += 1
        tok = (id(sb), v)
        self.hist[tok] = self.known[queue]
        for b in writes:
            b.lw = tok
            b.rd = {}
        for b in reads:
            if b.rd.get(tok[0], 0) < v:
                b.rd[tok[0]] = v
        return tok

    def barrier(self):
        deps = {}
        for e in ("pe", "act", "dve", "pool"):
            if self.cnt[e]:
                deps[e] = self.cnt[e]
        for b in self.dmabufs:
            deps[id(b)] = b.semcnt
        for e in ENG:
            d = {k: v for k, v in deps.items() if k != e}
            self._emit_waits(e, d)

    def emit(self):
        pass


WNAMES = ["ffn1_pre_g", "ffn1_w_gate", "ffn1_w_up", "ffn1_w_down", "ffn1_post_g", "mix_pre_g", "w_in",
          "mla_q_norm_g", "mla_w_uq", "mla_kv_norm_g", "mla_w_ukv", "mla_out_norm_g", "gdn_conv_w",
          "gdn_a_log", "gdn_dt_bias", "gdn_out_norm_g", "w_out", "mix_post_g", "ffn2_pre_g",
          "ffn2_w_gate", "ffn2_w_up", "ffn2_w_down", "ffn2_post_g", "final_norm_g"]
WSHAPES = {"ffn1_pre_g": [1, D], "ffn1_w_gate": [D, DFF], "ffn1_w_up": [D, DFF], "ffn1_w_down": [DFF, D],
           "ffn1_post_g": [1, D], "mix_pre_g": [1, D], "w_in": [D, INC], "mla_q_norm_g": [1, 384],
           "mla_w_uq": [384, 768], "mla_kv_norm_g": [1, 256], "mla_w_ukv": [256, 1024],
           "mla_out_norm_g": [1, 512], "gdn_conv_w": [5, 1536], "gdn_a_log": [1, 8], "gdn_dt_bias": [1, 8],
           "gdn_out_norm_g": [1, 128], "w_out": [D, D], "mix_post_g": [1, D], "ffn2_pre_g": [1, D],
           "ffn2_w_gate": [D, DFF], "ffn2_w_up": [D, DFF], "ffn2_w_down": [DFF, D], "ffn2_post_g": [1, D],
           "final_norm_g": [1, D]}


def build(Lp, Ls, debug=False, phases="S A M B G H C"):
    phases = phases.split()
    nc = bass.Bass("TRN2", target_bir_lowering=False)
    Lmax = max(Lp, Ls)
    SEQ = (("p", Lp), ("s", Ls))

    def din(name, shape, dt=F32):
        return nc.dram_tensor(name, list(shape), dt, kind="ExternalInput").ap()

    def dscr(name, shape, dt=F32):
        return nc.dram_tensor(name, list(shape), dt, kind="ExternalOutput" if debug else "Internal").ap()

    X = {"p": din("x_p", [Lp, D]), "s": din("x_s", [Ls, D])}
    W = {n: din(n, WSHAPES[n]) for n in WNAMES}
    ROPE = din("rope_tab", [Lmax, 64])
    GMASK = din("gdn_masks", [64, 5, 8, 64])
    Y = {"p": nc.dram_tensor("y_p", [Lp, D], F32, kind="ExternalOutput").ap(),
         "s": nc.dram_tensor("y_s", [Ls, D], F32, kind="ExternalOutput").ap()}
    WB = {n: dscr("bf_" + n, WSHAPES[n], BF16) for n in
          ["ffn1_w_gate", "ffn1_w_up", "ffn1_w_down", "w_in", "ffn2_w_gate", "ffn2_w_up", "ffn2_w_down"]}
    H1 = {s: dscr("h1_" + s, [L, D]) for s, L in SEQ}
    LAT = {s: dscr("lat_" + s, [L, 672]) for s, L in SEQ}
    QKVT = {s: dscr("qkvT_" + s, [1536, L]) for s, L in SEQ}
    ZS = {s: dscr("z_" + s, [L, 512]) for s, L in SEQ}
    GBL = {s: dscr("gbl_" + s, [2, L, 12]) for s, L in SEQ}
    QT = {s: dscr("QT_" + s, [8, 96, L], BF16) for s, L in SEQ}
    KT = {s: dscr("KT_" + s, [8, 96, L], BF16) for s, L in SEQ}
    VV = {s: dscr("V_" + s, [8, 128, L // 128, 64], BF16) for s, L in SEQ}
    YA = {s: dscr("ya_" + s, [L, 512]) for s, L in SEQ}
    TOK = {s: dscr("tok_" + s, [L, 1536], BF16) for s, L in SEQ}
    OF = {s: dscr("of_" + s, [2, L, 512]) for s, L in SEQ}

    with ExitStack() as st:
        P = Prog(nc, st)
        ARENA = 52800
        arena = st.enter_context(nc.sbuf_tensor("arena", [128, ARENA], F32))
        psum = st.enter_context(nc.psum_tensor("psum", [128, 4096], F32))
        psum_bf = psum.bitcast(BF16)
        BK = [Buf("bank%d" % i, psum[:, i * 512:(i + 1) * 512]) for i in range(8)]

        def pf(i, a=0, b=512):
            return psum[:, i * 512 + a:i * 512 + b]

        def pb(i, a=0, b=1024):
            return psum_bf[:, i * 1024 + a:i * 1024 + b]

        off = [0]
        perm_end = [0]

        def alloc(name, ncols, dt=F32, parts=128):
            n32 = ncols if dt == F32 else (ncols + 1) // 2
            assert off[0] + n32 <= ARENA, ("SBUF arena overflow", name, off[0] + n32)
            a = arena[0:parts, off[0]:off[0] + n32]
            off[0] += n32
            if dt != F32:
                a = a.bitcast(dt)[:, 0:ncols]
            return Buf(name, a)

        def phase_reset():
            P.barrier()
            off[0] = perm_end[0]

        op = P.op
        dma = P.dma

        identf = alloc("identf", 128)
        identb = alloc("identb", 128, BF16)
        onesf = alloc("onesf", 128)
        gT = {n: alloc("gT_" + n, 8) for n in ("ffn1_pre_g", "mix_pre_g", "ffn2_pre_g")}
        gB = {n: alloc("gB_" + n, D) for n in ("ffn1_post_g", "mix_post_g", "ffn2_post_g", "final_norm_g")}
        small = alloc("small", 64)
        perm_end[0] = off[0]

        op("pool", lambda E: E.memset(identf[:], 0.0), writes=[identf])
        op("pool", lambda E: E.affine_select(out=identf[:], in_=identf[:], pattern=[[-1, 128]], compare_op=ALU.not_equal,
                                             fill=1.0, base=0, channel_multiplier=1), reads=[identf], writes=[identf])
        op("dve", lambda E: E.tensor_copy(out=identb[:], in_=identf[:]), reads=[identf], writes=[identb])
        op("pool", lambda E: E.memset(onesf[:], 1.0), writes=[onesf])
        for n, b in gT.items():
            dma(b[:], W[n].rearrange("o (c p) -> p (o c)", p=128), writes=[b], sb=b, allow_slow_non_contiguous=True)
        for n, b in gB.items():
            dma(b[:], W[n].broadcast_to([128, D]), writes=[b], sb=b)
        for n in ("ffn1_post_g", "ffn2_post_g"):
            b = gB[n]
            op("pool", lambda E, b=b: E.tensor_scalar(out=b[:], in0=b[:], scalar1=0.5, scalar2=None, op0=ALU.mult),
               reads=[b], writes=[b])
        sm_q = Buf("sm_q", small.ap)
        dma(small[:, 0:3], W["mla_q_norm_g"].rearrange("o (c p) -> p (o c)", p=128), writes=[small], sb=small,
            allow_slow_non_contiguous=True)
        dma(small[:, 3:5], W["mla_kv_norm_g"].rearrange("o (c p) -> p (o c)", p=128), reads=[small], sb=small,
            allow_slow_non_contiguous=True)
        dma(small[:, 5:9], W["mla_out_norm_g"].rearrange("o (c p) -> p (o c)", p=128), reads=[small], sb=small,
            allow_slow_non_contiguous=True)
        dma(small[:, 9:10], W["gdn_out_norm_g"].rearrange("o (c p) -> p (o c)", p=128), reads=[small], sb=small,
            allow_slow_non_contiguous=True)
        dma(small[:, 16:24], W["gdn_dt_bias"].broadcast_to([128, 8]), reads=[small], sb=small)
        dma(small[:, 24:32], W["gdn_a_log"].broadcast_to([128, 8]), reads=[small], sb=small)
        small.lw = (id(small), small.semcnt)
        op("act", lambda E: E.activation(out=small[:, 24:32], in_=small[:, 24:32], func=AF.Exp), reads=[small], writes=[small])
        op("dve", lambda E: E.tensor_scalar(out=small[:, 24:32], in0=small[:, 24:32], scalar1=-1.0, scalar2=None, op0=ALU.mult),
           reads=[small], writes=[small])

        def rstd_chain(ss, k, n, eps=EPS, post=None):
            op("dve", lambda E: E.tensor_scalar(out=ss[:, 0:k], in0=ss[:, 0:k], scalar1=1.0 / n, scalar2=eps, op0=ALU.mult, op1=ALU.add),
               reads=[ss], writes=[ss])
            op("act", lambda E: E.activation(out=ss[:, 0:k], in_=ss[:, 0:k], func=AF.Ln), reads=[ss], writes=[ss])
            op("act", lambda E: E.activation(out=ss[:, 0:k], in_=ss[:, 0:k], func=AF.Exp, scale=-0.5), reads=[ss], writes=[ss])

        cast_rr = [0]

        def cast(out_ap, in_ap, reads, writes):
            e = ("dve", "pool", "act")[cast_rr[0] % 3]
            cast_rr[0] += 1
            if e == "act":
                op("act", lambda E: E.activation(out=out_ap, in_=in_ap, func=AF.Copy), reads=reads, writes=writes)
            else:
                op(e, lambda E: E.tensor_copy(out=out_ap, in_=in_ap), reads=reads, writes=writes)

        if "S" in phases:
            stf = [alloc("stf%d" % i, DFF) for i in range(2)]
            stb = [alloc("stb%d" % i, DFF, BF16) for i in range(2)]
            it = 0
            for n in WB:
                K_, N_ = WSHAPES[n]
                for kc in range(K_ // 128):
                    f, b = stf[it % 2], stb[it % 2]
                    dma(f[:, 0:N_], W[n][kc * 128:(kc + 1) * 128, :], writes=[f], sb=f)
                    cast(b[:, 0:N_], f[:, 0:N_], [f], [b])
                    dma(WB[n][kc * 128:(kc + 1) * 128, :], b[:, 0:N_], reads=[b], sb=b, queue="pool")
                    it += 1
        phase_reset()

        def norm_T(srcs, src_bufs, gTb, xT, ss, xnb, junk):
            for j in range(4):
                op("act", lambda E, j=j: E.activation(out=junk[:], in_=srcs[j], func=AF.Square, accum_out=ss[:, j:j + 1]),
                   reads=[src_bufs[j]], writes=[ss])
            rstd_chain(ss, 4, D)
            for j in range(4):
                xb = xnb[j % 2]
                if j % 2 == 0:
                    op("dve", lambda E, j=j, xb=xb: E.tensor_scalar(out=xb[:], in0=srcs[j], scalar1=ss[:, j:j + 1], scalar2=None, op0=ALU.mult),
                       reads=[src_bufs[j], ss], writes=[xb])
                else:
                    op("act", lambda E, j=j, xb=xb: E.activation(out=xb[:], in_=srcs[j], func=AF.Copy, scale=ss[:, j:j + 1]),
                       reads=[src_bufs[j], ss], writes=[xb])
                bk = j % 2
                for dc in range(8):
                    op("pe", lambda E, dc=dc, bk=bk, xb=xb: E.transpose(out=pb(bk, dc * 128, (dc + 1) * 128), in_=xb[:, dc * 128:(dc + 1) * 128], identity=identb[:]),
                       reads=[xb, identb], writes=[BK[bk]])
                op("dve", lambda E, j=j, bk=bk: E.tensor_tensor(
                    out=xT.ap[:, :, j * 128:(j + 1) * 128], in0=pb(bk).rearrange("p (c t) -> p c t", c=8),
                    in1=gTb[:, 0:8].unsqueeze(2).to_broadcast([128, 8, 128]), op=ALU.mult),
                    reads=[gTb], writes=[xT, BK[bk]])

        SLABS = [(0, 512), (512, 512), (1024, 512), (1536, 512), (2048, 512), (2560, 256)]

        def ffn(hbufs, hsrcs, pre_g, wg, wu, wd_res, post_bc, T):
            norm_T(hsrcs, hbufs, gT[pre_g], T["xT"], T["ss"], T["xnb"], T["junk"])
            xT, act = T["xT"], T["act"]
            for si, (c0, w) in enumerate(SLABS):
                sg = T["slab"][T["slabi"] % 3]
                su = T["slab"][(T["slabi"] + 1) % 3]
                T["slabi"] += 2
                dma(sg.ap[:, :, 0:w], wg[:, c0:c0 + w].rearrange("(dc p) f -> p dc f", p=128), writes=[sg], sb=sg)
                dma(su.ap[:, :, 0:w], wu[:, c0:c0 + w].rearrange("(dc p) f -> p dc f", p=128), writes=[su], sb=su)
                for fl in range(w // 128):
                    f = c0 // 128 + fl
                    g_, u_ = 2 + f % 2, 4 + f % 2
                    for dc in range(8):
                        op("pe", lambda E, dc=dc, fl=fl, g_=g_, sg=sg: E.matmul(pf(g_), lhsT=sg.ap[:, dc, fl * 128:(fl + 1) * 128], rhs=xT.ap[:, dc, :],
                                                                                  start=(dc == 0), stop=(dc == 7)), reads=[sg, xT], writes=[BK[g_]])
                    for dc in range(8):
                        op("pe", lambda E, dc=dc, fl=fl, u_=u_, su=su: E.matmul(pf(u_), lhsT=su.ap[:, dc, fl * 128:(fl + 1) * 128], rhs=xT.ap[:, dc, :],
                                                                                  start=(dc == 0), stop=(dc == 7)), reads=[su, xT], writes=[BK[u_]])
                    et = T["etmp"][f % 2]
                    op("act", lambda E, et=et, g_=g_: E.activation(out=et[:], in_=pf(g_), func=AF.Exp, scale=-1.0), writes=[et, BK[g_]])
                    op("act", lambda E, et=et: E.activation(out=et[:], in_=et[:], func=AF.Ln, bias=1.0), reads=[et], writes=[et])
                    op("act", lambda E, et=et: E.activation(out=et[:], in_=et[:], func=AF.Exp, scale=-1.0), reads=[et], writes=[et])
                    op("dve", lambda E, et=et, g_=g_: E.tensor_tensor(out=et[:], in0=et[:], in1=pf(g_), op=ALU.mult), reads=[et], writes=[et, BK[g_]])
                    op("dve", lambda E, et=et, u_=u_, f=f: E.tensor_tensor(out=act.ap[:, f, :], in0=et[:], in1=pf(u_), op=ALU.mult),
                       reads=[et], writes=[T["actc"][f], BK[u_]])
            ss2 = T["ss2"]
            for j in range(4):
                b0 = 2 if j % 2 == 0 else 4
                for hf in range(2):
                    for f in range(NF):
                        op("pe", lambda E, f=f, hf=hf, b0=b0, j=j: E.matmul(pf(b0 + hf), lhsT=act.ap[:, f, j * 128:(j + 1) * 128],
                                                                             rhs=wd_res.ap[:, f, hf * 512:(hf + 1) * 512], start=(f == 0), stop=(f == NF - 1)),
                           reads=[T["actc"][f], wd_res], writes=[BK[b0 + hf]])
                yps = psum[:, b0 * 512:(b0 + 2) * 512]
                op("act", lambda E, j=j, yps=yps: E.activation(out=T["junk"][:], in_=yps, func=AF.Square, accum_out=ss2[:, j:j + 1]),
                   writes=[ss2, BK[b0], BK[b0 + 1]])
                op("dve", lambda E, j=j: E.tensor_scalar(out=ss2[:, 4 + j:5 + j], in0=ss2[:, j:j + 1], scalar1=1.0 / D, scalar2=EPS, op0=ALU.mult, op1=ALU.add),
                   reads=[ss2], writes=[ss2])
                op("act", lambda E, j=j: E.activation(out=ss2[:, 4 + j:5 + j], in_=ss2[:, 4 + j:5 + j], func=AF.Ln), reads=[ss2], writes=[ss2])
                op("act", lambda E, j=j: E.activation(out=ss2[:, 4 + j:5 + j], in_=ss2[:, 4 + j:5 + j], func=AF.Exp, scale=-0.5), reads=[ss2], writes=[ss2])
                tmp = T["tmp"][j % 2]
                op("dve", lambda E, j=j, yps=yps, tmp=tmp: E.scalar_tensor_tensor(out=tmp[:], in0=yps, scalar=ss2[:, 4 + j:5 + j], in1=post_bc[:],
                                                                                    op0=ALU.mult, op1=ALU.mult),
                   reads=[ss2, post_bc], writes=[tmp, BK[b0], BK[b0 + 1]])
                op("pool", lambda E, j=j, tmp=tmp: E.tensor_tensor(out=hsrcs[j], in0=hsrcs[j], in1=tmp[:], op=ALU.add),
                   reads=[tmp], writes=[hbufs[j]])

        def ffn_bufs():
            T = {}
            T["wd"] = alloc("wd_res", NF * D, BF16)
            T["wd"].ap = T["wd"].ap.rearrange("p (f d) -> p f d", f=NF)
            T["xT"] = alloc("xT", 8 * 512, BF16)
            T["xT"].ap = T["xT"].ap.rearrange("p (c t) -> p c t", c=8)
            T["act"] = alloc("act", NF * 512, BF16)
            T["act"].ap = T["act"].ap.rearrange("p (f t) -> p f t", f=NF)
            T["actc"] = [Buf("act%d" % f, T["act"].ap[:, f, :]) for f in range(NF)]
            T["slab"] = []
            for i in range(3):
                b = alloc("slab%d" % i, 8 * 512, BF16)
                b.ap = b.ap.rearrange("p (c f) -> p c f", c=8)
                T["slab"].append(b)
            T["slabi"] = 0
            T["etmp"] = [alloc("etmp%d" % i, 512) for i in range(2)]
            T["tmp"] = [alloc("tmp%d" % i, D) for i in range(2)]
            T["xnb"] = [alloc("xnb%d" % i, D, BF16) for i in range(2)]
            T["junk"] = alloc("junk", D, BF16)
            T["ss"] = alloc("ss", 8)
            T["ss2"] = alloc("ss2", 8)
            T["h"] = alloc("h", 4 * D)
            T["h"].ap = T["h"].ap.rearrange("p (j d) -> p j d", j=4)
            T["hb"] = [Buf("h%d" % j, T["h"].ap[:, j, :]) for j in range(4)]
            T["hs"] = [T["h"].ap[:, j, :] for j in range(4)]
            return T

        def load_wd(T, name):
            wd = T["wd"]
            for f0 in range(0, NF, 2):
                dma(wd.ap[:, f0:f0 + 2, :], WB[name][f0 * 128:(f0 + 2) * 128, :].rearrange("(f p) d -> p f d", p=128),
                    writes=[wd] if f0 == 0 else [], reads=[] if f0 == 0 else [], sb=wd)
            wd.lw = (id(wd), wd.semcnt)

        if "A" in phases:
            T = ffn_bufs()
            load_wd(T, "ffn1_w_down")
            pst = [alloc("pst%d" % i, 1200) for i in range(2)]
            qst = [alloc("qst%d" % i, 512) for i in range(3)]
            gst = [alloc("gst%d" % i, 64) for i in range(2)]
            pi = 0
            qi = 0
            for s, L in SEQ:
                for ti in range(L // 512):
                    r0 = ti * 512
                    h = T["h"]
                    dma(h.ap, X[s][r0:r0 + 512, :].rearrange("(j p) d -> p j d", p=128), writes=T["hb"], sb=h)
                    for hb in T["hb"]:
                        hb.lw = (id(h), h.semcnt)
                    ffn(T["hb"], T["hs"], "ffn1_pre_g", WB["ffn1_w_gate"], WB["ffn1_w_up"], T["wd"], gB["ffn1_post_g"], T)
                    dma(H1[s][r0:r0 + 512, :].rearrange("(j p) d -> p j d", p=128), h.ap, reads=T["hb"], sb=h, queue="pool")
                    norm_T(T["hs"], T["hb"], gT["mix_pre_g"], T["xT"], T["ss"], T["xnb"], T["junk"])
                    xT = T["xT"]
                    for q3 in range(3):
                        sl = T["slab"][T["slabi"] % 3]
                        T["slabi"] += 1
                        c0 = 672 + q3 * 512
                        dma(sl.ap[:, :, 0:512], WB["w_in"][:, c0:c0 + 512].rearrange("(dc p) f -> p dc f", p=128), writes=[sl], sb=sl)
                        for fl in range(4):
                            ch = q3 * 4 + fl
                            bk = 2 + ch % 2
                            for dc in range(8):
                                op("pe", lambda E, dc=dc, fl=fl, bk=bk, sl=sl: E.matmul(pf(bk), lhsT=sl.ap[:, dc, fl * 128:(fl + 1) * 128], rhs=xT.ap[:, dc, :],
                                                                                          start=(dc == 0), stop=(dc == 7)), reads=[sl, xT], writes=[BK[bk]])
                            qs = qst[qi % 3]
                            qi += 1
                            if ch % 2 == 0:
                                op("act", lambda E, qs=qs, bk=bk: E.activation(out=qs[:], in_=pf(bk), func=AF.Copy), writes=[qs, BK[bk]])
                            else:
                                op("dve", lambda E, qs=qs, bk=bk: E.tensor_copy(out=qs[:], in_=pf(bk)), writes=[qs, BK[bk]])
                            dma(QKVT[s][ch * 128:(ch + 1) * 128, r0:r0 + 512], qs[:], reads=[qs], sb=qs, queue="pool")
                    sA = T["slab"][T["slabi"] % 3]
                    sB = T["slab"][(T["slabi"] + 1) % 3]
                    sC = T["slab"][(T["slabi"] + 2) % 3]
                    T["slabi"] += 3
                    dma(sA.ap[:, :, 0:512], WB["w_in"][:, 0:512].rearrange("(dc p) f -> p dc f", p=128), writes=[sA], sb=sA)
                    dma(sB.ap[:, :, 0:512], WB["w_in"][:, 2208:2720].rearrange("(dc p) f -> p dc f", p=128), writes=[sB], sb=sB)
                    dma(sC.ap[:, :, 0:160], WB["w_in"][:, 512:672].rearrange("(dc p) f -> p dc f", p=128), writes=[sC], sb=sC)
                    dma(sC.ap[:, :, 160:176], WB["w_in"][:, 2720:2736].rearrange("(dc p) f -> p dc f", p=128), reads=[sC], sb=sC)
                    sC.lw = (id(sC), sC.semcnt)
                    for j in range(4):
                        for dc in range(8):
                            lhs = xT.ap[:, dc, j * 128:(j + 1) * 128]
                            op("pe", lambda E, dc=dc, lhs=lhs: E.matmul(pf(6), lhsT=lhs, rhs=sA.ap[:, dc, 0:512], start=(dc == 0), stop=(dc == 7)),
                               reads=[sA, xT], writes=[BK[6]])
                        for dc in range(8):
                            lhs = xT.ap[:, dc, j * 128:(j + 1) * 128]
                            op("pe", lambda E, dc=dc, lhs=lhs: E.matmul(pf(7), lhsT=lhs, rhs=sB.ap[:, dc, 0:512], start=(dc == 0), stop=(dc == 7)),
                               reads=[sB, xT], writes=[BK[7]])
                        for dc in range(8):
                            lhs = xT.ap[:, dc, j * 128:(j + 1) * 128]
                            op("pe", lambda E, dc=dc, lhs=lhs: E.matmul(pf(0, 0, 176), lhsT=lhs, rhs=sC.ap[:, dc, 0:176], start=(dc == 0), stop=(dc == 7)),
                               reads=[sC, xT], writes=[BK[0]])
                        ps_ = pst[pi % 2]
                        gs_ = gst[pi % 2]
                        pi += 1
                        op("act", lambda E, ps_=ps_: E.activation(out=ps_[:, 0:512], in_=pf(6), func=AF.Copy), writes=[ps_, BK[6]])
                        op("dve", lambda E, ps_=ps_: E.tensor_copy(out=ps_[:, 672:1184], in_=pf(7)), reads=[ps_], writes=[ps_, BK[7]])
                        op("dve", lambda E, ps_=ps_: E.tensor_copy(out=ps_[:, 512:672], in_=pf(0, 0, 160)), reads=[ps_], writes=[ps_, BK[0]])
                        op("dve", lambda E, gs_=gs_: E.tensor_tensor(out=gs_[:, 0:8], in0=pf(0, 160, 168), in1=small[:, 16:24], op=ALU.add),
                           reads=[small], writes=[gs_, BK[0]])
                        op("act", lambda E, gs_=gs_: E.activation(out=gs_[:, 0:8], in_=gs_[:, 0:8], func=AF.Exp), reads=[gs_], writes=[gs_])
                        op("act", lambda E, gs_=gs_: E.activation(out=gs_[:, 8:16], in_=pf(0, 168, 176), func=AF.Exp, scale=-1.0), reads=[gs_], writes=[gs_, BK[0]])
                        op("act", lambda E, gs_=gs_: E.activation(out=gs_[:, 0:16], in_=gs_[:, 0:16], func=AF.Ln, bias=1.0), reads=[gs_], writes=[gs_])
                        gv = gs_.ap[:, 32:56].rearrange("p (d k) -> p d k", d=2)
                        op("dve", lambda E, gs_=gs_, gv=gv: E.tensor_tensor(out=gv[:, :, 0:4], in0=gs_.ap[:, 0:8].rearrange("p (d k) -> p d k", d=2),
                                                                            in1=small.ap[:, 24:32].rearrange("p (d k) -> p d k", d=2), op=ALU.mult),
                           reads=[gs_, small], writes=[gs_])
                        op("dve", lambda E, gs_=gs_, gv=gv: E.tensor_scalar(out=gv[:, :, 4:8], in0=gs_.ap[:, 8:16].rearrange("p (d k) -> p d k", d=2),
                                                                            scalar1=-1.0, scalar2=None, op0=ALU.mult), reads=[gs_], writes=[gs_])
                        op("act", lambda E, gs_=gs_, gv=gv: E.activation(out=gv[:, :, 8:12], in_=gs_.ap[:, 8:16].rearrange("p (d k) -> p d k", d=2),
                                                                         func=AF.Exp, scale=-1.0), reads=[gs_], writes=[gs_])
                        rr = slice(r0 + j * 128, r0 + (j + 1) * 128)
                        dma(LAT[s][rr, :], ps_[:, 0:672], reads=[ps_], sb=ps_, queue="pool")
                        dma(ZS[s][rr, :], ps_[:, 672:1184], reads=[ps_], sb=ps_, queue="pool")
                        dma(GBL[s][:, rr, :].rearrange("d p k -> p d k"), gv, reads=[gs_], sb=gs_, queue="pool")
            phase_reset()

        if "M" in phases:
            wst = alloc("wst", 1024)
            wuq = alloc("wuq", 3 * 768, BF16)
            wuq.ap = wuq.ap.rearrange("p (c f) -> p c f", c=3)
            wk = alloc("wk", 2 * 512, BF16)
            wk.ap = wk.ap.rearrange("p (c f) -> p c f", c=2)
            wv = alloc("wv", 2 * 512, BF16)
            wv.ap = wv.ap.rearrange("p (c f) -> p c f", c=2)
            for c in range(3):
                dma(wst[:, 0:768], W["mla_w_uq"][c * 128:(c + 1) * 128, :], writes=[wst], sb=wst)
                op("dve", lambda E, c=c: E.tensor_scalar(out=wuq.ap[:, c, :], in0=wst[:, 0:768], scalar1=small[:, c:c + 1], scalar2=None, op0=ALU.mult),
                   reads=[wst, small], writes=[wuq])
            for c in range(2):
                dma(wst[:, 0:1024], W["mla_w_ukv"][c * 128:(c + 1) * 128, :], writes=[wst], sb=wst)
                wv4 = wst.ap[:, 0:1024].rearrange("p (h e) -> p h e", h=8)
                op("dve", lambda E, c=c, wv4=wv4: E.tensor_scalar(out=wk.ap[:, c, :].rearrange("p (h e) -> p h e", h=8), in0=wv4[:, :, 0:64],
                                                                  scalar1=small[:, 3 + c:4 + c], scalar2=None, op0=ALU.mult), reads=[wst, small], writes=[wk])
                op("dve", lambda E, c=c, wv4=wv4: E.tensor_scalar(out=wv.ap[:, c, :].rearrange("p (h e) -> p h e", h=8), in0=wv4[:, :, 64:128],
                                                                  scalar1=small[:, 3 + c:4 + c], scalar2=None, op0=ALU.mult), reads=[wst, small], writes=[wv])
            lat = [alloc("lat%d" % i, 672) for i in range(2)]
            rp = [alloc("rp%d" % i, 64) for i in range(2)]
            lnb = [alloc("lnb%d" % i, 736, BF16) for i in range(2)]
            for b in lnb:
                op("pool", lambda E, b=b: E.memset(b[:, 640:736], 0.0), writes=[b])
            ssm = [alloc("ssm%d" % i, 4) for i in range(2)]
            junkm = alloc("junkm", 384, BF16)
            cT = [alloc("cT%d" % i, 3 * 128, BF16) for i in range(2)]
            ckvT = alloc("ckvT", 2 * 512, BF16)
            ckvT.ap = ckvT.ap.rearrange("p (c t) -> p c t", c=2)
            ckvc = [Buf("ckv%d" % j, ckvT.ap[:, :, j * 128:(j + 1) * 128]) for j in range(4)]
            kpst = alloc("kpst", 512, BF16)
            kpc = [Buf("kp%d" % j, kpst.ap[:, j * 128:(j + 1) * 128]) for j in range(4)]
            qr = [alloc("qr%d" % i, 768, BF16) for i in range(2)]
            rtmp = [alloc("rtmp%d" % i, 512) for i in range(2)]
            qtst = alloc("qtst", 8 * 512, BF16)
            qtst.ap = qtst.ap.rearrange("p (h t) -> p h t", h=8)
            qtc = [Buf("qt%d" % j, qtst.ap[:, :, j * 128:(j + 1) * 128]) for j in range(4)]
            ktst = alloc("ktst", 4 * 512, BF16)
            ktst.ap = ktst.ap.rearrange("p (a t) -> p a t", a=4)
            vst = alloc("vst", 4 * 512, BF16)
            vst.ap = vst.ap.rearrange("p (j f) -> p j f", j=4)
            vsc = [Buf("vs%d" % j, vst.ap[:, j, :]) for j in range(4)]
            li = 0
            for s, L in SEQ:
                for ti in range(L // 512):
                    r0 = ti * 512
                    for j in range(4):
                        rr = slice(r0 + j * 128, r0 + (j + 1) * 128)
                        lt, rpt, lb, sm_, ct = lat[li % 2], rp[li % 2], lnb[li % 2], ssm[li % 2], cT[li % 2]
                        qrt, rt = qr[li % 2], rtmp[li % 2]
                        li += 1
                        dma(lt[:], LAT[s][rr, :], writes=[lt], sb=lt)
                        dma(rpt[:], ROPE[rr, :], writes=[rpt], sb=rpt)
                        op("act", lambda E, lt=lt, sm_=sm_: E.activation(out=junkm[:, 0:384], in_=lt[:, 0:384], func=AF.Square, accum_out=sm_[:, 0:1]),
                           reads=[lt], writes=[sm_])
                        op("act", lambda E, lt=lt, sm_=sm_: E.activation(out=junkm[:, 0:256], in_=lt[:, 384:640], func=AF.Square, accum_out=sm_[:, 1:2]),
                           reads=[lt], writes=[sm_])
                        op("dve", lambda E, sm_=sm_: E.tensor_scalar(out=sm_[:, 0:1], in0=sm_[:, 0:1], scalar1=1.0 / 384, scalar2=EPS, op0=ALU.mult, op1=ALU.add),
                           reads=[sm_], writes=[sm_])
                        op("dve", lambda E, sm_=sm_: E.tensor_scalar(out=sm_[:, 1:2], in0=sm_[:, 1:2], scalar1=1.0 / 256, scalar2=EPS, op0=ALU.mult, op1=ALU.add),
                           reads=[sm_], writes=[sm_])
                        op("act", lambda E, sm_=sm_: E.activation(out=sm_[:, 0:2], in_=sm_[:, 0:2], func=AF.Ln), reads=[sm_], writes=[sm_])
                        op("act", lambda E, sm_=sm_: E.activation(out=sm_[:, 0:2], in_=sm_[:, 0:2], func=AF.Exp, scale=-0.5), reads=[sm_], writes=[sm_])
                        op("dve", lambda E, lt=lt, lb=lb, sm_=sm_: E.tensor_scalar(out=lb[:, 0:384], in0=lt[:, 0:384], scalar1=sm_[:, 0:1], scalar2=None, op0=ALU.mult),
                           reads=[lt, sm_], writes=[lb])
                        op("act", lambda E, lt=lt, lb=lb, sm_=sm_: E.activation(out=lb[:, 384:640], in_=lt[:, 384:640], func=AF.Copy, scale=sm_[:, 1:2]),
                           reads=[lt, sm_, lb], writes=[lb])
                        op("pool", lambda E, lt=lt, rt=rt, rpt=rpt: E.tensor_tensor(out=rt[:, 0:32], in0=lt[:, 640:672], in1=rpt[:, 0:32], op=ALU.mult),
                           reads=[lt, rpt], writes=[rt])
                        op("pool", lambda E, lt=lt, rt=rt, rpt=rpt: E.tensor_tensor(out=rt[:, 32:48], in0=lt[:, 656:672], in1=rpt[:, 32:48], op=ALU.mult),
                           reads=[lt, rpt, rt], writes=[rt])
                        op("pool", lambda E, lt=lt, rt=rt, rpt=rpt: E.tensor_tensor(out=rt[:, 48:64], in0=lt[:, 640:656], in1=rpt[:, 48:64], op=ALU.mult),
                           reads=[lt, rpt, rt], writes=[rt])
                        op("pool", lambda E, rt=rt, lb=lb: E.tensor_tensor(out=lb[:, 704:736], in0=rt[:, 0:32], in1=rt[:, 32:64], op=ALU.add),
                           reads=[rt, lb], writes=[lb])
                        for c in range(5):
                            op("pe", lambda E, c=c, lb=lb: E.transpose(out=pb(0, c * 128, (c + 1) * 128), in_=lb[:, c * 128:(c + 1) * 128], identity=identb[:]),
                               reads=[lb, identb], writes=[BK[0]])
                        op("pe", lambda E, lb=lb: E.transpose(out=psum_bf[0:96, 640:768], in_=lb[:, 640:736], identity=identb[:]),
                           reads=[lb, identb], writes=[BK[0]])
                        op("dve", lambda E, ct=ct: E.tensor_copy(out=ct[:], in_=pb(0, 0, 384)), writes=[ct, BK[0]])
                        op("act", lambda E, j=j: E.activation(out=ckvT.ap[:, :, j * 128:(j + 1) * 128], in_=pb(0, 384, 640).rearrange("p (c t) -> p c t", c=2), func=AF.Copy),
                           writes=[ckvc[j], BK[0]])
                        op("dve", lambda E, j=j: E.tensor_copy(out=kpst.ap[64:96, j * 128:(j + 1) * 128], in_=psum_bf[64:96, 640:768]), writes=[kpc[j], BK[0]])
                        ctv = ct.ap.rearrange("p (c t) -> p c t", c=3)
                        for c in range(3):
                            op("pe", lambda E, c=c, ctv=ctv: E.matmul(pf(1, 0, 480), lhsT=ctv[:, c, :], rhs=wuq.ap[:, c, 0:480], start=(c == 0), stop=(c == 2)),
                               reads=[ct, wuq], writes=[BK[1]])
                        for c in range(3):
                            op("pe", lambda E, c=c, ctv=ctv: E.matmul(pf(2, 0, 288), lhsT=ctv[:, c, :], rhs=wuq.ap[:, c, 480:768], start=(c == 0), stop=(c == 2)),
                               reads=[ct, wuq], writes=[BK[2]])
                        for (bk, h0, nh) in ((1, 0, 5), (2, 5, 3)):
                            pv = pf(bk, 0, nh * 96).rearrange("p (h e) -> p h e", h=nh)
                            qv = qrt.ap[:, h0 * 96:(h0 + nh) * 96].rearrange("p (h e) -> p h e", h=nh)
                            tv = rt.ap[:, 64:64 + nh * 64].rearrange("p (h e) -> p h e", h=nh)
                            cs = rpt.ap[:, 0:32].unsqueeze(1).to_broadcast([128, nh, 32])
                            sn1 = rpt.ap[:, 32:48].unsqueeze(1).to_broadcast([128, nh, 16])
                            sn2 = rpt.ap[:, 48:64].unsqueeze(1).to_broadcast([128, nh, 16])
                            op("act", lambda E, pv=pv, qv=qv: E.activation(out=qv[:, :, 0:64], in_=pv[:, :, 0:64], func=AF.Copy), reads=[], writes=[qrt, BK[bk]])
                            op("dve", lambda E, pv=pv, tv=tv, cs=cs: E.tensor_tensor(out=tv[:, :, 0:32], in0=pv[:, :, 64:96], in1=cs, op=ALU.mult),
                               reads=[rpt], writes=[rt, BK[bk]])
                            op("dve", lambda E, pv=pv, tv=tv, sn1=sn1: E.tensor_tensor(out=tv[:, :, 32:48], in0=pv[:, :, 80:96], in1=sn1, op=ALU.mult),
                               reads=[rpt], writes=[rt, BK[bk]])
                            op("dve", lambda E, pv=pv, tv=tv, sn2=sn2: E.tensor_tensor(out=tv[:, :, 48:64], in0=pv[:, :, 64:80], in1=sn2, op=ALU.mult),
                               reads=[rpt], writes=[rt, BK[bk]])
                            op("pool", lambda E, tv=tv, qv=qv: E.tensor_tensor(out=qv[:, :, 64:96], in0=tv[:, :, 0:32], in1=tv[:, :, 32:64], op=ALU.add),
                               reads=[rt], writes=[qrt])
                        for hh in range(8):
                            op("pe", lambda E, hh=hh, qrt=qrt: E.transpose(out=psum_bf[0:96, 3 * 1024 + hh * 128:3 * 1024 + (hh + 1) * 128], in_=qrt[:, hh * 96:(hh + 1) * 96],
                                                                           identity=identb[:]), reads=[qrt, identb], writes=[BK[3]])
                        op("act", lambda E, j=j: E.activation(out=qtst.ap[0:96, :, j * 128:(j + 1) * 128],
                                                              in_=psum_bf[0:96, 3 * 1024:4 * 1024].rearrange("p (h t) -> p h t", h=8), func=AF.Copy),
                           writes=[qtc[j], BK[3]])
                        for c in range(2):
                            op("pe", lambda E, c=c, j=j: E.matmul(pf(4), lhsT=ckvT.ap[:, c, j * 128:(j + 1) * 128], rhs=wv.ap[:, c, :], start=(c == 0), stop=(c == 1)),
                               reads=[ckvc[j], wv], writes=[BK[4]])
                        op("dve", lambda E, j=j: E.tensor_copy(out=vst.ap[:, j, :], in_=pf(4)), writes=[vsc[j], BK[4]])
                    for pr in range(4):
                        bk = 5 + pr % 2
                        for c in range(2):
                            op("pe", lambda E, c=c, pr=pr, bk=bk: E.matmul(pf(bk), lhsT=wk.ap[:, c, pr * 128:(pr + 1) * 128], rhs=ckvT.ap[:, c, :], start=(c == 0), stop=(c == 1)),
                               reads=ckvc + [wk], writes=[BK[bk]])
                        if pr % 2 == 0:
                            op("act", lambda E, pr=pr, bk=bk: E.activation(out=ktst.ap[:, pr, :], in_=pf(bk), func=AF.Copy), writes=[ktst, BK[bk]])
                        else:
                            op("dve", lambda E, pr=pr, bk=bk: E.tensor_copy(out=ktst.ap[:, pr, :], in_=pf(bk)), reads=[ktst], writes=[ktst, BK[bk]])
                    cc = slice(r0, r0 + 512)
                    for hh in range(8):
                        pr, hi = hh // 2, hh % 2
                        dma(KT[s][hh, 0:64, cc], ktst.ap[hi * 64:(hi + 1) * 64, pr, :], reads=[ktst], sb=ktst, queue="pool")
                        dma(KT[s][hh, 64:96, cc], kpst.ap[64:96, :], reads=kpc, sb=kpst, queue="pool")
                    dma(QT[s][:, :, cc].rearrange("h r t -> r h t"), qtst.ap[0:96, :, :], reads=qtc, sb=qtst, queue="pool")
                    for j in range(4):
                        dma(VV[s][:, :, ti * 4 + j, :].rearrange("h p e -> p h e"),
                            vst.ap[:, j, :].rearrange("p (h e) -> p h e", h=8), reads=vsc, sb=vst, queue="pool")
            phase_reset()

        if "B" in phases:
            SC = 96 ** -0.5
            for s, L in SEQ:
                nkb = L // 128
                off_seq = off[0]
                ktb = []
                vtb = []
                for i in range(2):
                    b = alloc("ktb%d" % i, L, BF16)
                    ktb.append(b)
                    v = alloc("vtb%d" % i, nkb * 65, BF16)
                    v.ap = v.ap.rearrange("p (k e) -> p k e", e=65)
                    op("pool", lambda E, v=v: E.memset(v.ap[:, :, 64:65], 1.0), writes=[v])
                    vtb.append(v)
                qtb = [alloc("qtb%d" % i, 512, BF16) for i in range(3)]
                ptb = [alloc("ptb%d" % i, 512, BF16) for i in range(4)]
                osb = [alloc("osb%d" % i, 512) for i in range(2)]
                yst = [alloc("yst%d" % i, 4 * 64) for i in range(2)]
                rdn = [alloc("rdn%d" % i, 4) for i in range(2)]
                qi = 0
                pi = 0
                for hh in range(8):
                    kt, vt = ktb[hh % 2], vtb[hh % 2]
                    dma(kt[0:96, :], KT[s][hh, :, :], writes=[kt], sb=kt)
                    dma(vt.ap[:, :, 0:64], VV[s][hh, :, :, :], writes=[vt], sb=vt)
                    for qt in range(L // 512):
                        qb = qtb[qi % 3]
                        ob, ys, rd = osb[qi % 2], yst[qi % 2], rdn[qi % 2]
                        obk = 4 + qi % 2
                        qi += 1
                        dma(qb[0:96, :], QT[s][hh, :, qt * 512:(qt + 1) * 512], writes=[qb], sb=qb)

                        def smm(kb, qb=qb, kt=kt):
                            bk = kb % 4
                            op("pe", lambda E: E.matmul(pf(bk), lhsT=kt[0:96, kb * 128:(kb + 1) * 128], rhs=qb[0:96, :], start=True, stop=True),
                               reads=[kt, qb], writes=[BK[bk]])
                        smm(0)
                        if nkb > 1:
                            smm(1)
                        for kb in range(nkb):
                            if kb + 2 < nkb:
                                smm(kb + 2)
                            pt = ptb[pi % 4]
                            pi += 1
                            bk = kb % 4
                            op("act", lambda E, pt=pt, bk=bk: E.activation(out=pt[:], in_=pf(bk), func=AF.Exp, scale=SC), writes=[pt, BK[bk]])
                            op("pe", lambda E, pt=pt, kb=kb, vt=vt, obk=obk: E.matmul(psum[0:65, obk * 512:(obk + 1) * 512], lhsT=vt.ap[:, kb, 0:65], rhs=pt[:],
                                                                                         start=(kb == 0), stop=(kb == nkb - 1)), reads=[vt, pt], writes=[BK[obk]])
                        op("dve", lambda E, ob=ob, obk=obk: E.tensor_copy(out=ob[0:65, :], in_=psum[0:65, obk * 512:(obk + 1) * 512]), writes=[ob, BK[obk]])
                        for j in range(4):
                            op("pe", lambda E, j=j, ob=ob: E.matmul(pf(6, j * 65, (j + 1) * 65), lhsT=ob[0:65, j * 128:(j + 1) * 128], rhs=identf[0:65, 0:65], start=True, stop=True),
                               reads=[ob, identf], writes=[BK[6]])
                        p6 = pf(6, 0, 260).rearrange("p (j e) -> p j e", j=4)
                        op("dve", lambda E, rd=rd, p6=p6: E.reciprocal(out=rd.ap[:, 0:4].unsqueeze(2), in_=p6[:, :, 64:65]), writes=[rd, BK[6]])
                        op("dve", lambda E, rd=rd, ys=ys, p6=p6: E.tensor_tensor(out=ys.ap.rearrange("p (j e) -> p j e", j=4), in0=p6[:, :, 0:64],
                                                                                in1=rd.ap[:, 0:4].unsqueeze(2).to_broadcast([128, 4, 64]), op=ALU.mult),
                           reads=[rd], writes=[ys, BK[6]])
                        dma(YA[s][qt * 512:(qt + 1) * 512, hh * 64:(hh + 1) * 64].rearrange("(j p) e -> p j e", p=128),
                            ys.ap.rearrange("p (j e) -> p j e", j=4), reads=[ys], sb=ys, queue="pool")
                P.barrier()
                off[0] = off_seq
            phase_reset()

        if "G" in phases:
            cw = alloc("cw", 12 * 5)
            for c in range(12):
                dma(cw[:, c * 5:(c + 1) * 5], W["gdn_conv_w"][:, c * 128:(c + 1) * 128].rearrange("k p -> p k"), writes=[cw] if c == 0 else [],
                    reads=[] if c == 0 else [cw], sb=cw, allow_slow_non_contiguous=True)
            cw.lw = (id(cw), cw.semcnt)
            xin = [alloc("gxin%d" % i, 516) for i in range(3)]
            acc = [alloc("gacc%d" % i, 512) for i in range(2)]
            ex = [alloc("gex%d" % i, 512) for i in range(2)]
            sb_ = [alloc("gsb%d" % i, 512, BF16) for i in range(2)]
            tokst = alloc("tokst", 4 * 1536, BF16)
            tokst.ap = tokst.ap.rearrange("p (j c) -> p j c", j=4)
            tokc = [Buf("tokc%d" % c, tokst.ap[:, :, c * 128:(c + 1) * 128]) for c in range(12)]
            sq = alloc("gsq", 1024)
            ssg = alloc("ssg", 32)
            ci = 0
            for s, L in SEQ:
                for ti in range(L // 512):
                    r0 = ti * 512
                    for c in range(12):
                        xi, ac, e_, sbb = xin[ci % 3], acc[ci % 2], ex[ci % 2], sb_[ci % 2]
                        ci += 1
                        lo, hi = max(r0 - 2, 0), min(r0 + 514, L)
                        if r0 == 0:
                            op("pool", lambda E, xi=xi: E.memset(xi[:, 0:2], 0.0), writes=[xi])
                        if r0 + 512 == L:
                            op("pool", lambda E, xi=xi: E.memset(xi[:, 514:516], 0.0), writes=[xi])
                        dma(xi[:, lo - (r0 - 2):hi - (r0 - 2)], QKVT[s][c * 128:(c + 1) * 128, lo:hi], writes=[xi], sb=xi)
                        e1 = "dve"
                        op(e1, lambda E, xi=xi, ac=ac, c=c: E.tensor_scalar(out=ac[:], in0=xi[:, 0:512], scalar1=cw[:, c * 5:c * 5 + 1], scalar2=None, op0=ALU.mult),
                           reads=[xi, cw], writes=[ac])
                        for k in range(1, 5):
                            op(e1, lambda E, xi=xi, ac=ac, c=c, k=k: E.scalar_tensor_tensor(out=ac[:], in0=xi[:, k:k + 512], scalar=cw[:, c * 5 + k:c * 5 + k + 1], in1=ac[:],
                                                                                           op0=ALU.mult, op1=ALU.add), reads=[xi, cw, ac], writes=[ac])
                        op("act", lambda E, ac=ac, e_=e_: E.activation(out=e_[:], in_=ac[:], func=AF.Exp, scale=-1.0), reads=[ac], writes=[e_])
                        op("act", lambda E, e_=e_: E.activation(out=e_[:], in_=e_[:], func=AF.Ln, bias=1.0), reads=[e_], writes=[e_])
                        op("act", lambda E, e_=e_: E.activation(out=e_[:], in_=e_[:], func=AF.Exp, scale=-1.0), reads=[e_], writes=[e_])
                        op("dve", lambda E, ac=ac, e_=e_, sbb=sbb: E.tensor_tensor(out=sbb[:], in0=ac[:], in1=e_[:], op=ALU.mult), reads=[ac, e_], writes=[sbb])
                        bk = c % 2
                        for j in range(4):
                            op("pe", lambda E, j=j, sbb=sbb, bk=bk: E.transpose(out=pb(bk, j * 128, (j + 1) * 128), in_=sbb[:, j * 128:(j + 1) * 128], identity=identb[:]),
                               reads=[sbb, identb], writes=[BK[bk]])
                        if c % 2 == 0:
                            op("act", lambda E, c=c, bk=bk: E.activation(out=tokst.ap[:, :, c * 128:(c + 1) * 128], in_=pb(bk, 0, 512).rearrange("p (j d) -> p j d", j=4), func=AF.Copy),
                               writes=[tokc[c], BK[bk]])
                        else:
                            op("dve", lambda E, c=c, bk=bk: E.tensor_copy(out=tokst.ap[:, :, c * 128:(c + 1) * 128], in_=pb(bk, 0, 512).rearrange("p (j d) -> p j d", j=4)),
                               writes=[tokc[c], BK[bk]])
                    for j in range(4):
                        tv = tokst.ap[:, j, 0:1024]
                        op("dve", lambda E, tv=tv: E.tensor_tensor(out=sq[:], in0=tv, in1=tv, op=ALU.mult), reads=tokc[0:8], writes=[sq])
                        op("dve", lambda E, j=j: E.tensor_reduce(out=ssg[:, j * 8:(j + 1) * 8], in_=sq.ap.rearrange("p (h d) -> p h d", h=8), axis=AX.X, op=ALU.add),
                           reads=[sq], writes=[ssg])
                    op("dve", lambda E: E.tensor_scalar(out=ssg[:, 0:32], in0=ssg[:, 0:32], scalar1=EPS, scalar2=None, op0=ALU.add), reads=[ssg], writes=[ssg])
                    op("act", lambda E: E.activation(out=ssg[:, 0:32], in_=ssg[:, 0:32], func=AF.Ln), reads=[ssg], writes=[ssg])
                    op("act", lambda E: E.activation(out=ssg[:, 0:32], in_=ssg[:, 0:32], func=AF.Exp, scale=-0.5), reads=[ssg], writes=[ssg])
                    sv = ssg.ap[:, 0:32].rearrange("p (j h) -> p j h", j=4)
                    op("dve", lambda E, sv=sv: E.tensor_scalar(out=sv[:, :, 0:4], in0=sv[:, :, 0:4], scalar1=128 ** -0.5, scalar2=None, op0=ALU.mult), reads=[ssg], writes=[ssg])
                    for j in range(4):
                        tv = tokst.ap[:, j, 0:1024].rearrange("p (h d) -> p h d", h=8)
                        e1 = "dve" if j % 2 == 0 else "pool"
                        op(e1, lambda E, tv=tv, j=j: E.tensor_tensor(out=tv, in0=tv, in1=ssg.ap[:, j * 8:(j + 1) * 8].unsqueeze(2).to_broadcast([128, 8, 128]), op=ALU.mult),
                           reads=[ssg] + tokc[0:8], writes=tokc[0:8])
                    dma(TOK[s][r0:r0 + 512, :].rearrange("(j p) c -> p j c", p=128), tokst.ap, reads=tokc, sb=tokst, queue="pool")
            phase_reset()

        if "H" in phases:
            gm = alloc("gm", 5 * 512)
            gm.ap = gm.ap.rearrange("p (m h j) -> p m h j", m=5, h=8)
            dma(gm.ap[0:64], GMASK[:, :, :, :], writes=[gm], sb=gm)
            NA, NAT, NQK = gm.ap[0:64, 0], gm.ap[0:64, 1], gm.ap[0:64, 2]
            trif, trib = gm.ap[0:64, 3, 0, :], gm.ap[0:64, 3, 1, :]
            idb8 = identb.ap[0:64, 0:64].unsqueeze(1).to_broadcast([64, 8, 64])
            idf8 = identf.ap[0:64, 0:64].unsqueeze(1).to_broadcast([64, 8, 64])

            def A3(name, n, dt=F32, parts=128):
                b = alloc(name, 8 * n, dt)
                b.ap = b.ap.rearrange("p (h x) -> p h x", h=8)
                return b
            NSET = 2
            tok = [alloc("tk%d" % i, 2 * 1536, BF16) for i in range(NSET)]
            gsel = [alloc("gsel%d" % i, 24) for i in range(NSET)]
            sm8 = [alloc("sm8_%d" % i, 64) for i in range(NSET)]
            ost = [alloc("ost%d" % i, 1024) for i in range(NSET)]
            SETS = []
            for i in range(NSET):
                d_ = {}
                d_["Dg"] = A3("Dg%d" % i, 64)
                d_["Dc"] = A3("Dc%d" % i, 64)
                d_["De"] = A3("De%d" % i, 64, BF16)
                kq_ = alloc("kqT%d" % i, 16 * 64, BF16)
                kq_.ap = kq_.ap.rearrange("p (h x) -> p h x", h=16)
                d_["kqT"] = kq_
                for nm in ("dA", "dAT", "dQK"):
                    d_[nm] = A3(nm + str(i), 64)
                d_["Xb"] = [A3("Xb%d_%d" % (i, k), 64, BF16) for k in range(2)]
                d_["Yb"] = [A3("Yb%d_%d" % (i, k), 64, BF16) for k in range(2)]
                d_["Zb"] = [A3("Zb%d_%d" % (i, k), 64, BF16) for k in range(2)]
                for nm, n_, dt_ in (("qkT", 64, BF16), ("kbg", 128, BF16), ("vb", 128, BF16), ("kg", 128, BF16), ("wT", 64, BF16),
                                   ("qgT", 64, BF16), ("uu", 128, F32), ("vnew", 128, BF16)):
                    d_[nm] = A3(nm + str(i), n_, dt_)
                SETS.append(d_)
            S = A3("S", 128)
            Sb = A3("Sb", 128, BF16)

            def step(s, N, n):
                st_ = SETS[n % NSET]
                c0 = 4 * (n % 2)
                c1, c2, c3 = c0 + 1, c0 + 2, c0 + 3
                Dg, Dc, De, kqT, dA, dAT, dQK = st_["Dg"], st_["Dc"], st_["De"], st_["kqT"], st_["dA"], st_["dAT"], st_["dQK"]
                Xb, Yb, Zb = st_["Xb"], st_["Yb"], st_["Zb"]
                qkT, kbg, vb, kg, wT, qgT, uu, vnew = (st_[k_] for k_ in ("qkT", "kbg", "vb", "kg", "wT", "qgT", "uu", "vnew"))
                cf, cb = n, N - 1 - n
                tk, gs, m8, os_ = tok[n % NSET], gsel[n % NSET], sm8[n % NSET], ost[n % NSET]
                tkv = tk.ap.rearrange("p (d c) -> p d c", d=2)
                for d, ch in ((0, cf), (1, cb)):
                    dma(tkv[0:64, d, :], TOK[s][ch * 64:(ch + 1) * 64, :], writes=[tk] if d == 0 else [], reads=[] if d == 0 else [tk], sb=tk)
                    dma(gs.ap[0:64, d * 12:(d + 1) * 12], GBL[s][d, ch * 64:(ch + 1) * 64, :], writes=[gs] if d == 0 else [], reads=[] if d == 0 else [gs], sb=gs)
                tk.lw = (id(tk), tk.semcnt)
                gs.lw = (id(gs), gs.semcnt)
                gv = gs.ap[0:64, :].rearrange("p (d k) -> p d k", d=2)
                g8, lnb8, beta8 = gv[:, :, 0:4], gv[:, :, 4:8], gv[:, :, 8:12]
                m = m8.ap[0:64, :]

                def v8(a, b):
                    return m8.ap[0:64, a:b].rearrange("p (d k) -> p d k", d=2)
                yield None
                op("pe", lambda E, gs=gs: E.matmul(psum[0:64, c0 * 512:c0 * 512 + 4], lhsT=trif, rhs=gs.ap[0:64, 0:4], start=True, stop=True), reads=[gm, gs], writes=[BK[c0]])
                op("pe", lambda E, gs=gs: E.matmul(psum[0:64, c0 * 512 + 4:c0 * 512 + 8], lhsT=trib, rhs=gs.ap[0:64, 12:16], start=True, stop=True), reads=[gm, gs], writes=[BK[c0]])
                op("pe", lambda E, g8=g8: E.matmul(psum[:, c0 * 512 + 8:c0 * 512 + 16].rearrange("p (d k) -> p d k", d=2), lhsT=onesf[0:64, :], rhs=g8, start=True, stop=True),
                   reads=[onesf, gs], writes=[BK[c0]])
                yield None
                op("dve", lambda E, m8=m8: E.tensor_copy(out=m8[0:64, 0:8], in_=psum[0:64, c0 * 512:c0 * 512 + 8]), writes=[m8, BK[c0]])
                op("dve", lambda E, m8=m8, lnb8=lnb8, v8=v8: E.tensor_tensor(out=v8(8, 16), in0=v8(0, 8), in1=lnb8, op=ALU.add), reads=[m8, gs], writes=[m8])
                op("act", lambda E, m8=m8: E.activation(out=m8[0:64, 16:24], in_=m8[0:64, 0:8], func=AF.Exp), reads=[m8], writes=[m8])
                op("dve", lambda E, m8=m8, beta8=beta8, v8=v8: E.tensor_tensor(out=v8(24, 32), in0=v8(16, 24), in1=beta8, op=ALU.mult), reads=[m8, gs], writes=[m8])
                op("dve", lambda E, m8=m8: E.tensor_tensor(out=m8[0:64, 40:48], in0=psum[0:64, c0 * 512 + 8:c0 * 512 + 16], in1=m8[0:64, 0:8], op=ALU.subtract), reads=[m8], writes=[m8, BK[c0]])
                op("act", lambda E, m8=m8: E.activation(out=m8[0:64, 32:40], in_=m8[0:64, 40:48], func=AF.Exp), reads=[m8], writes=[m8])
                op("act", lambda E, m8=m8: E.activation(out=m8[:, 48:56], in_=psum[:, c0 * 512 + 8:c0 * 512 + 16], func=AF.Exp), reads=[m8], writes=[m8, BK[c0]])
                yield None
                op("pool", lambda E, m8=m8: E.tensor_tensor(out=Dg.ap[0:64], in0=idf8, in1=m8.ap[0:64, 0:8].unsqueeze(2).to_broadcast([64, 8, 64]), op=ALU.mult),
                   reads=[identf, m8], writes=[Dg])
                op("pool", lambda E, m8=m8: E.tensor_tensor(out=Dc.ap[0:64], in0=idf8, in1=m8.ap[0:64, 8:16].unsqueeze(2).to_broadcast([64, 8, 64]), op=ALU.mult),
                   reads=[identf, m8], writes=[Dc])
                op("pool", lambda E, m8=m8: E.tensor_tensor(out=De.ap[0:64], in0=idf8, in1=m8.ap[0:64, 16:24].unsqueeze(2).to_broadcast([64, 8, 64]), op=ALU.mult),
                   reads=[identf, m8], writes=[De])
                yield None
                op("pe", lambda E: E.matmul(psum[0:64, c1 * 512:(c1 + 1) * 512], lhsT=onesf[0:64, 0:64], rhs=Dg.ap[0:64].rearrange("p h x -> p (h x)"), start=True, stop=True),
                   reads=[onesf, Dg], writes=[BK[c1]])
                op("pe", lambda E: E.matmul(psum[0:64, c2 * 512:(c2 + 1) * 512], lhsT=onesf[0:64, 0:64], rhs=Dc.ap[0:64].rearrange("p h x -> p (h x)"), start=True, stop=True),
                   reads=[onesf, Dc], writes=[BK[c2]])
                yield None
                for d in range(2):
                    for hh in range(4):
                        hd = d * 4 + hh
                        op("pe", lambda E, d=d, hh=hh, hd=hd, tkv=tkv: E.transpose(out=psum_bf[:, c3 * 1024 + hd * 64:c3 * 1024 + (hd + 1) * 64],
                                                                                 in_=tkv[0:64, d, 512 + hh * 128:512 + (hh + 1) * 128], identity=identb[0:64, 0:64]),
                           reads=[tk, identb], writes=[BK[c3]])
                        op("pe", lambda E, d=d, hh=hh, hd=hd, tkv=tkv: E.transpose(out=psum_bf[:, c3 * 1024 + 512 + hd * 64:c3 * 1024 + 512 + (hd + 1) * 64],
                                                                                 in_=tkv[0:64, d, hh * 128:(hh + 1) * 128], identity=identb[0:64, 0:64]),
                           reads=[tk, identb], writes=[BK[c3]])
                yield None
                op("act", lambda E: E.activation(out=kqT.ap, in_=pb(c3).rearrange("p (h x) -> p h x", h=16), func=AF.Copy), writes=[kqT, BK[c3]])
                yield None
                for hd in range(8):
                    op("pe", lambda E, hd=hd: E.matmul(psum[0:64, c0 * 512 + hd * 64:c0 * 512 + (hd + 1) * 64], lhsT=kqT.ap[:, hd, :], rhs=kqT.ap[:, hd, :], start=True, stop=True),
                       reads=[kqT], writes=[BK[c0]])
                for hd in range(8):
                    op("pe", lambda E, hd=hd: E.matmul(psum[0:64, c3 * 512 + hd * 64:c3 * 512 + (hd + 1) * 64], lhsT=kqT.ap[:, hd, :], rhs=kqT.ap[:, 8 + hd, :], start=True, stop=True),
                       reads=[kqT], writes=[BK[c3]])
                P1 = psum[0:64, c1 * 512:(c1 + 1) * 512].rearrange("p (h x) -> p h x", h=8)
                P2 = psum[0:64, c2 * 512:(c2 + 1) * 512].rearrange("p (h x) -> p h x", h=8)
                PG = psum[0:64, c0 * 512:(c0 + 1) * 512].rearrange("p (h x) -> p h x", h=8)
                PQ = psum[0:64, c3 * 512:(c3 + 1) * 512].rearrange("p (h x) -> p h x", h=8)

                def bc8(a):
                    return m8.ap[0:64, a:a + 8].unsqueeze(2).to_broadcast([64, 8, 64])
                yield None
                op("dve", lambda E: E.scalar_tensor_tensor(out=dA.ap[0:64], in0=P1, scalar=-1.0, in1=NA, op0=ALU.mult, op1=ALU.add), reads=[gm], writes=[dA, BK[c1]])
                op("pool", lambda E, bc8=bc8: E.tensor_tensor(out=dA.ap[0:64], in0=dA.ap[0:64], in1=bc8(8), op=ALU.add), reads=[m8, dA], writes=[dA])
                op("act", lambda E: E.activation(out=dA.ap[0:64], in_=dA.ap[0:64], func=AF.Exp), reads=[dA], writes=[dA])
                yield None
                op("dve", lambda E: E.tensor_tensor(out=dAT.ap[0:64], in0=P2, in1=NAT, op=ALU.add), reads=[gm], writes=[dAT, BK[c2]])
                op("pool", lambda E, bc8=bc8: E.tensor_tensor(out=dAT.ap[0:64], in0=dAT.ap[0:64], in1=bc8(0), op=ALU.subtract), reads=[m8, dAT], writes=[dAT])
                op("act", lambda E: E.activation(out=dAT.ap[0:64], in_=dAT.ap[0:64], func=AF.Exp), reads=[dAT], writes=[dAT])
                yield None
                op("dve", lambda E: E.tensor_tensor(out=dQK.ap[0:64], in0=P1, in1=NQK, op=ALU.add), reads=[gm], writes=[dQK, BK[c1]])
                op("pool", lambda E, bc8=bc8: E.tensor_tensor(out=dQK.ap[0:64], in0=dQK.ap[0:64], in1=bc8(0), op=ALU.subtract), reads=[m8, dQK], writes=[dQK])
                op("act", lambda E: E.activation(out=dQK.ap[0:64], in_=dQK.ap[0:64], func=AF.Exp), reads=[dQK], writes=[dQK])
                yield None
                X0, Y0, Z0 = Xb[0], Yb[0], Zb[0]
                op("dve", lambda E, X0=X0: E.scalar_tensor_tensor(out=X0.ap[0:64], in0=dA.ap[0:64], scalar=-1.0, in1=PG, op0=ALU.mult, op1=ALU.mult), reads=[dA], writes=[X0, BK[c0]])
                op("dve", lambda E, Y0=Y0: E.scalar_tensor_tensor(out=Y0.ap[0:64], in0=dAT.ap[0:64], scalar=-1.0, in1=PG, op0=ALU.mult, op1=ALU.mult), reads=[dAT], writes=[Y0, BK[c0]])
                op("dve", lambda E: E.tensor_tensor(out=qkT.ap[0:64], in0=dQK.ap[0:64], in1=PQ, op=ALU.mult), reads=[dQK], writes=[qkT, BK[c3]])
                op("pool", lambda E, Y0=Y0, Z0=Z0: E.tensor_tensor(out=Z0.ap[0:64], in0=Y0.ap[0:64], in1=idb8, op=ALU.add), reads=[Y0, identb], writes=[Z0])
                yield None
                for k in range(1, 6):
                    Xp, Yp, Zp = Xb[(k - 1) % 2], Yb[(k - 1) % 2], Zb[(k - 1) % 2]
                    Xn, Yn, Zn = Xb[k % 2], Yb[k % 2], Zb[k % 2]
                    for hd in range(8):
                        op("pe", lambda E, hd=hd, Xp=Xp, Yp=Yp: E.matmul(psum[0:64, c1 * 512 + hd * 64:c1 * 512 + (hd + 1) * 64], lhsT=Yp.ap[0:64, hd, :], rhs=Xp.ap[0:64, hd, :],
                                                                         start=True, stop=True), reads=[Xp, Yp], writes=[BK[c1]])
                    if k < 5:
                        for hd in range(8):
                            op("pe", lambda E, hd=hd, Xp=Xp, Yp=Yp: E.matmul(psum[0:64, c2 * 512 + hd * 64:c2 * 512 + (hd + 1) * 64], lhsT=Xp.ap[0:64, hd, :], rhs=Yp.ap[0:64, hd, :],
                                                                             start=True, stop=True), reads=[Xp, Yp], writes=[BK[c2]])
                    yield None
                    op("act", lambda E, Xn=Xn: E.activation(out=Xn.ap[0:64], in_=P1, func=AF.Copy), writes=[Xn, BK[c1]])
                    if k < 5:
                        op("dve", lambda E, Yn=Yn: E.tensor_copy(out=Yn.ap[0:64], in_=P2), writes=[Yn, BK[c2]])
                    yield None
                    for hd in range(8):
                        op("pe", lambda E, hd=hd, Xn=Xn, Zp=Zp: E.matmul(psum[0:64, c3 * 512 + hd * 64:c3 * 512 + (hd + 1) * 64], lhsT=Xn.ap[0:64, hd, :], rhs=Zp.ap[0:64, hd, :],
                                                                         start=True, stop=True), reads=[Xn, Zp], writes=[BK[c3]])
                    op("dve", lambda E, Zn=Zn, Zp=Zp: E.tensor_tensor(out=Zn.ap[0:64], in0=psum[0:64, c3 * 512:(c3 + 1) * 512].rearrange("p (h x) -> p h x", h=8), in1=Zp.ap[0:64], op=ALU.add),
                       reads=[Zp], writes=[Zn, BK[c3]])
                    yield None
                Z = Zb[5 % 2]
                yield None
                kv4 = tkv[0:64, :, 512:1024].rearrange("p d (h x) -> p d h x", h=4)
                vv4 = tkv[0:64, :, 1024:1536].rearrange("p d (h x) -> p d h x", h=4)

                def b4(a):
                    return m8.ap[0:64, a:a + 8].rearrange("p (d h) -> p d h", d=2).unsqueeze(3).to_broadcast([64, 2, 4, 128])
                op("pool", lambda E, kv4=kv4, b4=b4: E.tensor_tensor(out=kbg.ap[0:64].rearrange("p (d h) x -> p d h x", d=2), in0=kv4, in1=b4(24), op=ALU.mult),
                   reads=[tk, m8], writes=[kbg])
                op("dve", lambda E, vv4=vv4, gs=gs: E.tensor_tensor(out=vb.ap[0:64].rearrange("p (d h) x -> p d h x", d=2), in0=vv4,
                                                                    in1=gs.ap[0:64, :].rearrange("p (d k) -> p d k", d=2)[:, :, 8:12].unsqueeze(3).to_broadcast([64, 2, 4, 128]), op=ALU.mult),
                   reads=[tk, gs], writes=[vb])
                op("pool", lambda E, kv4=kv4, b4=b4: E.tensor_tensor(out=kg.ap[0:64].rearrange("p (d h) x -> p d h x", d=2), in0=kv4, in1=b4(32), op=ALU.mult),
                   reads=[tk, m8], writes=[kg])
                yield None
                for hd in range(8):
                    op("pe", lambda E, hd=hd, Z=Z: E.matmul(psum[:, c0 * 512 + hd * 64:c0 * 512 + (hd + 1) * 64], lhsT=kbg.ap[0:64, hd, :], rhs=Z.ap[0:64, hd, :], start=True, stop=True),
                       reads=[kbg, Z], writes=[BK[c0]])
                op("act", lambda E: E.activation(out=wT.ap, in_=pf(c0).rearrange("p (h x) -> p h x", h=8), func=AF.Copy), writes=[wT, BK[c0]])
                yield None
                for hd in range(8):
                    bk = c2 + hd // 4
                    op("pe", lambda E, hd=hd, Z=Z: E.matmul(psum[0:64, c2 * 512 + hd * 128:c2 * 512 + (hd + 1) * 128], lhsT=Z.ap[0:64, hd, :], rhs=vb.ap[0:64, hd, :], start=True, stop=True),
                       reads=[vb, Z], writes=[BK[bk]])
                op("act", lambda E: E.activation(out=uu.ap[0:64], in_=psum[0:64, c2 * 512:(c2 + 2) * 512].rearrange("p (h x) -> p h x", h=8), func=AF.Copy), writes=[uu, BK[c2], BK[c3]])
                yield None
                for d in range(2):
                    for hh in range(4):
                        hd = d * 4 + hh
                        op("pe", lambda E, d=d, hh=hh, hd=hd, tkv=tkv: E.matmul(psum[:, c1 * 512 + hd * 64:c1 * 512 + (hd + 1) * 64], lhsT=tkv[0:64, d, hh * 128:(hh + 1) * 128],
                                                                                rhs=De.ap[0:64, hd, :], start=True, stop=True), reads=[tk, De], writes=[BK[c1]])
                op("dve", lambda E: E.tensor_copy(out=qgT.ap, in_=pf(c1).rearrange("p (h x) -> p h x", h=8)), writes=[qgT, BK[c1]])
                yield "SCAN"
                for hd in range(8):
                    bk = c0 + hd // 4
                    op("pe", lambda E, hd=hd: E.matmul(psum[0:64, c0 * 512 + hd * 128:c0 * 512 + (hd + 1) * 128], lhsT=wT.ap[:, hd, :], rhs=Sb.ap[:, hd, :], start=True, stop=True),
                       reads=[wT, Sb], writes=[BK[bk]])
                yield None
                op("dve", lambda E: E.tensor_tensor(out=vnew.ap[0:64], in0=uu.ap[0:64], in1=psum[0:64, c0 * 512:(c0 + 2) * 512].rearrange("p (h x) -> p h x", h=8), op=ALU.subtract),
                   reads=[uu], writes=[vnew, BK[c0], BK[c1]])
                yield None
                for hd in range(8):
                    bk = c2 + hd // 4
                    op("pe", lambda E, hd=hd: E.matmul(psum[0:64, c2 * 512 + hd * 128:c2 * 512 + (hd + 1) * 128], lhsT=qgT.ap[:, hd, :], rhs=Sb.ap[:, hd, :], start=True, stop=False),
                       reads=[qgT, Sb], writes=[BK[bk]])
                    op("pe", lambda E, hd=hd: E.matmul(psum[0:64, c2 * 512 + hd * 128:c2 * 512 + (hd + 1) * 128], lhsT=qkT.ap[0:64, hd, :], rhs=vnew.ap[0:64, hd, :], start=False, stop=True),
                       reads=[qkT, vnew], writes=[BK[bk]])
                yield None
                op("act", lambda E, os_=os_: E.activation(out=os_[0:64, :], in_=psum[0:64, c2 * 512:(c2 + 2) * 512], func=AF.Copy), writes=[os_, BK[c2], BK[c3]])
                dma(OF[s][0, cf * 64:(cf + 1) * 64, :], os_[0:64, 0:512], reads=[os_], sb=os_, queue="pool")
                dma(OF[s][1, cb * 64:(cb + 1) * 64, :], os_[0:64, 512:1024], reads=[os_], sb=os_, queue="pool")
                yield None
                for hd in range(8):
                    bk = c0 + hd // 4
                    op("pe", lambda E, hd=hd: E.matmul(psum[:, c0 * 512 + hd * 128:c0 * 512 + (hd + 1) * 128], lhsT=kg.ap[0:64, hd, :], rhs=vnew.ap[0:64, hd, :], start=True, stop=True),
                       reads=[kg, vnew], writes=[BK[bk]])
                yield None
                op("pool", lambda E, m8=m8: E.tensor_tensor(out=S.ap, in0=S.ap, in1=m8.ap[:, 48:56].unsqueeze(2).to_broadcast([128, 8, 128]), op=ALU.mult),
                   reads=[m8, S], writes=[S])
                op("dve", lambda E: E.tensor_tensor(out=S.ap, in0=S.ap, in1=psum[:, c0 * 512:(c0 + 2) * 512].rearrange("p (h x) -> p h x", h=8), op=ALU.add),
                   reads=[S], writes=[S, BK[c0], BK[c1]])
                op("act", lambda E: E.activation(out=Sb.ap, in_=S.ap, func=AF.Copy), reads=[S], writes=[Sb])
            WIN = 2
            for s, L in SEQ:
                N = L // 64
                op("pool", lambda E: E.memset(S.ap, 0.0), writes=[S])
                op("pool", lambda E: E.memset(Sb.ap, 0.0), writes=[Sb])
                nxt = 0
                active = []
                while nxt < N or active:
                    while len(active) < WIN and nxt < N:
                        active.append([nxt, step(s, N, nxt), False])
                        nxt += 1
                    oldest = min(a_[0] for a_ in active)
                    for a_ in list(active):
                        if a_[2] and a_[0] != oldest:
                            continue
                        try:
                            r_ = next(a_[1])
                            a_[2] = (r_ == "SCAN")
                        except StopIteration:
                            active.remove(a_)
            phase_reset()

        if "C" in phases:
            T = ffn_bufs()
            load_wd(T, "ffn2_w_down")
            wst = alloc("wstc", 1024)
            wo = alloc("wo", 8 * 1024, BF16)
            wo.ap = wo.ap.rearrange("p (c f) -> p c f", c=8)
            for c in range(8):
                dma(wst[:], W["w_out"][c * 128:(c + 1) * 128, :], writes=[wst], sb=wst)
                sc = small[:, 5 + c:6 + c] if c < 4 else small[:, 9:10]
                op("dve", lambda E, c=c, sc=sc: E.tensor_scalar(out=wo.ap[:, c, :], in0=wst[:], scalar1=sc, scalar2=None, op0=ALU.mult),
                   reads=[wst, small], writes=[wo])
            ya = alloc("ya", 4 * 512)
            ya.ap = ya.ap.rearrange("p (j e) -> p j e", j=4)
            ofb = [alloc("ofb%d" % i, 4 * 512) for i in range(2)]
            for b in ofb:
                b.ap = b.ap.rearrange("p (j e) -> p j e", j=4)
            zz = alloc("zz", 4 * 512)
            zz.ap = zz.ap.rearrange("p (j e) -> p j e", j=4)
            mixb = [alloc("mixb%d" % i, 1024, BF16) for i in range(2)]
            sq = T["tmp"][1]
            ssc = alloc("ssc", 32)
            for s, L in SEQ:
                for ti in range(L // 512):
                    r0 = ti * 512
                    rows = slice(r0, r0 + 512)
                    h = T["h"]
                    dma(h.ap, H1[s][rows, :].rearrange("(j p) d -> p j d", p=128), writes=T["hb"], sb=h)
                    for hb in T["hb"]:
                        hb.lw = (id(h), h.semcnt)
                    dma(ya.ap, YA[s][rows, :].rearrange("(j p) e -> p j e", p=128), writes=[ya], sb=ya)
                    dma(ofb[0].ap, OF[s][0, rows, :].rearrange("(j p) e -> p j e", p=128), writes=[ofb[0]], sb=ofb[0])
                    dma(ofb[1].ap, OF[s][1, rows, :].rearrange("(j p) e -> p j e", p=128), writes=[ofb[1]], sb=ofb[1])
                    dma(zz.ap, ZS[s][rows, :].rearrange("(j p) e -> p j e", p=128), writes=[zz], sb=zz)
                    o = ofb[0]
                    op("pool", lambda E: E.tensor_tensor(out=o.ap, in0=o.ap, in1=ofb[1].ap, op=ALU.add), reads=[ofb[1], o], writes=[o])
                    e2 = ofb[1]
                    op("act", lambda E: E.activation(out=e2.ap, in_=zz.ap, func=AF.Exp, scale=-1.0), reads=[zz], writes=[e2])
                    op("act", lambda E: E.activation(out=e2.ap, in_=e2.ap, func=AF.Ln, bias=1.0), reads=[e2], writes=[e2])
                    op("act", lambda E: E.activation(out=e2.ap, in_=e2.ap, func=AF.Exp, scale=-1.0), reads=[e2], writes=[e2])
                    op("pool", lambda E: E.tensor_tensor(out=zz.ap, in0=zz.ap, in1=e2.ap, op=ALU.mult), reads=[e2, zz], writes=[zz])
                    for j in range(4):
                        op("dve", lambda E, j=j: E.tensor_tensor(out=sq[:, 0:512], in0=o.ap[:, j, :], in1=o.ap[:, j, :], op=ALU.mult), reads=[o], writes=[sq])
                        op("dve", lambda E, j=j: E.tensor_reduce(out=ssc[:, j * 8:j * 8 + 4], in_=sq.ap[:, 0:512].rearrange("p (h d) -> p h d", h=4), axis=AX.X, op=ALU.add),
                           reads=[sq], writes=[ssc])
                        op("act", lambda E, j=j: E.activation(out=T["junk"][:, 0:512], in_=ya.ap[:, j, :], func=AF.Square, accum_out=ssc[:, j * 8 + 4:j * 8 + 5]),
                           reads=[ya], writes=[ssc])
                    sv = ssc.ap[:, 0:32].rearrange("p (j k) -> p j k", j=4)
                    op("dve", lambda E, sv=sv: E.tensor_scalar(out=sv[:, :, 0:4], in0=sv[:, :, 0:4], scalar1=1.0 / 128, scalar2=EPS, op0=ALU.mult, op1=ALU.add), reads=[ssc], writes=[ssc])
                    op("dve", lambda E, sv=sv: E.tensor_scalar(out=sv[:, :, 4:5], in0=sv[:, :, 4:5], scalar1=1.0 / 512, scalar2=EPS, op0=ALU.mult, op1=ALU.add), reads=[ssc], writes=[ssc])
                    op("act", lambda E, sv=sv: E.activation(out=sv[:, :, 0:5], in_=sv[:, :, 0:5], func=AF.Ln), reads=[ssc], writes=[ssc])
                    op("act", lambda E, sv=sv: E.activation(out=sv[:, :, 0:5], in_=sv[:, :, 0:5], func=AF.Exp, scale=-0.5), reads=[ssc], writes=[ssc])
                    xT = T["xT"]
                    for j in range(4):
                        mb = mixb[j % 2]
                        op("act", lambda E, j=j, mb=mb: E.activation(out=mb[:, 0:512], in_=ya.ap[:, j, :], func=AF.Copy, scale=ssc[:, j * 8 + 4:j * 8 + 5]),
                           reads=[ya, ssc], writes=[mb])
                        op("dve", lambda E, j=j: E.tensor_tensor(out=o.ap[:, j, :].rearrange("p (h d) -> p h d", h=4), in0=o.ap[:, j, :].rearrange("p (h d) -> p h d", h=4),
                                                                 in1=ssc.ap[:, j * 8:j * 8 + 4].unsqueeze(2).to_broadcast([128, 4, 128]), op=ALU.mult), reads=[ssc, o], writes=[o])
                        op("pool", lambda E, j=j, mb=mb: E.tensor_tensor(out=mb[:, 512:1024], in0=o.ap[:, j, :], in1=zz.ap[:, j, :], op=ALU.mult), reads=[o, zz, mb], writes=[mb])
                        bk = j % 2
                        for dc in range(8):
                            op("pe", lambda E, dc=dc, bk=bk, mb=mb: E.transpose(out=pb(bk, dc * 128, (dc + 1) * 128), in_=mb[:, dc * 128:(dc + 1) * 128], identity=identb[:]),
                               reads=[mb, identb], writes=[BK[bk]])
                        op("dve", lambda E, j=j, bk=bk: E.tensor_copy(out=xT.ap[:, :, j * 128:(j + 1) * 128], in_=pb(bk).rearrange("p (c t) -> p c t", c=8)),
                           writes=[xT, BK[bk]])
                    ss2 = T["ss2"]
                    for j in range(4):
                        b0 = 2 if j % 2 == 0 else 4
                        for hf in range(2):
                            for c in range(8):
                                op("pe", lambda E, c=c, hf=hf, b0=b0, j=j: E.matmul(pf(b0 + hf), lhsT=xT.ap[:, c, j * 128:(j + 1) * 128], rhs=wo.ap[:, c, hf * 512:(hf + 1) * 512],
                                                                                     start=(c == 0), stop=(c == 7)), reads=[xT, wo], writes=[BK[b0 + hf]])
                        yps = psum[:, b0 * 512:(b0 + 2) * 512]
                        op("act", lambda E, j=j, yps=yps: E.activation(out=T["junk"][:], in_=yps, func=AF.Square, accum_out=ss2[:, j:j + 1]), writes=[ss2, BK[b0], BK[b0 + 1]])
                        op("dve", lambda E, j=j: E.tensor_scalar(out=ss2[:, 4 + j:5 + j], in0=ss2[:, j:j + 1], scalar1=1.0 / D, scalar2=EPS, op0=ALU.mult, op1=ALU.add), reads=[ss2], writes=[ss2])
                        op("act", lambda E, j=j: E.activation(out=ss2[:, 4 + j:5 + j], in_=ss2[:, 4 + j:5 + j], func=AF.Ln), reads=[ss2], writes=[ss2])
                        op("act", lambda E, j=j: E.activation(out=ss2[:, 4 + j:5 + j], in_=ss2[:, 4 + j:5 + j], func=AF.Exp, scale=-0.5), reads=[ss2], writes=[ss2])
                        tmp = T["tmp"][j % 2]
                        op("dve", lambda E, j=j, yps=yps, tmp=tmp: E.scalar_tensor_tensor(out=tmp[:], in0=yps, scalar=ss2[:, 4 + j:5 + j], in1=gB["mix_post_g"][:], op0=ALU.mult, op1=ALU.mult),
                           reads=[ss2, gB["mix_post_g"]], writes=[tmp, BK[b0], BK[b0 + 1]])
                        op("pool", lambda E, j=j, tmp=tmp: E.tensor_tensor(out=T["hs"][j], in0=T["hs"][j], in1=tmp[:], op=ALU.add), reads=[tmp], writes=[T["hb"][j]])
                    ffn(T["hb"], T["hs"], "ffn2_pre_g", WB["ffn2_w_gate"], WB["ffn2_w_up"], T["wd"], gB["ffn2_post_g"], T)
                    ss = T["ss"]
                    for j in range(4):
                        op("act", lambda E, j=j: E.activation(out=T["junk"][:], in_=T["hs"][j], func=AF.Square, accum_out=ss[:, j:j + 1]), reads=[T["hb"][j]], writes=[ss])
                    rstd_chain(ss, 4, D)
                    for j in range(4):
                        e1 = "dve"
                        op(e1, lambda E, j=j: E.scalar_tensor_tensor(out=T["hs"][j], in0=T["hs"][j], scalar=ss[:, j:j + 1], in1=gB["final_norm_g"][:], op0=ALU.mult, op1=ALU.mult),
                           reads=[ss, gB["final_norm_g"]], writes=[T["hb"][j]])
                    dma(Y[s][rows, :].rearrange("(j p) d -> p j d", p=128), h.ap, reads=T["hb"], sb=h, queue="pool")
            phase_reset()
        P.barrier()
        P.emit()
        stats = dict(nins=P.nins, nwaits=P.nwaits, nsem=len(P.dmabufs) + 5)
    return nc, stats


def rope_table(L):
    inv = 10000.0 ** (-np.arange(0, 32, 2, dtype=np.float32) / 32)
    ang = np.arange(L, dtype=np.float32)[:, None] * inv[None, :].astype(np.float32)
    c, s = np.cos(ang).astype(np.float32), np.sin(ang).astype(np.float32)
    return np.ascontiguousarray(np.concatenate([c, c, -s, s], axis=1).astype(np.float32))


def gdn_masks():
    i = np.arange(64)
    m = np.zeros((64, 5, 8, 64), np.float32)
    for hd in range(8):
        fwd = hd < 4
        al = (i[:, None] > i[None, :]) if fwd else (i[:, None] < i[None, :])
        m[:, 0, hd, :] = np.where(al, 0.0, NEG)
        al = (i[None, :] > i[:, None]) if fwd else (i[None, :] < i[:, None])
        m[:, 1, hd, :] = np.where(al, 0.0, NEG)
        al = (i[None, :] >= i[:, None]) if fwd else (i[None, :] <= i[:, None])
        m[:, 2, hd, :] = np.where(al, 0.0, NEG)
    m[:, 3, 0, :] = (i[:, None] <= i[None, :]).astype(np.float32)
    m[:, 3, 1, :] = (i[:, None] >= i[None, :]).astype(np.float32)
    return m


_CACHE = {}


def kernel(**inputs):
    xp = np.asarray(inputs["x_prompt"], np.float32)
    xs = np.asarray(inputs["x_sample"], np.float32)
    B, Lp, _ = xp.shape
    Ls = xs.shape[1]
    assert B == 8 and xs.shape[0] == 8
    key = (Lp, Ls)
    if key not in _CACHE:
        _CACHE[key] = build(Lp, Ls)[0]
    nc = _CACHE[key]
    shared = {n: np.ascontiguousarray(np.asarray(inputs[n], np.float32).reshape(WSHAPES[n])) for n in WNAMES}
    shared["rope_tab"] = rope_table(max(Lp, Ls))
    shared["gdn_masks"] = gdn_masks()
    in_maps = []
    for c in range(8):
        m = dict(shared)
        m["x_p"] = np.ascontiguousarray(xp[c])
        m["x_s"] = np.ascontiguousarray(xs[c])
        in_maps.append(m)
    res = run_bass_kernel_spmd(nc, in_maps, core_ids=list(range(8)))
    yp = np.stack([np.asarray(r["y_p"], np.float32) for r in res.results], 0)
    ys = np.stack([np.asarray(r["y_s"], np.float32) for r in res.results], 0)
    return (yp, ys)
```

```python
from contextlib import ExitStack
import numpy as np
import concourse.bass as bass
import concourse.mybir as mybir
from concourse.bass_utils import run_bass_kernel_spmd

F32 = mybir.dt.float32
BF16 = mybir.dt.bfloat16
ALU = mybir.AluOpType
AF = mybir.ActivationFunctionType
AX = mybir.AxisListType

D = 1024
DFF = 2816
NF = DFF // 128
INC = 2736
EPS = 1e-6
NEG = -30000.0
ENG = ("pe", "act", "dve", "pool", "sp")


class Buf:
    __slots__ = ("name", "ap", "lw", "rd", "sem", "semcnt")

    def __init__(self, name, ap):
        self.name = name
        self.ap = ap
        self.lw = None
        self.rd = {}
        self.sem = None
        self.semcnt = 0

    def __getitem__(self, k):
        return self.ap[k]


class Prog:
    def __init__(self, nc, stack):
        self.nc = nc
        self.stack = stack
        self.q = {e: [] for e in ENG}
        self.cnt = {e: 0 for e in ENG}
        self.known = {e: {} for e in ENG}
        self.hist = {}
        self.esem = {e: stack.enter_context(nc.semaphore("s_" + e)) for e in ENG}
        self.semobj = {e: self.esem[e] for e in ENG}
        self.dmabufs = []
        self.nwaits = 0
        self.nins = 0
        self.E = {"pe": nc.tensor, "act": nc.scalar, "dve": nc.vector, "pool": nc.gpsimd, "sp": nc.sync}

    def _need(self, deps, tok):
        if tok is None:
            return
        k, v = tok
        if deps.get(k, 0) < v:
            deps[k] = v

    def _collect(self, reads, writes):
        deps = {}
        for b in reads:
            self._need(deps, b.lw)
        for b in writes:
            self._need(deps, b.lw)
            for k, v in b.rd.items():
                self._need(deps, (k, v))
        return deps

    def _emit_waits(self, eng, deps):
        kn = self.known[eng]
        new = None
        for k, v in deps.items():
            if k == eng:
                if eng == "pe" or eng == "sp":
                    continue
                if self.cnt[eng] - v > 1:
                    continue
            cur = new if new is not None else kn
            if cur.get(k, 0) >= v:
                continue
            sem = self.semobj[k]
            self.E[eng].wait_ge(sem, v)
            self.nwaits += 1
            if new is None:
                new = dict(kn)
            new[k] = v
            h = self.hist.get((k, v))
            if h:
                for k2, v2 in h.items():
                    if k2 != eng and new.get(k2, 0) < v2:
                        new[k2] = v2
        if new is not None:
            self.known[eng] = new

    def op(self, eng, fn, reads=(), writes=()):
        deps = self._collect(reads, writes)
        self._emit_waits(eng, deps)
        sem = self.esem[eng]
        self.cnt[eng] += 1
        v = self.cnt[eng]
        fn(self.E[eng]).then_inc(sem, 1)
        self.nins += 1
        tok = (eng, v)
        self.hist[tok] = self.known[eng]
        for b in writes:
            b.lw = tok
            b.rd = {}
        for b in reads:
            if b.rd.get(eng, 0) < v:
                b.rd[eng] = v
        return tok

    def dma(self, out_ap, in_ap, reads=(), writes=(), sb=None, queue="sp", **kw):
        deps = self._collect(reads, writes)
        self._emit_waits(queue, deps)
        if sb.sem is None:
            sb.sem = self.stack.enter_context(self.nc.semaphore("d%d_%s" % (len(self.dmabufs), sb.name)))
            self.semobj[id(sb)] = sb.sem
            self.dmabufs.append(sb)
        sb.semcnt += 16
        v = sb.semcnt
        sem = sb.sem
        self.E[queue].dma_start(out=out_ap, in_=in_ap, **kw).then_inc(sem, 16)
        self.nins += 1
        tok = (id(sb), v)
        self.hist[tok] = self.known[queue]
        for b in writes:
            b.lw = tok
            b.rd = {}
        for b in reads:
            if b.rd.get(tok[0], 0) < v:
                b.rd[tok[0]] = v
        return tok

    def barrier(self):
        deps = {}
        for e in ("pe", "act", "dve", "pool"):
            if self.cnt[e]:
                deps[e] = self.cnt[e]
        for b in self.dmabufs:
            deps[id(b)] = b.semcnt
        for e in ENG:
            d = {k: v for k, v in deps.items() if k != e}
            self._emit_waits(e, d)

    def emit(self):
        pass


WNAMES = ["ffn1_pre_g", "ffn1_w_gate", "ffn1_w_up", "ffn1_w_down", "ffn1_post_g", "mix_pre_g", "w_in",
          "mla_q_norm_g", "mla_w_uq", "mla_kv_norm_g", "mla_w_ukv", "mla_out_norm_g", "gdn_conv_w",
          "gdn_a_log", "gdn_dt_bias", "gdn_out_norm_g", "w_out", "mix_post_g", "ffn2_pre_g",
          "ffn2_w_gate", "ffn2_w_up", "ffn2_w_down", "ffn2_post_g", "final_norm_g"]
WSHAPES = {"ffn1_pre_g": [1, D], "ffn1_w_gate": [D, DFF], "ffn1_w_up": [D, DFF], "ffn1_w_down": [DFF, D],
           "ffn1_post_g": [1, D], "mix_pre_g": [1, D], "w_in": [D, INC], "mla_q_norm_g": [1, 384],
           "mla_w_uq": [384, 768], "mla_kv_norm_g": [1, 256], "mla_w_ukv": [256, 1024],
           "mla_out_norm_g": [1, 512], "gdn_conv_w": [5, 1536], "gdn_a_log": [1, 8], "gdn_dt_bias": [1, 8],
           "gdn_out_norm_g": [1, 128], "w_out": [D, D], "mix_post_g": [1, D], "ffn2_pre_g": [1, D],
           "ffn2_w_gate": [D, DFF], "ffn2_w_up": [D, DFF], "ffn2_w_down": [DFF, D], "ffn2_post_g": [1, D],
           "final_norm_g": [1, D]}


def build(Lp, Ls, debug=False, phases="S A M B G H C"):
    phases = phases.split()
    nc = bass.Bass("TRN2", target_bir_lowering=False)
    Lmax = max(Lp, Ls)
    SEQ = (("p", Lp), ("s", Ls))

    def din(name, shape, dt=F32):
        return nc.dram_tensor(name, list(shape), dt, kind="ExternalInput").ap()

    def dscr(name, shape, dt=F32):
        return nc.dram_tensor(name, list(shape), dt, kind="ExternalOutput" if debug else "Internal").ap()

    X = {"p": din("x_p", [Lp, D]), "s": din("x_s", [Ls, D])}
    W = {n: din(n, WSHAPES[n]) for n in WNAMES}
    ROPE = din("rope_tab", [Lmax, 64])
    GMASK = din("gdn_masks", [64, 5, 8, 64])
    Y = {"p": nc.dram_tensor("y_p", [Lp, D], F32, kind="ExternalOutput").ap(),
         "s": nc.dram_tensor("y_s", [Ls, D], F32, kind="ExternalOutput").ap()}
    WB = {n: dscr("bf_" + n, WSHAPES[n], BF16) for n in
          ["ffn1_w_gate", "ffn1_w_up", "ffn1_w_down", "w_in", "ffn2_w_gate", "ffn2_w_up", "ffn2_w_down"]}
    H1 = {s: dscr("h1_" + s, [L, D]) for s, L in SEQ}
    LAT = {s: dscr("lat_" + s, [L, 672]) for s, L in SEQ}
    QKVT = {s: dscr("qkvT_" + s, [1536, L]) for s, L in SEQ}
    ZS = {s: dscr("z_" + s, [L, 512]) for s, L in SEQ}
    GBL = {s: dscr("gbl_" + s, [2, L, 12]) for s, L in SEQ}
    QT = {s: dscr("QT_" + s, [8, 96, L], BF16) for s, L in SEQ}
    KT = {s: dscr("KT_" + s, [8, 96, L], BF16) for s, L in SEQ}
    VV = {s: dscr("V_" + s, [8, 128, L // 128, 64], BF16) for s, L in SEQ}
    YA = {s: dscr("ya_" + s, [L, 512]) for s, L in SEQ}
    TOK = {s: dscr("tok_" + s, [L, 1536], BF16) for s, L in SEQ}
    OF = {s: dscr("of_" + s, [2, L, 512]) for s, L in SEQ}

    with ExitStack() as st:
        P = Prog(nc, st)
        ARENA = 52800
        arena = st.enter_context(nc.sbuf_tensor("arena", [128, ARENA], F32))
        psum = st.enter_context(nc.psum_tensor("psum", [128, 4096], F32))
        psum_bf = psum.bitcast(BF16)
        BK = [Buf("bank%d" % i, psum[:, i * 512:(i + 1) * 512]) for i in range(8)]

        def pf(i, a=0, b=512):
            return psum[:, i * 512 + a:i * 512 + b]

        def pb(i, a=0, b=1024):
            return psum_bf[:, i * 1024 + a:i * 1024 + b]

        off = [0]
        perm_end = [0]

        def alloc(name, ncols, dt=F32, parts=128):
            n32 = ncols if dt == F32 else (ncols + 1) // 2
            assert off[0] + n32 <= ARENA, ("SBUF arena overflow", name, off[0] + n32)
            a = arena[0:parts, off[0]:off[0] + n32]
            off[0] += n32
            if dt != F32:
                a = a.bitcast(dt)[:, 0:ncols]
            return Buf(name, a)

        def phase_reset():
            P.barrier()
            off[0] = perm_end[0]

        op = P.op
        dma = P.dma

        identf = alloc("identf", 128)
        identb = alloc("identb", 128, BF16)
        onesf = alloc("onesf", 128)
        gT = {n: alloc("gT_" + n, 8) for n in ("ffn1_pre_g", "mix_pre_g", "ffn2_pre_g")}
        gB = {n: alloc("gB_" + n, D) for n in ("ffn1_post_g", "mix_post_g", "ffn2_post_g", "final_norm_g")}
        small = alloc("small", 64)
        perm_end[0] = off[0]

        op("pool", lambda E: E.memset(identf[:], 0.0), writes=[identf])
        op("pool", lambda E: E.affine_select(out=identf[:], in_=identf[:], pattern=[[-1, 128]], compare_op=ALU.not_equal,
                                             fill=1.0, base=0, channel_multiplier=1), reads=[identf], writes=[identf])
        op("dve", lambda E: E.tensor_copy(out=identb[:], in_=identf[:]), reads=[identf], writes=[identb])
        op("pool", lambda E: E.memset(onesf[:], 1.0), writes=[onesf])
        for n, b in gT.items():
            dma(b[:], W[n].rearrange("o (c p) -> p (o c)", p=128), writes=[b], sb=b, allow_slow_non_contiguous=True)
        for n, b in gB.items():
            dma(b[:], W[n].broadcast_to([128, D]), writes=[b], sb=b)
        for n in ("ffn1_post_g", "ffn2_post_g"):
            b = gB[n]
            op("pool", lambda E, b=b: E.tensor_scalar(out=b[:], in0=b[:], scalar1=0.5, scalar2=None, op0=ALU.mult),
               reads=[b], writes=[b])
        sm_q = Buf("sm_q", small.ap)
        dma(small[:, 0:3], W["mla_q_norm_g"].rearrange("o (c p) -> p (o c)", p=128), writes=[small], sb=small,
            allow_slow_non_contiguous=True)
        dma(small[:, 3:5], W["mla_kv_norm_g"].rearrange("o (c p) -> p (o c)", p=128), reads=[small], sb=small,
            allow_slow_non_contiguous=True)
        dma(small[:, 5:9], W["mla_out_norm_g"].rearrange("o (c p) -> p (o c)", p=128), reads=[small], sb=small,
            allow_slow_non_contiguous=True)
        dma(small[:, 9:10], W["gdn_out_norm_g"].rearrange("o (c p) -> p (o c)", p=128), reads=[small], sb=small,
            allow_slow_non_contiguous=True)
        dma(small[:, 16:24], W["gdn_dt_bias"].broadcast_to([128, 8]), reads=[small], sb=small)
        dma(small[:, 24:32], W["gdn_a_log"].broadcast_to([128, 8]), reads=[small], sb=small)
        small.lw = (id(small), small.semcnt)
        op("act", lambda E: E.activation(out=small[:, 24:32], in_=small[:, 24:32], func=AF.Exp), reads=[small], writes=[small])
        op("dve", lambda E: E.tensor_scalar(out=small[:, 24:32], in0=small[:, 24:32], scalar1=-1.0, scalar2=None, op0=ALU.mult),
           reads=[small], writes=[small])

        def rstd_chain(ss, k, n, eps=EPS, post=None):
            op("dve", lambda E: E.tensor_scalar(out=ss[:, 0:k], in0=ss[:, 0:k], scalar1=1.0 / n, scalar2=eps, op0=ALU.mult, op1=ALU.add),
               reads=[ss], writes=[ss])
            op("act", lambda E: E.activation(out=ss[:, 0:k], in_=ss[:, 0:k], func=AF.Ln), reads=[ss], writes=[ss])
            op("act", lambda E: E.activation(out=ss[:, 0:k], in_=ss[:, 0:k], func=AF.Exp, scale=-0.5), reads=[ss], writes=[ss])

        cast_rr = [0]

        def cast(out_ap, in_ap, reads, writes):
            e = ("dve", "pool", "act")[cast_rr[0] % 3]
            cast_rr[0] += 1
            if e == "act":
                op("act", lambda E: E.activation(out=out_ap, in_=in_ap, func=AF.Copy), reads=reads, writes=writes)
            else:
                op(e, lambda E: E.tensor_copy(out=out_ap, in_=in_ap), reads=reads, writes=writes)

        if "S" in phases:
            stf = [alloc("stf%d" % i, DFF) for i in range(2)]
            stb = [alloc("stb%d" % i, DFF, BF16) for i in range(2)]
            it = 0
            for n in WB:
                K_, N_ = WSHAPES[n]
                for kc in range(K_ // 128):
                    f, b = stf[it % 2], stb[it % 2]
                    dma(f[:, 0:N_], W[n][kc * 128:(kc + 1) * 128, :], writes=[f], sb=f)
                    cast(b[:, 0:N_], f[:, 0:N_], [f], [b])
                    dma(WB[n][kc * 128:(kc + 1) * 128, :], b[:, 0:N_], reads=[b], sb=b, queue="pool")
                    it += 1
        phase_reset()

        def norm_T(srcs, src_bufs, gTb, xT, ss, xnb, junk):
            for j in range(4):
                op("act", lambda E, j=j: E.activation(out=junk[:], in_=srcs[j], func=AF.Square, accum_out=ss[:, j:j + 1]),
                   reads=[src_bufs[j]], writes=[ss])
            rstd_chain(ss, 4, D)
            for j in range(4):
                xb = xnb[j % 2]
                if j % 2 == 0:
                    op("dve", lambda E, j=j, xb=xb: E.tensor_scalar(out=xb[:], in0=srcs[j], scalar1=ss[:, j:j + 1], scalar2=None, op0=ALU.mult),
                       reads=[src_bufs[j], ss], writes=[xb])
                else:
                    op("act", lambda E, j=j, xb=xb: E.activation(out=xb[:], in_=srcs[j], func=AF.Copy, scale=ss[:, j:j + 1]),
                       reads=[src_bufs[j], ss], writes=[xb])
                bk = j % 2
                for dc in range(8):
                    op("pe", lambda E, dc=dc, bk=bk, xb=xb: E.transpose(out=pb(bk, dc * 128, (dc + 1) * 128), in_=xb[:, dc * 128:(dc + 1) * 128], identity=identb[:]),
                       reads=[xb, identb], writes=[BK[bk]])
                op("dve", lambda E, j=j, bk=bk: E.tensor_tensor(
                    out=xT.ap[:, :, j * 128:(j + 1) * 128], in0=pb(bk).rearrange("p (c t) -> p c t", c=8),
                    in1=gTb[:, 0:8].unsqueeze(2).to_broadcast([128, 8, 128]), op=ALU.mult),
                    reads=[gTb], writes=[xT, BK[bk]])

        PIECES = [(c0_, 256) for c0_ in range(0, DFF, 256)]

        def ffn(hbufs, hsrcs, pre_g, wg, wu, wd_res, post_bc, T):
            norm_T(hsrcs, hbufs, gT[pre_g], T["xT"], T["ss"], T["xnb"], T["junk"])
            xT, act = T["xT"], T["act"]
            for si, (c0, w) in enumerate(PIECES):
                sg = T["hsl"][T["hi"] % 6]
                su = T["hsl"][(T["hi"] + 1) % 6]
                T["hi"] += 2
                dma(sg.ap[:, :, 0:w], wg[:, c0:c0 + w].rearrange("(dc p) f -> p dc f", p=128), writes=[sg], sb=sg)
                dma(su.ap[:, :, 0:w], wu[:, c0:c0 + w].rearrange("(dc p) f -> p dc f", p=128), writes=[su], sb=su)
                for fl in range(w // 128):
                    f = c0 // 128 + fl
                    g_, u_ = 2 + f % 2, 4 + f % 2
                    for dc in range(8):
                        op("pe", lambda E, dc=dc, fl=fl, g_=g_, sg=sg: E.matmul(pf(g_), lhsT=sg.ap[:, dc, fl * 128:(fl + 1) * 128], rhs=xT.ap[:, dc, :],
                                                                                  start=(dc == 0), stop=(dc == 7)), reads=[sg, xT], writes=[BK[g_]])
                    for dc in range(8):
                        op("pe", lambda E, dc=dc, fl=fl, u_=u_, su=su: E.matmul(pf(u_), lhsT=su.ap[:, dc, fl * 128:(fl + 1) * 128], rhs=xT.ap[:, dc, :],
                                                                                  start=(dc == 0), stop=(dc == 7)), reads=[su, xT], writes=[BK[u_]])
                    et = T["etmp"][f % 2]
                    op("act", lambda E, et=et, g_=g_: E.activation(out=et[:], in_=pf(g_), func=AF.Exp, scale=-1.0), writes=[et, BK[g_]])
                    op("act", lambda E, et=et: E.activation(out=et[:], in_=et[:], func=AF.Ln, bias=1.0), reads=[et], writes=[et])
                    op("act", lambda E, et=et: E.activation(out=et[:], in_=et[:], func=AF.Exp, scale=-1.0), reads=[et], writes=[et])
                    op("dve", lambda E, et=et, g_=g_: E.tensor_tensor(out=et[:], in0=et[:], in1=pf(g_), op=ALU.mult), reads=[et], writes=[et, BK[g_]])
                    op("dve", lambda E, et=et, u_=u_, f=f: E.tensor_tensor(out=act.ap[:, f, :], in0=et[:], in1=pf(u_), op=ALU.mult),
                       reads=[et], writes=[T["actc"][f], BK[u_]])
            ss2 = T["ss2"]
            for j in range(4):
                b0 = 2 if j % 2 == 0 else 4
                for hf in range(2):
                    for f in range(NF):
                        op("pe", lambda E, f=f, hf=hf, b0=b0, j=j: E.matmul(pf(b0 + hf), lhsT=act.ap[:, f, j * 128:(j + 1) * 128],
                                                                             rhs=wd_res.ap[:, f, hf * 512:(hf + 1) * 512], start=(f == 0), stop=(f == NF - 1)),
                           reads=[T["actc"][f], wd_res], writes=[BK[b0 + hf]])
                yps = psum[:, b0 * 512:(b0 + 2) * 512]
                op("act", lambda E, j=j, yps=yps: E.activation(out=T["junk"][:], in_=yps, func=AF.Square, accum_out=ss2[:, j:j + 1]),
                   writes=[ss2, BK[b0], BK[b0 + 1]])
                op("dve", lambda E, j=j: E.tensor_scalar(out=ss2[:, 4 + j:5 + j], in0=ss2[:, j:j + 1], scalar1=1.0 / D, scalar2=EPS, op0=ALU.mult, op1=ALU.add),
                   reads=[ss2], writes=[ss2])
                op("act", lambda E, j=j: E.activation(out=ss2[:, 4 + j:5 + j], in_=ss2[:, 4 + j:5 + j], func=AF.Ln), reads=[ss2], writes=[ss2])
                op("act", lambda E, j=j: E.activation(out=ss2[:, 4 + j:5 + j], in_=ss2[:, 4 + j:5 + j], func=AF.Exp, scale=-0.5), reads=[ss2], writes=[ss2])
                tmp = T["tmp"][j % 2]
                op("dve", lambda E, j=j, yps=yps, tmp=tmp: E.scalar_tensor_tensor(out=tmp[:], in0=yps, scalar=ss2[:, 4 + j:5 + j], in1=post_bc[:],
                                                                                    op0=ALU.mult, op1=ALU.mult),
                   reads=[ss2, post_bc], writes=[tmp, BK[b0], BK[b0 + 1]])
                op("pool", lambda E, j=j, tmp=tmp: E.tensor_tensor(out=hsrcs[j], in0=hsrcs[j], in1=tmp[:], op=ALU.add),
                   reads=[tmp], writes=[hbufs[j]])

        def ffn_bufs(with_full=False):
            T = {}
            T["wd"] = alloc("wd_res", NF * D, BF16)
            T["wd"].ap = T["wd"].ap.rearrange("p (f d) -> p f d", f=NF)
            T["xT"] = alloc("xT", 8 * 512, BF16)
            T["xT"].ap = T["xT"].ap.rearrange("p (c t) -> p c t", c=8)
            T["act_off"] = off[0]
            T["act"] = alloc("act", NF * 512, BF16)
            T["act"].ap = T["act"].ap.rearrange("p (f t) -> p f t", f=NF)
            T["actc"] = [Buf("act%d" % f, T["act"].ap[:, f, :]) for f in range(NF)]
            T["slab"] = []
            for i in range(3 if with_full else 0):
                b = alloc("slab%d" % i, 8 * 512, BF16)
                b.ap = b.ap.rearrange("p (c f) -> p c f", c=8)
                T["slab"].append(b)
            T["slabi"] = 0
            T["hsl"] = []
            for i in range(6):
                b = alloc("hsl%d" % i, 8 * 256, BF16)
                b.ap = b.ap.rearrange("p (c f) -> p c f", c=8)
                T["hsl"].append(b)
            T["hi"] = 0
            T["etmp"] = [alloc("etmp%d" % i, 512) for i in range(2)]
            T["tmp"] = [alloc("tmp%d" % i, D) for i in range(2)]
            T["xnb"] = [alloc("xnb%d" % i, D, BF16) for i in range(2)]
            T["junk"] = alloc("junk", D, BF16)
            T["ss"] = alloc("ss", 8)
            T["ss2"] = alloc("ss2", 8)
            T["h"] = alloc("h", 4 * D)
            T["h"].ap = T["h"].ap.rearrange("p (j d) -> p j d", j=4)
            T["hb"] = [Buf("h%d" % j, T["h"].ap[:, j, :]) for j in range(4)]
            T["hs"] = [T["h"].ap[:, j, :] for j in range(4)]
            return T

        def load_wd(T, name):
            wd = T["wd"]
            for f0 in range(0, NF, 2):
                dma(wd.ap[:, f0:f0 + 2, :], WB[name][f0 * 128:(f0 + 2) * 128, :].rearrange("(f p) d -> p f d", p=128),
                    writes=[wd] if f0 == 0 else [], reads=[] if f0 == 0 else [], sb=wd)
            wd.lw = (id(wd), wd.semcnt)

        if "A" in phases:
            T = ffn_bufs(True)
            load_wd(T, "ffn1_w_down")
            pst = [alloc("pst%d" % i, 1200) for i in range(2)]
            qst = [alloc("qst%d" % i, 512) for i in range(3)]
            gst = [alloc("gst%d" % i, 64) for i in range(2)]
            pi = 0
            qi = 0
            tilesA = [(s, L, ti) for s, L in SEQ for ti in range(L // 512)]

            def load_x(ix):
                s_, L_, ti_ = tilesA[ix]
                dma(T["h"].ap, X[s_][ti_ * 512:(ti_ + 1) * 512, :].rearrange("(j p) d -> p j d", p=128), writes=T["hb"], sb=T["h"])
            load_x(0)
            for ix, (s, L, ti) in enumerate(tilesA):
                if True:
                    r0 = ti * 512
                    h = T["h"]
                    ffn(T["hb"], T["hs"], "ffn1_pre_g", WB["ffn1_w_gate"], WB["ffn1_w_up"], T["wd"], gB["ffn1_post_g"], T)
                    dma(H1[s][r0:r0 + 512, :].rearrange("(j p) d -> p j d", p=128), h.ap, reads=T["hb"], sb=h, queue="pool")
                    norm_T(T["hs"], T["hb"], gT["mix_pre_g"], T["xT"], T["ss"], T["xnb"], T["junk"])
                    if ix + 1 < len(tilesA):
                        load_x(ix + 1)
                    xT = T["xT"]
                    for q3 in range(3):
                        sl = T["slab"][T["slabi"] % 3]
                        T["slabi"] += 1
                        c0 = 672 + q3 * 512
                        dma(sl.ap[:, :, 0:512], WB["w_in"][:, c0:c0 + 512].rearrange("(dc p) f -> p dc f", p=128), writes=[sl], sb=sl)
                        for fl in range(4):
                            ch = q3 * 4 + fl
                            bk = 2 + ch % 2
                            for dc in range(8):
                                op("pe", lambda E, dc=dc, fl=fl, bk=bk, sl=sl: E.matmul(pf(bk), lhsT=sl.ap[:, dc, fl * 128:(fl + 1) * 128], rhs=xT.ap[:, dc, :],
                                                                                          start=(dc == 0), stop=(dc == 7)), reads=[sl, xT], writes=[BK[bk]])
                            qs = qst[qi % 3]
                            qi += 1
                            if ch % 2 == 0:
                                op("act", lambda E, qs=qs, bk=bk: E.activation(out=qs[:], in_=pf(bk), func=AF.Copy), writes=[qs, BK[bk]])
                            else:
                                op("dve", lambda E, qs=qs, bk=bk: E.tensor_copy(out=qs[:], in_=pf(bk)), writes=[qs, BK[bk]])
                            dma(QKVT[s][ch * 128:(ch + 1) * 128, r0:r0 + 512], qs[:], reads=[qs], sb=qs, queue="pool")
                    sA = T["slab"][T["slabi"] % 3]
                    sB = T["slab"][(T["slabi"] + 1) % 3]
                    sC = T["slab"][(T["slabi"] + 2) % 3]
                    T["slabi"] += 3
                    dma(sA.ap[:, :, 0:512], WB["w_in"][:, 0:512].rearrange("(dc p) f -> p dc f", p=128), writes=[sA], sb=sA)
                    dma(sB.ap[:, :, 0:512], WB["w_in"][:, 2208:2720].rearrange("(dc p) f -> p dc f", p=128), writes=[sB], sb=sB)
                    dma(sC.ap[:, :, 0:160], WB["w_in"][:, 512:672].rearrange("(dc p) f -> p dc f", p=128), writes=[sC], sb=sC)
                    dma(sC.ap[:, :, 160:176], WB["w_in"][:, 2720:2736].rearrange("(dc p) f -> p dc f", p=128), reads=[sC], sb=sC)
                    sC.lw = (id(sC), sC.semcnt)
                    for j in range(4):
                        for dc in range(8):
                            lhs = xT.ap[:, dc, j * 128:(j + 1) * 128]
                            op("pe", lambda E, dc=dc, lhs=lhs: E.matmul(pf(6), lhsT=lhs, rhs=sA.ap[:, dc, 0:512], start=(dc == 0), stop=(dc == 7)),
                               reads=[sA, xT], writes=[BK[6]])
                        for dc in range(8):
                            lhs = xT.ap[:, dc, j * 128:(j + 1) * 128]
                            op("pe", lambda E, dc=dc, lhs=lhs: E.matmul(pf(7), lhsT=lhs, rhs=sB.ap[:, dc, 0:512], start=(dc == 0), stop=(dc == 7)),
                               reads=[sB, xT], writes=[BK[7]])
                        for dc in range(8):
                            lhs = xT.ap[:, dc, j * 128:(j + 1) * 128]
                            op("pe", lambda E, dc=dc, lhs=lhs: E.matmul(pf(0, 0, 176), lhsT=lhs, rhs=sC.ap[:, dc, 0:176], start=(dc == 0), stop=(dc == 7)),
                               reads=[sC, xT], writes=[BK[0]])
                        ps_ = pst[pi % 2]
                        gs_ = gst[pi % 2]
                        pi += 1
                        op("act", lambda E, ps_=ps_: E.activation(out=ps_[:, 0:512], in_=pf(6), func=AF.Copy), writes=[ps_, BK[6]])
                        op("dve", lambda E, ps_=ps_: E.tensor_copy(out=ps_[:, 672:1184], in_=pf(7)), reads=[ps_], writes=[ps_, BK[7]])
                        op("dve", lambda E, ps_=ps_: E.tensor_copy(out=ps_[:, 512:672], in_=pf(0, 0, 160)), reads=[ps_], writes=[ps_, BK[0]])
                        op("dve", lambda E, gs_=gs_: E.tensor_tensor(out=gs_[:, 0:8], in0=pf(0, 160, 168), in1=small[:, 16:24], op=ALU.add),
                           reads=[small], writes=[gs_, BK[0]])
                        op("act", lambda E, gs_=gs_: E.activation(out=gs_[:, 0:8], in_=gs_[:, 0:8], func=AF.Exp), reads=[gs_], writes=[gs_])
                        op("act", lambda E, gs_=gs_: E.activation(out=gs_[:, 8:16], in_=pf(0, 168, 176), func=AF.Exp, scale=-1.0), reads=[gs_], writes=[gs_, BK[0]])
                        op("act", lambda E, gs_=gs_: E.activation(out=gs_[:, 0:16], in_=gs_[:, 0:16], func=AF.Ln, bias=1.0), reads=[gs_], writes=[gs_])
                        gv = gs_.ap[:, 32:56].rearrange("p (d k) -> p d k", d=2)
                        op("dve", lambda E, gs_=gs_, gv=gv: E.tensor_tensor(out=gv[:, :, 0:4], in0=gs_.ap[:, 0:8].rearrange("p (d k) -> p d k", d=2),
                                                                            in1=small.ap[:, 24:32].rearrange("p (d k) -> p d k", d=2), op=ALU.mult),
                           reads=[gs_, small], writes=[gs_])
                        op("dve", lambda E, gs_=gs_, gv=gv: E.tensor_scalar(out=gv[:, :, 4:8], in0=gs_.ap[:, 8:16].rearrange("p (d k) -> p d k", d=2),
                                                                            scalar1=-1.0, scalar2=None, op0=ALU.mult), reads=[gs_], writes=[gs_])
                        op("act", lambda E, gs_=gs_, gv=gv: E.activation(out=gv[:, :, 8:12], in_=gs_.ap[:, 8:16].rearrange("p (d k) -> p d k", d=2),
                                                                         func=AF.Exp, scale=-1.0), reads=[gs_], writes=[gs_])
                        rr = slice(r0 + j * 128, r0 + (j + 1) * 128)
                        dma(LAT[s][rr, :], ps_[:, 0:672], reads=[ps_], sb=ps_, queue="pool")
                        dma(ZS[s][rr, :], ps_[:, 672:1184], reads=[ps_], sb=ps_, queue="pool")
                        dma(GBL[s][:, rr, :].rearrange("d p k -> p d k"), gv, reads=[gs_], sb=gs_, queue="pool")
            phase_reset()

        if "M" in phases:
            wst = alloc("wst", 1024)
            wuq = alloc("wuq", 3 * 768, BF16)
            wuq.ap = wuq.ap.rearrange("p (c f) -> p c f", c=3)
            wk = alloc("wk", 2 * 512, BF16)
            wk.ap = wk.ap.rearrange("p (c f) -> p c f", c=2)
            wv = alloc("wv", 2 * 512, BF16)
            wv.ap = wv.ap.rearrange("p (c f) -> p c f", c=2)
            for c in range(3):
                dma(wst[:, 0:768], W["mla_w_uq"][c * 128:(c + 1) * 128, :], writes=[wst], sb=wst)
                op("dve", lambda E, c=c: E.tensor_scalar(out=wuq.ap[:, c, :], in0=wst[:, 0:768], scalar1=small[:, c:c + 1], scalar2=None, op0=ALU.mult),
                   reads=[wst, small], writes=[wuq])
            for c in range(2):
                dma(wst[:, 0:1024], W["mla_w_ukv"][c * 128:(c + 1) * 128, :], writes=[wst], sb=wst)
                wv4 = wst.ap[:, 0:1024].rearrange("p (h e) -> p h e", h=8)
                op("dve", lambda E, c=c, wv4=wv4: E.tensor_scalar(out=wk.ap[:, c, :].rearrange("p (h e) -> p h e", h=8), in0=wv4[:, :, 0:64],
                                                                  scalar1=small[:, 3 + c:4 + c], scalar2=None, op0=ALU.mult), reads=[wst, small], writes=[wk])
                op("dve", lambda E, c=c, wv4=wv4: E.tensor_scalar(out=wv.ap[:, c, :].rearrange("p (h e) -> p h e", h=8), in0=wv4[:, :, 64:128],
                                                                  scalar1=small[:, 3 + c:4 + c], scalar2=None, op0=ALU.mult), reads=[wst, small], writes=[wv])
            lat = [alloc("lat%d" % i, 672) for i in range(2)]
            rp = [alloc("rp%d" % i, 64) for i in range(2)]
            lnb = [alloc("lnb%d" % i, 736, BF16) for i in range(2)]
            for b in lnb:
                op("pool", lambda E, b=b: E.memset(b[:, 640:736], 0.0), writes=[b])
            ssm = [alloc("ssm%d" % i, 4) for i in range(2)]
            junkm = alloc("junkm", 384, BF16)
            cT = [alloc("cT%d" % i, 3 * 128, BF16) for i in range(2)]
            ckvT = alloc("ckvT", 2 * 512, BF16)
            ckvT.ap = ckvT.ap.rearrange("p (c t) -> p c t", c=2)
            ckvc = [Buf("ckv%d" % j, ckvT.ap[:, :, j * 128:(j + 1) * 128]) for j in range(4)]
            kpst = alloc("kpst", 512, BF16)
            kpc = [Buf("kp%d" % j, kpst.ap[:, j * 128:(j + 1) * 128]) for j in range(4)]
            qr = [alloc("qr%d" % i, 768, BF16) for i in range(2)]
            rtmp = [alloc("rtmp%d" % i, 512) for i in range(2)]
            qtst = alloc("qtst", 8 * 512, BF16)
            qtst.ap = qtst.ap.rearrange("p (h t) -> p h t", h=8)
            qtc = [Buf("qt%d" % j, qtst.ap[:, :, j * 128:(j + 1) * 128]) for j in range(4)]
            ktst = alloc("ktst", 4 * 512, BF16)
            ktst.ap = ktst.ap.rearrange("p (a t) -> p a t", a=4)
            vst = alloc("vst", 4 * 512, BF16)
            vst.ap = vst.ap.rearrange("p (j f) -> p j f", j=4)
            vsc = [Buf("vs%d" % j, vst.ap[:, j, :]) for j in range(4)]
            li = 0
            for s, L in SEQ:
                for ti in range(L // 512):
                    r0 = ti * 512
                    for j in range(4):
                        rr = slice(r0 + j * 128, r0 + (j + 1) * 128)
                        lt, rpt, lb, sm_, ct = lat[li % 2], rp[li % 2], lnb[li % 2], ssm[li % 2], cT[li % 2]
                        qrt, rt = qr[li % 2], rtmp[li % 2]
                        li += 1
                        dma(lt[:], LAT[s][rr, :], writes=[lt], sb=lt)
                        dma(rpt[:], ROPE[rr, :], writes=[rpt], sb=rpt)
                        op("act", lambda E, lt=lt, sm_=sm_: E.activation(out=junkm[:, 0:384], in_=lt[:, 0:384], func=AF.Square, accum_out=sm_[:, 0:1]),
                           reads=[lt], writes=[sm_])
                        op("act", lambda E, lt=lt, sm_=sm_: E.activation(out=junkm[:, 0:256], in_=lt[:, 384:640], func=AF.Square, accum_out=sm_[:, 1:2]),
                           reads=[lt], writes=[sm_])
                        op("dve", lambda E, sm_=sm_: E.tensor_scalar(out=sm_[:, 0:1], in0=sm_[:, 0:1], scalar1=1.0 / 384, scalar2=EPS, op0=ALU.mult, op1=ALU.add),
                           reads=[sm_], writes=[sm_])
                        op("dve", lambda E, sm_=sm_: E.tensor_scalar(out=sm_[:, 1:2], in0=sm_[:, 1:2], scalar1=1.0 / 256, scalar2=EPS, op0=ALU.mult, op1=ALU.add),
                           reads=[sm_], writes=[sm_])
                        op("act", lambda E, sm_=sm_: E.activation(out=sm_[:, 0:2], in_=sm_[:, 0:2], func=AF.Ln), reads=[sm_], writes=[sm_])
                        op("act", lambda E, sm_=sm_: E.activation(out=sm_[:, 0:2], in_=sm_[:, 0:2], func=AF.Exp, scale=-0.5), reads=[sm_], writes=[sm_])
                        op("dve", lambda E, lt=lt, lb=lb, sm_=sm_: E.tensor_scalar(out=lb[:, 0:384], in0=lt[:, 0:384], scalar1=sm_[:, 0:1], scalar2=None, op0=ALU.mult),
                           reads=[lt, sm_], writes=[lb])
                        op("act", lambda E, lt=lt, lb=lb, sm_=sm_: E.activation(out=lb[:, 384:640], in_=lt[:, 384:640], func=AF.Copy, scale=sm_[:, 1:2]),
                           reads=[lt, sm_, lb], writes=[lb])
                        op("pool", lambda E, lt=lt, rt=rt, rpt=rpt: E.tensor_tensor(out=rt[:, 0:32], in0=lt[:, 640:672], in1=rpt[:, 0:32], op=ALU.mult),
                           reads=[lt, rpt], writes=[rt])
                        op("pool", lambda E, lt=lt, rt=rt, rpt=rpt: E.tensor_tensor(out=rt[:, 32:48], in0=lt[:, 656:672], in1=rpt[:, 32:48], op=ALU.mult),
                           reads=[lt, rpt, rt], writes=[rt])
                        op("pool", lambda E, lt=lt, rt=rt, rpt=rpt: E.tensor_tensor(out=rt[:, 48:64], in0=lt[:, 640:656], in1=rpt[:, 48:64], op=ALU.mult),
                           reads=[lt, rpt, rt], writes=[rt])
                        op("pool", lambda E, rt=rt, lb=lb: E.tensor_tensor(out=lb[:, 704:736], in0=rt[:, 0:32], in1=rt[:, 32:64], op=ALU.add),
                           reads=[rt, lb], writes=[lb])
                        for c in range(5):
                            op("pe", lambda E, c=c, lb=lb: E.transpose(out=pb(0, c * 128, (c + 1) * 128), in_=lb[:, c * 128:(c + 1) * 128], identity=identb[:]),
                               reads=[lb, identb], writes=[BK[0]])
                        op("pe", lambda E, lb=lb: E.transpose(out=psum_bf[0:96, 640:768], in_=lb[:, 640:736], identity=identb[:]),
                           reads=[lb, identb], writes=[BK[0]])
                        op("dve", lambda E, ct=ct: E.tensor_copy(out=ct[:], in_=pb(0, 0, 384)), writes=[ct, BK[0]])
                        op("act", lambda E, j=j: E.activation(out=ckvT.ap[:, :, j * 128:(j + 1) * 128], in_=pb(0, 384, 640).rearrange("p (c t) -> p c t", c=2), func=AF.Copy),
                           writes=[ckvc[j], BK[0]])
                        op("dve", lambda E, j=j: E.tensor_copy(out=kpst.ap[64:96, j * 128:(j + 1) * 128], in_=psum_bf[64:96, 640:768]), writes=[kpc[j], BK[0]])
                        ctv = ct.ap.rearrange("p (c t) -> p c t", c=3)
                        for c in range(3):
                            op("pe", lambda E, c=c, ctv=ctv: E.matmul(pf(1, 0, 480), lhsT=ctv[:, c, :], rhs=wuq.ap[:, c, 0:480], start=(c == 0), stop=(c == 2)),
                               reads=[ct, wuq], writes=[BK[1]])
                        for c in range(3):
                            op("pe", lambda E, c=c, ctv=ctv: E.matmul(pf(2, 0, 288), lhsT=ctv[:, c, :], rhs=wuq.ap[:, c, 480:768], start=(c == 0), stop=(c == 2)),
                               reads=[ct, wuq], writes=[BK[2]])
                        for (bk, h0, nh) in ((1, 0, 5), (2, 5, 3)):
                            pv = pf(bk, 0, nh * 96).rearrange("p (h e) -> p h e", h=nh)
                            qv = qrt.ap[:, h0 * 96:(h0 + nh) * 96].rearrange("p (h e) -> p h e", h=nh)
                            tv = rt.ap[:, 64:64 + nh * 64].rearrange("p (h e) -> p h e", h=nh)
                            cs = rpt.ap[:, 0:32].unsqueeze(1).to_broadcast([128, nh, 32])
                            sn1 = rpt.ap[:, 32:48].unsqueeze(1).to_broadcast([128, nh, 16])
                            sn2 = rpt.ap[:, 48:64].unsqueeze(1).to_broadcast([128, nh, 16])
                            op("act", lambda E, pv=pv, qv=qv: E.activation(out=qv[:, :, 0:64], in_=pv[:, :, 0:64], func=AF.Copy), reads=[], writes=[qrt, BK[bk]])
                            op("dve", lambda E, pv=pv, tv=tv, cs=cs: E.tensor_tensor(out=tv[:, :, 0:32], in0=pv[:, :, 64:96], in1=cs, op=ALU.mult),
                               reads=[rpt], writes=[rt, BK[bk]])
                            op("dve", lambda E, pv=pv, tv=tv, sn1=sn1: E.tensor_tensor(out=tv[:, :, 32:48], in0=pv[:, :, 80:96], in1=sn1, op=ALU.mult),
                               reads=[rpt], writes=[rt, BK[bk]])
                            op("dve", lambda E, pv=pv, tv=tv, sn2=sn2: E.tensor_tensor(out=tv[:, :, 48:64], in0=pv[:, :, 64:80], in1=sn2, op=ALU.mult),
                               reads=[rpt], writes=[rt, BK[bk]])
                            op("pool", lambda E, tv=tv, qv=qv: E.tensor_tensor(out=qv[:, :, 64:96], in0=tv[:, :, 0:32], in1=tv[:, :, 32:64], op=ALU.add),
                               reads=[rt], writes=[qrt])
                        for hh in range(8):
                            op("pe", lambda E, hh=hh, qrt=qrt: E.transpose(out=psum_bf[0:96, 3 * 1024 + hh * 128:3 * 1024 + (hh + 1) * 128], in_=qrt[:, hh * 96:(hh + 1) * 96],
                                                                           identity=identb[:]), reads=[qrt, identb], writes=[BK[3]])
                        op("act", lambda E, j=j: E.activation(out=qtst.ap[0:96, :, j * 128:(j + 1) * 128],
                                                              in_=psum_bf[0:96, 3 * 1024:4 * 1024].rearrange("p (h t) -> p h t", h=8), func=AF.Copy),
                           writes=[qtc[j], BK[3]])
                        for c in range(2):
                            op("pe", lambda E, c=c, j=j: E.matmul(pf(4), lhsT=ckvT.ap[:, c, j * 128:(j + 1) * 128], rhs=wv.ap[:, c, :], start=(c == 0), stop=(c == 1)),
                               reads=[ckvc[j], wv], writes=[BK[4]])
                        op("dve", lambda E, j=j: E.tensor_copy(out=vst.ap[:, j, :], in_=pf(4)), writes=[vsc[j], BK[4]])
                    for pr in range(4):
                        bk = 5 + pr % 2
                        for c in range(2):
                            op("pe", lambda E, c=c, pr=pr, bk=bk: E.matmul(pf(bk), lhsT=wk.ap[:, c, pr * 128:(pr + 1) * 128], rhs=ckvT.ap[:, c, :], start=(c == 0), stop=(c == 1)),
                               reads=ckvc + [wk], writes=[BK[bk]])
                        if pr % 2 == 0:
                            op("act", lambda E, pr=pr, bk=bk: E.activation(out=ktst.ap[:, pr, :], in_=pf(bk), func=AF.Copy), writes=[ktst, BK[bk]])
                        else:
                            op("dve", lambda E, pr=pr, bk=bk: E.tensor_copy(out=ktst.ap[:, pr, :], in_=pf(bk)), reads=[ktst], writes=[ktst, BK[bk]])
                    cc = slice(r0, r0 + 512)
                    for hh in range(8):
                        pr, hi = hh // 2, hh % 2
                        dma(KT[s][hh, 0:64, cc], ktst.ap[hi * 64:(hi + 1) * 64, pr, :], reads=[ktst], sb=ktst, queue="pool")
                        dma(KT[s][hh, 64:96, cc], kpst.ap[64:96, :], reads=kpc, sb=kpst, queue="pool")
                    dma(QT[s][:, :, cc].rearrange("h r t -> r h t"), qtst.ap[0:96, :, :], reads=qtc, sb=qtst, queue="pool")
                    for j in range(4):
                        dma(VV[s][:, :, ti * 4 + j, :].rearrange("h p e -> p h e"),
                            vst.ap[:, j, :].rearrange("p (h e) -> p h e", h=8), reads=vsc, sb=vst, queue="pool")
            phase_reset()

        if "B" in phases:
            SC = 96 ** -0.5
            for s, L in SEQ:
                nkb = L // 128
                off_seq = off[0]
                ktb = []
                vtb = []
                for i in range(2):
                    b = alloc("ktb%d" % i, L, BF16)
                    ktb.append(b)
                    v = alloc("vtb%d" % i, nkb * 65, BF16)
                    v.ap = v.ap.rearrange("p (k e) -> p k e", e=65)
                    op("pool", lambda E, v=v: E.memset(v.ap[:, :, 64:65], 1.0), writes=[v])
                    vtb.append(v)
                qtb = [alloc("qtb%d" % i, 512, BF16) for i in range(3)]
                ptb = [alloc("ptb%d" % i, 512, BF16) for i in range(4)]
                osb = [alloc("osb%d" % i, 512) for i in range(2)]
                yst = [alloc("yst%d" % i, 4 * 64) for i in range(2)]
                rdn = [alloc("rdn%d" % i, 4) for i in range(2)]
                qi = 0
                pi = 0
                for hh in range(8):
                    kt, vt = ktb[hh % 2], vtb[hh % 2]
                    dma(kt[0:96, :], KT[s][hh, :, :], writes=[kt], sb=kt)
                    dma(vt.ap[:, :, 0:64], VV[s][hh, :, :, :], writes=[vt], sb=vt)
                    for qt in range(L // 512):
                        qb = qtb[qi % 3]
                        ob, ys, rd = osb[qi % 2], yst[qi % 2], rdn[qi % 2]
                        obk = 4 + qi % 2
                        qi += 1
                        dma(qb[0:96, :], QT[s][hh, :, qt * 512:(qt + 1) * 512], writes=[qb], sb=qb)

                        def smm(kb, qb=qb, kt=kt):
                            bk = kb % 4
                            op("pe", lambda E: E.matmul(pf(bk), lhsT=kt[0:96, kb * 128:(kb + 1) * 128], rhs=qb[0:96, :], start=True, stop=True),
                               reads=[kt, qb], writes=[BK[bk]])
                        smm(0)
                        if nkb > 1:
                            smm(1)
                        for kb in range(nkb):
                            if kb + 2 < nkb:
                                smm(kb + 2)
                            pt = ptb[pi % 4]
                            pi += 1
                            bk = kb % 4
                            op("act", lambda E, pt=pt, bk=bk: E.activation(out=pt[:], in_=pf(bk), func=AF.Exp, scale=SC), writes=[pt, BK[bk]])
                            op("pe", lambda E, pt=pt, kb=kb, vt=vt, obk=obk: E.matmul(psum[0:65, obk * 512:(obk + 1) * 512], lhsT=vt.ap[:, kb, 0:65], rhs=pt[:],
                                                                                         start=(kb == 0), stop=(kb == nkb - 1)), reads=[vt, pt], writes=[BK[obk]])
                        op("dve", lambda E, ob=ob, obk=obk: E.tensor_copy(out=ob[0:65, :], in_=psum[0:65, obk * 512:(obk + 1) * 512]), writes=[ob, BK[obk]])
                        for j in range(4):
                            op("pe", lambda E, j=j, ob=ob: E.matmul(pf(6, j * 65, (j + 1) * 65), lhsT=ob[0:65, j * 128:(j + 1) * 128], rhs=identf[0:65, 0:65], start=True, stop=True),
                               reads=[ob, identf], writes=[BK[6]])
                        p6 = pf(6, 0, 260).rearrange("p (j e) -> p j e", j=4)
                        op("dve", lambda E, rd=rd, p6=p6: E.reciprocal(out=rd.ap[:, 0:4].unsqueeze(2), in_=p6[:, :, 64:65]), writes=[rd, BK[6]])
                        op("dve", lambda E, rd=rd, ys=ys, p6=p6: E.tensor_tensor(out=ys.ap.rearrange("p (j e) -> p j e", j=4), in0=p6[:, :, 0:64],
                                                                                in1=rd.ap[:, 0:4].unsqueeze(2).to_broadcast([128, 4, 64]), op=ALU.mult),
                           reads=[rd], writes=[ys, BK[6]])
                        dma(YA[s][qt * 512:(qt + 1) * 512, hh * 64:(hh + 1) * 64].rearrange("(j p) e -> p j e", p=128),
                            ys.ap.rearrange("p (j e) -> p j e", j=4), reads=[ys], sb=ys, queue="pool")
                P.barrier()
                off[0] = off_seq
            phase_reset()

        def run_window(items, W):
            nxt = 0
            active = []
            while nxt < len(items) or active:
                while len(active) < W and nxt < len(items):
                    if items[nxt][1] and active:
                        break
                    active.append(items[nxt][0])
                    nxt += 1
                for g_ in list(active):
                    try:
                        next(g_)
                    except StopIteration:
                        active.remove(g_)

        if "G" in phases:
            cw = alloc("cw", 12 * 5)
            for c in range(12):
                dma(cw[:, c * 5:(c + 1) * 5], W["gdn_conv_w"][:, c * 128:(c + 1) * 128].rearrange("k p -> p k"), writes=[cw] if c == 0 else [],
                    reads=[] if c == 0 else [cw], sb=cw, allow_slow_non_contiguous=True)
            cw.lw = (id(cw), cw.semcnt)
            dg = alloc("dg", 60 * 128, BF16)
            dg.ap = dg.ap.rearrange("p (k f) -> p k f", k=60)
            for k in range(60):
                op("dve", lambda E, k=k: E.tensor_scalar(out=dg.ap[:, k, :], in0=identb[:], scalar1=cw[:, k:k + 1], scalar2=None, op0=ALU.mult),
                   reads=[identb, cw], writes=[dg] if k == 0 else [])
            dg.lw = ("dve", P.cnt["dve"])
            GW = 3
            xin = [alloc("gxin%d" % i, 516) for i in range(GW)]
            xbf = [alloc("gxbf%d" % i, 516, BF16) for i in range(GW)]
            ex = [alloc("gex%d" % i, 512) for i in range(GW)]
            sb_ = [alloc("gsb%d" % i, 512, BF16) for i in range(GW)]
            tokst2 = []
            tokc2 = []
            for i in range(2):
                t_ = alloc("tokst%d" % i, 4 * 1536, BF16)
                t_.ap = t_.ap.rearrange("p (j c) -> p j c", j=4)
                tokst2.append(t_)
                tokc2.append([Buf("tokc%d_%d" % (i, c), t_.ap[:, :, c * 128:(c + 1) * 128]) for c in range(12)])
            sq = alloc("gsq", 1024)
            ssg2 = [alloc("ssg%d" % i, 32) for i in range(2)]

            def gchunk(s, L, r0, c, k, tokst, tokc):
                xi, xb, e_, sbb = xin[k % GW], xbf[k % GW], ex[k % GW], sb_[k % GW]
                cb_, tb_ = 2 * (k % GW), 2 * (k % GW) + 1
                lo, hi = max(r0 - 2, 0), min(r0 + 514, L)
                if r0 == 0:
                    op("pool", lambda E: E.memset(xi[:, 0:2], 0.0), writes=[xi])
                if r0 + 512 == L:
                    op("pool", lambda E: E.memset(xi[:, 514:516], 0.0), writes=[xi])
                dma(xi[:, lo - (r0 - 2):hi - (r0 - 2)], QKVT[s][c * 128:(c + 1) * 128, lo:hi], writes=[xi], sb=xi)
                yield
                op("pool", lambda E: E.tensor_copy(out=xb[:], in_=xi[:]), reads=[xi], writes=[xb])
                yield
                for t5 in range(5):
                    op("pe", lambda E, t5=t5: E.matmul(pf(cb_), lhsT=dg.ap[:, c * 5 + t5, :], rhs=xb[:, t5:t5 + 512], start=(t5 == 0), stop=(t5 == 4)),
                       reads=[dg, xb], writes=[BK[cb_]])
                yield
                op("act", lambda E: E.activation(out=e_[:], in_=pf(cb_), func=AF.Exp, scale=-1.0), writes=[e_, BK[cb_]])
                op("act", lambda E: E.activation(out=e_[:], in_=e_[:], func=AF.Ln, bias=1.0), reads=[e_], writes=[e_])
                op("act", lambda E: E.activation(out=e_[:], in_=e_[:], func=AF.Exp, scale=-1.0), reads=[e_], writes=[e_])
                yield
                op("dve", lambda E: E.tensor_tensor(out=sbb[:], in0=e_[:], in1=pf(cb_), op=ALU.mult), reads=[e_], writes=[sbb, BK[cb_]])
                yield
                for j in range(4):
                    op("pe", lambda E, j=j: E.transpose(out=pb(tb_, j * 128, (j + 1) * 128), in_=sbb[:, j * 128:(j + 1) * 128], identity=identb[:]),
                       reads=[sbb, identb], writes=[BK[tb_]])
                yield
                if c % 2 == 0:
                    op("act", lambda E: E.activation(out=tokst.ap[:, :, c * 128:(c + 1) * 128], in_=pb(tb_, 0, 512).rearrange("p (j d) -> p j d", j=4), func=AF.Copy),
                       writes=[tokc[c], BK[tb_]])
                else:
                    op("dve", lambda E: E.tensor_copy(out=tokst.ap[:, :, c * 128:(c + 1) * 128], in_=pb(tb_, 0, 512).rearrange("p (j d) -> p j d", j=4)),
                       writes=[tokc[c], BK[tb_]])

            def gtail(s, r0, tokst, tokc, ssg):
                for j in range(4):
                    tv = tokst.ap[:, j, 0:1024]
                    op("dve", lambda E, tv=tv: E.tensor_tensor(out=sq[:], in0=tv, in1=tv, op=ALU.mult), reads=tokc[0:8], writes=[sq])
                    op("dve", lambda E, j=j: E.tensor_reduce(out=ssg[:, j * 8:(j + 1) * 8], in_=sq.ap.rearrange("p (h d) -> p h d", h=8), axis=AX.X, op=ALU.add),
                       reads=[sq], writes=[ssg])
                    yield
                op("dve", lambda E: E.tensor_scalar(out=ssg[:, 0:32], in0=ssg[:, 0:32], scalar1=EPS, scalar2=None, op0=ALU.add), reads=[ssg], writes=[ssg])
                op("act", lambda E: E.activation(out=ssg[:, 0:32], in_=ssg[:, 0:32], func=AF.Ln), reads=[ssg], writes=[ssg])
                op("act", lambda E: E.activation(out=ssg[:, 0:32], in_=ssg[:, 0:32], func=AF.Exp, scale=-0.5), reads=[ssg], writes=[ssg])
                sv = ssg.ap[:, 0:32].rearrange("p (j h) -> p j h", j=4)
                op("dve", lambda E: E.tensor_scalar(out=sv[:, :, 0:4], in0=sv[:, :, 0:4], scalar1=128 ** -0.5, scalar2=None, op0=ALU.mult), reads=[ssg], writes=[ssg])
                yield
                for j in range(4):
                    tv = tokst.ap[:, j, 0:1024].rearrange("p (h d) -> p h d", h=8)
                    e1 = "dve" if j % 2 == 0 else "pool"
                    op(e1, lambda E, tv=tv, j=j: E.tensor_tensor(out=tv, in0=tv, in1=ssg.ap[:, j * 8:(j + 1) * 8].unsqueeze(2).to_broadcast([128, 8, 128]), op=ALU.mult),
                       reads=[ssg] + tokc[0:8], writes=tokc[0:8])
                    yield
                dma(TOK[s][r0:r0 + 512, :].rearrange("(j p) c -> p j c", p=128), tokst.ap, reads=tokc, sb=tokst, queue="pool")

            items = []
            k = 0
            tix = 0
            for s, L in SEQ:
                for ti in range(L // 512):
                    r0 = ti * 512
                    tkst, tkc, ssg = tokst2[tix % 2], tokc2[tix % 2], ssg2[tix % 2]
                    tix += 1
                    for c in range(12):
                        items.append((gchunk(s, L, r0, c, k, tkst, tkc), False))
                        k += 1
                    items.append((gtail(s, r0, tkst, tkc, ssg), True))
            run_window(items, GW)
            phase_reset()

        if "H" in phases:
            gm = alloc("gm", 5 * 512)
            gm.ap = gm.ap.rearrange("p (m h j) -> p m h j", m=5, h=8)
            dma(gm.ap[0:64], GMASK[:, :, :, :], writes=[gm], sb=gm)
            NA, NAT, NQK = gm.ap[0:64, 0], gm.ap[0:64, 1], gm.ap[0:64, 2]
            trif, trib = gm.ap[0:64, 3, 0, :], gm.ap[0:64, 3, 1, :]
            idb8 = identb.ap[0:64, 0:64].unsqueeze(1).to_broadcast([64, 8, 64])
            idf8 = identf.ap[0:64, 0:64].unsqueeze(1).to_broadcast([64, 8, 64])

            def A3(name, n, dt=F32, parts=128):
                b = alloc(name, 8 * n, dt)
                b.ap = b.ap.rearrange("p (h x) -> p h x", h=8)
                return b
            NSET = 2
            tok = [alloc("tk%d" % i, 2 * 1536, BF16) for i in range(NSET)]
            gsel = [alloc("gsel%d" % i, 24) for i in range(NSET)]
            sm8 = [alloc("sm8_%d" % i, 64) for i in range(NSET)]
            ost = [alloc("ost%d" % i, 1024) for i in range(NSET)]
            SETS = []
            for i in range(NSET):
                d_ = {}
                d_["Dg"] = A3("Dg%d" % i, 64)
                d_["Dc"] = A3("Dc%d" % i, 64)
                d_["De"] = A3("De%d" % i, 64, BF16)
                kq_ = alloc("kqT%d" % i, 16 * 64, BF16)
                kq_.ap = kq_.ap.rearrange("p (h x) -> p h x", h=16)
                d_["kqT"] = kq_
                for nm in ("dA", "dAT", "dQK"):
                    d_[nm] = A3(nm + str(i), 64)
                d_["Xb"] = [A3("Xb%d_%d" % (i, k), 64, BF16) for k in range(2)]
                d_["Yb"] = [A3("Yb%d_%d" % (i, k), 64, BF16) for k in range(2)]
                d_["Zb"] = [A3("Zb%d_%d" % (i, k), 64, BF16) for k in range(2)]
                for nm, n_, dt_ in (("qkT", 64, BF16), ("kbg", 128, BF16), ("vb", 128, BF16), ("kg", 128, BF16), ("wT", 64, BF16),
                                   ("qgT", 64, BF16), ("uu", 128, F32), ("vnew", 128, BF16)):
                    d_[nm] = A3(nm + str(i), n_, dt_)
                SETS.append(d_)
            S = A3("S", 128)
            Sb = A3("Sb", 128, BF16)

            def step(s, N, n):
                st_ = SETS[n % NSET]
                c0 = 4 * (n % 2)
                c1, c2, c3 = c0 + 1, c0 + 2, c0 + 3
                Dg, Dc, De, kqT, dA, dAT, dQK = st_["Dg"], st_["Dc"], st_["De"], st_["kqT"], st_["dA"], st_["dAT"], st_["dQK"]
                Xb, Yb, Zb = st_["Xb"], st_["Yb"], st_["Zb"]
                qkT, kbg, vb, kg, wT, qgT, uu, vnew = (st_[k_] for k_ in ("qkT", "kbg", "vb", "kg", "wT", "qgT", "uu", "vnew"))
                cf, cb = n, N - 1 - n
                tk, gs, m8, os_ = tok[n % NSET], gsel[n % NSET], sm8[n % NSET], ost[n % NSET]
                tkv = tk.ap.rearrange("p (d c) -> p d c", d=2)
                for d, ch in ((0, cf), (1, cb)):
                    dma(tkv[0:64, d, :], TOK[s][ch * 64:(ch + 1) * 64, :], writes=[tk] if d == 0 else [], reads=[] if d == 0 else [tk], sb=tk)
                    dma(gs.ap[0:64, d * 12:(d + 1) * 12], GBL[s][d, ch * 64:(ch + 1) * 64, :], writes=[gs] if d == 0 else [], reads=[] if d == 0 else [gs], sb=gs)
                tk.lw = (id(tk), tk.semcnt)
                gs.lw = (id(gs), gs.semcnt)
                gv = gs.ap[0:64, :].rearrange("p (d k) -> p d k", d=2)
                g8, lnb8, beta8 = gv[:, :, 0:4], gv[:, :, 4:8], gv[:, :, 8:12]
                m = m8.ap[0:64, :]

                def v8(a, b):
                    return m8.ap[0:64, a:b].rearrange("p (d k) -> p d k", d=2)
                yield None
                op("pe", lambda E, gs=gs: E.matmul(psum[0:64, c0 * 512:c0 * 512 + 4], lhsT=trif, rhs=gs.ap[0:64, 0:4], start=True, stop=True), reads=[gm, gs], writes=[BK[c0]])
                op("pe", lambda E, gs=gs: E.matmul(psum[0:64, c0 * 512 + 4:c0 * 512 + 8], lhsT=trib, rhs=gs.ap[0:64, 12:16], start=True, stop=True), reads=[gm, gs], writes=[BK[c0]])
                op("pe", lambda E, g8=g8: E.matmul(psum[:, c0 * 512 + 8:c0 * 512 + 16].rearrange("p (d k) -> p d k", d=2), lhsT=onesf[0:64, :], rhs=g8, start=True, stop=True),
                   reads=[onesf, gs], writes=[BK[c0]])
                yield None
                op("dve", lambda E, m8=m8: E.tensor_copy(out=m8[0:64, 0:8], in_=psum[0:64, c0 * 512:c0 * 512 + 8]), writes=[m8, BK[c0]])
                op("dve", lambda E, m8=m8, lnb8=lnb8, v8=v8: E.tensor_tensor(out=v8(8, 16), in0=v8(0, 8), in1=lnb8, op=ALU.add), reads=[m8, gs], writes=[m8])
                op("act", lambda E, m8=m8: E.activation(out=m8[0:64, 16:24], in_=m8[0:64, 0:8], func=AF.Exp), reads=[m8], writes=[m8])
                op("dve", lambda E, m8=m8, beta8=beta8, v8=v8: E.tensor_tensor(out=v8(24, 32), in0=v8(16, 24), in1=beta8, op=ALU.mult), reads=[m8, gs], writes=[m8])
                op("dve", lambda E, m8=m8: E.tensor_tensor(out=m8[0:64, 40:48], in0=psum[0:64, c0 * 512 + 8:c0 * 512 + 16], in1=m8[0:64, 0:8], op=ALU.subtract), reads=[m8], writes=[m8, BK[c0]])
                op("act", lambda E, m8=m8: E.activation(out=m8[0:64, 32:40], in_=m8[0:64, 40:48], func=AF.Exp), reads=[m8], writes=[m8])
                op("act", lambda E, m8=m8: E.activation(out=m8[:, 48:56], in_=psum[:, c0 * 512 + 8:c0 * 512 + 16], func=AF.Exp), reads=[m8], writes=[m8, BK[c0]])
                yield None
                op("pool", lambda E, m8=m8: E.tensor_tensor(out=Dg.ap[0:64], in0=idf8, in1=m8.ap[0:64, 0:8].unsqueeze(2).to_broadcast([64, 8, 64]), op=ALU.mult),
                   reads=[identf, m8], writes=[Dg])
                op("pool", lambda E, m8=m8: E.tensor_tensor(out=Dc.ap[0:64], in0=idf8, in1=m8.ap[0:64, 8:16].unsqueeze(2).to_broadcast([64, 8, 64]), op=ALU.mult),
                   reads=[identf, m8], writes=[Dc])
                op("pool", lambda E, m8=m8: E.tensor_tensor(out=De.ap[0:64], in0=idf8, in1=m8.ap[0:64, 16:24].unsqueeze(2).to_broadcast([64, 8, 64]), op=ALU.mult),
                   reads=[identf, m8], writes=[De])
                yield None
                op("pe", lambda E: E.matmul(psum[0:64, c1 * 512:(c1 + 1) * 512], lhsT=onesf[0:64, 0:64], rhs=Dg.ap[0:64].rearrange("p h x -> p (h x)"), start=True, stop=True),
                   reads=[onesf, Dg], writes=[BK[c1]])
                op("pe", lambda E: E.matmul(psum[0:64, c2 * 512:(c2 + 1) * 512], lhsT=onesf[0:64, 0:64], rhs=Dc.ap[0:64].rearrange("p h x -> p (h x)"), start=True, stop=True),
                   reads=[onesf, Dc], writes=[BK[c2]])
                yield None
                for d in range(2):
                    for hh in range(4):
                        hd = d * 4 + hh
                        op("pe", lambda E, d=d, hh=hh, hd=hd, tkv=tkv: E.transpose(out=psum_bf[:, c3 * 1024 + hd * 64:c3 * 1024 + (hd + 1) * 64],
                                                                                 in_=tkv[0:64, d, 512 + hh * 128:512 + (hh + 1) * 128], identity=identb[0:64, 0:64]),
                           reads=[tk, identb], writes=[BK[c3]])
                        op("pe", lambda E, d=d, hh=hh, hd=hd, tkv=tkv: E.transpose(out=psum_bf[:, c3 * 1024 + 512 + hd * 64:c3 * 1024 + 512 + (hd + 1) * 64],
                                                                                 in_=tkv[0:64, d, hh * 128:(hh + 1) * 128], identity=identb[0:64, 0:64]),
                           reads=[tk, identb], writes=[BK[c3]])
                yield None
                op("act", lambda E: E.activation(out=kqT.ap, in_=pb(c3).rearrange("p (h x) -> p h x", h=16), func=AF.Copy), writes=[kqT, BK[c3]])
                yield None
                for hd in range(8):
                    op("pe", lambda E, hd=hd: E.matmul(psum[0:64, c0 * 512 + hd * 64:c0 * 512 + (hd + 1) * 64], lhsT=kqT.ap[:, hd, :], rhs=kqT.ap[:, hd, :], start=True, stop=True),
                       reads=[kqT], writes=[BK[c0]])
                for hd in range(8):
                    op("pe", lambda E, hd=hd: E.matmul(psum[0:64, c3 * 512 + hd * 64:c3 * 512 + (hd + 1) * 64], lhsT=kqT.ap[:, hd, :], rhs=kqT.ap[:, 8 + hd, :], start=True, stop=True),
                       reads=[kqT], writes=[BK[c3]])
                P1 = psum[0:64, c1 * 512:(c1 + 1) * 512].rearrange("p (h x) -> p h x", h=8)
                P2 = psum[0:64, c2 * 512:(c2 + 1) * 512].rearrange("p (h x) -> p h x", h=8)
                PG = psum[0:64, c0 * 512:(c0 + 1) * 512].rearrange("p (h x) -> p h x", h=8)
                PQ = psum[0:64, c3 * 512:(c3 + 1) * 512].rearrange("p (h x) -> p h x", h=8)

                def bc8(a):
                    return m8.ap[0:64, a:a + 8].unsqueeze(2).to_broadcast([64, 8, 64])
                yield None
                op("dve", lambda E: E.scalar_tensor_tensor(out=dA.ap[0:64], in0=P1, scalar=-1.0, in1=NA, op0=ALU.mult, op1=ALU.add), reads=[gm], writes=[dA, BK[c1]])
                op("pool", lambda E, bc8=bc8: E.tensor_tensor(out=dA.ap[0:64], in0=dA.ap[0:64], in1=bc8(8), op=ALU.add), reads=[m8, dA], writes=[dA])
                op("act", lambda E: E.activation(out=dA.ap[0:64], in_=dA.ap[0:64], func=AF.Exp), reads=[dA], writes=[dA])
                yield None
                op("dve", lambda E: E.tensor_tensor(out=dAT.ap[0:64], in0=P2, in1=NAT, op=ALU.add), reads=[gm], writes=[dAT, BK[c2]])
                op("pool", lambda E, bc8=bc8: E.tensor_tensor(out=dAT.ap[0:64], in0=dAT.ap[0:64], in1=bc8(0), op=ALU.subtract), reads=[m8, dAT], writes=[dAT])
                op("act", lambda E: E.activation(out=dAT.ap[0:64], in_=dAT.ap[0:64], func=AF.Exp), reads=[dAT], writes=[dAT])
                yield None
                op("dve", lambda E: E.tensor_tensor(out=dQK.ap[0:64], in0=P1, in1=NQK, op=ALU.add), reads=[gm], writes=[dQK, BK[c1]])
                op("pool", lambda E, bc8=bc8: E.tensor_tensor(out=dQK.ap[0:64], in0=dQK.ap[0:64], in1=bc8(0), op=ALU.subtract), reads=[m8, dQK], writes=[dQK])
                op("act", lambda E: E.activation(out=dQK.ap[0:64], in_=dQK.ap[0:64], func=AF.Exp), reads=[dQK], writes=[dQK])
                yield None
                X0, Y0, Z0 = Xb[0], Yb[0], Zb[0]
                op("dve", lambda E, X0=X0: E.scalar_tensor_tensor(out=X0.ap[0:64], in0=dA.ap[0:64], scalar=-1.0, in1=PG, op0=ALU.mult, op1=ALU.mult), reads=[dA], writes=[X0, BK[c0]])
                op("dve", lambda E, Y0=Y0: E.scalar_tensor_tensor(out=Y0.ap[0:64], in0=dAT.ap[0:64], scalar=-1.0, in1=PG, op0=ALU.mult, op1=ALU.mult), reads=[dAT], writes=[Y0, BK[c0]])
                op("dve", lambda E: E.tensor_tensor(out=qkT.ap[0:64], in0=dQK.ap[0:64], in1=PQ, op=ALU.mult), reads=[dQK], writes=[qkT, BK[c3]])
                op("pool", lambda E, Y0=Y0, Z0=Z0: E.tensor_tensor(out=Z0.ap[0:64], in0=Y0.ap[0:64], in1=idb8, op=ALU.add), reads=[Y0, identb], writes=[Z0])
                yield None
                for k in range(1, 6):
                    Xp, Yp, Zp = Xb[(k - 1) % 2], Yb[(k - 1) % 2], Zb[(k - 1) % 2]
                    Xn, Yn, Zn = Xb[k % 2], Yb[k % 2], Zb[k % 2]
                    for hd in range(8):
                        op("pe", lambda E, hd=hd, Xp=Xp, Yp=Yp: E.matmul(psum[0:64, c1 * 512 + hd * 64:c1 * 512 + (hd + 1) * 64], lhsT=Yp.ap[0:64, hd, :], rhs=Xp.ap[0:64, hd, :],
                                                                         start=True, stop=True), reads=[Xp, Yp], writes=[BK[c1]])
                    if k < 5:
                        for hd in range(8):
                            op("pe", lambda E, hd=hd, Xp=Xp, Yp=Yp: E.matmul(psum[0:64, c2 * 512 + hd * 64:c2 * 512 + (hd + 1) * 64], lhsT=Xp.ap[0:64, hd, :], rhs=Yp.ap[0:64, hd, :],
                                                                             start=True, stop=True), reads=[Xp, Yp], writes=[BK[c2]])
                    yield None
                    op("act", lambda E, Xn=Xn: E.activation(out=Xn.ap[0:64], in_=P1, func=AF.Copy), writes=[Xn, BK[c1]])
                    if k < 5:
                        op("dve", lambda E, Yn=Yn: E.tensor_copy(out=Yn.ap[0:64], in_=P2), writes=[Yn, BK[c2]])
                    yield None
                    for hd in range(8):
                        op("pe", lambda E, hd=hd, Xn=Xn, Zp=Zp: E.matmul(psum[0:64, c3 * 512 + hd * 64:c3 * 512 + (hd + 1) * 64], lhsT=Xn.ap[0:64, hd, :], rhs=Zp.ap[0:64, hd, :],
                                                                         start=True, stop=True), reads=[Xn, Zp], writes=[BK[c3]])
                    op("dve", lambda E, Zn=Zn, Zp=Zp: E.tensor_tensor(out=Zn.ap[0:64], in0=psum[0:64, c3 * 512:(c3 + 1) * 512].rearrange("p (h x) -> p h x", h=8), in1=Zp.ap[0:64], op=ALU.add),
                       reads=[Zp], writes=[Zn, BK[c3]])
                    yield None
                Z = Zb[5 % 2]
                yield None
                kv4 = tkv[0:64, :, 512:1024].rearrange("p d (h x) -> p d h x", h=4)
                vv4 = tkv[0:64, :, 1024:1536].rearrange("p d (h x) -> p d h x", h=4)

                def b4(a):
                    return m8.ap[0:64, a:a + 8].rearrange("p (d h) -> p d h", d=2).unsqueeze(3).to_broadcast([64, 2, 4, 128])
                op("pool", lambda E, kv4=kv4, b4=b4: E.tensor_tensor(out=kbg.ap[0:64].rearrange("p (d h) x -> p d h x", d=2), in0=kv4, in1=b4(24), op=ALU.mult),
                   reads=[tk, m8], writes=[kbg])
                op("dve", lambda E, vv4=vv4, gs=gs: E.tensor_tensor(out=vb.ap[0:64].rearrange("p (d h) x -> p d h x", d=2), in0=vv4,
                                                                    in1=gs.ap[0:64, :].rearrange("p (d k) -> p d k", d=2)[:, :, 8:12].unsqueeze(3).to_broadcast([64, 2, 4, 128]), op=ALU.mult),
                   reads=[tk, gs], writes=[vb])
                op("pool", lambda E, kv4=kv4, b4=b4: E.tensor_tensor(out=kg.ap[0:64].rearrange("p (d h) x -> p d h x", d=2), in0=kv4, in1=b4(32), op=ALU.mult),
                   reads=[tk, m8], writes=[kg])
                yield None
                for hd in range(8):
                    op("pe", lambda E, hd=hd, Z=Z: E.matmul(psum[:, c0 * 512 + hd * 64:c0 * 512 + (hd + 1) * 64], lhsT=kbg.ap[0:64, hd, :], rhs=Z.ap[0:64, hd, :], start=True, stop=True),
                       reads=[kbg, Z], writes=[BK[c0]])
                op("act", lambda E: E.activation(out=wT.ap, in_=pf(c0).rearrange("p (h x) -> p h x", h=8), func=AF.Copy), writes=[wT, BK[c0]])
                yield None
                for hd in range(8):
                    bk = c2 + hd // 4
                    op("pe", lambda E, hd=hd, Z=Z: E.matmul(psum[0:64, c2 * 512 + hd * 128:c2 * 512 + (hd + 1) * 128], lhsT=Z.ap[0:64, hd, :], rhs=vb.ap[0:64, hd, :], start=True, stop=True),
                       reads=[vb, Z], writes=[BK[bk]])
                op("act", lambda E: E.activation(out=uu.ap[0:64], in_=psum[0:64, c2 * 512:(c2 + 2) * 512].rearrange("p (h x) -> p h x", h=8), func=AF.Copy), writes=[uu, BK[c2], BK[c3]])
                yield None
                for d in range(2):
                    for hh in range(4):
                        hd = d * 4 + hh
                        op("pe", lambda E, d=d, hh=hh, hd=hd, tkv=tkv: E.matmul(psum[:, c1 * 512 + hd * 64:c1 * 512 + (hd + 1) * 64], lhsT=tkv[0:64, d, hh * 128:(hh + 1) * 128],
                                                                                rhs=De.ap[0:64, hd, :], start=True, stop=True), reads=[tk, De], writes=[BK[c1]])
                op("dve", lambda E: E.tensor_copy(out=qgT.ap, in_=pf(c1).rearrange("p (h x) -> p h x", h=8)), writes=[qgT, BK[c1]])
                yield "SCAN"
                for hd in range(8):
                    bk = c0 + hd // 4
                    op("pe", lambda E, hd=hd: E.matmul(psum[0:64, c0 * 512 + hd * 128:c0 * 512 + (hd + 1) * 128], lhsT=wT.ap[:, hd, :], rhs=Sb.ap[:, hd, :], start=True, stop=True),
                       reads=[wT, Sb], writes=[BK[bk]])
                yield None
                op("dve", lambda E: E.tensor_tensor(out=vnew.ap[0:64], in0=uu.ap[0:64], in1=psum[0:64, c0 * 512:(c0 + 2) * 512].rearrange("p (h x) -> p h x", h=8), op=ALU.subtract),
                   reads=[uu], writes=[vnew, BK[c0], BK[c1]])
                yield None
                for hd in range(8):
                    bk = c2 + hd // 4
                    op("pe", lambda E, hd=hd: E.matmul(psum[0:64, c2 * 512 + hd * 128:c2 * 512 + (hd + 1) * 128], lhsT=qgT.ap[:, hd, :], rhs=Sb.ap[:, hd, :], start=True, stop=False),
                       reads=[qgT, Sb], writes=[BK[bk]])
                    op("pe", lambda E, hd=hd: E.matmul(psum[0:64, c2 * 512 + hd * 128:c2 * 512 + (hd + 1) * 128], lhsT=qkT.ap[0:64, hd, :], rhs=vnew.ap[0:64, hd, :], start=False, stop=True),
                       reads=[qkT, vnew], writes=[BK[bk]])
                yield None
                op("act", lambda E, os_=os_: E.activation(out=os_[0:64, :], in_=psum[0:64, c2 * 512:(c2 + 2) * 512], func=AF.Copy), writes=[os_, BK[c2], BK[c3]])
                dma(OF[s][0, cf * 64:(cf + 1) * 64, :], os_[0:64, 0:512], reads=[os_], sb=os_, queue="pool")
                dma(OF[s][1, cb * 64:(cb + 1) * 64, :], os_[0:64, 512:1024], reads=[os_], sb=os_, queue="pool")
                yield None
                for hd in range(8):
                    bk = c0 + hd // 4
                    op("pe", lambda E, hd=hd: E.matmul(psum[:, c0 * 512 + hd * 128:c0 * 512 + (hd + 1) * 128], lhsT=kg.ap[0:64, hd, :], rhs=vnew.ap[0:64, hd, :], start=True, stop=True),
                       reads=[kg, vnew], writes=[BK[bk]])
                yield None
                op("pool", lambda E, m8=m8: E.tensor_tensor(out=S.ap, in0=S.ap, in1=m8.ap[:, 48:56].unsqueeze(2).to_broadcast([128, 8, 128]), op=ALU.mult),
                   reads=[m8, S], writes=[S])
                op("dve", lambda E: E.tensor_tensor(out=S.ap, in0=S.ap, in1=psum[:, c0 * 512:(c0 + 2) * 512].rearrange("p (h x) -> p h x", h=8), op=ALU.add),
                   reads=[S], writes=[S, BK[c0], BK[c1]])
                op("act", lambda E: E.activation(out=Sb.ap, in_=S.ap, func=AF.Copy), reads=[S], writes=[Sb])
            WIN = 2
            for s, L in SEQ:
                N = L // 64
                op("pool", lambda E: E.memset(S.ap, 0.0), writes=[S])
                op("pool", lambda E: E.memset(Sb.ap, 0.0), writes=[Sb])
                nxt = 0
                active = []
                while nxt < N or active:
                    while len(active) < WIN and nxt < N:
                        active.append([nxt, step(s, N, nxt), False])
                        nxt += 1
                    oldest = min(a_[0] for a_ in active)
                    for a_ in list(active):
                        if a_[2] and a_[0] != oldest:
                            continue
                        try:
                            r_ = next(a_[1])
                            a_[2] = (r_ == "SCAN")
                        except StopIteration:
                            active.remove(a_)
            phase_reset()

        if "C" in phases:
            T = ffn_bufs()
            load_wd(T, "ffn2_w_down")
            wst = alloc("wstc", 1024)
            wo = alloc("wo", 8 * 1024, BF16)
            wo.ap = wo.ap.rearrange("p (c f) -> p c f", c=8)
            for c in range(8):
                dma(wst[:], W["w_out"][c * 128:(c + 1) * 128, :], writes=[wst], sb=wst)
                sc = small[:, 5 + c:6 + c] if c < 4 else small[:, 9:10]
                op("dve", lambda E, c=c, sc=sc: E.tensor_scalar(out=wo.ap[:, c, :], in0=wst[:], scalar1=sc, scalar2=None, op0=ALU.mult),
                   reads=[wst, small], writes=[wo])
            ya = alloc("ya", 4 * 512)
            ya.ap = ya.ap.rearrange("p (j e) -> p j e", j=4)
            ofb = [alloc("ofb%d" % i, 4 * 512) for i in range(2)]
            for b in ofb:
                b.ap = b.ap.rearrange("p (j e) -> p j e", j=4)
            zz = alloc("zz", 4 * 512)
            zz.ap = zz.ap.rearrange("p (j e) -> p j e", j=4)
            mixb = [alloc("mixb%d" % i, 1024, BF16) for i in range(2)]
            sq = T["tmp"][1]
            ssc = alloc("ssc", 32)
            tilesC = [(s, L, ti) for s, L in SEQ for ti in range(L // 512)]
            ystg = Buf("ystg", arena[:, T["act_off"]:T["act_off"] + 4 * D].rearrange("p (j d) -> p j d", j=4))

            def load_h(ix):
                s_, L_, ti_ = tilesC[ix]
                dma(T["h"].ap, H1[s_][ti_ * 512:(ti_ + 1) * 512, :].rearrange("(j p) d -> p j d", p=128), writes=T["hb"], sb=T["h"])

            def load_front(ix):
                s_, L_, ti_ = tilesC[ix]
                rw = slice(ti_ * 512, (ti_ + 1) * 512)
                dma(ya.ap, YA[s_][rw, :].rearrange("(j p) e -> p j e", p=128), writes=[ya], sb=ya)
                dma(ofb[0].ap, OF[s_][0, rw, :].rearrange("(j p) e -> p j e", p=128), writes=[ofb[0]], sb=ofb[0])
                dma(ofb[1].ap, OF[s_][1, rw, :].rearrange("(j p) e -> p j e", p=128), writes=[ofb[1]], sb=ofb[1])
                dma(zz.ap, ZS[s_][rw, :].rearrange("(j p) e -> p j e", p=128), writes=[zz], sb=zz)
            load_h(0)
            load_front(0)
            for ix, (s, L, ti) in enumerate(tilesC):
                if True:
                    r0 = ti * 512
                    rows = slice(r0, r0 + 512)
                    h = T["h"]
                    o = ofb[0]
                    op("pool", lambda E: E.tensor_tensor(out=o.ap, in0=o.ap, in1=ofb[1].ap, op=ALU.add), reads=[ofb[1], o], writes=[o])
                    e2 = ofb[1]
                    op("act", lambda E: E.activation(out=e2.ap, in_=zz.ap, func=AF.Exp, scale=-1.0), reads=[zz], writes=[e2])
                    op("act", lambda E: E.activation(out=e2.ap, in_=e2.ap, func=AF.Ln, bias=1.0), reads=[e2], writes=[e2])
                    op("act", lambda E: E.activation(out=e2.ap, in_=e2.ap, func=AF.Exp, scale=-1.0), reads=[e2], writes=[e2])
                    op("pool", lambda E: E.tensor_tensor(out=zz.ap, in0=zz.ap, in1=e2.ap, op=ALU.mult), reads=[e2, zz], writes=[zz])
                    for j in range(4):
                        op("dve", lambda E, j=j: E.tensor_tensor(out=sq[:, 0:512], in0=o.ap[:, j, :], in1=o.ap[:, j, :], op=ALU.mult), reads=[o], writes=[sq])
                        op("dve", lambda E, j=j: E.tensor_reduce(out=ssc[:, j * 8:j * 8 + 4], in_=sq.ap[:, 0:512].rearrange("p (h d) -> p h d", h=4), axis=AX.X, op=ALU.add),
                           reads=[sq], writes=[ssc])
                        op("act", lambda E, j=j: E.activation(out=T["junk"][:, 0:512], in_=ya.ap[:, j, :], func=AF.Square, accum_out=ssc[:, j * 8 + 4:j * 8 + 5]),
                           reads=[ya], writes=[ssc])
                    sv = ssc.ap[:, 0:32].rearrange("p (j k) -> p j k", j=4)
                    op("dve", lambda E, sv=sv: E.tensor_scalar(out=sv[:, :, 0:4], in0=sv[:, :, 0:4], scalar1=1.0 / 128, scalar2=EPS, op0=ALU.mult, op1=ALU.add), reads=[ssc], writes=[ssc])
                    op("dve", lambda E, sv=sv: E.tensor_scalar(out=sv[:, :, 4:5], in0=sv[:, :, 4:5], scalar1=1.0 / 512, scalar2=EPS, op0=ALU.mult, op1=ALU.add), reads=[ssc], writes=[ssc])
                    op("act", lambda E, sv=sv: E.activation(out=sv[:, :, 0:5], in_=sv[:, :, 0:5], func=AF.Ln), reads=[ssc], writes=[ssc])
                    op("act", lambda E, sv=sv: E.activation(out=sv[:, :, 0:5], in_=sv[:, :, 0:5], func=AF.Exp, scale=-0.5), reads=[ssc], writes=[ssc])
                    xT = T["xT"]
                    for j in range(4):
                        mb = mixb[j % 2]
                        op("act", lambda E, j=j, mb=mb: E.activation(out=mb[:, 0:512], in_=ya.ap[:, j, :], func=AF.Copy, scale=ssc[:, j * 8 + 4:j * 8 + 5]),
                           reads=[ya, ssc], writes=[mb])
                        op("dve", lambda E, j=j: E.tensor_tensor(out=o.ap[:, j, :].rearrange("p (h d) -> p h d", h=4), in0=o.ap[:, j, :].rearrange("p (h d) -> p h d", h=4),
                                                                 in1=ssc.ap[:, j * 8:j * 8 + 4].unsqueeze(2).to_broadcast([128, 4, 128]), op=ALU.mult), reads=[ssc, o], writes=[o])
                        op("pool", lambda E, j=j, mb=mb: E.tensor_tensor(out=mb[:, 512:1024], in0=o.ap[:, j, :], in1=zz.ap[:, j, :], op=ALU.mult), reads=[o, zz, mb], writes=[mb])
                        bk = j % 2
                        for dc in range(8):
                            op("pe", lambda E, dc=dc, bk=bk, mb=mb: E.transpose(out=pb(bk, dc * 128, (dc + 1) * 128), in_=mb[:, dc * 128:(dc + 1) * 128], identity=identb[:]),
                               reads=[mb, identb], writes=[BK[bk]])
                        op("dve", lambda E, j=j, bk=bk: E.tensor_copy(out=xT.ap[:, :, j * 128:(j + 1) * 128], in_=pb(bk).rearrange("p (c t) -> p c t", c=8)),
                           writes=[xT, BK[bk]])
                    if ix + 1 < len(tilesC):
                        load_front(ix + 1)
                    ss2 = T["ss2"]
                    for j in range(4):
                        b0 = 2 if j % 2 == 0 else 4
                        for hf in range(2):
                            for c in range(8):
                                op("pe", lambda E, c=c, hf=hf, b0=b0, j=j: E.matmul(pf(b0 + hf), lhsT=xT.ap[:, c, j * 128:(j + 1) * 128], rhs=wo.ap[:, c, hf * 512:(hf + 1) * 512],
                                                                                     start=(c == 0), stop=(c == 7)), reads=[xT, wo], writes=[BK[b0 + hf]])
                        yps = psum[:, b0 * 512:(b0 + 2) * 512]
                        op("act", lambda E, j=j, yps=yps: E.activation(out=T["junk"][:], in_=yps, func=AF.Square, accum_out=ss2[:, j:j + 1]), writes=[ss2, BK[b0], BK[b0 + 1]])
                        op("dve", lambda E, j=j: E.tensor_scalar(out=ss2[:, 4 + j:5 + j], in0=ss2[:, j:j + 1], scalar1=1.0 / D, scalar2=EPS, op0=ALU.mult, op1=ALU.add), reads=[ss2], writes=[ss2])
                        op("act", lambda E, j=j: E.activation(out=ss2[:, 4 + j:5 + j], in_=ss2[:, 4 + j:5 + j], func=AF.Ln), reads=[ss2], writes=[ss2])
                        op("act", lambda E, j=j: E.activation(out=ss2[:, 4 + j:5 + j], in_=ss2[:, 4 + j:5 + j], func=AF.Exp, scale=-0.5), reads=[ss2], writes=[ss2])
                        tmp = T["tmp"][j % 2]
                        op("dve", lambda E, j=j, yps=yps, tmp=tmp: E.scalar_tensor_tensor(out=tmp[:], in0=yps, scalar=ss2[:, 4 + j:5 + j], in1=gB["mix_post_g"][:], op0=ALU.mult, op1=ALU.mult),
                           reads=[ss2, gB["mix_post_g"]], writes=[tmp, BK[b0], BK[b0 + 1]])
                        op("pool", lambda E, j=j, tmp=tmp: E.tensor_tensor(out=T["hs"][j], in0=T["hs"][j], in1=tmp[:], op=ALU.add), reads=[tmp], writes=[T["hb"][j]])
                    ffn(T["hb"], T["hs"], "ffn2_pre_g", WB["ffn2_w_gate"], WB["ffn2_w_up"], T["wd"], gB["ffn2_post_g"], T)
                    ss = T["ss"]
                    for j in range(4):
                        op("act", lambda E, j=j: E.activation(out=T["junk"][:], in_=T["hs"][j], func=AF.Square, accum_out=ss[:, j:j + 1]), reads=[T["hb"][j]], writes=[ss])
                    rstd_chain(ss, 4, D)
                    for j in range(4):
                        e1 = "dve"
                        op(e1, lambda E, j=j: E.scalar_tensor_tensor(out=ystg.ap[:, j, :], in0=T["hs"][j], scalar=ss[:, j:j + 1], in1=gB["final_norm_g"][:], op0=ALU.mult, op1=ALU.mult),
                           reads=[ss, gB["final_norm_g"], T["hb"][j]], writes=T["actc"][4 * j:4 * j + 4])
                    if ix + 1 < len(tilesC):
                        load_h(ix + 1)
                    dma(Y[s][rows, :].rearrange("(j p) d -> p j d", p=128), ystg.ap, reads=T["actc"][0:16], sb=ystg, queue="pool")
            phase_reset()
        P.barrier()
        P.emit()
        stats = dict(nins=P.nins, nwaits=P.nwaits, nsem=len(P.dmabufs) + 5)
    return nc, stats


def rope_table(L):
    inv = 10000.0 ** (-np.arange(0, 32, 2, dtype=np.float32) / 32)
    ang = np.arange(L, dtype=np.float32)[:, None] * inv[None, :].astype(np.float32)
    c, s = np.cos(ang).astype(np.float32), np.sin(ang).astype(np.float32)
    return np.ascontiguousarray(np.concatenate([c, c, -s, s], axis=1).astype(np.float32))


def gdn_masks():
    i = np.arange(64)
    m = np.zeros((64, 5, 8, 64), np.float32)
    for hd in range(8):
        fwd = hd < 4
        al = (i[:, None] > i[None, :]) if fwd else (i[:, None] < i[None, :])
        m[:, 0, hd, :] = np.where(al, 0.0, NEG)
        al = (i[None, :] > i[:, None]) if fwd else (i[None, :] < i[:, None])
        m[:, 1, hd, :] = np.where(al, 0.0, NEG)
        al = (i[None, :] >= i[:, None]) if fwd else (i[None, :] <= i[:, None])
        m[:, 2, hd, :] = np.where(al, 0.0, NEG)
    m[:, 3, 0, :] = (i[:, None] <= i[None, :]).astype(np.float32)
    m[:, 3, 1, :] = (i[:, None] >= i[None, :]).astype(np.float32)
    return m


_CACHE = {}


def kernel(**inputs):
    xp = np.asarray(inputs["x_prompt"], np.float32)
    xs = np.asarray(inputs["x_sample"], np.float32)
    B, Lp, _ = xp.shape
    Ls = xs.shape[1]
    assert B == 8 and xs.shape[0] == 8
    key = (Lp, Ls)
    if key not in _CACHE:
        _CACHE[key] = build(Lp, Ls)[0]
    nc = _CACHE[key]
    shared = {n: np.ascontiguousarray(np.asarray(inputs[n], np.float32).reshape(WSHAPES[n])) for n in WNAMES}
    shared["rope_tab"] = rope_table(max(Lp, Ls))
    shared["gdn_masks"] = gdn_masks()
    in_maps = []
    for c in range(8):
        m = dict(shared)
        m["x_p"] = np.ascontiguousarray(xp[c])
        m["x_s"] = np.ascontiguousarray(xs[c])
        in_maps.append(m)
    res = run_bass_kernel_spmd(nc, in_maps, core_ids=list(range(8)))
    yp = np.stack([np.asarray(r["y_p"], np.float32) for r in res.results], 0)
    ys = np.stack([np.asarray(r["y_s"], np.float32) for r in res.results], 0)
    return (yp, ys)
```

```python
from contextlib import ExitStack
import numpy as np
import concourse.bass as bass
import concourse.mybir as mybir
from concourse.bass_utils import run_bass_kernel_spmd

F32 = mybir.dt.float32
BF16 = mybir.dt.bfloat16
ALU = mybir.AluOpType
AF = mybir.ActivationFunctionType
AX = mybir.AxisListType

D = 1024
DFF = 2816
NF = DFF // 128
INC = 2736
EPS = 1e-6
NEG = -30000.0
ENG = ("pe", "act", "dve", "pool", "sp")


class Buf:
    __slots__ = ("name", "ap", "lw", "rd", "sem", "semcnt")

    def __init__(self, name, ap):
        self.name = name
        self.ap = ap
        self.lw = None
        self.rd = {}
        self.sem = None
        self.semcnt = 0

    def __getitem__(self, k):
        return self.ap[k]


class Prog:
    def __init__(self, nc, stack):
        self.nc = nc
        self.stack = stack
        self.q = {e: [] for e in ENG}
        self.cnt = {e: 0 for e in ENG}
        self.known = {e: {} for e in ENG}
        self.hist = {}
        self.esem = {e: stack.enter_context(nc.semaphore("s_" + e)) for e in ENG}
        self.semobj = {e: self.esem[e] for e in ENG}
        self.dmabufs = []
        self.nwaits = 0
        self.nins = 0
        self.E = {"pe": nc.tensor, "act": nc.scalar, "dve": nc.vector, "pool": nc.gpsimd, "sp": nc.sync}

    def _need(self, deps, tok):
        if tok is None:
            return
        k, v = tok
        if deps.get(k, 0) < v:
            deps[k] = v

    def _collect(self, reads, writes):
        deps = {}
        for b in reads:
            self._need(deps, b.lw)
        for b in writes:
            self._need(deps, b.lw)
            for k, v in b.rd.items():
                self._need(deps, (k, v))
        return deps

    def _emit_waits(self, eng, deps, defer=False):
        kn = self.known[eng]
        new = None
        pend = []
        for k, v in deps.items():
            if k == eng:
                if eng == "pe" or eng == "sp":
                    continue
                if self.cnt[eng] - v > 1:
                    continue
            cur = new if new is not None else kn
            if cur.get(k, 0) >= v:
                continue
            sem = self.semobj[k]
            pend.append((sem, v))
            self.nwaits += 1
            if new is None:
                new = dict(kn)
            new[k] = v
            h = self.hist.get((k, v))
            if h:
                for k2, v2 in h.items():
                    if k2 != eng and new.get(k2, 0) < v2:
                        new[k2] = v2
        if new is not None:
            self.known[eng] = new
        last = pend.pop() if (defer and pend) else None
        for sem, v in pend:
            self.E[eng].wait_ge(sem, v)
        return last

    def op(self, eng, fn, reads=(), writes=()):
        deps = self._collect(reads, writes)
        last = self._emit_waits(eng, deps, defer=True)
        sem = self.esem[eng]
        self.cnt[eng] += 1
        v = self.cnt[eng]
        ins = fn(self.E[eng])
        if last is not None:
            ins._wait_ge(last[0], last[1])
        ins.then_inc(sem, 1)
        self.nins += 1
        tok = (eng, v)
        self.hist[tok] = self.known[eng]
        for b in writes:
            b.lw = tok
            b.rd = {}
        for b in reads:
            if b.rd.get(eng, 0) < v:
                b.rd[eng] = v
        return tok

    def dma(self, out_ap, in_ap, reads=(), writes=(), sb=None, queue="sp", **kw):
        deps = self._collect(reads, writes)
        last = self._emit_waits(queue, deps, defer=True)
        if sb.sem is None:
            sb.sem = self.stack.enter_context(self.nc.semaphore("d%d_%s" % (len(self.dmabufs), sb.name)))
            self.semobj[id(sb)] = sb.sem
            self.dmabufs.append(sb)
        sb.semcnt += 16
        v = sb.semcnt
        sem = sb.sem
        ins = self.E[queue].dma_start(out=out_ap, in_=in_ap, **kw)
        if last is not None:
            ins._wait_ge(last[0], last[1])
        ins.then_inc(sem, 16)
        self.nins += 1
        tok = (id(sb), v)
        self.hist[tok] = self.known[queue]
        for b in writes:
            b.lw = tok
            b.rd = {}
        for b in reads:
            if b.rd.get(tok[0], 0) < v:
                b.rd[tok[0]] = v
        return tok

    def barrier(self):
        deps = {}
        for e in ("pe", "act", "dve", "pool"):
            if self.cnt[e]:
                deps[e] = self.cnt[e]
        for b in self.dmabufs:
            deps[id(b)] = b.semcnt
        for e in ENG:
            d = {k: v for k, v in deps.items() if k != e}
            self._emit_waits(e, d)

    def emit(self):
        pass


WNAMES = ["ffn1_pre_g", "ffn1_w_gate", "ffn1_w_up", "ffn1_w_down", "ffn1_post_g", "mix_pre_g", "w_in",
          "mla_q_norm_g", "mla_w_uq", "mla_kv_norm_g", "mla_w_ukv", "mla_out_norm_g", "gdn_conv_w",
          "gdn_a_log", "gdn_dt_bias", "gdn_out_norm_g", "w_out", "mix_post_g", "ffn2_pre_g",
          "ffn2_w_gate", "ffn2_w_up", "ffn2_w_down", "ffn2_post_g", "final_norm_g"]
WSHAPES = {"ffn1_pre_g": [1, D], "ffn1_w_gate": [D, DFF], "ffn1_w_up": [D, DFF], "ffn1_w_down": [DFF, D],
           "ffn1_post_g": [1, D], "mix_pre_g": [1, D], "w_in": [D, INC], "mla_q_norm_g": [1, 384],
           "mla_w_uq": [384, 768], "mla_kv_norm_g": [1, 256], "mla_w_ukv": [256, 1024],
           "mla_out_norm_g": [1, 512], "gdn_conv_w": [5, 1536], "gdn_a_log": [1, 8], "gdn_dt_bias": [1, 8],
           "gdn_out_norm_g": [1, 128], "w_out": [D, D], "mix_post_g": [1, D], "ffn2_pre_g": [1, D],
           "ffn2_w_gate": [D, DFF], "ffn2_w_up": [D, DFF], "ffn2_w_down": [DFF, D], "ffn2_post_g": [1, D],
           "final_norm_g": [1, D]}


def build(Lp, Ls, debug=False, phases="S A M B G H C"):
    phases = phases.split()
    nc = bass.Bass("TRN2", target_bir_lowering=False)
    Lmax = max(Lp, Ls)
    SEQ = (("p", Lp), ("s", Ls))

    def din(name, shape, dt=F32):
        return nc.dram_tensor(name, list(shape), dt, kind="ExternalInput").ap()

    def dscr(name, shape, dt=F32):
        return nc.dram_tensor(name, list(shape), dt, kind="ExternalOutput" if debug else "Internal").ap()

    X = {"p": din("x_p", [Lp, D]), "s": din("x_s", [Ls, D])}
    W = {n: din(n, WSHAPES[n]) for n in WNAMES}
    ROPE = din("rope_tab", [Lmax, 64])
    GMASK = din("gdn_masks", [64, 5, 8, 64])
    Y = {"p": nc.dram_tensor("y_p", [Lp, D], F32, kind="ExternalOutput").ap(),
         "s": nc.dram_tensor("y_s", [Ls, D], F32, kind="ExternalOutput").ap()}
    WB = {n: dscr("bf_" + n, WSHAPES[n], BF16) for n in
          ["ffn1_w_gate", "ffn1_w_up", "ffn1_w_down", "w_in", "ffn2_w_gate", "ffn2_w_up", "ffn2_w_down"]}
    H1 = {s: dscr("h1_" + s, [L, D]) for s, L in SEQ}
    LAT = {s: dscr("lat_" + s, [L, 672]) for s, L in SEQ}
    QKVT = {s: dscr("qkvT_" + s, [1536, L]) for s, L in SEQ}
    ZS = {s: dscr("z_" + s, [L, 512]) for s, L in SEQ}
    GBL = {s: dscr("gbl_" + s, [2, L, 12]) for s, L in SEQ}
    QT = {s: dscr("QT_" + s, [8, 96, L], BF16) for s, L in SEQ}
    KT = {s: dscr("KT_" + s, [8, 96, L], BF16) for s, L in SEQ}
    VV = {s: dscr("V_" + s, [8, 128, L // 128, 64], BF16) for s, L in SEQ}
    YA = {s: dscr("ya_" + s, [L, 512]) for s, L in SEQ}
    TOK = {s: dscr("tok_" + s, [L, 1536], BF16) for s, L in SEQ}
    OF = {s: dscr("of_" + s, [2, L, 512]) for s, L in SEQ}

    with ExitStack() as st:
        P = Prog(nc, st)
        ARENA = 52800
        arena = st.enter_context(nc.sbuf_tensor("arena", [128, ARENA], F32))
        psum = st.enter_context(nc.psum_tensor("psum", [128, 4096], F32))
        psum_bf = psum.bitcast(BF16)
        BK = [Buf("bank%d" % i, psum[:, i * 512:(i + 1) * 512]) for i in range(8)]

        def pf(i, a=0, b=512):
            return psum[:, i * 512 + a:i * 512 + b]

        def pb(i, a=0, b=1024):
            return psum_bf[:, i * 1024 + a:i * 1024 + b]

        off = [0]
        perm_end = [0]

        def alloc(name, ncols, dt=F32, parts=128):
            n32 = ncols if dt == F32 else (ncols + 1) // 2
            assert off[0] + n32 <= ARENA, ("SBUF arena overflow", name, off[0] + n32)
            a = arena[0:parts, off[0]:off[0] + n32]
            off[0] += n32
            if dt != F32:
                a = a.bitcast(dt)[:, 0:ncols]
            return Buf(name, a)

        def phase_reset():
            P.barrier()
            off[0] = perm_end[0]

        op = P.op
        dma = P.dma

        identf = alloc("identf", 128)
        identb = alloc("identb", 128, BF16)
        onesf = alloc("onesf", 128)
        gT = {n: alloc("gT_" + n, 8) for n in ("ffn1_pre_g", "mix_pre_g", "ffn2_pre_g")}
        gB = {n: alloc("gB_" + n, D) for n in ("ffn1_post_g", "mix_post_g", "ffn2_post_g", "final_norm_g")}
        small = alloc("small", 64)
        perm_end[0] = off[0]

        op("pool", lambda E: E.memset(identf[:], 0.0), writes=[identf])
        op("pool", lambda E: E.affine_select(out=identf[:], in_=identf[:], pattern=[[-1, 128]], compare_op=ALU.not_equal,
                                             fill=1.0, base=0, channel_multiplier=1), reads=[identf], writes=[identf])
        op("dve", lambda E: E.tensor_copy(out=identb[:], in_=identf[:]), reads=[identf], writes=[identb])
        op("pool", lambda E: E.memset(onesf[:], 1.0), writes=[onesf])
        for n, b in gT.items():
            dma(b[:], W[n].rearrange("o (c p) -> p (o c)", p=128), writes=[b], sb=b, allow_slow_non_contiguous=True)
        for n, b in gB.items():
            dma(b[:], W[n].broadcast_to([128, D]), writes=[b], sb=b)
        for n in ("ffn1_post_g", "ffn2_post_g"):
            b = gB[n]
            op("pool", lambda E, b=b: E.tensor_scalar(out=b[:], in0=b[:], scalar1=0.5, scalar2=None, op0=ALU.mult),
               reads=[b], writes=[b])
        sm_q = Buf("sm_q", small.ap)
        dma(small[:, 0:3], W["mla_q_norm_g"].rearrange("o (c p) -> p (o c)", p=128), writes=[small], sb=small,
            allow_slow_non_contiguous=True)
        dma(small[:, 3:5], W["mla_kv_norm_g"].rearrange("o (c p) -> p (o c)", p=128), reads=[small], sb=small,
            allow_slow_non_contiguous=True)
        dma(small[:, 5:9], W["mla_out_norm_g"].rearrange("o (c p) -> p (o c)", p=128), reads=[small], sb=small,
            allow_slow_non_contiguous=True)
        dma(small[:, 9:10], W["gdn_out_norm_g"].rearrange("o (c p) -> p (o c)", p=128), reads=[small], sb=small,
            allow_slow_non_contiguous=True)
        dma(small[:, 16:24], W["gdn_dt_bias"].broadcast_to([128, 8]), reads=[small], sb=small)
        dma(small[:, 24:32], W["gdn_a_log"].broadcast_to([128, 8]), reads=[small], sb=small)
        small.lw = (id(small), small.semcnt)
        op("act", lambda E: E.activation(out=small[:, 24:32], in_=small[:, 24:32], func=AF.Exp), reads=[small], writes=[small])
        op("dve", lambda E: E.tensor_scalar(out=small[:, 24:32], in0=small[:, 24:32], scalar1=-1.0, scalar2=None, op0=ALU.mult),
           reads=[small], writes=[small])

        def rstd_chain(ss, k, n, eps=EPS, post=None):
            op("dve", lambda E: E.tensor_scalar(out=ss[:, 0:k], in0=ss[:, 0:k], scalar1=1.0 / n, scalar2=eps, op0=ALU.mult, op1=ALU.add),
               reads=[ss], writes=[ss])
            op("act", lambda E: E.activation(out=ss[:, 0:k], in_=ss[:, 0:k], func=AF.Ln), reads=[ss], writes=[ss])
            op("act", lambda E: E.activation(out=ss[:, 0:k], in_=ss[:, 0:k], func=AF.Exp, scale=-0.5), reads=[ss], writes=[ss])

        cast_rr = [0]

        def cast(out_ap, in_ap, reads, writes):
            e = ("dve", "pool", "act")[cast_rr[0] % 3]
            cast_rr[0] += 1
            if e == "act":
                op("act", lambda E: E.activation(out=out_ap, in_=in_ap, func=AF.Copy), reads=reads, writes=writes)
            else:
                op(e, lambda E: E.tensor_copy(out=out_ap, in_=in_ap), reads=reads, writes=writes)

        if "S" in phases:
            stf = [alloc("stf%d" % i, DFF) for i in range(2)]
            stb = [alloc("stb%d" % i, DFF, BF16) for i in range(2)]
            it = 0
            for n in WB:
                K_, N_ = WSHAPES[n]
                for kc in range(K_ // 128):
                    f, b = stf[it % 2], stb[it % 2]
                    dma(f[:, 0:N_], W[n][kc * 128:(kc + 1) * 128, :], writes=[f], sb=f)
                    cast(b[:, 0:N_], f[:, 0:N_], [f], [b])
                    dma(WB[n][kc * 128:(kc + 1) * 128, :], b[:, 0:N_], reads=[b], sb=b, queue="pool")
                    it += 1
        phase_reset()

        def norm_T(srcs, src_bufs, gTb, xT, ss, xnb, junk):
            for j in range(4):
                op("act", lambda E, j=j: E.activation(out=junk[:], in_=srcs[j], func=AF.Square, accum_out=ss[:, j:j + 1]),
                   reads=[src_bufs[j]], writes=[ss])
            rstd_chain(ss, 4, D)
            for j in range(4):
                xb = xnb[j % 2]
                if j % 2 == 0:
                    op("dve", lambda E, j=j, xb=xb: E.tensor_scalar(out=xb[:], in0=srcs[j], scalar1=ss[:, j:j + 1], scalar2=None, op0=ALU.mult),
                       reads=[src_bufs[j], ss], writes=[xb])
                else:
                    op("act", lambda E, j=j, xb=xb: E.activation(out=xb[:], in_=srcs[j], func=AF.Copy, scale=ss[:, j:j + 1]),
                       reads=[src_bufs[j], ss], writes=[xb])
                bk = j % 2
                for dc in range(8):
                    op("pe", lambda E, dc=dc, bk=bk, xb=xb: E.transpose(out=pb(bk, dc * 128, (dc + 1) * 128), in_=xb[:, dc * 128:(dc + 1) * 128], identity=identb[:]),
                       reads=[xb, identb], writes=[BK[bk]])
                op("dve", lambda E, j=j, bk=bk: E.tensor_tensor(
                    out=xT.ap[:, :, j * 128:(j + 1) * 128], in0=pb(bk).rearrange("p (c t) -> p c t", c=8),
                    in1=gTb[:, 0:8].unsqueeze(2).to_broadcast([128, 8, 128]), op=ALU.mult),
                    reads=[gTb], writes=[xT, BK[bk]])

        PIECES = [(c0_, 256) for c0_ in range(0, DFF, 256)]

        def ffn(hbufs, hsrcs, pre_g, wg, wu, wd_res, post_bc, T):
            norm_T(hsrcs, hbufs, gT[pre_g], T["xT"], T["ss"], T["xnb"], T["junk"])
            xT, act = T["xT"], T["act"]
            for si, (c0, w) in enumerate(PIECES):
                sg = T["hsl"][T["hi"] % 6]
                su = T["hsl"][(T["hi"] + 1) % 6]
                T["hi"] += 2
                dma(sg.ap[:, :, 0:w], wg[:, c0:c0 + w].rearrange("(dc p) f -> p dc f", p=128), writes=[sg], sb=sg)
                dma(su.ap[:, :, 0:w], wu[:, c0:c0 + w].rearrange("(dc p) f -> p dc f", p=128), writes=[su], sb=su)
                for fl in range(w // 128):
                    f = c0 // 128 + fl
                    g_, u_ = 2 + f % 2, 4 + f % 2
                    for dc in range(8):
                        op("pe", lambda E, dc=dc, fl=fl, g_=g_, sg=sg: E.matmul(pf(g_), lhsT=sg.ap[:, dc, fl * 128:(fl + 1) * 128], rhs=xT.ap[:, dc, :],
                                                                                  start=(dc == 0), stop=(dc == 7)), reads=[sg, xT], writes=[BK[g_]])
                    for dc in range(8):
                        op("pe", lambda E, dc=dc, fl=fl, u_=u_, su=su: E.matmul(pf(u_), lhsT=su.ap[:, dc, fl * 128:(fl + 1) * 128], rhs=xT.ap[:, dc, :],
                                                                                  start=(dc == 0), stop=(dc == 7)), reads=[su, xT], writes=[BK[u_]])
                    et = T["etmp"][f % 2]
                    op("act", lambda E, et=et, g_=g_: E.activation(out=et[:], in_=pf(g_), func=AF.Exp, scale=-1.0), writes=[et, BK[g_]])
                    op("act", lambda E, et=et: E.activation(out=et[:], in_=et[:], func=AF.Ln, bias=1.0), reads=[et], writes=[et])
                    op("act", lambda E, et=et: E.activation(out=et[:], in_=et[:], func=AF.Exp, scale=-1.0), reads=[et], writes=[et])
                    op("dve", lambda E, et=et, g_=g_: E.tensor_tensor(out=et[:], in0=et[:], in1=pf(g_), op=ALU.mult), reads=[et], writes=[et, BK[g_]])
                    op("dve", lambda E, et=et, u_=u_, f=f: E.tensor_tensor(out=act.ap[:, f, :], in0=et[:], in1=pf(u_), op=ALU.mult),
                       reads=[et], writes=[T["actc"][f], BK[u_]])
            ss2 = T["ss2"]
            for j in range(4):
                b0 = 2 if j % 2 == 0 else 4
                for hf in range(2):
                    for f in range(NF):
                        op("pe", lambda E, f=f, hf=hf, b0=b0, j=j: E.matmul(pf(b0 + hf), lhsT=act.ap[:, f, j * 128:(j + 1) * 128],
                                                                             rhs=wd_res.ap[:, f, hf * 512:(hf + 1) * 512], start=(f == 0), stop=(f == NF - 1)),
                           reads=[T["actc"][f], wd_res], writes=[BK[b0 + hf]])
                yps = psum[:, b0 * 512:(b0 + 2) * 512]
                op("act", lambda E, j=j, yps=yps: E.activation(out=T["junk"][:], in_=yps, func=AF.Square, accum_out=ss2[:, j:j + 1]),
                   writes=[ss2, BK[b0], BK[b0 + 1]])
                op("dve", lambda E, j=j: E.tensor_scalar(out=ss2[:, 4 + j:5 + j], in0=ss2[:, j:j + 1], scalar1=1.0 / D, scalar2=EPS, op0=ALU.mult, op1=ALU.add),
                   reads=[ss2], writes=[ss2])
                op("act", lambda E, j=j: E.activation(out=ss2[:, 4 + j:5 + j], in_=ss2[:, 4 + j:5 + j], func=AF.Ln), reads=[ss2], writes=[ss2])
                op("act", lambda E, j=j: E.activation(out=ss2[:, 4 + j:5 + j], in_=ss2[:, 4 + j:5 + j], func=AF.Exp, scale=-0.5), reads=[ss2], writes=[ss2])
                tmp = T["tmp"][j % 2]
                op("dve", lambda E, j=j, yps=yps, tmp=tmp: E.scalar_tensor_tensor(out=tmp[:], in0=yps, scalar=ss2[:, 4 + j:5 + j], in1=post_bc[:],
                                                                                    op0=ALU.mult, op1=ALU.mult),
                   reads=[ss2, post_bc], writes=[tmp, BK[b0], BK[b0 + 1]])
                op("pool", lambda E, j=j, tmp=tmp: E.tensor_tensor(out=hsrcs[j], in0=hsrcs[j], in1=tmp[:], op=ALU.add),
                   reads=[tmp], writes=[hbufs[j]])

        def ffn_bufs(with_full=False):
            T = {}
            T["wd"] = alloc("wd_res", NF * D, BF16)
            T["wd"].ap = T["wd"].ap.rearrange("p (f d) -> p f d", f=NF)
            T["xT"] = alloc("xT", 8 * 512, BF16)
            T["xT"].ap = T["xT"].ap.rearrange("p (c t) -> p c t", c=8)
            T["act_off"] = off[0]
            T["act"] = alloc("act", NF * 512, BF16)
            T["act"].ap = T["act"].ap.rearrange("p (f t) -> p f t", f=NF)
            T["actc"] = [Buf("act%d" % f, T["act"].ap[:, f, :]) for f in range(NF)]
            T["slab"] = []
            for i in range(3 if with_full else 0):
                b = alloc("slab%d" % i, 8 * 512, BF16)
                b.ap = b.ap.rearrange("p (c f) -> p c f", c=8)
                T["slab"].append(b)
            T["slabi"] = 0
            T["hsl"] = []
            for i in range(6):
                b = alloc("hsl%d" % i, 8 * 256, BF16)
                b.ap = b.ap.rearrange("p (c f) -> p c f", c=8)
                T["hsl"].append(b)
            T["hi"] = 0
            T["etmp"] = [alloc("etmp%d" % i, 512) for i in range(2)]
            T["tmp"] = [alloc("tmp%d" % i, D) for i in range(2)]
            T["xnb"] = [alloc("xnb%d" % i, D, BF16) for i in range(2)]
            T["junk"] = alloc("junk", D, BF16)
            T["ss"] = alloc("ss", 8)
            T["ss2"] = alloc("ss2", 8)
            T["h"] = alloc("h", 4 * D)
            T["h"].ap = T["h"].ap.rearrange("p (j d) -> p j d", j=4)
            T["hb"] = [Buf("h%d" % j, T["h"].ap[:, j, :]) for j in range(4)]
            T["hs"] = [T["h"].ap[:, j, :] for j in range(4)]
            return T

        def load_wd(T, name):
            wd = T["wd"]
            for f0 in range(0, NF, 2):
                dma(wd.ap[:, f0:f0 + 2, :], WB[name][f0 * 128:(f0 + 2) * 128, :].rearrange("(f p) d -> p f d", p=128),
                    writes=[wd] if f0 == 0 else [], reads=[] if f0 == 0 else [], sb=wd)
            wd.lw = (id(wd), wd.semcnt)

        if "A" in phases:
            T = ffn_bufs(True)
            load_wd(T, "ffn1_w_down")
            pst = [alloc("pst%d" % i, 1200) for i in range(2)]
            qst = [alloc("qst%d" % i, 512) for i in range(3)]
            gst = [alloc("gst%d" % i, 64) for i in range(2)]
            pi = 0
            qi = 0
            tilesA = [(s, L, ti) for s, L in SEQ for ti in range(L // 512)]

            def load_x(ix):
                s_, L_, ti_ = tilesA[ix]
                dma(T["h"].ap, X[s_][ti_ * 512:(ti_ + 1) * 512, :].rearrange("(j p) d -> p j d", p=128), writes=T["hb"], sb=T["h"])
            load_x(0)
            for ix, (s, L, ti) in enumerate(tilesA):
                if True:
                    r0 = ti * 512
                    h = T["h"]
                    ffn(T["hb"], T["hs"], "ffn1_pre_g", WB["ffn1_w_gate"], WB["ffn1_w_up"], T["wd"], gB["ffn1_post_g"], T)
                    dma(H1[s][r0:r0 + 512, :].rearrange("(j p) d -> p j d", p=128), h.ap, reads=T["hb"], sb=h, queue="pool")
                    norm_T(T["hs"], T["hb"], gT["mix_pre_g"], T["xT"], T["ss"], T["xnb"], T["junk"])
                    if ix + 1 < len(tilesA):
                        load_x(ix + 1)
                    xT = T["xT"]
                    for q3 in range(3):
                        sl = T["slab"][T["slabi"] % 3]
                        T["slabi"] += 1
                        c0 = 672 + q3 * 512
                        dma(sl.ap[:, :, 0:512], WB["w_in"][:, c0:c0 + 512].rearrange("(dc p) f -> p dc f", p=128), writes=[sl], sb=sl)
                        for fl in range(4):
                            ch = q3 * 4 + fl
                            bk = 2 + ch % 2
                            for dc in range(8):
                                op("pe", lambda E, dc=dc, fl=fl, bk=bk, sl=sl: E.matmul(pf(bk), lhsT=sl.ap[:, dc, fl * 128:(fl + 1) * 128], rhs=xT.ap[:, dc, :],
                                                                                          start=(dc == 0), stop=(dc == 7)), reads=[sl, xT], writes=[BK[bk]])
                            qs = qst[qi % 3]
                            qi += 1
                            if ch % 2 == 0:
                                op("act", lambda E, qs=qs, bk=bk: E.activation(out=qs[:], in_=pf(bk), func=AF.Copy), writes=[qs, BK[bk]])
                            else:
                                op("dve", lambda E, qs=qs, bk=bk: E.tensor_copy(out=qs[:], in_=pf(bk)), writes=[qs, BK[bk]])
                            dma(QKVT[s][ch * 128:(ch + 1) * 128, r0:r0 + 512], qs[:], reads=[qs], sb=qs, queue="pool")
                    sA = T["slab"][T["slabi"] % 3]
                    sB = T["slab"][(T["slabi"] + 1) % 3]
                    sC = T["slab"][(T["slabi"] + 2) % 3]
                    T["slabi"] += 3
                    dma(sA.ap[:, :, 0:512], WB["w_in"][:, 0:512].rearrange("(dc p) f -> p dc f", p=128), writes=[sA], sb=sA)
                    dma(sB.ap[:, :, 0:512], WB["w_in"][:, 2208:2720].rearrange("(dc p) f -> p dc f", p=128), writes=[sB], sb=sB)
                    dma(sC.ap[:, :, 0:160], WB["w_in"][:, 512:672].rearrange("(dc p) f -> p dc f", p=128), writes=[sC], sb=sC)
                    dma(sC.ap[:, :, 160:176], WB["w_in"][:, 2720:2736].rearrange("(dc p) f -> p dc f", p=128), reads=[sC], sb=sC)
                    sC.lw = (id(sC), sC.semcnt)
                    for j in range(4):
                        for dc in range(8):
                            lhs = xT.ap[:, dc, j * 128:(j + 1) * 128]
                            op("pe", lambda E, dc=dc, lhs=lhs: E.matmul(pf(6), lhsT=lhs, rhs=sA.ap[:, dc, 0:512], start=(dc == 0), stop=(dc == 7)),
                               reads=[sA, xT], writes=[BK[6]])
                        for dc in range(8):
                            lhs = xT.ap[:, dc, j * 128:(j + 1) * 128]
                            op("pe", lambda E, dc=dc, lhs=lhs: E.matmul(pf(7), lhsT=lhs, rhs=sB.ap[:, dc, 0:512], start=(dc == 0), stop=(dc == 7)),
                               reads=[sB, xT], writes=[BK[7]])
                        for dc in range(8):
                            lhs = xT.ap[:, dc, j * 128:(j + 1) * 128]
                            op("pe", lambda E, dc=dc, lhs=lhs: E.matmul(pf(0, 0, 176), lhsT=lhs, rhs=sC.ap[:, dc, 0:176], start=(dc == 0), stop=(dc == 7)),
                               reads=[sC, xT], writes=[BK[0]])
                        ps_ = pst[pi % 2]
                        gs_ = gst[pi % 2]
                        pi += 1
                        op("act", lambda E, ps_=ps_: E.activation(out=ps_[:, 0:512], in_=pf(6), func=AF.Copy), writes=[ps_, BK[6]])
                        op("dve", lambda E, ps_=ps_: E.tensor_copy(out=ps_[:, 672:1184], in_=pf(7)), reads=[ps_], writes=[ps_, BK[7]])
                        op("dve", lambda E, ps_=ps_: E.tensor_copy(out=ps_[:, 512:672], in_=pf(0, 0, 160)), reads=[ps_], writes=[ps_, BK[0]])
                        op("dve", lambda E, gs_=gs_: E.tensor_tensor(out=gs_[:, 0:8], in0=pf(0, 160, 168), in1=small[:, 16:24], op=ALU.add),
                           reads=[small], writes=[gs_, BK[0]])
                        op("act", lambda E, gs_=gs_: E.activation(out=gs_[:, 0:8], in_=gs_[:, 0:8], func=AF.Exp), reads=[gs_], writes=[gs_])
                        op("act", lambda E, gs_=gs_: E.activation(out=gs_[:, 8:16], in_=pf(0, 168, 176), func=AF.Exp, scale=-1.0), reads=[gs_], writes=[gs_, BK[0]])
                        op("act", lambda E, gs_=gs_: E.activation(out=gs_[:, 0:16], in_=gs_[:, 0:16], func=AF.Ln, bias=1.0), reads=[gs_], writes=[gs_])
                        gv = gs_.ap[:, 32:56].rearrange("p (d k) -> p d k", d=2)
                        op("dve", lambda E, gs_=gs_, gv=gv: E.tensor_tensor(out=gv[:, :, 0:4], in0=gs_.ap[:, 0:8].rearrange("p (d k) -> p d k", d=2),
                                                                            in1=small.ap[:, 24:32].rearrange("p (d k) -> p d k", d=2), op=ALU.mult),
                           reads=[gs_, small], writes=[gs_])
                        op("dve", lambda E, gs_=gs_, gv=gv: E.tensor_scalar(out=gv[:, :, 4:8], in0=gs_.ap[:, 8:16].rearrange("p (d k) -> p d k", d=2),
                                                                            scalar1=-1.0, scalar2=None, op0=ALU.mult), reads=[gs_], writes=[gs_])
                        op("act", lambda E, gs_=gs_, gv=gv: E.activation(out=gv[:, :, 8:12], in_=gs_.ap[:, 8:16].rearrange("p (d k) -> p d k", d=2),
                                                                         func=AF.Exp, scale=-1.0), reads=[gs_], writes=[gs_])
                        rr = slice(r0 + j * 128, r0 + (j + 1) * 128)
                        dma(LAT[s][rr, :], ps_[:, 0:672], reads=[ps_], sb=ps_, queue="pool")
                        dma(ZS[s][rr, :], ps_[:, 672:1184], reads=[ps_], sb=ps_, queue="pool")
                        dma(GBL[s][:, rr, :].rearrange("d p k -> p d k"), gv, reads=[gs_], sb=gs_, queue="pool")
            phase_reset()

        if "M" in phases:
            wst = alloc("wst", 1024)
            wuq = alloc("wuq", 3 * 768, BF16)
            wuq.ap = wuq.ap.rearrange("p (c f) -> p c f", c=3)
            wk = alloc("wk", 2 * 512, BF16)
            wk.ap = wk.ap.rearrange("p (c f) -> p c f", c=2)
            wv = alloc("wv", 2 * 512, BF16)
            wv.ap = wv.ap.rearrange("p (c f) -> p c f", c=2)
            for c in range(3):
                dma(wst[:, 0:768], W["mla_w_uq"][c * 128:(c + 1) * 128, :], writes=[wst], sb=wst)
                op("dve", lambda E, c=c: E.tensor_scalar(out=wuq.ap[:, c, :], in0=wst[:, 0:768], scalar1=small[:, c:c + 1], scalar2=None, op0=ALU.mult),
                   reads=[wst, small], writes=[wuq])
            for c in range(2):
                dma(wst[:, 0:1024], W["mla_w_ukv"][c * 128:(c + 1) * 128, :], writes=[wst], sb=wst)
                wv4 = wst.ap[:, 0:1024].rearrange("p (h e) -> p h e", h=8)
                op("dve", lambda E, c=c, wv4=wv4: E.tensor_scalar(out=wk.ap[:, c, :].rearrange("p (h e) -> p h e", h=8), in0=wv4[:, :, 0:64],
                                                                  scalar1=small[:, 3 + c:4 + c], scalar2=None, op0=ALU.mult), reads=[wst, small], writes=[wk])
                op("dve", lambda E, c=c, wv4=wv4: E.tensor_scalar(out=wv.ap[:, c, :].rearrange("p (h e) -> p h e", h=8), in0=wv4[:, :, 64:128],
                                                                  scalar1=small[:, 3 + c:4 + c], scalar2=None, op0=ALU.mult), reads=[wst, small], writes=[wv])
            lat = [alloc("lat%d" % i, 672) for i in range(2)]
            rp = [alloc("rp%d" % i, 64) for i in range(2)]
            lnb = [alloc("lnb%d" % i, 736, BF16) for i in range(2)]
            for b in lnb:
                op("pool", lambda E, b=b: E.memset(b[:, 640:736], 0.0), writes=[b])
            ssm = [alloc("ssm%d" % i, 4) for i in range(2)]
            junkm = alloc("junkm", 384, BF16)
            cT = [alloc("cT%d" % i, 3 * 128, BF16) for i in range(2)]
            ckvT = alloc("ckvT", 2 * 512, BF16)
            ckvT.ap = ckvT.ap.rearrange("p (c t) -> p c t", c=2)
            ckvc = [Buf("ckv%d" % j, ckvT.ap[:, :, j * 128:(j + 1) * 128]) for j in range(4)]
            kpst = alloc("kpst", 512, BF16)
            kpc = [Buf("kp%d" % j, kpst.ap[:, j * 128:(j + 1) * 128]) for j in range(4)]
            qr = [alloc("qr%d" % i, 768, BF16) for i in range(2)]
            rtmp = [alloc("rtmp%d" % i, 512) for i in range(2)]
            qtst = alloc("qtst", 8 * 512, BF16)
            qtst.ap = qtst.ap.rearrange("p (h t) -> p h t", h=8)
            qtc = [Buf("qt%d" % j, qtst.ap[:, :, j * 128:(j + 1) * 128]) for j in range(4)]
            ktst = alloc("ktst", 4 * 512, BF16)
            ktst.ap = ktst.ap.rearrange("p (a t) -> p a t", a=4)
            vst = alloc("vst", 4 * 512, BF16)
            vst.ap = vst.ap.rearrange("p (j f) -> p j f", j=4)
            vsc = [Buf("vs%d" % j, vst.ap[:, j, :]) for j in range(4)]
            li = 0
            for s, L in SEQ:
                for ti in range(L // 512):
                    r0 = ti * 512
                    for j in range(4):
                        rr = slice(r0 + j * 128, r0 + (j + 1) * 128)
                        lt, rpt, lb, sm_, ct = lat[li % 2], rp[li % 2], lnb[li % 2], ssm[li % 2], cT[li % 2]
                        qrt, rt = qr[li % 2], rtmp[li % 2]
                        li += 1
                        dma(lt[:], LAT[s][rr, :], writes=[lt], sb=lt)
                        dma(rpt[:], ROPE[rr, :], writes=[rpt], sb=rpt)
                        op("act", lambda E, lt=lt, sm_=sm_: E.activation(out=junkm[:, 0:384], in_=lt[:, 0:384], func=AF.Square, accum_out=sm_[:, 0:1]),
                           reads=[lt], writes=[sm_])
                        op("act", lambda E, lt=lt, sm_=sm_: E.activation(out=junkm[:, 0:256], in_=lt[:, 384:640], func=AF.Square, accum_out=sm_[:, 1:2]),
                           reads=[lt], writes=[sm_])
                        op("dve", lambda E, sm_=sm_: E.tensor_scalar(out=sm_[:, 0:1], in0=sm_[:, 0:1], scalar1=1.0 / 384, scalar2=EPS, op0=ALU.mult, op1=ALU.add),
                           reads=[sm_], writes=[sm_])
                        op("dve", lambda E, sm_=sm_: E.tensor_scalar(out=sm_[:, 1:2], in0=sm_[:, 1:2], scalar1=1.0 / 256, scalar2=EPS, op0=ALU.mult, op1=ALU.add),
                           reads=[sm_], writes=[sm_])
                        op("act", lambda E, sm_=sm_: E.activation(out=sm_[:, 0:2], in_=sm_[:, 0:2], func=AF.Ln), reads=[sm_], writes=[sm_])
                        op("act", lambda E, sm_=sm_: E.activation(out=sm_[:, 0:2], in_=sm_[:, 0:2], func=AF.Exp, scale=-0.5), reads=[sm_], writes=[sm_])
                        op("dve", lambda E, lt=lt, lb=lb, sm_=sm_: E.tensor_scalar(out=lb[:, 0:384], in0=lt[:, 0:384], scalar1=sm_[:, 0:1], scalar2=None, op0=ALU.mult),
                           reads=[lt, sm_], writes=[lb])
                        op("act", lambda E, lt=lt, lb=lb, sm_=sm_: E.activation(out=lb[:, 384:640], in_=lt[:, 384:640], func=AF.Copy, scale=sm_[:, 1:2]),
                           reads=[lt, sm_, lb], writes=[lb])
                        op("pool", lambda E, lt=lt, rt=rt, rpt=rpt: E.tensor_tensor(out=rt[:, 0:32], in0=lt[:, 640:672], in1=rpt[:, 0:32], op=ALU.mult),
                           reads=[lt, rpt], writes=[rt])
                        op("pool", lambda E, lt=lt, rt=rt, rpt=rpt: E.tensor_tensor(out=rt[:, 32:48], in0=lt[:, 656:672], in1=rpt[:, 32:48], op=ALU.mult),
                           reads=[lt, rpt, rt], writes=[rt])
                        op("pool", lambda E, lt=lt, rt=rt, rpt=rpt: E.tensor_tensor(out=rt[:, 48:64], in0=lt[:, 640:656], in1=rpt[:, 48:64], op=ALU.mult),
                           reads=[lt, rpt, rt], writes=[rt])
                        op("pool", lambda E, rt=rt, lb=lb: E.tensor_tensor(out=lb[:, 704:736], in0=rt[:, 0:32], in1=rt[:, 32:64], op=ALU.add),
                           reads=[rt, lb], writes=[lb])
                        for c in range(5):
                            op("pe", lambda E, c=c, lb=lb: E.transpose(out=pb(0, c * 128, (c + 1) * 128), in_=lb[:, c * 128:(c + 1) * 128], identity=identb[:]),
                               reads=[lb, identb], writes=[BK[0]])
                        op("pe", lambda E, lb=lb: E.transpose(out=psum_bf[0:96, 640:768], in_=lb[:, 640:736], identity=identb[:]),
                           reads=[lb, identb], writes=[BK[0]])
                        op("dve", lambda E, ct=ct: E.tensor_copy(out=ct[:], in_=pb(0, 0, 384)), writes=[ct, BK[0]])
                        op("act", lambda E, j=j: E.activation(out=ckvT.ap[:, :, j * 128:(j + 1) * 128], in_=pb(0, 384, 640).rearrange("p (c t) -> p c t", c=2), func=AF.Copy),
                           writes=[ckvc[j], BK[0]])
                        op("dve", lambda E, j=j: E.tensor_copy(out=kpst.ap[64:96, j * 128:(j + 1) * 128], in_=psum_bf[64:96, 640:768]), writes=[kpc[j], BK[0]])
                        ctv = ct.ap.rearrange("p (c t) -> p c t", c=3)
                        for c in range(3):
                            op("pe", lambda E, c=c, ctv=ctv: E.matmul(pf(1, 0, 480), lhsT=ctv[:, c, :], rhs=wuq.ap[:, c, 0:480], start=(c == 0), stop=(c == 2)),
                               reads=[ct, wuq], writes=[BK[1]])
                        for c in range(3):
                            op("pe", lambda E, c=c, ctv=ctv: E.matmul(pf(2, 0, 288), lhsT=ctv[:, c, :], rhs=wuq.ap[:, c, 480:768], start=(c == 0), stop=(c == 2)),
                               reads=[ct, wuq], writes=[BK[2]])
                        for (bk, h0, nh) in ((1, 0, 5), (2, 5, 3)):
                            pv = pf(bk, 0, nh * 96).rearrange("p (h e) -> p h e", h=nh)
                            qv = qrt.ap[:, h0 * 96:(h0 + nh) * 96].rearrange("p (h e) -> p h e", h=nh)
                            tv = rt.ap[:, 64:64 + nh * 64].rearrange("p (h e) -> p h e", h=nh)
                            cs = rpt.ap[:, 0:32].unsqueeze(1).to_broadcast([128, nh, 32])
                            sn1 = rpt.ap[:, 32:48].unsqueeze(1).to_broadcast([128, nh, 16])
                            sn2 = rpt.ap[:, 48:64].unsqueeze(1).to_broadcast([128, nh, 16])
                            op("act", lambda E, pv=pv, qv=qv: E.activation(out=qv[:, :, 0:64], in_=pv[:, :, 0:64], func=AF.Copy), reads=[], writes=[qrt, BK[bk]])
                            op("dve", lambda E, pv=pv, tv=tv, cs=cs: E.tensor_tensor(out=tv[:, :, 0:32], in0=pv[:, :, 64:96], in1=cs, op=ALU.mult),
                               reads=[rpt], writes=[rt, BK[bk]])
                            op("dve", lambda E, pv=pv, tv=tv, sn1=sn1: E.tensor_tensor(out=tv[:, :, 32:48], in0=pv[:, :, 80:96], in1=sn1, op=ALU.mult),
                               reads=[rpt], writes=[rt, BK[bk]])
                            op("dve", lambda E, pv=pv, tv=tv, sn2=sn2: E.tensor_tensor(out=tv[:, :, 48:64], in0=pv[:, :, 64:80], in1=sn2, op=ALU.mult),
                               reads=[rpt], writes=[rt, BK[bk]])
                            op("pool", lambda E, tv=tv, qv=qv: E.tensor_tensor(out=qv[:, :, 64:96], in0=tv[:, :, 0:32], in1=tv[:, :, 32:64], op=ALU.add),
                               reads=[rt], writes=[qrt])
                        for hh in range(8):
                            op("pe", lambda E, hh=hh, qrt=qrt: E.transpose(out=psum_bf[0:96, 3 * 1024 + hh * 128:3 * 1024 + (hh + 1) * 128], in_=qrt[:, hh * 96:(hh + 1) * 96],
                                                                           identity=identb[:]), reads=[qrt, identb], writes=[BK[3]])
                        op("act", lambda E, j=j: E.activation(out=qtst.ap[0:96, :, j * 128:(j + 1) * 128],
                                                              in_=psum_bf[0:96, 3 * 1024:4 * 1024].rearrange("p (h t) -> p h t", h=8), func=AF.Copy),
                           writes=[qtc[j], BK[3]])
                        for c in range(2):
                            op("pe", lambda E, c=c, j=j: E.matmul(pf(4), lhsT=ckvT.ap[:, c, j * 128:(j + 1) * 128], rhs=wv.ap[:, c, :], start=(c == 0), stop=(c == 1)),
                               reads=[ckvc[j], wv], writes=[BK[4]])
                        op("dve", lambda E, j=j: E.tensor_copy(out=vst.ap[:, j, :], in_=pf(4)), writes=[vsc[j], BK[4]])
                    for pr in range(4):
                        bk = 5 + pr % 2
                        for c in range(2):
                            op("pe", lambda E, c=c, pr=pr, bk=bk: E.matmul(pf(bk), lhsT=wk.ap[:, c, pr * 128:(pr + 1) * 128], rhs=ckvT.ap[:, c, :], start=(c == 0), stop=(c == 1)),
                               reads=ckvc + [wk], writes=[BK[bk]])
                        if pr % 2 == 0:
                            op("act", lambda E, pr=pr, bk=bk: E.activation(out=ktst.ap[:, pr, :], in_=pf(bk), func=AF.Copy), writes=[ktst, BK[bk]])
                        else:
                            op("dve", lambda E, pr=pr, bk=bk: E.tensor_copy(out=ktst.ap[:, pr, :], in_=pf(bk)), reads=[ktst], writes=[ktst, BK[bk]])
                    cc = slice(r0, r0 + 512)
                    for hh in range(8):
                        pr, hi = hh // 2, hh % 2
                        dma(KT[s][hh, 0:64, cc], ktst.ap[hi * 64:(hi + 1) * 64, pr, :], reads=[ktst], sb=ktst, queue="pool")
                        dma(KT[s][hh, 64:96, cc], kpst.ap[64:96, :], reads=kpc, sb=kpst, queue="pool")
                    dma(QT[s][:, :, cc].rearrange("h r t -> r h t"), qtst.ap[0:96, :, :], reads=qtc, sb=qtst, queue="pool")
                    for j in range(4):
                        dma(VV[s][:, :, ti * 4 + j, :].rearrange("h p e -> p h e"),
                            vst.ap[:, j, :].rearrange("p (h e) -> p h e", h=8), reads=vsc, sb=vst, queue="pool")
            phase_reset()

        if "B" in phases:
            SC = 96 ** -0.5
            for s, L in SEQ:
                nkb = L // 128
                off_seq = off[0]
                ktb = []
                vtb = []
                for i in range(2):
                    b = alloc("ktb%d" % i, L, BF16)
                    ktb.append(b)
                    v = alloc("vtb%d" % i, nkb * 65, BF16)
                    v.ap = v.ap.rearrange("p (k e) -> p k e", e=65)
                    op("pool", lambda E, v=v: E.memset(v.ap[:, :, 64:65], 1.0), writes=[v])
                    vtb.append(v)
                qtb = [alloc("qtb%d" % i, 512, BF16) for i in range(3)]
                ptb = [alloc("ptb%d" % i, 512, BF16) for i in range(4)]
                osb = [alloc("osb%d" % i, 512) for i in range(2)]
                yst = [alloc("yst%d" % i, 4 * 64) for i in range(2)]
                rdn = [alloc("rdn%d" % i, 4) for i in range(2)]
                qi = 0
                pi = 0
                for hh in range(8):
                    kt, vt = ktb[hh % 2], vtb[hh % 2]
                    dma(kt[0:96, :], KT[s][hh, :, :], writes=[kt], sb=kt)
                    dma(vt.ap[:, :, 0:64], VV[s][hh, :, :, :], writes=[vt], sb=vt)
                    for qt in range(L // 512):
                        qb = qtb[qi % 3]
                        ob, ys, rd = osb[qi % 2], yst[qi % 2], rdn[qi % 2]
                        obk = 4 + qi % 2
                        qi += 1
                        dma(qb[0:96, :], QT[s][hh, :, qt * 512:(qt + 1) * 512], writes=[qb], sb=qb)

                        def smm(kb, qb=qb, kt=kt):
                            bk = kb % 4
                            op("pe", lambda E: E.matmul(pf(bk), lhsT=kt[0:96, kb * 128:(kb + 1) * 128], rhs=qb[0:96, :], start=True, stop=True),
                               reads=[kt, qb], writes=[BK[bk]])
                        smm(0)
                        if nkb > 1:
                            smm(1)
                        for kb in range(nkb):
                            if kb + 2 < nkb:
                                smm(kb + 2)
                            pt = ptb[pi % 4]
                            pi += 1
                            bk = kb % 4
                            op("act", lambda E, pt=pt, bk=bk: E.activation(out=pt[:], in_=pf(bk), func=AF.Exp, scale=SC), writes=[pt, BK[bk]])
                            op("pe", lambda E, pt=pt, kb=kb, vt=vt, obk=obk: E.matmul(psum[0:65, obk * 512:(obk + 1) * 512], lhsT=vt.ap[:, kb, 0:65], rhs=pt[:],
                                                                                         start=(kb == 0), stop=(kb == nkb - 1)), reads=[vt, pt], writes=[BK[obk]])
                        op("dve", lambda E, ob=ob, obk=obk: E.tensor_copy(out=ob[0:65, :], in_=psum[0:65, obk * 512:(obk + 1) * 512]), writes=[ob, BK[obk]])
                        for j in range(4):
                            op("pe", lambda E, j=j, ob=ob: E.matmul(pf(6, j * 65, (j + 1) * 65), lhsT=ob[0:65, j * 128:(j + 1) * 128], rhs=identf[0:65, 0:65], start=True, stop=True),
                               reads=[ob, identf], writes=[BK[6]])
                        p6 = pf(6, 0, 260).rearrange("p (j e) -> p j e", j=4)
                        op("dve", lambda E, rd=rd, p6=p6: E.reciprocal(out=rd.ap[:, 0:4].unsqueeze(2), in_=p6[:, :, 64:65]), writes=[rd, BK[6]])
                        op("dve", lambda E, rd=rd, ys=ys, p6=p6: E.tensor_tensor(out=ys.ap.rearrange("p (j e) -> p j e", j=4), in0=p6[:, :, 0:64],
                                                                                in1=rd.ap[:, 0:4].unsqueeze(2).to_broadcast([128, 4, 64]), op=ALU.mult),
                           reads=[rd], writes=[ys, BK[6]])
                        dma(YA[s][qt * 512:(qt + 1) * 512, hh * 64:(hh + 1) * 64].rearrange("(j p) e -> p j e", p=128),
                            ys.ap.rearrange("p (j e) -> p j e", j=4), reads=[ys], sb=ys, queue="pool")
                P.barrier()
                off[0] = off_seq
            phase_reset()

        def run_window(items, W):
            nxt = 0
            active = []
            while nxt < len(items) or active:
                while len(active) < W and nxt < len(items):
                    if items[nxt][1] and active:
                        break
                    active.append(items[nxt][0])
                    nxt += 1
                for g_ in list(active):
                    try:
                        next(g_)
                    except StopIteration:
                        active.remove(g_)

        if "G" in phases:
            cw = alloc("cw", 12 * 5)
            for c in range(12):
                dma(cw[:, c * 5:(c + 1) * 5], W["gdn_conv_w"][:, c * 128:(c + 1) * 128].rearrange("k p -> p k"), writes=[cw] if c == 0 else [],
                    reads=[] if c == 0 else [cw], sb=cw, allow_slow_non_contiguous=True)
            cw.lw = (id(cw), cw.semcnt)
            dg = alloc("dg", 60 * 128, BF16)
            dg.ap = dg.ap.rearrange("p (k f) -> p k f", k=60)
            for k in range(60):
                op("dve", lambda E, k=k: E.tensor_scalar(out=dg.ap[:, k, :], in0=identb[:], scalar1=cw[:, k:k + 1], scalar2=None, op0=ALU.mult),
                   reads=[identb, cw], writes=[dg] if k == 0 else [])
            dg.lw = ("dve", P.cnt["dve"])
            GW = 3
            xin = [alloc("gxin%d" % i, 516) for i in range(GW)]
            xbf = [alloc("gxbf%d" % i, 516, BF16) for i in range(GW)]
            ex = [alloc("gex%d" % i, 512) for i in range(GW)]
            sb_ = [alloc("gsb%d" % i, 512, BF16) for i in range(GW)]
            tokst2 = []
            tokc2 = []
            for i in range(2):
                t_ = alloc("tokst%d" % i, 4 * 1536, BF16)
                t_.ap = t_.ap.rearrange("p (j c) -> p j c", j=4)
                tokst2.append(t_)
                tokc2.append([Buf("tokc%d_%d" % (i, c), t_.ap[:, :, c * 128:(c + 1) * 128]) for c in range(12)])
            sq = alloc("gsq", 1024)
            ssg2 = [alloc("ssg%d" % i, 32) for i in range(2)]

            def gchunk(s, L, r0, c, k, tokst, tokc):
                xi, xb, e_, sbb = xin[k % GW], xbf[k % GW], ex[k % GW], sb_[k % GW]
                cb_, tb_ = 2 * (k % GW), 2 * (k % GW) + 1
                lo, hi = max(r0 - 2, 0), min(r0 + 514, L)
                if r0 == 0:
                    op("pool", lambda E: E.memset(xi[:, 0:2], 0.0), writes=[xi])
                if r0 + 512 == L:
                    op("pool", lambda E: E.memset(xi[:, 514:516], 0.0), writes=[xi])
                dma(xi[:, lo - (r0 - 2):hi - (r0 - 2)], QKVT[s][c * 128:(c + 1) * 128, lo:hi], writes=[xi], sb=xi)
                yield
                op("pool", lambda E: E.tensor_copy(out=xb[:], in_=xi[:]), reads=[xi], writes=[xb])
                yield
                for t5 in range(5):
                    op("pe", lambda E, t5=t5: E.matmul(pf(cb_), lhsT=dg.ap[:, c * 5 + t5, :], rhs=xb[:, t5:t5 + 512], start=(t5 == 0), stop=(t5 == 4)),
                       reads=[dg, xb], writes=[BK[cb_]])
                yield
                op("act", lambda E: E.activation(out=e_[:], in_=pf(cb_), func=AF.Exp, scale=-1.0), writes=[e_, BK[cb_]])
                op("act", lambda E: E.activation(out=e_[:], in_=e_[:], func=AF.Ln, bias=1.0), reads=[e_], writes=[e_])
                op("act", lambda E: E.activation(out=e_[:], in_=e_[:], func=AF.Exp, scale=-1.0), reads=[e_], writes=[e_])
                yield
                op("dve", lambda E: E.tensor_tensor(out=sbb[:], in0=e_[:], in1=pf(cb_), op=ALU.mult), reads=[e_], writes=[sbb, BK[cb_]])
                yield
                for j in range(4):
                    op("pe", lambda E, j=j: E.transpose(out=pb(tb_, j * 128, (j + 1) * 128), in_=sbb[:, j * 128:(j + 1) * 128], identity=identb[:]),
                       reads=[sbb, identb], writes=[BK[tb_]])
                yield
                if c % 2 == 0:
                    op("act", lambda E: E.activation(out=tokst.ap[:, :, c * 128:(c + 1) * 128], in_=pb(tb_, 0, 512).rearrange("p (j d) -> p j d", j=4), func=AF.Copy),
                       writes=[tokc[c], BK[tb_]])
                else:
                    op("dve", lambda E: E.tensor_copy(out=tokst.ap[:, :, c * 128:(c + 1) * 128], in_=pb(tb_, 0, 512).rearrange("p (j d) -> p j d", j=4)),
                       writes=[tokc[c], BK[tb_]])

            def gtail(s, r0, tokst, tokc, ssg):
                for j in range(4):
                    tv = tokst.ap[:, j, 0:1024]
                    op("dve", lambda E, tv=tv: E.tensor_tensor(out=sq[:], in0=tv, in1=tv, op=ALU.mult), reads=tokc[0:8], writes=[sq])
                    op("dve", lambda E, j=j: E.tensor_reduce(out=ssg[:, j * 8:(j + 1) * 8], in_=sq.ap.rearrange("p (h d) -> p h d", h=8), axis=AX.X, op=ALU.add),
                       reads=[sq], writes=[ssg])
                    yield
                op("dve", lambda E: E.tensor_scalar(out=ssg[:, 0:32], in0=ssg[:, 0:32], scalar1=EPS, scalar2=None, op0=ALU.add), reads=[ssg], writes=[ssg])
                op("act", lambda E: E.activation(out=ssg[:, 0:32], in_=ssg[:, 0:32], func=AF.Ln), reads=[ssg], writes=[ssg])
                op("act", lambda E: E.activation(out=ssg[:, 0:32], in_=ssg[:, 0:32], func=AF.Exp, scale=-0.5), reads=[ssg], writes=[ssg])
                sv = ssg.ap[:, 0:32].rearrange("p (j h) -> p j h", j=4)
                op("dve", lambda E: E.tensor_scalar(out=sv[:, :, 0:4], in0=sv[:, :, 0:4], scalar1=128 ** -0.5, scalar2=None, op0=ALU.mult), reads=[ssg], writes=[ssg])
                yield
                for j in range(4):
                    tv = tokst.ap[:, j, 0:1024].rearrange("p (h d) -> p h d", h=8)
                    e1 = "dve" if j % 2 == 0 else "pool"
                    op(e1, lambda E, tv=tv, j=j: E.tensor_tensor(out=tv, in0=tv, in1=ssg.ap[:, j * 8:(j + 1) * 8].unsqueeze(2).to_broadcast([128, 8, 128]), op=ALU.mult),
                       reads=[ssg] + tokc[0:8], writes=tokc[0:8])
                    yield
                dma(TOK[s][r0:r0 + 512, :].rearrange("(j p) c -> p j c", p=128), tokst.ap, reads=tokc, sb=tokst, queue="pool")

            items = []
            k = 0
            tix = 0
            for s, L in SEQ:
                for ti in range(L // 512):
                    r0 = ti * 512
                    tkst, tkc, ssg = tokst2[tix % 2], tokc2[tix % 2], ssg2[tix % 2]
                    tix += 1
                    for c in range(12):
                        items.append((gchunk(s, L, r0, c, k, tkst, tkc), False))
                        k += 1
                    items.append((gtail(s, r0, tkst, tkc, ssg), True))
            run_window(items, GW)
            phase_reset()

        if "H" in phases:
            gm = alloc("gm", 5 * 512)
            gm.ap = gm.ap.rearrange("p (m h j) -> p m h j", m=5, h=8)
            dma(gm.ap[0:64], GMASK[:, :, :, :], writes=[gm], sb=gm)
            NA, NAT, NQK = gm.ap[0:64, 0], gm.ap[0:64, 1], gm.ap[0:64, 2]
            trif, trib = gm.ap[0:64, 3, 0, :], gm.ap[0:64, 3, 1, :]
            idb8 = identb.ap[0:64, 0:64].unsqueeze(1).to_broadcast([64, 8, 64])
            idf8 = identf.ap[0:64, 0:64].unsqueeze(1).to_broadcast([64, 8, 64])

            def A3(name, n, dt=F32, parts=128):
                b = alloc(name, 8 * n, dt)
                b.ap = b.ap.rearrange("p (h x) -> p h x", h=8)
                return b
            NSET = 2
            tok = [alloc("tk%d" % i, 2 * 1536, BF16) for i in range(NSET)]
            gsel = [alloc("gsel%d" % i, 24) for i in range(NSET)]
            sm8 = [alloc("sm8_%d" % i, 64) for i in range(NSET)]
            ost = [alloc("ost%d" % i, 1024) for i in range(NSET)]
            SETS = []
            for i in range(NSET):
                d_ = {}
                d_["Dg"] = A3("Dg%d" % i, 64)
                d_["Dc"] = A3("Dc%d" % i, 64)
                d_["De"] = A3("De%d" % i, 64, BF16)
                kq_ = alloc("kqT%d" % i, 16 * 64, BF16)
                kq_.ap = kq_.ap.rearrange("p (h x) -> p h x", h=16)
                d_["kqT"] = kq_
                for nm in ("dA", "dAT", "dQK"):
                    d_[nm] = A3(nm + str(i), 64)
                d_["Xb"] = [A3("Xb%d_%d" % (i, k), 64, BF16) for k in range(2)]
                d_["Yb"] = [A3("Yb%d_%d" % (i, k), 64, BF16) for k in range(2)]
                d_["Zb"] = [A3("Zb%d_%d" % (i, k), 64, BF16) for k in range(2)]
                for nm, n_, dt_ in (("qkT", 64, BF16), ("kbg", 128, BF16), ("vb", 128, BF16), ("kg", 128, BF16), ("wT", 64, BF16),
                                   ("qgT", 64, BF16), ("uu", 128, F32), ("vnew", 128, BF16)):
                    d_[nm] = A3(nm + str(i), n_, dt_)
                SETS.append(d_)
            S = A3("S", 128)
            Sb = A3("Sb", 128, BF16)

            def step(s, N, n):
                st_ = SETS[n % NSET]
                c0 = 4 * (n % 2)
                c1, c2, c3 = c0 + 1, c0 + 2, c0 + 3
                Dg, Dc, De, kqT, dA, dAT, dQK = st_["Dg"], st_["Dc"], st_["De"], st_["kqT"], st_["dA"], st_["dAT"], st_["dQK"]
                Xb, Yb, Zb = st_["Xb"], st_["Yb"], st_["Zb"]
                qkT, kbg, vb, kg, wT, qgT, uu, vnew = (st_[k_] for k_ in ("qkT", "kbg", "vb", "kg", "wT", "qgT", "uu", "vnew"))
                cf, cb = n, N - 1 - n
                tk, gs, m8, os_ = tok[n % NSET], gsel[n % NSET], sm8[n % NSET], ost[n % NSET]
                tkv = tk.ap.rearrange("p (d c) -> p d c", d=2)
                for d, ch in ((0, cf), (1, cb)):
                    dma(tkv[0:64, d, :], TOK[s][ch * 64:(ch + 1) * 64, :], writes=[tk] if d == 0 else [], reads=[] if d == 0 else [tk], sb=tk)
                    dma(gs.ap[0:64, d * 12:(d + 1) * 12], GBL[s][d, ch * 64:(ch + 1) * 64, :], writes=[gs] if d == 0 else [], reads=[] if d == 0 else [gs], sb=gs)
                tk.lw = (id(tk), tk.semcnt)
                gs.lw = (id(gs), gs.semcnt)
                gv = gs.ap[0:64, :].rearrange("p (d k) -> p d k", d=2)
                g8, lnb8, beta8 = gv[:, :, 0:4], gv[:, :, 4:8], gv[:, :, 8:12]
                m = m8.ap[0:64, :]

                def v8(a, b):
                    return m8.ap[0:64, a:b].rearrange("p (d k) -> p d k", d=2)
                yield None
                op("pe", lambda E, gs=gs: E.matmul(psum[0:64, c0 * 512:c0 * 512 + 4], lhsT=trif, rhs=gs.ap[0:64, 0:4], start=True, stop=True), reads=[gm, gs], writes=[BK[c0]])
                op("pe", lambda E, gs=gs: E.matmul(psum[0:64, c0 * 512 + 4:c0 * 512 + 8], lhsT=trib, rhs=gs.ap[0:64, 12:16], start=True, stop=True), reads=[gm, gs], writes=[BK[c0]])
                op("pe", lambda E, g8=g8: E.matmul(psum[:, c0 * 512 + 8:c0 * 512 + 16].rearrange("p (d k) -> p d k", d=2), lhsT=onesf[0:64, :], rhs=g8, start=True, stop=True),
                   reads=[onesf, gs], writes=[BK[c0]])
                yield None
                op("dve", lambda E, m8=m8: E.tensor_copy(out=m8[0:64, 0:8], in_=psum[0:64, c0 * 512:c0 * 512 + 8]), writes=[m8, BK[c0]])
                op("dve", lambda E, m8=m8, lnb8=lnb8, v8=v8: E.tensor_tensor(out=v8(8, 16), in0=v8(0, 8), in1=lnb8, op=ALU.add), reads=[m8, gs], writes=[m8])
                op("act", lambda E, m8=m8: E.activation(out=m8[0:64, 16:24], in_=m8[0:64, 0:8], func=AF.Exp), reads=[m8], writes=[m8])
                op("dve", lambda E, m8=m8, beta8=beta8, v8=v8: E.tensor_tensor(out=v8(24, 32), in0=v8(16, 24), in1=beta8, op=ALU.mult), reads=[m8, gs], writes=[m8])
                op("dve", lambda E, m8=m8: E.tensor_tensor(out=m8[0:64, 40:48], in0=psum[0:64, c0 * 512 + 8:c0 * 512 + 16], in1=m8[0:64, 0:8], op=ALU.subtract), reads=[m8], writes=[m8, BK[c0]])
                op("act", lambda E, m8=m8: E.activation(out=m8[0:64, 32:40], in_=m8[0:64, 40:48], func=AF.Exp), reads=[m8], writes=[m8])
                op("act", lambda E, m8=m8: E.activation(out=m8[:, 48:56], in_=psum[:, c0 * 512 + 8:c0 * 512 + 16], func=AF.Exp), reads=[m8], writes=[m8, BK[c0]])
                yield None
                op("pool", lambda E, m8=m8: E.tensor_tensor(out=Dg.ap[0:64], in0=idf8, in1=m8.ap[0:64, 0:8].unsqueeze(2).to_broadcast([64, 8, 64]), op=ALU.mult),
                   reads=[identf, m8], writes=[Dg])
                op("pool", lambda E, m8=m8: E.tensor_tensor(out=Dc.ap[0:64], in0=idf8, in1=m8.ap[0:64, 8:16].unsqueeze(2).to_broadcast([64, 8, 64]), op=ALU.mult),
                   reads=[identf, m8], writes=[Dc])
                op("pool", lambda E, m8=m8: E.tensor_tensor(out=De.ap[0:64], in0=idf8, in1=m8.ap[0:64, 16:24].unsqueeze(2).to_broadcast([64, 8, 64]), op=ALU.mult),
                   reads=[identf, m8], writes=[De])
                yield None
                op("pe", lambda E: E.matmul(psum[0:64, c1 * 512:(c1 + 1) * 512], lhsT=onesf[0:64, 0:64], rhs=Dg.ap[0:64].rearrange("p h x -> p (h x)"), start=True, stop=True),
                   reads=[onesf, Dg], writes=[BK[c1]])
                op("pe", lambda E: E.matmul(psum[0:64, c2 * 512:(c2 + 1) * 512], lhsT=onesf[0:64, 0:64], rhs=Dc.ap[0:64].rearrange("p h x -> p (h x)"), start=True, stop=True),
                   reads=[onesf, Dc], writes=[BK[c2]])
                yield None
                for d in range(2):
                    for hh in range(4):
                        hd = d * 4 + hh
                        op("pe", lambda E, d=d, hh=hh, hd=hd, tkv=tkv: E.transpose(out=psum_bf[:, c3 * 1024 + hd * 64:c3 * 1024 + (hd + 1) * 64],
                                                                                 in_=tkv[0:64, d, 512 + hh * 128:512 + (hh + 1) * 128], identity=identb[0:64, 0:64]),
                           reads=[tk, identb], writes=[BK[c3]])
                        op("pe", lambda E, d=d, hh=hh, hd=hd, tkv=tkv: E.transpose(out=psum_bf[:, c3 * 1024 + 512 + hd * 64:c3 * 1024 + 512 + (hd + 1) * 64],
                                                                                 in_=tkv[0:64, d, hh * 128:(hh + 1) * 128], identity=identb[0:64, 0:64]),
                           reads=[tk, identb], writes=[BK[c3]])
                yield None
                op("act", lambda E: E.activation(out=kqT.ap, in_=pb(c3).rearrange("p (h x) -> p h x", h=16), func=AF.Copy), writes=[kqT, BK[c3]])
                yield None
                for hd in range(8):
                    op("pe", lambda E, hd=hd: E.matmul(psum[0:64, c0 * 512 + hd * 64:c0 * 512 + (hd + 1) * 64], lhsT=kqT.ap[:, hd, :], rhs=kqT.ap[:, hd, :], start=True, stop=True),
                       reads=[kqT], writes=[BK[c0]])
                for hd in range(8):
                    op("pe", lambda E, hd=hd: E.matmul(psum[0:64, c3 * 512 + hd * 64:c3 * 512 + (hd + 1) * 64], lhsT=kqT.ap[:, hd, :], rhs=kqT.ap[:, 8 + hd, :], start=True, stop=True),
                       reads=[kqT], writes=[BK[c3]])
                P1 = psum[0:64, c1 * 512:(c1 + 1) * 512].rearrange("p (h x) -> p h x", h=8)
                P2 = psum[0:64, c2 * 512:(c2 + 1) * 512].rearrange("p (h x) -> p h x", h=8)
                PG = psum[0:64, c0 * 512:(c0 + 1) * 512].rearrange("p (h x) -> p h x", h=8)
                PQ = psum[0:64, c3 * 512:(c3 + 1) * 512].rearrange("p (h x) -> p h x", h=8)

                def bc8(a):
                    return m8.ap[0:64, a:a + 8].unsqueeze(2).to_broadcast([64, 8, 64])
                yield None
                op("dve", lambda E: E.scalar_tensor_tensor(out=dA.ap[0:64], in0=P1, scalar=-1.0, in1=NA, op0=ALU.mult, op1=ALU.add), reads=[gm], writes=[dA, BK[c1]])
                op("pool", lambda E, bc8=bc8: E.tensor_tensor(out=dA.ap[0:64], in0=dA.ap[0:64], in1=bc8(8), op=ALU.add), reads=[m8, dA], writes=[dA])
                op("act", lambda E: E.activation(out=dA.ap[0:64], in_=dA.ap[0:64], func=AF.Exp), reads=[dA], writes=[dA])
                yield None
                op("dve", lambda E: E.tensor_tensor(out=dAT.ap[0:64], in0=P2, in1=NAT, op=ALU.add), reads=[gm], writes=[dAT, BK[c2]])
                op("pool", lambda E, bc8=bc8: E.tensor_tensor(out=dAT.ap[0:64], in0=dAT.ap[0:64], in1=bc8(0), op=ALU.subtract), reads=[m8, dAT], writes=[dAT])
                op("act", lambda E: E.activation(out=dAT.ap[0:64], in_=dAT.ap[0:64], func=AF.Exp), reads=[dAT], writes=[dAT])
                yield None
                op("dve", lambda E: E.tensor_tensor(out=dQK.ap[0:64], in0=P1, in1=NQK, op=ALU.add), reads=[gm], writes=[dQK, BK[c1]])
                op("pool", lambda E, bc8=bc8: E.tensor_tensor(out=dQK.ap[0:64], in0=dQK.ap[0:64], in1=bc8(0), op=ALU.subtract), reads=[m8, dQK], writes=[dQK])
                op("act", lambda E: E.activation(out=dQK.ap[0:64], in_=dQK.ap[0:64], func=AF.Exp), reads=[dQK], writes=[dQK])
                yield None
                X0, Y0, Z0 = Xb[0], Yb[0], Zb[0]
                op("dve", lambda E, X0=X0: E.scalar_tensor_tensor(out=X0.ap[0:64], in0=dA.ap[0:64], scalar=-1.0, in1=PG, op0=ALU.mult, op1=ALU.mult), reads=[dA], writes=[X0, BK[c0]])
                op("dve", lambda E, Y0=Y0: E.scalar_tensor_tensor(out=Y0.ap[0:64], in0=dAT.ap[0:64], scalar=-1.0, in1=PG, op0=ALU.mult, op1=ALU.mult), reads=[dAT], writes=[Y0, BK[c0]])
                op("dve", lambda E: E.tensor_tensor(out=qkT.ap[0:64], in0=dQK.ap[0:64], in1=PQ, op=ALU.mult), reads=[dQK], writes=[qkT, BK[c3]])
                op("pool", lambda E, Y0=Y0, Z0=Z0: E.tensor_tensor(out=Z0.ap[0:64], in0=Y0.ap[0:64], in1=idb8, op=ALU.add), reads=[Y0, identb], writes=[Z0])
                yield None
                for k in range(1, 6):
                    Xp, Yp, Zp = Xb[(k - 1) % 2], Yb[(k - 1) % 2], Zb[(k - 1) % 2]
                    Xn, Yn, Zn = Xb[k % 2], Yb[k % 2], Zb[k % 2]
                    for hd in range(8):
                        op("pe", lambda E, hd=hd, Xp=Xp, Yp=Yp: E.matmul(psum[0:64, c1 * 512 + hd * 64:c1 * 512 + (hd + 1) * 64], lhsT=Yp.ap[0:64, hd, :], rhs=Xp.ap[0:64, hd, :],
                                                                         start=True, stop=True), reads=[Xp, Yp], writes=[BK[c1]])
                    if k < 5:
                        for hd in range(8):
                            op("pe", lambda E, hd=hd, Xp=Xp, Yp=Yp: E.matmul(psum[0:64, c2 * 512 + hd * 64:c2 * 512 + (hd + 1) * 64], lhsT=Xp.ap[0:64, hd, :], rhs=Yp.ap[0:64, hd, :],
                                                                             start=True, stop=True), reads=[Xp, Yp], writes=[BK[c2]])
                    yield None
                    op("act", lambda E, Xn=Xn: E.activation(out=Xn.ap[0:64], in_=P1, func=AF.Copy), writes=[Xn, BK[c1]])
                    if k < 5:
                        op("dve", lambda E, Yn=Yn: E.tensor_copy(out=Yn.ap[0:64], in_=P2), writes=[Yn, BK[c2]])
                    yield None
                    for hd in range(8):
                        op("pe", lambda E, hd=hd, Xn=Xn, Zp=Zp: E.matmul(psum[0:64, c3 * 512 + hd * 64:c3 * 512 + (hd + 1) * 64], lhsT=Xn.ap[0:64, hd, :], rhs=Zp.ap[0:64, hd, :],
                                                                         start=True, stop=True), reads=[Xn, Zp], writes=[BK[c3]])
                    op("dve", lambda E, Zn=Zn, Zp=Zp: E.tensor_tensor(out=Zn.ap[0:64], in0=psum[0:64, c3 * 512:(c3 + 1) * 512].rearrange("p (h x) -> p h x", h=8), in1=Zp.ap[0:64], op=ALU.add),
                       reads=[Zp], writes=[Zn, BK[c3]])
                    yield None
                Z = Zb[5 % 2]
                yield None
                kv4 = tkv[0:64, :, 512:1024].rearrange("p d (h x) -> p d h x", h=4)
                vv4 = tkv[0:64, :, 1024:1536].rearrange("p d (h x) -> p d h x", h=4)

                def b4(a):
                    return m8.ap[0:64, a:a + 8].rearrange("p (d h) -> p d h", d=2).unsqueeze(3).to_broadcast([64, 2, 4, 128])
                op("pool", lambda E, kv4=kv4, b4=b4: E.tensor_tensor(out=kbg.ap[0:64].rearrange("p (d h) x -> p d h x", d=2), in0=kv4, in1=b4(24), op=ALU.mult),
                   reads=[tk, m8], writes=[kbg])
                op("dve", lambda E, vv4=vv4, gs=gs: E.tensor_tensor(out=vb.ap[0:64].rearrange("p (d h) x -> p d h x", d=2), in0=vv4,
                                                                    in1=gs.ap[0:64, :].rearrange("p (d k) -> p d k", d=2)[:, :, 8:12].unsqueeze(3).to_broadcast([64, 2, 4, 128]), op=ALU.mult),
                   reads=[tk, gs], writes=[vb])
                op("pool", lambda E, kv4=kv4, b4=b4: E.tensor_tensor(out=kg.ap[0:64].rearrange("p (d h) x -> p d h x", d=2), in0=kv4, in1=b4(32), op=ALU.mult),
                   reads=[tk, m8], writes=[kg])
                yield None
                for hd in range(8):
                    op("pe", lambda E, hd=hd, Z=Z: E.matmul(psum[:, c0 * 512 + hd * 64:c0 * 512 + (hd + 1) * 64], lhsT=kbg.ap[0:64, hd, :], rhs=Z.ap[0:64, hd, :], start=True, stop=True),
                       reads=[kbg, Z], writes=[BK[c0]])
                op("act", lambda E: E.activation(out=wT.ap, in_=pf(c0).rearrange("p (h x) -> p h x", h=8), func=AF.Copy), writes=[wT, BK[c0]])
                yield None
                for hd in range(8):
                    bk = c2 + hd // 4
                    op("pe", lambda E, hd=hd, Z=Z: E.matmul(psum[0:64, c2 * 512 + hd * 128:c2 * 512 + (hd + 1) * 128], lhsT=Z.ap[0:64, hd, :], rhs=vb.ap[0:64, hd, :], start=True, stop=True),
                       reads=[vb, Z], writes=[BK[bk]])
                op("act", lambda E: E.activation(out=uu.ap[0:64], in_=psum[0:64, c2 * 512:(c2 + 2) * 512].rearrange("p (h x) -> p h x", h=8), func=AF.Copy), writes=[uu, BK[c2], BK[c3]])
                yield None
                for d in range(2):
                    for hh in range(4):
                        hd = d * 4 + hh
                        op("pe", lambda E, d=d, hh=hh, hd=hd, tkv=tkv: E.matmul(psum[:, c1 * 512 + hd * 64:c1 * 512 + (hd + 1) * 64], lhsT=tkv[0:64, d, hh * 128:(hh + 1) * 128],
                                                                                rhs=De.ap[0:64, hd, :], start=True, stop=True), reads=[tk, De], writes=[BK[c1]])
                op("dve", lambda E: E.tensor_copy(out=qgT.ap, in_=pf(c1).rearrange("p (h x) -> p h x", h=8)), writes=[qgT, BK[c1]])
                yield "SCAN"
                for hd in range(8):
                    bk = c0 + hd // 4
                    op("pe", lambda E, hd=hd: E.matmul(psum[0:64, c0 * 512 + hd * 128:c0 * 512 + (hd + 1) * 128], lhsT=wT.ap[:, hd, :], rhs=Sb.ap[:, hd, :], start=True, stop=True),
                       reads=[wT, Sb], writes=[BK[bk]])
                yield None
                op("dve", lambda E: E.tensor_tensor(out=vnew.ap[0:64], in0=uu.ap[0:64], in1=psum[0:64, c0 * 512:(c0 + 2) * 512].rearrange("p (h x) -> p h x", h=8), op=ALU.subtract),
                   reads=[uu], writes=[vnew, BK[c0], BK[c1]])
                yield None
                for hd in range(8):
                    bk = c2 + hd // 4
                    op("pe", lambda E, hd=hd: E.matmul(psum[0:64, c2 * 512 + hd * 128:c2 * 512 + (hd + 1) * 128], lhsT=qgT.ap[:, hd, :], rhs=Sb.ap[:, hd, :], start=True, stop=False),
                       reads=[qgT, Sb], writes=[BK[bk]])
                    op("pe", lambda E, hd=hd: E.matmul(psum[0:64, c2 * 512 + hd * 128:c2 * 512 + (hd + 1) * 128], lhsT=qkT.ap[0:64, hd, :], rhs=vnew.ap[0:64, hd, :], start=False, stop=True),
                       reads=[qkT, vnew], writes=[BK[bk]])
                yield None
                op("act", lambda E, os_=os_: E.activation(out=os_[0:64, :], in_=psum[0:64, c2 * 512:(c2 + 2) * 512], func=AF.Copy), writes=[os_, BK[c2], BK[c3]])
                dma(OF[s][0, cf * 64:(cf + 1) * 64, :], os_[0:64, 0:512], reads=[os_], sb=os_, queue="pool")
                dma(OF[s][1, cb * 64:(cb + 1) * 64, :], os_[0:64, 512:1024], reads=[os_], sb=os_, queue="pool")
                yield None
                for hd in range(8):
                    bk = c0 + hd // 4
                    op("pe", lambda E, hd=hd: E.matmul(psum[:, c0 * 512 + hd * 128:c0 * 512 + (hd + 1) * 128], lhsT=kg.ap[0:64, hd, :], rhs=vnew.ap[0:64, hd, :], start=True, stop=True),
                       reads=[kg, vnew], writes=[BK[bk]])
                yield None
                op("pool", lambda E, m8=m8: E.tensor_tensor(out=S.ap, in0=S.ap, in1=m8.ap[:, 48:56].unsqueeze(2).to_broadcast([128, 8, 128]), op=ALU.mult),
                   reads=[m8, S], writes=[S])
                op("dve", lambda E: E.tensor_tensor(out=S.ap, in0=S.ap, in1=psum[:, c0 * 512:(c0 + 2) * 512].rearrange("p (h x) -> p h x", h=8), op=ALU.add),
                   reads=[S], writes=[S, BK[c0], BK[c1]])
                op("act", lambda E: E.activation(out=Sb.ap, in_=S.ap, func=AF.Copy), reads=[S], writes=[Sb])
            WIN = 2
            for s, L in SEQ:
                N = L // 64
                op("pool", lambda E: E.memset(S.ap, 0.0), writes=[S])
                op("pool", lambda E: E.memset(Sb.ap, 0.0), writes=[Sb])
                nxt = 0
                active = []
                while nxt < N or active:
                    while len(active) < WIN and nxt < N:
                        active.append([nxt, step(s, N, nxt), False])
                        nxt += 1
                    oldest = min(a_[0] for a_ in active)
                    for a_ in list(active):
                        if a_[2] and a_[0] != oldest:
                            continue
                        try:
                            r_ = next(a_[1])
                            a_[2] = (r_ == "SCAN")
                        except StopIteration:
                            active.remove(a_)
            phase_reset()

        if "C" in phases:
            T = ffn_bufs()
            load_wd(T, "ffn2_w_down")
            wst = alloc("wstc", 1024)
            wo = alloc("wo", 8 * 1024, BF16)
            wo.ap = wo.ap.rearrange("p (c f) -> p c f", c=8)
            for c in range(8):
                dma(wst[:], W["w_out"][c * 128:(c + 1) * 128, :], writes=[wst], sb=wst)
                sc = small[:, 5 + c:6 + c] if c < 4 else small[:, 9:10]
                op("dve", lambda E, c=c, sc=sc: E.tensor_scalar(out=wo.ap[:, c, :], in0=wst[:], scalar1=sc, scalar2=None, op0=ALU.mult),
                   reads=[wst, small], writes=[wo])
            ya = alloc("ya", 4 * 512)
            ya.ap = ya.ap.rearrange("p (j e) -> p j e", j=4)
            ofb = [alloc("ofb%d" % i, 4 * 512) for i in range(2)]
            for b in ofb:
                b.ap = b.ap.rearrange("p (j e) -> p j e", j=4)
            zz = alloc("zz", 4 * 512)
            zz.ap = zz.ap.rearrange("p (j e) -> p j e", j=4)
            mixb = [alloc("mixb%d" % i, 1024, BF16) for i in range(2)]
            sq = T["tmp"][1]
            ssc = alloc("ssc", 32)
            tilesC = [(s, L, ti) for s, L in SEQ for ti in range(L // 512)]
            ystg = Buf("ystg", arena[:, T["act_off"]:T["act_off"] + 4 * D].rearrange("p (j d) -> p j d", j=4))

            def load_h(ix):
                s_, L_, ti_ = tilesC[ix]
                dma(T["h"].ap, H1[s_][ti_ * 512:(ti_ + 1) * 512, :].rearrange("(j p) d -> p j d", p=128), writes=T["hb"], sb=T["h"])

            def load_front(ix):
                s_, L_, ti_ = tilesC[ix]
                rw = slice(ti_ * 512, (ti_ + 1) * 512)
                dma(ya.ap, YA[s_][rw, :].rearrange("(j p) e -> p j e", p=128), writes=[ya], sb=ya)
                dma(ofb[0].ap, OF[s_][0, rw, :].rearrange("(j p) e -> p j e", p=128), writes=[ofb[0]], sb=ofb[0])
                dma(ofb[1].ap, OF[s_][1, rw, :].rearrange("(j p) e -> p j e", p=128), writes=[ofb[1]], sb=ofb[1])
                dma(zz.ap, ZS[s_][rw, :].rearrange("(j p) e -> p j e", p=128), writes=[zz], sb=zz)
            load_h(0)
            load_front(0)
            for ix, (s, L, ti) in enumerate(tilesC):
                if True:
                    r0 = ti * 512
                    rows = slice(r0, r0 + 512)
                    h = T["h"]
                    o = ofb[0]
                    op("pool", lambda E: E.tensor_tensor(out=o.ap, in0=o.ap, in1=ofb[1].ap, op=ALU.add), reads=[ofb[1], o], writes=[o])
                    e2 = ofb[1]
                    op("act", lambda E: E.activation(out=e2.ap, in_=zz.ap, func=AF.Exp, scale=-1.0), reads=[zz], writes=[e2])
                    op("act", lambda E: E.activation(out=e2.ap, in_=e2.ap, func=AF.Ln, bias=1.0), reads=[e2], writes=[e2])
                    op("act", lambda E: E.activation(out=e2.ap, in_=e2.ap, func=AF.Exp, scale=-1.0), reads=[e2], writes=[e2])
                    op("pool", lambda E: E.tensor_tensor(out=zz.ap, in0=zz.ap, in1=e2.ap, op=ALU.mult), reads=[e2, zz], writes=[zz])
                    for j in range(4):
                        op("dve", lambda E, j=j: E.tensor_tensor(out=sq[:, 0:512], in0=o.ap[:, j, :], in1=o.ap[:, j, :], op=ALU.mult), reads=[o], writes=[sq])
                        op("dve", lambda E, j=j: E.tensor_reduce(out=ssc[:, j * 8:j * 8 + 4], in_=sq.ap[:, 0:512].rearrange("p (h d) -> p h d", h=4), axis=AX.X, op=ALU.add),
                           reads=[sq], writes=[ssc])
                        op("act", lambda E, j=j: E.activation(out=T["junk"][:, 0:512], in_=ya.ap[:, j, :], func=AF.Square, accum_out=ssc[:, j * 8 + 4:j * 8 + 5]),
                           reads=[ya], writes=[ssc])
                    sv = ssc.ap[:, 0:32].rearrange("p (j k) -> p j k", j=4)
                    op("dve", lambda E, sv=sv: E.tensor_scalar(out=sv[:, :, 0:4], in0=sv[:, :, 0:4], scalar1=1.0 / 128, scalar2=EPS, op0=ALU.mult, op1=ALU.add), reads=[ssc], writes=[ssc])
                    op("dve", lambda E, sv=sv: E.tensor_scalar(out=sv[:, :, 4:5], in0=sv[:, :, 4:5], scalar1=1.0 / 512, scalar2=EPS, op0=ALU.mult, op1=ALU.add), reads=[ssc], writes=[ssc])
                    op("act", lambda E, sv=sv: E.activation(out=sv[:, :, 0:5], in_=sv[:, :, 0:5], func=AF.Ln), reads=[ssc], writes=[ssc])
                    op("act", lambda E, sv=sv: E.activation(out=sv[:, :, 0:5], in_=sv[:, :, 0:5], func=AF.Exp, scale=-0.5), reads=[ssc], writes=[ssc])
                    xT = T["xT"]
                    for j in range(4):
                        mb = mixb[j % 2]
                        op("act", lambda E, j=j, mb=mb: E.activation(out=mb[:, 0:512], in_=ya.ap[:, j, :], func=AF.Copy, scale=ssc[:, j * 8 + 4:j * 8 + 5]),
                           reads=[ya, ssc], writes=[mb])
                        op("dve", lambda E, j=j: E.tensor_tensor(out=o.ap[:, j, :].rearrange("p (h d) -> p h d", h=4), in0=o.ap[:, j, :].rearrange("p (h d) -> p h d", h=4),
                                                                 in1=ssc.ap[:, j * 8:j * 8 + 4].unsqueeze(2).to_broadcast([128, 4, 128]), op=ALU.mult), reads=[ssc, o], writes=[o])
                        op("pool", lambda E, j=j, mb=mb: E.tensor_tensor(out=mb[:, 512:1024], in0=o.ap[:, j, :], in1=zz.ap[:, j, :], op=ALU.mult), reads=[o, zz, mb], writes=[mb])
                        bk = j % 2
                        for dc in range(8):
                            op("pe", lambda E, dc=dc, bk=bk, mb=mb: E.transpose(out=pb(bk, dc * 128, (dc + 1) * 128), in_=mb[:, dc * 128:(dc + 1) * 128], identity=identb[:]),
                               reads=[mb, identb], writes=[BK[bk]])
                        op("dve", lambda E, j=j, bk=bk: E.tensor_copy(out=xT.ap[:, :, j * 128:(j + 1) * 128], in_=pb(bk).rearrange("p (c t) -> p c t", c=8)),
                           writes=[xT, BK[bk]])
                    if ix + 1 < len(tilesC):
                        load_front(ix + 1)
                    ss2 = T["ss2"]
                    for j in range(4):
                        b0 = 2 if j % 2 == 0 else 4
                        for hf in range(2):
                            for c in range(8):
                                op("pe", lambda E, c=c, hf=hf, b0=b0, j=j: E.matmul(pf(b0 + hf), lhsT=xT.ap[:, c, j * 128:(j + 1) * 128], rhs=wo.ap[:, c, hf * 512:(hf + 1) * 512],
                                                                                     start=(c == 0), stop=(c == 7)), reads=[xT, wo], writes=[BK[b0 + hf]])
                        yps = psum[:, b0 * 512:(b0 + 2) * 512]
                        op("act", lambda E, j=j, yps=yps: E.activation(out=T["junk"][:], in_=yps, func=AF.Square, accum_out=ss2[:, j:j + 1]), writes=[ss2, BK[b0], BK[b0 + 1]])
                        op("dve", lambda E, j=j: E.tensor_scalar(out=ss2[:, 4 + j:5 + j], in0=ss2[:, j:j + 1], scalar1=1.0 / D, scalar2=EPS, op0=ALU.mult, op1=ALU.add), reads=[ss2], writes=[ss2])
                        op("act", lambda E, j=j: E.activation(out=ss2[:, 4 + j:5 + j], in_=ss2[:, 4 + j:5 + j], func=AF.Ln), reads=[ss2], writes=[ss2])
                        op("act", lambda E, j=j: E.activation(out=ss2[:, 4 + j:5 + j], in_=ss2[:, 4 + j:5 + j], func=AF.Exp, scale=-0.5), reads=[ss2], writes=[ss2])
                        tmp = T["tmp"][j % 2]
                        op("dve", lambda E, j=j, yps=yps, tmp=tmp: E.scalar_tensor_tensor(out=tmp[:], in0=yps, scalar=ss2[:, 4 + j:5 + j], in1=gB["mix_post_g"][:], op0=ALU.mult, op1=ALU.mult),
                           reads=[ss2, gB["mix_post_g"]], writes=[tmp, BK[b0], BK[b0 + 1]])
                        op("pool", lambda E, j=j, tmp=tmp: E.tensor_tensor(out=T["hs"][j], in0=T["hs"][j], in1=tmp[:], op=ALU.add), reads=[tmp], writes=[T["hb"][j]])
                    ffn(T["hb"], T["hs"], "ffn2_pre_g", WB["ffn2_w_gate"], WB["ffn2_w_up"], T["wd"], gB["ffn2_post_g"], T)
                    ss = T["ss"]
                    for j in range(4):
                        op("act", lambda E, j=j: E.activation(out=T["junk"][:], in_=T["hs"][j], func=AF.Square, accum_out=ss[:, j:j + 1]), reads=[T["hb"][j]], writes=[ss])
                    rstd_chain(ss, 4, D)
                    for j in range(4):
                        e1 = "dve"
                        op(e1, lambda E, j=j: E.scalar_tensor_tensor(out=ystg.ap[:, j, :], in0=T["hs"][j], scalar=ss[:, j:j + 1], in1=gB["final_norm_g"][:], op0=ALU.mult, op1=ALU.mult),
                           reads=[ss, gB["final_norm_g"], T["hb"][j]], writes=T["actc"][4 * j:4 * j + 4])
                    if ix + 1 < len(tilesC):
                        load_h(ix + 1)
                    dma(Y[s][rows, :].rearrange("(j p) d -> p j d", p=128), ystg.ap, reads=T["actc"][0:16], sb=ystg, queue="pool")
            phase_reset()
        P.barrier()
        P.emit()
        stats = dict(nins=P.nins, nwaits=P.nwaits, nsem=len(P.dmabufs) + 5)
    return nc, stats


def rope_table(L):
    inv = 10000.0 ** (-np.arange(0, 32, 2, dtype=np.float32) / 32)
    ang = np.arange(L, dtype=np.float32)[:, None] * inv[None, :].astype(np.float32)
    c, s = np.cos(ang).astype(np.float32), np.sin(ang).astype(np.float32)
    return np.ascontiguousarray(np.concatenate([c, c, -s, s], axis=1).astype(np.float32))


def gdn_masks():
    i = np.arange(64)
    m = np.zeros((64, 5, 8, 64), np.float32)
    for hd in range(8):
        fwd = hd < 4
        al = (i[:, None] > i[None, :]) if fwd else (i[:, None] < i[None, :])
        m[:, 0, hd, :] = np.where(al, 0.0, NEG)
        al = (i[None, :] > i[:, None]) if fwd else (i[None, :] < i[:, None])
        m[:, 1, hd, :] = np.where(al, 0.0, NEG)
        al = (i[None, :] >= i[:, None]) if fwd else (i[None, :] <= i[:, None])
        m[:, 2, hd, :] = np.where(al, 0.0, NEG)
    m[:, 3, 0, :] = (i[:, None] <= i[None, :]).astype(np.float32)
    m[:, 3, 1, :] = (i[:, None] >= i[None, :]).astype(np.float32)
    return m


_CACHE = {}


def kernel(**inputs):
    xp = np.asarray(inputs["x_prompt"], np.float32)
    xs = np.asarray(inputs["x_sample"], np.float32)
    B, Lp, _ = xp.shape
    Ls = xs.shape[1]
    assert B == 8 and xs.shape[0] == 8
    key = (Lp, Ls)
    if key not in _CACHE:
        _CACHE[key] = build(Lp, Ls)[0]
    nc = _CACHE[key]
    shared = {n: np.ascontiguousarray(np.asarray(inputs[n], np.float32).reshape(WSHAPES[n])) for n in WNAMES}
    shared["rope_tab"] = rope_table(max(Lp, Ls))
    shared["gdn_masks"] = gdn_masks()
    in_maps = []
    for c in range(8):
        m = dict(shared)
        m["x_p"] = np.ascontiguousarray(xp[c])
        m["x_s"] = np.ascontiguousarray(xs[c])
        in_maps.append(m)
    res = run_bass_kernel_spmd(nc, in_maps, core_ids=list(range(8)))
    yp = np.stack([np.asarray(r["y_p"], np.float32) for r in res.results], 0)
    ys = np.stack([np.asarray(r["y_s"], np.float32) for r in res.results], 0)
    return (yp, ys)
```

```python
from contextlib import ExitStack
import numpy as np
import concourse.bass as bass
import concourse.mybir as mybir
from concourse.bass_utils import run_bass_kernel_spmd

F32 = mybir.dt.float32
BF16 = mybir.dt.bfloat16
ALU = mybir.AluOpType
AF = mybir.ActivationFunctionType
AX = mybir.AxisListType

D = 1024
DFF = 2816
NF = DFF // 128
INC = 2736
EPS = 1e-6
NEG = -30000.0
ENG = ("pe", "act", "dve", "pool", "sp")


class Buf:
    __slots__ = ("name", "ap", "lw", "rd", "sem", "semcnt")

    def __init__(self, name, ap):
        self.name = name
        self.ap = ap
        self.lw = None
        self.rd = {}
        self.sem = None
        self.semcnt = 0

    def __getitem__(self, k):
        return self.ap[k]


class Prog:
    def __init__(self, nc, stack):
        self.nc = nc
        self.stack = stack
        self.q = {e: [] for e in ENG}
        self.cnt = {e: 0 for e in ENG}
        self.known = {e: {} for e in ENG}
        self.hist = {}
        self.esem = {e: stack.enter_context(nc.semaphore("s_" + e)) for e in ENG}
        self.semobj = {e: self.esem[e] for e in ENG}
        self.dmabufs = []
        self.nwaits = 0
        self.nins = 0
        self.E = {"pe": nc.tensor, "act": nc.scalar, "dve": nc.vector, "pool": nc.gpsimd, "sp": nc.sync}

    def _need(self, deps, tok):
        if tok is None:
            return
        k, v = tok
        if deps.get(k, 0) < v:
            deps[k] = v

    def _collect(self, reads, writes):
        deps = {}
        for b in reads:
            self._need(deps, b.lw)
        for b in writes:
            self._need(deps, b.lw)
            for k, v in b.rd.items():
                self._need(deps, (k, v))
        return deps

    def _emit_waits(self, eng, deps, defer=False):
        kn = self.known[eng]
        new = None
        pend = []
        for k, v in deps.items():
            if k == eng:
                if eng == "pe" or eng == "sp":
                    continue
                if self.cnt[eng] - v > 1:
                    continue
            cur = new if new is not None else kn
            if cur.get(k, 0) >= v:
                continue
            sem = self.semobj[k]
            pend.append((sem, v))
            self.nwaits += 1
            if new is None:
                new = dict(kn)
            new[k] = v
            h = self.hist.get((k, v))
            if h:
                for k2, v2 in h.items():
                    if k2 != eng and new.get(k2, 0) < v2:
                        new[k2] = v2
        if new is not None:
            self.known[eng] = new
        last = pend.pop() if (defer and pend) else None
        for sem, v in pend:
            self.E[eng].wait_ge(sem, v)
        return last

    def op(self, eng, fn, reads=(), writes=()):
        deps = self._collect(reads, writes)
        last = self._emit_waits(eng, deps, defer=True)
        sem = self.esem[eng]
        self.cnt[eng] += 1
        v = self.cnt[eng]
        ins = fn(self.E[eng])
        if last is not None:
            ins._wait_ge(last[0], last[1])
        ins.then_inc(sem, 1)
        self.nins += 1
        tok = (eng, v)
        self.hist[tok] = self.known[eng]
        for b in writes:
            b.lw = tok
            b.rd = {}
        for b in reads:
            if b.rd.get(eng, 0) < v:
                b.rd[eng] = v
        return tok

    def dma(self, out_ap, in_ap, reads=(), writes=(), sb=None, queue="sp", **kw):
        deps = self._collect(reads, writes)
        last = self._emit_waits(queue, deps, defer=True)
        if sb.sem is None:
            sb.sem = self.stack.enter_context(self.nc.semaphore("d%d_%s" % (len(self.dmabufs), sb.name)))
            self.semobj[id(sb)] = sb.sem
            self.dmabufs.append(sb)
        sb.semcnt += 16
        v = sb.semcnt
        sem = sb.sem
        ins = self.E[queue].dma_start(out=out_ap, in_=in_ap, **kw)
        if last is not None:
            ins._wait_ge(last[0], last[1])
        ins.then_inc(sem, 16)
        self.nins += 1
        tok = (id(sb), v)
        self.hist[tok] = self.known[queue]
        for b in writes:
            b.lw = tok
            b.rd = {}
        for b in reads:
            if b.rd.get(tok[0], 0) < v:
                b.rd[tok[0]] = v
        return tok

    def barrier(self):
        deps = {}
        for e in ("pe", "act", "dve", "pool"):
            if self.cnt[e]:
                deps[e] = self.cnt[e]
        for b in self.dmabufs:
            deps[id(b)] = b.semcnt
        for e in ENG:
            d = {k: v for k, v in deps.items() if k != e}
            self._emit_waits(e, d)

    def emit(self):
        pass


WNAMES = ["ffn1_pre_g", "ffn1_w_gate", "ffn1_w_up", "ffn1_w_down", "ffn1_post_g", "mix_pre_g", "w_in",
          "mla_q_norm_g", "mla_w_uq", "mla_kv_norm_g", "mla_w_ukv", "mla_out_norm_g", "gdn_conv_w",
          "gdn_a_log", "gdn_dt_bias", "gdn_out_norm_g", "w_out", "mix_post_g", "ffn2_pre_g",
          "ffn2_w_gate", "ffn2_w_up", "ffn2_w_down", "ffn2_post_g", "final_norm_g"]
WSHAPES = {"ffn1_pre_g": [1, D], "ffn1_w_gate": [D, DFF], "ffn1_w_up": [D, DFF], "ffn1_w_down": [DFF, D],
           "ffn1_post_g": [1, D], "mix_pre_g": [1, D], "w_in": [D, INC], "mla_q_norm_g": [1, 384],
           "mla_w_uq": [384, 768], "mla_kv_norm_g": [1, 256], "mla_w_ukv": [256, 1024],
           "mla_out_norm_g": [1, 512], "gdn_conv_w": [5, 1536], "gdn_a_log": [1, 8], "gdn_dt_bias": [1, 8],
           "gdn_out_norm_g": [1, 128], "w_out": [D, D], "mix_post_g": [1, D], "ffn2_pre_g": [1, D],
           "ffn2_w_gate": [D, DFF], "ffn2_w_up": [D, DFF], "ffn2_w_down": [DFF, D], "ffn2_post_g": [1, D],
           "final_norm_g": [1, D]}


def build(Lp, Ls, debug=False, phases="S A M B G H C"):
    phases = phases.split()
    nc = bass.Bass("TRN2", target_bir_lowering=False)
    Lmax = max(Lp, Ls)
    SEQ = (("p", Lp), ("s", Ls))

    def din(name, shape, dt=F32):
        return nc.dram_tensor(name, list(shape), dt, kind="ExternalInput").ap()

    def dscr(name, shape, dt=F32):
        return nc.dram_tensor(name, list(shape), dt, kind="ExternalOutput" if debug else "Internal").ap()

    X = {"p": din("x_p", [Lp, D]), "s": din("x_s", [Ls, D])}
    W = {n: din(n, WSHAPES[n]) for n in WNAMES}
    ROPE = din("rope_tab", [Lmax, 64])
    GMASK = din("gdn_masks", [64, 5, 8, 64])
    Y = {"p": nc.dram_tensor("y_p", [Lp, D], F32, kind="ExternalOutput").ap(),
         "s": nc.dram_tensor("y_s", [Ls, D], F32, kind="ExternalOutput").ap()}
    WB = {n: dscr("bf_" + n, WSHAPES[n], BF16) for n in
          ["ffn1_w_gate", "ffn1_w_up", "ffn1_w_down", "w_in", "ffn2_w_gate", "ffn2_w_up", "ffn2_w_down"]}
    H1 = {s: dscr("h1_" + s, [L, D]) for s, L in SEQ}
    LAT = {s: dscr("lat_" + s, [L, 672]) for s, L in SEQ}
    QKVT = {s: dscr("qkvT_" + s, [1536, L]) for s, L in SEQ}
    ZS = {s: dscr("z_" + s, [L, 512]) for s, L in SEQ}
    GBL = {s: dscr("gbl_" + s, [2, L, 12]) for s, L in SEQ}
    QT = {s: dscr("QT_" + s, [8, 96, L], BF16) for s, L in SEQ}
    KT = {s: dscr("KT_" + s, [8, 96, L], BF16) for s, L in SEQ}
    VV = {s: dscr("V_" + s, [8, 128, L // 128, 64], BF16) for s, L in SEQ}
    YA = {s: dscr("ya_" + s, [L, 512]) for s, L in SEQ}
    TOK = {s: dscr("tok_" + s, [L, 1536], BF16) for s, L in SEQ}
    OF = {s: dscr("of_" + s, [2, L, 512]) for s, L in SEQ}

    with ExitStack() as st:
        P = Prog(nc, st)
        ARENA = 52800
        arena = st.enter_context(nc.sbuf_tensor("arena", [128, ARENA], F32))
        psum = st.enter_context(nc.psum_tensor("psum", [128, 4096], F32))
        psum_bf = psum.bitcast(BF16)
        BK = [Buf("bank%d" % i, psum[:, i * 512:(i + 1) * 512]) for i in range(8)]

        def pf(i, a=0, b=512):
            return psum[:, i * 512 + a:i * 512 + b]

        def pb(i, a=0, b=1024):
            return psum_bf[:, i * 1024 + a:i * 1024 + b]

        off = [0]
        perm_end = [0]

        def alloc(name, ncols, dt=F32, parts=128):
            n32 = ncols if dt == F32 else (ncols + 1) // 2
            assert off[0] + n32 <= ARENA, ("SBUF arena overflow", name, off[0] + n32)
            a = arena[0:parts, off[0]:off[0] + n32]
            off[0] += n32
            if dt != F32:
                a = a.bitcast(dt)[:, 0:ncols]
            return Buf(name, a)

        def phase_reset():
            P.barrier()
            off[0] = perm_end[0]

        op = P.op
        dma = P.dma

        identf = alloc("identf", 128)
        identb = alloc("identb", 128, BF16)
        onesf = alloc("onesf", 128)
        gT = {n: alloc("gT_" + n, 8) for n in ("ffn1_pre_g", "mix_pre_g", "ffn2_pre_g")}
        gB = {n: alloc("gB_" + n, D) for n in ("ffn1_post_g", "mix_post_g", "ffn2_post_g", "final_norm_g")}
        small = alloc("small", 64)
        perm_end[0] = off[0]

        op("pool", lambda E: E.memset(identf[:], 0.0), writes=[identf])
        op("pool", lambda E: E.affine_select(out=identf[:], in_=identf[:], pattern=[[-1, 128]], compare_op=ALU.not_equal,
                                             fill=1.0, base=0, channel_multiplier=1), reads=[identf], writes=[identf])
        op("dve", lambda E: E.tensor_copy(out=identb[:], in_=identf[:]), reads=[identf], writes=[identb])
        op("pool", lambda E: E.memset(onesf[:], 1.0), writes=[onesf])
        for n, b in gT.items():
            dma(b[:], W[n].rearrange("o (c p) -> p (o c)", p=128), writes=[b], sb=b, allow_slow_non_contiguous=True)
        for n, b in gB.items():
            dma(b[:], W[n].broadcast_to([128, D]), writes=[b], sb=b)
        for n in ("ffn1_post_g", "ffn2_post_g"):
            b = gB[n]
            op("pool", lambda E, b=b: E.tensor_scalar(out=b[:], in0=b[:], scalar1=0.5, scalar2=None, op0=ALU.mult),
               reads=[b], writes=[b])
        sm_q = Buf("sm_q", small.ap)
        dma(small[:, 0:3], W["mla_q_norm_g"].rearrange("o (c p) -> p (o c)", p=128), writes=[small], sb=small,
            allow_slow_non_contiguous=True)
        dma(small[:, 3:5], W["mla_kv_norm_g"].rearrange("o (c p) -> p (o c)", p=128), reads=[small], sb=small,
            allow_slow_non_contiguous=True)
        dma(small[:, 5:9], W["mla_out_norm_g"].rearrange("o (c p) -> p (o c)", p=128), reads=[small], sb=small,
            allow_slow_non_contiguous=True)
        dma(small[:, 9:10], W["gdn_out_norm_g"].rearrange("o (c p) -> p (o c)", p=128), reads=[small], sb=small,
            allow_slow_non_contiguous=True)
        dma(small[:, 16:24], W["gdn_dt_bias"].broadcast_to([128, 8]), reads=[small], sb=small)
        dma(small[:, 24:32], W["gdn_a_log"].broadcast_to([128, 8]), reads=[small], sb=small)
        small.lw = (id(small), small.semcnt)
        op("act", lambda E: E.activation(out=small[:, 24:32], in_=small[:, 24:32], func=AF.Exp), reads=[small], writes=[small])
        op("dve", lambda E: E.tensor_scalar(out=small[:, 24:32], in0=small[:, 24:32], scalar1=-1.0, scalar2=None, op0=ALU.mult),
           reads=[small], writes=[small])

        def rstd_chain(ss, k, n, eps=EPS, post=None):
            op("dve", lambda E: E.tensor_scalar(out=ss[:, 0:k], in0=ss[:, 0:k], scalar1=1.0 / n, scalar2=eps, op0=ALU.mult, op1=ALU.add),
               reads=[ss], writes=[ss])
            op("act", lambda E: E.activation(out=ss[:, 0:k], in_=ss[:, 0:k], func=AF.Ln), reads=[ss], writes=[ss])
            op("act", lambda E: E.activation(out=ss[:, 0:k], in_=ss[:, 0:k], func=AF.Exp, scale=-0.5), reads=[ss], writes=[ss])

        cast_rr = [0]

        def cast(out_ap, in_ap, reads, writes):
            e = ("dve", "pool", "act")[cast_rr[0] % 3]
            cast_rr[0] += 1
            if e == "act":
                op("act", lambda E: E.activation(out=out_ap, in_=in_ap, func=AF.Copy), reads=reads, writes=writes)
            else:
                op(e, lambda E: E.tensor_copy(out=out_ap, in_=in_ap), reads=reads, writes=writes)

        if "S" in phases:
            stf = [alloc("stf%d" % i, DFF) for i in range(3)]
            stb = [alloc("stb%d" % i, DFF, BF16) for i in range(3)]
            it = 0
            for n in WB:
                K_, N_ = WSHAPES[n]
                for kc in range(K_ // 128):
                    f, b = stf[it % 3], stb[it % 3]
                    dma(f[:, 0:N_], W[n][kc * 128:(kc + 1) * 128, :], writes=[f], sb=f)
                    cast(b[:, 0:N_], f[:, 0:N_], [f], [b])
                    dma(WB[n][kc * 128:(kc + 1) * 128, :], b[:, 0:N_], reads=[b], sb=b, queue="pool")
                    it += 1
        phase_reset()

        def norm_T(srcs, src_bufs, gTb, xT, ss, xnb, junk):
            for j in range(4):
                op("act", lambda E, j=j: E.activation(out=junk[:], in_=srcs[j], func=AF.Square, accum_out=ss[:, j:j + 1]),
                   reads=[src_bufs[j]], writes=[ss])
            rstd_chain(ss, 4, D)
            for j in range(4):
                xb = xnb[j % 2]
                if j % 2 == 0:
                    op("dve", lambda E, j=j, xb=xb: E.tensor_scalar(out=xb[:], in0=srcs[j], scalar1=ss[:, j:j + 1], scalar2=None, op0=ALU.mult),
                       reads=[src_bufs[j], ss], writes=[xb])
                else:
                    op("act", lambda E, j=j, xb=xb: E.activation(out=xb[:], in_=srcs[j], func=AF.Copy, scale=ss[:, j:j + 1]),
                       reads=[src_bufs[j], ss], writes=[xb])
                bk = j % 2
                for dc in range(8):
                    op("pe", lambda E, dc=dc, bk=bk, xb=xb: E.transpose(out=pb(bk, dc * 128, (dc + 1) * 128), in_=xb[:, dc * 128:(dc + 1) * 128], identity=identb[:]),
                       reads=[xb, identb], writes=[BK[bk]])
                op("dve", lambda E, j=j, bk=bk: E.tensor_tensor(
                    out=xT.ap[:, :, j * 128:(j + 1) * 128], in0=pb(bk).rearrange("p (c t) -> p c t", c=8),
                    in1=gTb[:, 0:8].unsqueeze(2).to_broadcast([128, 8, 128]), op=ALU.mult),
                    reads=[gTb], writes=[xT, BK[bk]])

        PIECES = [(c0_, 256) for c0_ in range(0, DFF, 256)]

        def ffn(hbufs, hsrcs, pre_g, wg, wu, wd_res, post_bc, T):
            norm_T(hsrcs, hbufs, gT[pre_g], T["xT"], T["ss"], T["xnb"], T["junk"])
            xT, act = T["xT"], T["act"]
            for si, (c0, w) in enumerate(PIECES):
                sg = T["hsl"][T["hi"] % 6]
                su = T["hsl"][(T["hi"] + 1) % 6]
                T["hi"] += 2
                dma(sg.ap[:, :, 0:w], wg[:, c0:c0 + w].rearrange("(dc p) f -> p dc f", p=128), writes=[sg], sb=sg)
                dma(su.ap[:, :, 0:w], wu[:, c0:c0 + w].rearrange("(dc p) f -> p dc f", p=128), writes=[su], sb=su)
                for fl in range(w // 128):
                    f = c0 // 128 + fl
                    g_, u_ = 2 + f % 2, 4 + f % 2
                    for dc in range(8):
                        op("pe", lambda E, dc=dc, fl=fl, g_=g_, sg=sg: E.matmul(pf(g_), lhsT=sg.ap[:, dc, fl * 128:(fl + 1) * 128], rhs=xT.ap[:, dc, :],
                                                                                  start=(dc == 0), stop=(dc == 7)), reads=[sg, xT], writes=[BK[g_]])
                    for dc in range(8):
                        op("pe", lambda E, dc=dc, fl=fl, u_=u_, su=su: E.matmul(pf(u_), lhsT=su.ap[:, dc, fl * 128:(fl + 1) * 128], rhs=xT.ap[:, dc, :],
                                                                                  start=(dc == 0), stop=(dc == 7)), reads=[su, xT], writes=[BK[u_]])
                    et = T["etmp"][f % 2]
                    op("act", lambda E, et=et, g_=g_: E.activation(out=et[:], in_=pf(g_), func=AF.Exp, scale=-1.0), writes=[et, BK[g_]])
                    op("act", lambda E, et=et: E.activation(out=et[:], in_=et[:], func=AF.Ln, bias=1.0), reads=[et], writes=[et])
                    op("act", lambda E, et=et: E.activation(out=et[:], in_=et[:], func=AF.Exp, scale=-1.0), reads=[et], writes=[et])
                    op("dve", lambda E, et=et, g_=g_: E.tensor_tensor(out=et[:], in0=et[:], in1=pf(g_), op=ALU.mult), reads=[et], writes=[et, BK[g_]])
                    op("dve", lambda E, et=et, u_=u_, f=f: E.tensor_tensor(out=act.ap[:, f, :], in0=et[:], in1=pf(u_), op=ALU.mult),
                       reads=[et], writes=[T["actc"][f], BK[u_]])
            ss2 = T["ss2"]
            for j in range(4):
                b0 = 2 if j % 2 == 0 else 4
                for hf in range(2):
                    for f in range(NF):
                        op("pe", lambda E, f=f, hf=hf, b0=b0, j=j: E.matmul(pf(b0 + hf), lhsT=act.ap[:, f, j * 128:(j + 1) * 128],
                                                                             rhs=wd_res.ap[:, f, hf * 512:(hf + 1) * 512], start=(f == 0), stop=(f == NF - 1)),
                           reads=[T["actc"][f], wd_res], writes=[BK[b0 + hf]])
                yps = psum[:, b0 * 512:(b0 + 2) * 512]
                op("act", lambda E, j=j, yps=yps: E.activation(out=T["junk"][:], in_=yps, func=AF.Square, accum_out=ss2[:, j:j + 1]),
                   writes=[ss2, BK[b0], BK[b0 + 1]])
                op("dve", lambda E, j=j: E.tensor_scalar(out=ss2[:, 4 + j:5 + j], in0=ss2[:, j:j + 1], scalar1=1.0 / D, scalar2=EPS, op0=ALU.mult, op1=ALU.add),
                   reads=[ss2], writes=[ss2])
                op("act", lambda E, j=j: E.activation(out=ss2[:, 4 + j:5 + j], in_=ss2[:, 4 + j:5 + j], func=AF.Ln), reads=[ss2], writes=[ss2])
                op("act", lambda E, j=j: E.activation(out=ss2[:, 4 + j:5 + j], in_=ss2[:, 4 + j:5 + j], func=AF.Exp, scale=-0.5), reads=[ss2], writes=[ss2])
                tmp = T["tmp"][j % 2]
                op("dve", lambda E, j=j, yps=yps, tmp=tmp: E.scalar_tensor_tensor(out=tmp[:], in0=yps, scalar=ss2[:, 4 + j:5 + j], in1=post_bc[:],
                                                                                    op0=ALU.mult, op1=ALU.mult),
                   reads=[ss2, post_bc], writes=[tmp, BK[b0], BK[b0 + 1]])
                op("pool", lambda E, j=j, tmp=tmp: E.tensor_tensor(out=hsrcs[j], in0=hsrcs[j], in1=tmp[:], op=ALU.add),
                   reads=[tmp], writes=[hbufs[j]])

        def ffn_bufs(with_full=False):
            T = {}
            T["wd"] = alloc("wd_res", NF * D, BF16)
            T["wd"].ap = T["wd"].ap.rearrange("p (f d) -> p f d", f=NF)
            T["xT"] = alloc("xT", 8 * 512, BF16)
            T["xT"].ap = T["xT"].ap.rearrange("p (c t) -> p c t", c=8)
            T["act_off"] = off[0]
            T["act"] = alloc("act", NF * 512, BF16)
            T["act"].ap = T["act"].ap.rearrange("p (f t) -> p f t", f=NF)
            T["actc"] = [Buf("act%d" % f, T["act"].ap[:, f, :]) for f in range(NF)]
            T["slab"] = []
            for i in range(3 if with_full else 0):
                b = alloc("slab%d" % i, 8 * 512, BF16)
                b.ap = b.ap.rearrange("p (c f) -> p c f", c=8)
                T["slab"].append(b)
            T["slabi"] = 0
            T["hsl"] = []
            for i in range(6):
                b = alloc("hsl%d" % i, 8 * 256, BF16)
                b.ap = b.ap.rearrange("p (c f) -> p c f", c=8)
                T["hsl"].append(b)
            T["hi"] = 0
            T["etmp"] = [alloc("etmp%d" % i, 512) for i in range(2)]
            T["tmp"] = [alloc("tmp%d" % i, D) for i in range(2)]
            T["xnb"] = [alloc("xnb%d" % i, D, BF16) for i in range(2)]
            T["junk"] = alloc("junk", D, BF16)
            T["ss"] = alloc("ss", 8)
            T["ss2"] = alloc("ss2", 8)
            T["h"] = alloc("h", 4 * D)
            T["h"].ap = T["h"].ap.rearrange("p (j d) -> p j d", j=4)
            T["hb"] = [Buf("h%d" % j, T["h"].ap[:, j, :]) for j in range(4)]
            T["hs"] = [T["h"].ap[:, j, :] for j in range(4)]
            return T

        def load_wd(T, name):
            wd = T["wd"]
            for f0 in range(0, NF, 2):
                dma(wd.ap[:, f0:f0 + 2, :], WB[name][f0 * 128:(f0 + 2) * 128, :].rearrange("(f p) d -> p f d", p=128),
                    writes=[wd] if f0 == 0 else [], reads=[] if f0 == 0 else [], sb=wd)
            wd.lw = (id(wd), wd.semcnt)

        if "A" in phases:
            T = ffn_bufs(True)
            load_wd(T, "ffn1_w_down")
            pst = [alloc("pst%d" % i, 1200) for i in range(2)]
            qst = [alloc("qst%d" % i, 512) for i in range(3)]
            gst = [alloc("gst%d" % i, 64) for i in range(2)]
            pi = 0
            qi = 0
            tilesA = [(s, L, ti) for s, L in SEQ for ti in range(L // 512)]

            def load_x(ix):
                s_, L_, ti_ = tilesA[ix]
                dma(T["h"].ap, X[s_][ti_ * 512:(ti_ + 1) * 512, :].rearrange("(j p) d -> p j d", p=128), writes=T["hb"], sb=T["h"])
            load_x(0)
            for ix, (s, L, ti) in enumerate(tilesA):
                if True:
                    r0 = ti * 512
                    h = T["h"]
                    ffn(T["hb"], T["hs"], "ffn1_pre_g", WB["ffn1_w_gate"], WB["ffn1_w_up"], T["wd"], gB["ffn1_post_g"], T)
                    dma(H1[s][r0:r0 + 512, :].rearrange("(j p) d -> p j d", p=128), h.ap, reads=T["hb"], sb=h, queue="pool")
                    norm_T(T["hs"], T["hb"], gT["mix_pre_g"], T["xT"], T["ss"], T["xnb"], T["junk"])
                    if ix + 1 < len(tilesA):
                        load_x(ix + 1)
                    xT = T["xT"]
                    for q3 in range(3):
                        sl = T["slab"][T["slabi"] % 3]
                        T["slabi"] += 1
                        c0 = 672 + q3 * 512
                        dma(sl.ap[:, :, 0:512], WB["w_in"][:, c0:c0 + 512].rearrange("(dc p) f -> p dc f", p=128), writes=[sl], sb=sl)
                        for fl in range(4):
                            ch = q3 * 4 + fl
                            bk = 2 + ch % 2
                            for dc in range(8):
                                op("pe", lambda E, dc=dc, fl=fl, bk=bk, sl=sl: E.matmul(pf(bk), lhsT=sl.ap[:, dc, fl * 128:(fl + 1) * 128], rhs=xT.ap[:, dc, :],
                                                                                          start=(dc == 0), stop=(dc == 7)), reads=[sl, xT], writes=[BK[bk]])
                            qs = qst[qi % 3]
                            qi += 1
                            if ch % 2 == 0:
                                op("act", lambda E, qs=qs, bk=bk: E.activation(out=qs[:], in_=pf(bk), func=AF.Copy), writes=[qs, BK[bk]])
                            else:
                                op("dve", lambda E, qs=qs, bk=bk: E.tensor_copy(out=qs[:], in_=pf(bk)), writes=[qs, BK[bk]])
                            dma(QKVT[s][ch * 128:(ch + 1) * 128, r0:r0 + 512], qs[:], reads=[qs], sb=qs, queue="pool")
                    sA = T["slab"][T["slabi"] % 3]
                    sB = T["slab"][(T["slabi"] + 1) % 3]
                    sC = T["slab"][(T["slabi"] + 2) % 3]
                    T["slabi"] += 3
                    dma(sA.ap[:, :, 0:512], WB["w_in"][:, 0:512].rearrange("(dc p) f -> p dc f", p=128), writes=[sA], sb=sA)
                    dma(sB.ap[:, :, 0:512], WB["w_in"][:, 2208:2720].rearrange("(dc p) f -> p dc f", p=128), writes=[sB], sb=sB)
                    dma(sC.ap[:, :, 0:160], WB["w_in"][:, 512:672].rearrange("(dc p) f -> p dc f", p=128), writes=[sC], sb=sC)
                    dma(sC.ap[:, :, 160:176], WB["w_in"][:, 2720:2736].rearrange("(dc p) f -> p dc f", p=128), reads=[sC], sb=sC)
                    sC.lw = (id(sC), sC.semcnt)
                    for j in range(4):
                        for dc in range(8):
                            lhs = xT.ap[:, dc, j * 128:(j + 1) * 128]
                            op("pe", lambda E, dc=dc, lhs=lhs: E.matmul(pf(6), lhsT=lhs, rhs=sA.ap[:, dc, 0:512], start=(dc == 0), stop=(dc == 7)),
                               reads=[sA, xT], writes=[BK[6]])
                        for dc in range(8):
                            lhs = xT.ap[:, dc, j * 128:(j + 1) * 128]
                            op("pe", lambda E, dc=dc, lhs=lhs: E.matmul(pf(7), lhsT=lhs, rhs=sB.ap[:, dc, 0:512], start=(dc == 0), stop=(dc == 7)),
                               reads=[sB, xT], writes=[BK[7]])
                        for dc in range(8):
                            lhs = xT.ap[:, dc, j * 128:(j + 1) * 128]
                            op("pe", lambda E, dc=dc, lhs=lhs: E.matmul(pf(0, 0, 176), lhsT=lhs, rhs=sC.ap[:, dc, 0:176], start=(dc == 0), stop=(dc == 7)),
                               reads=[sC, xT], writes=[BK[0]])
                        ps_ = pst[pi % 2]
                        gs_ = gst[pi % 2]
                        pi += 1
                        op("act", lambda E, ps_=ps_: E.activation(out=ps_[:, 0:512], in_=pf(6), func=AF.Copy), writes=[ps_, BK[6]])
                        op("dve", lambda E, ps_=ps_: E.tensor_copy(out=ps_[:, 672:1184], in_=pf(7)), reads=[ps_], writes=[ps_, BK[7]])
                        op("dve", lambda E, ps_=ps_: E.tensor_copy(out=ps_[:, 512:672], in_=pf(0, 0, 160)), reads=[ps_], writes=[ps_, BK[0]])
                        op("dve", lambda E, gs_=gs_: E.tensor_tensor(out=gs_[:, 0:8], in0=pf(0, 160, 168), in1=small[:, 16:24], op=ALU.add),
                           reads=[small], writes=[gs_, BK[0]])
                        op("act", lambda E, gs_=gs_: E.activation(out=gs_[:, 0:8], in_=gs_[:, 0:8], func=AF.Exp), reads=[gs_], writes=[gs_])
                        op("act", lambda E, gs_=gs_: E.activation(out=gs_[:, 8:16], in_=pf(0, 168, 176), func=AF.Exp, scale=-1.0), reads=[gs_], writes=[gs_, BK[0]])
                        op("act", lambda E, gs_=gs_: E.activation(out=gs_[:, 0:16], in_=gs_[:, 0:16], func=AF.Ln, bias=1.0), reads=[gs_], writes=[gs_])
                        gv = gs_.ap[:, 32:56].rearrange("p (d k) -> p d k", d=2)
                        op("dve", lambda E, gs_=gs_, gv=gv: E.tensor_tensor(out=gv[:, :, 0:4], in0=gs_.ap[:, 0:8].rearrange("p (d k) -> p d k", d=2),
                                                                            in1=small.ap[:, 24:32].rearrange("p (d k) -> p d k", d=2), op=ALU.mult),
                           reads=[gs_, small], writes=[gs_])
                        op("dve", lambda E, gs_=gs_, gv=gv: E.tensor_scalar(out=gv[:, :, 4:8], in0=gs_.ap[:, 8:16].rearrange("p (d k) -> p d k", d=2),
                                                                            scalar1=-1.0, scalar2=None, op0=ALU.mult), reads=[gs_], writes=[gs_])
                        op("act", lambda E, gs_=gs_, gv=gv: E.activation(out=gv[:, :, 8:12], in_=gs_.ap[:, 8:16].rearrange("p (d k) -> p d k", d=2),
                                                                         func=AF.Exp, scale=-1.0), reads=[gs_], writes=[gs_])
                        rr = slice(r0 + j * 128, r0 + (j + 1) * 128)
                        dma(LAT[s][rr, :], ps_[:, 0:672], reads=[ps_], sb=ps_, queue="pool")
                        dma(ZS[s][rr, :], ps_[:, 672:1184], reads=[ps_], sb=ps_, queue="pool")
                        dma(GBL[s][:, rr, :].rearrange("d p k -> p d k"), gv, reads=[gs_], sb=gs_, queue="pool")
            phase_reset()

        if "M" in phases:
            wst = alloc("wst", 1024)
            wuq = alloc("wuq", 3 * 768, BF16)
            wuq.ap = wuq.ap.rearrange("p (c f) -> p c f", c=3)
            wk = alloc("wk", 2 * 512, BF16)
            wk.ap = wk.ap.rearrange("p (c f) -> p c f", c=2)
            wv = alloc("wv", 2 * 512, BF16)
            wv.ap = wv.ap.rearrange("p (c f) -> p c f", c=2)
            for c in range(3):
                dma(wst[:, 0:768], W["mla_w_uq"][c * 128:(c + 1) * 128, :], writes=[wst], sb=wst)
                op("dve", lambda E, c=c: E.tensor_scalar(out=wuq.ap[:, c, :], in0=wst[:, 0:768], scalar1=small[:, c:c + 1], scalar2=None, op0=ALU.mult),
                   reads=[wst, small], writes=[wuq])
            for c in range(2):
                dma(wst[:, 0:1024], W["mla_w_ukv"][c * 128:(c + 1) * 128, :], writes=[wst], sb=wst)
                wv4 = wst.ap[:, 0:1024].rearrange("p (h e) -> p h e", h=8)
                op("dve", lambda E, c=c, wv4=wv4: E.tensor_scalar(out=wk.ap[:, c, :].rearrange("p (h e) -> p h e", h=8), in0=wv4[:, :, 0:64],
                                                                  scalar1=small[:, 3 + c:4 + c], scalar2=None, op0=ALU.mult), reads=[wst, small], writes=[wk])
                op("dve", lambda E, c=c, wv4=wv4: E.tensor_scalar(out=wv.ap[:, c, :].rearrange("p (h e) -> p h e", h=8), in0=wv4[:, :, 64:128],
                                                                  scalar1=small[:, 3 + c:4 + c], scalar2=None, op0=ALU.mult), reads=[wst, small], writes=[wv])
            lat = [alloc("lat%d" % i, 672) for i in range(2)]
            rp = [alloc("rp%d" % i, 64) for i in range(2)]
            lnb = [alloc("lnb%d" % i, 736, BF16) for i in range(2)]
            for b in lnb:
                op("pool", lambda E, b=b: E.memset(b[:, 640:736], 0.0), writes=[b])
            ssm = [alloc("ssm%d" % i, 4) for i in range(2)]
            junkm = alloc("junkm", 384, BF16)
            cT = [alloc("cT%d" % i, 3 * 128, BF16) for i in range(2)]
            ckvT = alloc("ckvT", 2 * 512, BF16)
            ckvT.ap = ckvT.ap.rearrange("p (c t) -> p c t", c=2)
            ckvc = [Buf("ckv%d" % j, ckvT.ap[:, :, j * 128:(j + 1) * 128]) for j in range(4)]
            kpst = alloc("kpst", 512, BF16)
            kpc = [Buf("kp%d" % j, kpst.ap[:, j * 128:(j + 1) * 128]) for j in range(4)]
            qr = [alloc("qr%d" % i, 768, BF16) for i in range(2)]
            rtmp = [alloc("rtmp%d" % i, 512) for i in range(2)]
            qtst = alloc("qtst", 8 * 512, BF16)
            qtst.ap = qtst.ap.rearrange("p (h t) -> p h t", h=8)
            qtc = [Buf("qt%d" % j, qtst.ap[:, :, j * 128:(j + 1) * 128]) for j in range(4)]
            ktst = alloc("ktst", 4 * 512, BF16)
            ktst.ap = ktst.ap.rearrange("p (a t) -> p a t", a=4)
            vst = alloc("vst", 4 * 512, BF16)
            vst.ap = vst.ap.rearrange("p (j f) -> p j f", j=4)
            vsc = [Buf("vs%d" % j, vst.ap[:, j, :]) for j in range(4)]
            li = 0
            for s, L in SEQ:
                for ti in range(L // 512):
                    r0 = ti * 512
                    for j in range(4):
                        rr = slice(r0 + j * 128, r0 + (j + 1) * 128)
                        lt, rpt, lb, sm_, ct = lat[li % 2], rp[li % 2], lnb[li % 2], ssm[li % 2], cT[li % 2]
                        qrt, rt = qr[li % 2], rtmp[li % 2]
                        li += 1
                        dma(lt[:], LAT[s][rr, :], writes=[lt], sb=lt)
                        dma(rpt[:], ROPE[rr, :], writes=[rpt], sb=rpt)
                        op("act", lambda E, lt=lt, sm_=sm_: E.activation(out=junkm[:, 0:384], in_=lt[:, 0:384], func=AF.Square, accum_out=sm_[:, 0:1]),
                           reads=[lt], writes=[sm_])
                        op("act", lambda E, lt=lt, sm_=sm_: E.activation(out=junkm[:, 0:256], in_=lt[:, 384:640], func=AF.Square, accum_out=sm_[:, 1:2]),
                           reads=[lt], writes=[sm_])
                        op("dve", lambda E, sm_=sm_: E.tensor_scalar(out=sm_[:, 0:1], in0=sm_[:, 0:1], scalar1=1.0 / 384, scalar2=EPS, op0=ALU.mult, op1=ALU.add),
                           reads=[sm_], writes=[sm_])
                        op("dve", lambda E, sm_=sm_: E.tensor_scalar(out=sm_[:, 1:2], in0=sm_[:, 1:2], scalar1=1.0 / 256, scalar2=EPS, op0=ALU.mult, op1=ALU.add),
                           reads=[sm_], writes=[sm_])
                        op("act", lambda E, sm_=sm_: E.activation(out=sm_[:, 0:2], in_=sm_[:, 0:2], func=AF.Ln), reads=[sm_], writes=[sm_])
                        op("act", lambda E, sm_=sm_: E.activation(out=sm_[:, 0:2], in_=sm_[:, 0:2], func=AF.Exp, scale=-0.5), reads=[sm_], writes=[sm_])
                        op("dve", lambda E, lt=lt, lb=lb, sm_=sm_: E.tensor_scalar(out=lb[:, 0:384], in0=lt[:, 0:384], scalar1=sm_[:, 0:1], scalar2=None, op0=ALU.mult),
                           reads=[lt, sm_], writes=[lb])
                        op("act", lambda E, lt=lt, lb=lb, sm_=sm_: E.activation(out=lb[:, 384:640], in_=lt[:, 384:640], func=AF.Copy, scale=sm_[:, 1:2]),
                           reads=[lt, sm_, lb], writes=[lb])
                        op("pool", lambda E, lt=lt, rt=rt, rpt=rpt: E.tensor_tensor(out=rt[:, 0:32], in0=lt[:, 640:672], in1=rpt[:, 0:32], op=ALU.mult),
                           reads=[lt, rpt], writes=[rt])
                        op("pool", lambda E, lt=lt, rt=rt, rpt=rpt: E.tensor_tensor(out=rt[:, 32:48], in0=lt[:, 656:672], in1=rpt[:, 32:48], op=ALU.mult),
                           reads=[lt, rpt, rt], writes=[rt])
                        op("pool", lambda E, lt=lt, rt=rt, rpt=rpt: E.tensor_tensor(out=rt[:, 48:64], in0=lt[:, 640:656], in1=rpt[:, 48:64], op=ALU.mult),
                           reads=[lt, rpt, rt], writes=[rt])
                        op("pool", lambda E, rt=rt, lb=lb: E.tensor_tensor(out=lb[:, 704:736], in0=rt[:, 0:32], in1=rt[:, 32:64], op=ALU.add),
                           reads=[rt, lb], writes=[lb])
                        for c in range(5):
                            op("pe", lambda E, c=c, lb=lb: E.transpose(out=pb(0, c * 128, (c + 1) * 128), in_=lb[:, c * 128:(c + 1) * 128], identity=identb[:]),
                               reads=[lb, identb], writes=[BK[0]])
                        op("pe", lambda E, lb=lb: E.transpose(out=psum_bf[0:96, 640:768], in_=lb[:, 640:736], identity=identb[:]),
                           reads=[lb, identb], writes=[BK[0]])
                        op("dve", lambda E, ct=ct: E.tensor_copy(out=ct[:], in_=pb(0, 0, 384)), writes=[ct, BK[0]])
                        op("act", lambda E, j=j: E.activation(out=ckvT.ap[:, :, j * 128:(j + 1) * 128], in_=pb(0, 384, 640).rearrange("p (c t) -> p c t", c=2), func=AF.Copy),
                           writes=[ckvc[j], BK[0]])
                        op("dve", lambda E, j=j: E.tensor_copy(out=kpst.ap[64:96, j * 128:(j + 1) * 128], in_=psum_bf[64:96, 640:768]), writes=[kpc[j], BK[0]])
                        ctv = ct.ap.rearrange("p (c t) -> p c t", c=3)
                        for c in range(3):
                            op("pe", lambda E, c=c, ctv=ctv: E.matmul(pf(1, 0, 480), lhsT=ctv[:, c, :], rhs=wuq.ap[:, c, 0:480], start=(c == 0), stop=(c == 2)),
                               reads=[ct, wuq], writes=[BK[1]])
                        for c in range(3):
                            op("pe", lambda E, c=c, ctv=ctv: E.matmul(pf(2, 0, 288), lhsT=ctv[:, c, :], rhs=wuq.ap[:, c, 480:768], start=(c == 0), stop=(c == 2)),
                               reads=[ct, wuq], writes=[BK[2]])
                        for (bk, h0, nh) in ((1, 0, 5), (2, 5, 3)):
                            pv = pf(bk, 0, nh * 96).rearrange("p (h e) -> p h e", h=nh)
                            qv = qrt.ap[:, h0 * 96:(h0 + nh) * 96].rearrange("p (h e) -> p h e", h=nh)
                            tv = rt.ap[:, 64:64 + nh * 64].rearrange("p (h e) -> p h e", h=nh)
                            cs = rpt.ap[:, 0:32].unsqueeze(1).to_broadcast([128, nh, 32])
                            sn1 = rpt.ap[:, 32:48].unsqueeze(1).to_broadcast([128, nh, 16])
                            sn2 = rpt.ap[:, 48:64].unsqueeze(1).to_broadcast([128, nh, 16])
                            op("act", lambda E, pv=pv, qv=qv: E.activation(out=qv[:, :, 0:64], in_=pv[:, :, 0:64], func=AF.Copy), reads=[], writes=[qrt, BK[bk]])
                            op("dve", lambda E, pv=pv, tv=tv, cs=cs: E.tensor_tensor(out=tv[:, :, 0:32], in0=pv[:, :, 64:96], in1=cs, op=ALU.mult),
                               reads=[rpt], writes=[rt, BK[bk]])
                            op("dve", lambda E, pv=pv, tv=tv, sn1=sn1: E.tensor_tensor(out=tv[:, :, 32:48], in0=pv[:, :, 80:96], in1=sn1, op=ALU.mult),
                               reads=[rpt], writes=[rt, BK[bk]])
                            op("dve", lambda E, pv=pv, tv=tv, sn2=sn2: E.tensor_tensor(out=tv[:, :, 48:64], in0=pv[:, :, 64:80], in1=sn2, op=ALU.mult),
                               reads=[rpt], writes=[rt, BK[bk]])
                            op("pool", lambda E, tv=tv, qv=qv: E.tensor_tensor(out=qv[:, :, 64:96], in0=tv[:, :, 0:32], in1=tv[:, :, 32:64], op=ALU.add),
                               reads=[rt], writes=[qrt])
                        for hh in range(8):
                            op("pe", lambda E, hh=hh, qrt=qrt: E.transpose(out=psum_bf[0:96, 3 * 1024 + hh * 128:3 * 1024 + (hh + 1) * 128], in_=qrt[:, hh * 96:(hh + 1) * 96],
                                                                           identity=identb[:]), reads=[qrt, identb], writes=[BK[3]])
                        op("act", lambda E, j=j: E.activation(out=qtst.ap[0:96, :, j * 128:(j + 1) * 128],
                                                              in_=psum_bf[0:96, 3 * 1024:4 * 1024].rearrange("p (h t) -> p h t", h=8), func=AF.Copy),
                           writes=[qtc[j], BK[3]])
                        for c in range(2):
                            op("pe", lambda E, c=c, j=j: E.matmul(pf(4), lhsT=ckvT.ap[:, c, j * 128:(j + 1) * 128], rhs=wv.ap[:, c, :], start=(c == 0), stop=(c == 1)),
                               reads=[ckvc[j], wv], writes=[BK[4]])
                        op("dve", lambda E, j=j: E.tensor_copy(out=vst.ap[:, j, :], in_=pf(4)), writes=[vsc[j], BK[4]])
                    for pr in range(4):
                        bk = 5 + pr % 2
                        for c in range(2):
                            op("pe", lambda E, c=c, pr=pr, bk=bk: E.matmul(pf(bk), lhsT=wk.ap[:, c, pr * 128:(pr + 1) * 128], rhs=ckvT.ap[:, c, :], start=(c == 0), stop=(c == 1)),
                               reads=ckvc + [wk], writes=[BK[bk]])
                        if pr % 2 == 0:
                            op("act", lambda E, pr=pr, bk=bk: E.activation(out=ktst.ap[:, pr, :], in_=pf(bk), func=AF.Copy), writes=[ktst, BK[bk]])
                        else:
                            op("dve", lambda E, pr=pr, bk=bk: E.tensor_copy(out=ktst.ap[:, pr, :], in_=pf(bk)), reads=[ktst], writes=[ktst, BK[bk]])
                    cc = slice(r0, r0 + 512)
                    for hh in range(8):
                        pr, hi = hh // 2, hh % 2
                        dma(KT[s][hh, 0:64, cc], ktst.ap[hi * 64:(hi + 1) * 64, pr, :], reads=[ktst], sb=ktst, queue="pool")
                        dma(KT[s][hh, 64:96, cc], kpst.ap[64:96, :], reads=kpc, sb=kpst, queue="pool")
                    dma(QT[s][:, :, cc].rearrange("h r t -> r h t"), qtst.ap[0:96, :, :], reads=qtc, sb=qtst, queue="pool")
                    for j in range(4):
                        dma(VV[s][:, :, ti * 4 + j, :].rearrange("h p e -> p h e"),
                            vst.ap[:, j, :].rearrange("p (h e) -> p h e", h=8), reads=vsc, sb=vst, queue="pool")
            phase_reset()

        if "B" in phases:
            SC = 96 ** -0.5
            for s, L in SEQ:
                nkb = L // 128
                off_seq = off[0]
                ktb = []
                vtb = []
                for i in range(2):
                    b = alloc("ktb%d" % i, L, BF16)
                    ktb.append(b)
                    v = alloc("vtb%d" % i, nkb * 65, BF16)
                    v.ap = v.ap.rearrange("p (k e) -> p k e", e=65)
                    op("pool", lambda E, v=v: E.memset(v.ap[:, :, 64:65], 1.0), writes=[v])
                    vtb.append(v)
                qtb = [alloc("qtb%d" % i, 512, BF16) for i in range(3)]
                ptb = [alloc("ptb%d" % i, 512, BF16) for i in range(4)]
                osb = [alloc("osb%d" % i, 512) for i in range(2)]
                yst = [alloc("yst%d" % i, 4 * 64) for i in range(2)]
                rdn = [alloc("rdn%d" % i, 4) for i in range(2)]
                qi = 0
                pi = 0
                for hh in range(8):
                    kt, vt = ktb[hh % 2], vtb[hh % 2]
                    dma(kt[0:96, :], KT[s][hh, :, :], writes=[kt], sb=kt)
                    dma(vt.ap[:, :, 0:64], VV[s][hh, :, :, :], writes=[vt], sb=vt)
                    for qt in range(L // 512):
                        qb = qtb[qi % 3]
                        ob, ys, rd = osb[qi % 2], yst[qi % 2], rdn[qi % 2]
                        obk = 4 + qi % 2
                        qi += 1
                        dma(qb[0:96, :], QT[s][hh, :, qt * 512:(qt + 1) * 512], writes=[qb], sb=qb)

                        def smm(kb, qb=qb, kt=kt):
                            bk = kb % 4
                            op("pe", lambda E: E.matmul(pf(bk), lhsT=kt[0:96, kb * 128:(kb + 1) * 128], rhs=qb[0:96, :], start=True, stop=True),
                               reads=[kt, qb], writes=[BK[bk]])
                        smm(0)
                        if nkb > 1:
                            smm(1)
                        for kb in range(nkb):
                            if kb + 2 < nkb:
                                smm(kb + 2)
                            pt = ptb[pi % 4]
                            pi += 1
                            bk = kb % 4
                            op("act", lambda E, pt=pt, bk=bk: E.activation(out=pt[:], in_=pf(bk), func=AF.Exp, scale=SC), writes=[pt, BK[bk]])
                            op("pe", lambda E, pt=pt, kb=kb, vt=vt, obk=obk: E.matmul(psum[0:65, obk * 512:(obk + 1) * 512], lhsT=vt.ap[:, kb, 0:65], rhs=pt[:],
                                                                                         start=(kb == 0), stop=(kb == nkb - 1)), reads=[vt, pt], writes=[BK[obk]])
                        op("dve", lambda E, ob=ob, obk=obk: E.tensor_copy(out=ob[0:65, :], in_=psum[0:65, obk * 512:(obk + 1) * 512]), writes=[ob, BK[obk]])
                        for j in range(4):
                            op("pe", lambda E, j=j, ob=ob: E.matmul(pf(6, j * 65, (j + 1) * 65), lhsT=ob[0:65, j * 128:(j + 1) * 128], rhs=identf[0:65, 0:65], start=True, stop=True),
                               reads=[ob, identf], writes=[BK[6]])
                        p6 = pf(6, 0, 260).rearrange("p (j e) -> p j e", j=4)
                        op("dve", lambda E, rd=rd, p6=p6: E.reciprocal(out=rd.ap[:, 0:4].unsqueeze(2), in_=p6[:, :, 64:65]), writes=[rd, BK[6]])
                        op("dve", lambda E, rd=rd, ys=ys, p6=p6: E.tensor_tensor(out=ys.ap.rearrange("p (j e) -> p j e", j=4), in0=p6[:, :, 0:64],
                                                                                in1=rd.ap[:, 0:4].unsqueeze(2).to_broadcast([128, 4, 64]), op=ALU.mult),
                           reads=[rd], writes=[ys, BK[6]])
                        dma(YA[s][qt * 512:(qt + 1) * 512, hh * 64:(hh + 1) * 64].rearrange("(j p) e -> p j e", p=128),
                            ys.ap.rearrange("p (j e) -> p j e", j=4), reads=[ys], sb=ys, queue="pool")
                P.barrier()
                off[0] = off_seq
            phase_reset()

        def run_window(items, W):
            nxt = 0
            active = []
            while nxt < len(items) or active:
                while len(active) < W and nxt < len(items):
                    if items[nxt][1] and active:
                        break
                    active.append(items[nxt][0])
                    nxt += 1
                for g_ in list(active):
                    try:
                        next(g_)
                    except StopIteration:
                        active.remove(g_)

        if "G" in phases:
            cw = alloc("cw", 12 * 5)
            for c in range(12):
                dma(cw[:, c * 5:(c + 1) * 5], W["gdn_conv_w"][:, c * 128:(c + 1) * 128].rearrange("k p -> p k"), writes=[cw] if c == 0 else [],
                    reads=[] if c == 0 else [cw], sb=cw, allow_slow_non_contiguous=True)
            cw.lw = (id(cw), cw.semcnt)
            dg = alloc("dg", 60 * 128, BF16)
            dg.ap = dg.ap.rearrange("p (k f) -> p k f", k=60)
            for k in range(60):
                op("dve", lambda E, k=k: E.tensor_scalar(out=dg.ap[:, k, :], in0=identb[:], scalar1=cw[:, k:k + 1], scalar2=None, op0=ALU.mult),
                   reads=[identb, cw], writes=[dg] if k == 0 else [])
            dg.lw = ("dve", P.cnt["dve"])
            GW = 4
            xin = [alloc("gxin%d" % i, 516) for i in range(GW)]
            xbf = [alloc("gxbf%d" % i, 516, BF16) for i in range(GW)]
            ex = [alloc("gex%d" % i, 512) for i in range(GW)]
            sb_ = [alloc("gsb%d" % i, 512, BF16) for i in range(GW)]
            tokst2 = []
            tokc2 = []
            for i in range(2):
                t_ = alloc("tokst%d" % i, 4 * 1536, BF16)
                t_.ap = t_.ap.rearrange("p (j c) -> p j c", j=4)
                tokst2.append(t_)
                tokc2.append([Buf("tokc%d_%d" % (i, c), t_.ap[:, :, c * 128:(c + 1) * 128]) for c in range(12)])
            sq = alloc("gsq", 1024)
            ssg2 = [alloc("ssg%d" % i, 32) for i in range(2)]

            def gchunk(s, L, r0, c, k, tokst, tokc):
                xi, xb, e_, sbb = xin[k % GW], xbf[k % GW], ex[k % GW], sb_[k % GW]
                cb_, tb_ = 2 * (k % GW), 2 * (k % GW) + 1
                lo, hi = max(r0 - 2, 0), min(r0 + 514, L)
                if r0 == 0:
                    op("pool", lambda E: E.memset(xi[:, 0:2], 0.0), writes=[xi])
                if r0 + 512 == L:
                    op("pool", lambda E: E.memset(xi[:, 514:516], 0.0), writes=[xi])
                dma(xi[:, lo - (r0 - 2):hi - (r0 - 2)], QKVT[s][c * 128:(c + 1) * 128, lo:hi], writes=[xi], sb=xi)
                yield
                op("pool", lambda E: E.tensor_copy(out=xb[:], in_=xi[:]), reads=[xi], writes=[xb])
                yield
                for t5 in range(5):
                    op("pe", lambda E, t5=t5: E.matmul(pf(cb_), lhsT=dg.ap[:, c * 5 + t5, :], rhs=xb[:, t5:t5 + 512], start=(t5 == 0), stop=(t5 == 4)),
                       reads=[dg, xb], writes=[BK[cb_]])
                yield
                op("act", lambda E: E.activation(out=e_[:], in_=pf(cb_), func=AF.Exp, scale=-1.0), writes=[e_, BK[cb_]])
                op("act", lambda E: E.activation(out=e_[:], in_=e_[:], func=AF.Ln, bias=1.0), reads=[e_], writes=[e_])
                op("act", lambda E: E.activation(out=e_[:], in_=e_[:], func=AF.Exp, scale=-1.0), reads=[e_], writes=[e_])
                yield
                op("dve", lambda E: E.tensor_tensor(out=sbb[:], in0=e_[:], in1=pf(cb_), op=ALU.mult), reads=[e_], writes=[sbb, BK[cb_]])
                yield
                for j in range(4):
                    op("pe", lambda E, j=j: E.transpose(out=pb(tb_, j * 128, (j + 1) * 128), in_=sbb[:, j * 128:(j + 1) * 128], identity=identb[:]),
                       reads=[sbb, identb], writes=[BK[tb_]])
                yield
                if c % 2 == 0:
                    op("act", lambda E: E.activation(out=tokst.ap[:, :, c * 128:(c + 1) * 128], in_=pb(tb_, 0, 512).rearrange("p (j d) -> p j d", j=4), func=AF.Copy),
                       writes=[tokc[c], BK[tb_]])
                else:
                    op("dve", lambda E: E.tensor_copy(out=tokst.ap[:, :, c * 128:(c + 1) * 128], in_=pb(tb_, 0, 512).rearrange("p (j d) -> p j d", j=4)),
                       writes=[tokc[c], BK[tb_]])

            def gtail(s, r0, tokst, tokc, ssg):
                for j in range(4):
                    tv = tokst.ap[:, j, 0:1024]
                    op("dve", lambda E, tv=tv: E.tensor_tensor(out=sq[:], in0=tv, in1=tv, op=ALU.mult), reads=tokc[0:8], writes=[sq])
                    op("dve", lambda E, j=j: E.tensor_reduce(out=ssg[:, j * 8:(j + 1) * 8], in_=sq.ap.rearrange("p (h d) -> p h d", h=8), axis=AX.X, op=ALU.add),
                       reads=[sq], writes=[ssg])
                    yield
                op("dve", lambda E: E.tensor_scalar(out=ssg[:, 0:32], in0=ssg[:, 0:32], scalar1=EPS, scalar2=None, op0=ALU.add), reads=[ssg], writes=[ssg])
                op("act", lambda E: E.activation(out=ssg[:, 0:32], in_=ssg[:, 0:32], func=AF.Ln), reads=[ssg], writes=[ssg])
                op("act", lambda E: E.activation(out=ssg[:, 0:32], in_=ssg[:, 0:32], func=AF.Exp, scale=-0.5), reads=[ssg], writes=[ssg])
                sv = ssg.ap[:, 0:32].rearrange("p (j h) -> p j h", j=4)
                op("dve", lambda E: E.tensor_scalar(out=sv[:, :, 0:4], in0=sv[:, :, 0:4], scalar1=128 ** -0.5, scalar2=None, op0=ALU.mult), reads=[ssg], writes=[ssg])
                yield
                for j in range(4):
                    tv = tokst.ap[:, j, 0:1024].rearrange("p (h d) -> p h d", h=8)
                    e1 = "dve" if j % 2 == 0 else "pool"
                    op(e1, lambda E, tv=tv, j=j: E.tensor_tensor(out=tv, in0=tv, in1=ssg.ap[:, j * 8:(j + 1) * 8].unsqueeze(2).to_broadcast([128, 8, 128]), op=ALU.mult),
                       reads=[ssg] + tokc[0:8], writes=tokc[0:8])
                    yield
                dma(TOK[s][r0:r0 + 512, :].rearrange("(j p) c -> p j c", p=128), tokst.ap, reads=tokc, sb=tokst, queue="pool")

            items = []
            k = 0
            tix = 0
            for s, L in SEQ:
                for ti in range(L // 512):
                    r0 = ti * 512
                    tkst, tkc, ssg = tokst2[tix % 2], tokc2[tix % 2], ssg2[tix % 2]
                    tix += 1
                    for c in range(12):
                        items.append((gchunk(s, L, r0, c, k, tkst, tkc), False))
                        k += 1
                    items.append((gtail(s, r0, tkst, tkc, ssg), True))
            run_window(items, GW)
            phase_reset()

        if "H" in phases:
            gm = alloc("gm", 5 * 512)
            gm.ap = gm.ap.rearrange("p (m h j) -> p m h j", m=5, h=8)
            dma(gm.ap[0:64], GMASK[:, :, :, :], writes=[gm], sb=gm)
            NA, NAT, NQK = gm.ap[0:64, 0], gm.ap[0:64, 1], gm.ap[0:64, 2]
            trif, trib = gm.ap[0:64, 3, 0, :], gm.ap[0:64, 3, 1, :]
            idb8 = identb.ap[0:64, 0:64].unsqueeze(1).to_broadcast([64, 8, 64])
            idf8 = identf.ap[0:64, 0:64].unsqueeze(1).to_broadcast([64, 8, 64])

            def A3(name, n, dt=F32, parts=128):
                b = alloc(name, 8 * n, dt)
                b.ap = b.ap.rearrange("p (h x) -> p h x", h=8)
                return b
            NSET = 2
            tok = [alloc("tk%d" % i, 2 * 1536, BF16) for i in range(NSET)]
            gsel = [alloc("gsel%d" % i, 24) for i in range(NSET)]
            sm8 = [alloc("sm8_%d" % i, 64) for i in range(NSET)]
            ost = [alloc("ost%d" % i, 1024) for i in range(NSET)]
            SETS = []
            for i in range(NSET):
                d_ = {}
                d_["Dg"] = A3("Dg%d" % i, 64)
                d_["Dc"] = A3("Dc%d" % i, 64)
                d_["De"] = A3("De%d" % i, 64, BF16)
                kq_ = alloc("kqT%d" % i, 16 * 64, BF16)
                kq_.ap = kq_.ap.rearrange("p (h x) -> p h x", h=16)
                d_["kqT"] = kq_
                for nm in ("dA", "dAT", "dQK"):
                    d_[nm] = A3(nm + str(i), 64)
                d_["Xb"] = [A3("Xb%d_%d" % (i, k), 64, BF16) for k in range(2)]
                d_["Yb"] = [A3("Yb%d_%d" % (i, k), 64, BF16) for k in range(2)]
                d_["Zb"] = [A3("Zb%d_%d" % (i, k), 64, BF16) for k in range(2)]
                for nm, n_, dt_ in (("qkT", 64, BF16), ("kbg", 128, BF16), ("vb", 128, BF16), ("kg", 128, BF16), ("wT", 64, BF16),
                                   ("qgT", 64, BF16), ("uu", 128, F32), ("vnew", 128, BF16)):
                    d_[nm] = A3(nm + str(i), n_, dt_)
                SETS.append(d_)
            S = A3("S", 128)
            Sb = A3("Sb", 128, BF16)

            def step(s, N, n):
                st_ = SETS[n % NSET]
                c0 = 4 * (n % 2)
                c1, c2, c3 = c0 + 1, c0 + 2, c0 + 3
                Dg, Dc, De, kqT, dA, dAT, dQK = st_["Dg"], st_["Dc"], st_["De"], st_["kqT"], st_["dA"], st_["dAT"], st_["dQK"]
                Xb, Yb, Zb = st_["Xb"], st_["Yb"], st_["Zb"]
                qkT, kbg, vb, kg, wT, qgT, uu, vnew = (st_[k_] for k_ in ("qkT", "kbg", "vb", "kg", "wT", "qgT", "uu", "vnew"))
                cf, cb = n, N - 1 - n
                tk, gs, m8, os_ = tok[n % NSET], gsel[n % NSET], sm8[n % NSET], ost[n % NSET]
                tkv = tk.ap.rearrange("p (d c) -> p d c", d=2)
                for d, ch in ((0, cf), (1, cb)):
                    dma(tkv[0:64, d, :], TOK[s][ch * 64:(ch + 1) * 64, :], writes=[tk] if d == 0 else [], reads=[] if d == 0 else [tk], sb=tk)
                    dma(gs.ap[0:64, d * 12:(d + 1) * 12], GBL[s][d, ch * 64:(ch + 1) * 64, :], writes=[gs] if d == 0 else [], reads=[] if d == 0 else [gs], sb=gs)
                tk.lw = (id(tk), tk.semcnt)
                gs.lw = (id(gs), gs.semcnt)
                gv = gs.ap[0:64, :].rearrange("p (d k) -> p d k", d=2)
                g8, lnb8, beta8 = gv[:, :, 0:4], gv[:, :, 4:8], gv[:, :, 8:12]
                m = m8.ap[0:64, :]

                def v8(a, b):
                    return m8.ap[0:64, a:b].rearrange("p (d k) -> p d k", d=2)
                yield None
                op("pe", lambda E, gs=gs: E.matmul(psum[0:64, c0 * 512:c0 * 512 + 4], lhsT=trif, rhs=gs.ap[0:64, 0:4], start=True, stop=True), reads=[gm, gs], writes=[BK[c0]])
                op("pe", lambda E, gs=gs: E.matmul(psum[0:64, c0 * 512 + 4:c0 * 512 + 8], lhsT=trib, rhs=gs.ap[0:64, 12:16], start=True, stop=True), reads=[gm, gs], writes=[BK[c0]])
                op("pe", lambda E, g8=g8: E.matmul(psum[:, c0 * 512 + 8:c0 * 512 + 16].rearrange("p (d k) -> p d k", d=2), lhsT=onesf[0:64, :], rhs=g8, start=True, stop=True),
                   reads=[onesf, gs], writes=[BK[c0]])
                yield None
                op("dve", lambda E, m8=m8: E.tensor_copy(out=m8[0:64, 0:8], in_=psum[0:64, c0 * 512:c0 * 512 + 8]), writes=[m8, BK[c0]])
                op("dve", lambda E, m8=m8, lnb8=lnb8, v8=v8: E.tensor_tensor(out=v8(8, 16), in0=v8(0, 8), in1=lnb8, op=ALU.add), reads=[m8, gs], writes=[m8])
                op("act", lambda E, m8=m8: E.activation(out=m8[0:64, 16:24], in_=m8[0:64, 0:8], func=AF.Exp), reads=[m8], writes=[m8])
                op("dve", lambda E, m8=m8, beta8=beta8, v8=v8: E.tensor_tensor(out=v8(24, 32), in0=v8(16, 24), in1=beta8, op=ALU.mult), reads=[m8, gs], writes=[m8])
                op("dve", lambda E, m8=m8: E.tensor_tensor(out=m8[0:64, 40:48], in0=psum[0:64, c0 * 512 + 8:c0 * 512 + 16], in1=m8[0:64, 0:8], op=ALU.subtract), reads=[m8], writes=[m8, BK[c0]])
                op("act", lambda E, m8=m8: E.activation(out=m8[0:64, 32:40], in_=m8[0:64, 40:48], func=AF.Exp), reads=[m8], writes=[m8])
                op("act", lambda E, m8=m8: E.activation(out=m8[:, 48:56], in_=psum[:, c0 * 512 + 8:c0 * 512 + 16], func=AF.Exp), reads=[m8], writes=[m8, BK[c0]])
                yield None
                op("pool", lambda E, m8=m8: E.tensor_tensor(out=Dg.ap[0:64], in0=idf8, in1=m8.ap[0:64, 0:8].unsqueeze(2).to_broadcast([64, 8, 64]), op=ALU.mult),
                   reads=[identf, m8], writes=[Dg])
                op("pool", lambda E, m8=m8: E.tensor_tensor(out=Dc.ap[0:64], in0=idf8, in1=m8.ap[0:64, 8:16].unsqueeze(2).to_broadcast([64, 8, 64]), op=ALU.mult),
                   reads=[identf, m8], writes=[Dc])
                op("pool", lambda E, m8=m8: E.tensor_tensor(out=De.ap[0:64], in0=idf8, in1=m8.ap[0:64, 16:24].unsqueeze(2).to_broadcast([64, 8, 64]), op=ALU.mult),
                   reads=[identf, m8], writes=[De])
                yield None
                op("pe", lambda E: E.matmul(psum[0:64, c1 * 512:(c1 + 1) * 512], lhsT=onesf[0:64, 0:64], rhs=Dg.ap[0:64].rearrange("p h x -> p (h x)"), start=True, stop=True),
                   reads=[onesf, Dg], writes=[BK[c1]])
                op("pe", lambda E: E.matmul(psum[0:64, c2 * 512:(c2 + 1) * 512], lhsT=onesf[0:64, 0:64], rhs=Dc.ap[0:64].rearrange("p h x -> p (h x)"), start=True, stop=True),
                   reads=[onesf, Dc], writes=[BK[c2]])
                yield None
                for d in range(2):
                    for hh in range(4):
                        hd = d * 4 + hh
                        op("pe", lambda E, d=d, hh=hh, hd=hd, tkv=tkv: E.transpose(out=psum_bf[:, c3 * 1024 + hd * 64:c3 * 1024 + (hd + 1) * 64],
                                                                                 in_=tkv[0:64, d, 512 + hh * 128:512 + (hh + 1) * 128], identity=identb[0:64, 0:64]),
                           reads=[tk, identb], writes=[BK[c3]])
                        op("pe", lambda E, d=d, hh=hh, hd=hd, tkv=tkv: E.transpose(out=psum_bf[:, c3 * 1024 + 512 + hd * 64:c3 * 1024 + 512 + (hd + 1) * 64],
                                                                                 in_=tkv[0:64, d, hh * 128:(hh + 1) * 128], identity=identb[0:64, 0:64]),
                           reads=[tk, identb], writes=[BK[c3]])
                yield None
                op("act", lambda E: E.activation(out=kqT.ap, in_=pb(c3).rearrange("p (h x) -> p h x", h=16), func=AF.Copy), writes=[kqT, BK[c3]])
                yield None
                for hd in range(8):
                    op("pe", lambda E, hd=hd: E.matmul(psum[0:64, c0 * 512 + hd * 64:c0 * 512 + (hd + 1) * 64], lhsT=kqT.ap[:, hd, :], rhs=kqT.ap[:, hd, :], start=True, stop=True),
                       reads=[kqT], writes=[BK[c0]])
                for hd in range(8):
                    op("pe", lambda E, hd=hd: E.matmul(psum[0:64, c3 * 512 + hd * 64:c3 * 512 + (hd + 1) * 64], lhsT=kqT.ap[:, hd, :], rhs=kqT.ap[:, 8 + hd, :], start=True, stop=True),
                       reads=[kqT], writes=[BK[c3]])
                P1 = psum[0:64, c1 * 512:(c1 + 1) * 512].rearrange("p (h x) -> p h x", h=8)
                P2 = psum[0:64, c2 * 512:(c2 + 1) * 512].rearrange("p (h x) -> p h x", h=8)
                PG = psum[0:64, c0 * 512:(c0 + 1) * 512].rearrange("p (h x) -> p h x", h=8)
                PQ = psum[0:64, c3 * 512:(c3 + 1) * 512].rearrange("p (h x) -> p h x", h=8)

                def bc8(a):
                    return m8.ap[0:64, a:a + 8].unsqueeze(2).to_broadcast([64, 8, 64])
                yield None
                op("dve", lambda E: E.scalar_tensor_tensor(out=dA.ap[0:64], in0=P1, scalar=-1.0, in1=NA, op0=ALU.mult, op1=ALU.add), reads=[gm], writes=[dA, BK[c1]])
                op("pool", lambda E, bc8=bc8: E.tensor_tensor(out=dA.ap[0:64], in0=dA.ap[0:64], in1=bc8(8), op=ALU.add), reads=[m8, dA], writes=[dA])
                op("act", lambda E: E.activation(out=dA.ap[0:64], in_=dA.ap[0:64], func=AF.Exp), reads=[dA], writes=[dA])
                yield None
                op("dve", lambda E: E.tensor_tensor(out=dAT.ap[0:64], in0=P2, in1=NAT, op=ALU.add), reads=[gm], writes=[dAT, BK[c2]])
                op("pool", lambda E, bc8=bc8: E.tensor_tensor(out=dAT.ap[0:64], in0=dAT.ap[0:64], in1=bc8(0), op=ALU.subtract), reads=[m8, dAT], writes=[dAT])
                op("act", lambda E: E.activation(out=dAT.ap[0:64], in_=dAT.ap[0:64], func=AF.Exp), reads=[dAT], writes=[dAT])
                yield None
                op("dve", lambda E: E.tensor_tensor(out=dQK.ap[0:64], in0=P1, in1=NQK, op=ALU.add), reads=[gm], writes=[dQK, BK[c1]])
                op("pool", lambda E, bc8=bc8: E.tensor_tensor(out=dQK.ap[0:64], in0=dQK.ap[0:64], in1=bc8(0), op=ALU.subtract), reads=[m8, dQK], writes=[dQK])
                op("act", lambda E: E.activation(out=dQK.ap[0:64], in_=dQK.ap[0:64], func=AF.Exp), reads=[dQK], writes=[dQK])
                yield None
                X0, Y0, Z0 = Xb[0], Yb[0], Zb[0]
                op("dve", lambda E, X0=X0: E.scalar_tensor_tensor(out=X0.ap[0:64], in0=dA.ap[0:64], scalar=-1.0, in1=PG, op0=ALU.mult, op1=ALU.mult), reads=[dA], writes=[X0, BK[c0]])
                op("dve", lambda E, Y0=Y0: E.scalar_tensor_tensor(out=Y0.ap[0:64], in0=dAT.ap[0:64], scalar=-1.0, in1=PG, op0=ALU.mult, op1=ALU.mult), reads=[dAT], writes=[Y0, BK[c0]])
                op("dve", lambda E: E.tensor_tensor(out=qkT.ap[0:64], in0=dQK.ap[0:64], in1=PQ, op=ALU.mult), reads=[dQK], writes=[qkT, BK[c3]])
                op("pool", lambda E, Y0=Y0, Z0=Z0: E.tensor_tensor(out=Z0.ap[0:64], in0=Y0.ap[0:64], in1=idb8, op=ALU.add), reads=[Y0, identb], writes=[Z0])
                yield None
                for k in range(1, 6):
                    Xp, Yp, Zp = Xb[(k - 1) % 2], Yb[(k - 1) % 2], Zb[(k - 1) % 2]
                    Xn, Yn, Zn = Xb[k % 2], Yb[k % 2], Zb[k % 2]
                    for hd in range(8):
                        op("pe", lambda E, hd=hd, Xp=Xp, Yp=Yp: E.matmul(psum[0:64, c1 * 512 + hd * 64:c1 * 512 + (hd + 1) * 64], lhsT=Yp.ap[0:64, hd, :], rhs=Xp.ap[0:64, hd, :],
                                                                         start=True, stop=True), reads=[Xp, Yp], writes=[BK[c1]])
                    if k < 5:
                        for hd in range(8):
                            op("pe", lambda E, hd=hd, Xp=Xp, Yp=Yp: E.matmul(psum[0:64, c2 * 512 + hd * 64:c2 * 512 + (hd + 1) * 64], lhsT=Xp.ap[0:64, hd, :], rhs=Yp.ap[0:64, hd, :],
                                                                             start=True, stop=True), reads=[Xp, Yp], writes=[BK[c2]])
                    yield None
                    op("act", lambda E, Xn=Xn: E.activation(out=Xn.ap[0:64], in_=P1, func=AF.Copy), writes=[Xn, BK[c1]])
                    if k < 5:
                        op("dve", lambda E, Yn=Yn: E.tensor_copy(out=Yn.ap[0:64], in_=P2), writes=[Yn, BK[c2]])
                    yield None
                    for hd in range(8):
                        op("pe", lambda E, hd=hd, Xn=Xn, Zp=Zp: E.matmul(psum[0:64, c3 * 512 + hd * 64:c3 * 512 + (hd + 1) * 64], lhsT=Xn.ap[0:64, hd, :], rhs=Zp.ap[0:64, hd, :],
                                                                         start=True, stop=True), reads=[Xn, Zp], writes=[BK[c3]])
                    op("dve", lambda E, Zn=Zn, Zp=Zp: E.tensor_tensor(out=Zn.ap[0:64], in0=psum[0:64, c3 * 512:(c3 + 1) * 512].rearrange("p (h x) -> p h x", h=8), in1=Zp.ap[0:64], op=ALU.add),
                       reads=[Zp], writes=[Zn, BK[c3]])
                    yield None
                Z = Zb[5 % 2]
                yield None
                kv4 = tkv[0:64, :, 512:1024].rearrange("p d (h x) -> p d h x", h=4)
                vv4 = tkv[0:64, :, 1024:1536].rearrange("p d (h x) -> p d h x", h=4)

                def b4(a):
                    return m8.ap[0:64, a:a + 8].rearrange("p (d h) -> p d h", d=2).unsqueeze(3).to_broadcast([64, 2, 4, 128])
                op("pool", lambda E, kv4=kv4, b4=b4: E.tensor_tensor(out=kbg.ap[0:64].rearrange("p (d h) x -> p d h x", d=2), in0=kv4, in1=b4(24), op=ALU.mult),
                   reads=[tk, m8], writes=[kbg])
                op("dve", lambda E, vv4=vv4, gs=gs: E.tensor_tensor(out=vb.ap[0:64].rearrange("p (d h) x -> p d h x", d=2), in0=vv4,
                                                                    in1=gs.ap[0:64, :].rearrange("p (d k) -> p d k", d=2)[:, :, 8:12].unsqueeze(3).to_broadcast([64, 2, 4, 128]), op=ALU.mult),
                   reads=[tk, gs], writes=[vb])
                op("pool", lambda E, kv4=kv4, b4=b4: E.tensor_tensor(out=kg.ap[0:64].rearrange("p (d h) x -> p d h x", d=2), in0=kv4, in1=b4(32), op=ALU.mult),
                   reads=[tk, m8], writes=[kg])
                yield None
                for hd in range(8):
                    op("pe", lambda E, hd=hd, Z=Z: E.matmul(psum[:, c0 * 512 + hd * 64:c0 * 512 + (hd + 1) * 64], lhsT=kbg.ap[0:64, hd, :], rhs=Z.ap[0:64, hd, :], start=True, stop=True),
                       reads=[kbg, Z], writes=[BK[c0]])
                op("act", lambda E: E.activation(out=wT.ap, in_=pf(c0).rearrange("p (h x) -> p h x", h=8), func=AF.Copy), writes=[wT, BK[c0]])
                yield None
                for hd in range(8):
                    bk = c2 + hd // 4
                    op("pe", lambda E, hd=hd, Z=Z: E.matmul(psum[0:64, c2 * 512 + hd * 128:c2 * 512 + (hd + 1) * 128], lhsT=Z.ap[0:64, hd, :], rhs=vb.ap[0:64, hd, :], start=True, stop=True),
                       reads=[vb, Z], writes=[BK[bk]])
                op("act", lambda E: E.activation(out=uu.ap[0:64], in_=psum[0:64, c2 * 512:(c2 + 2) * 512].rearrange("p (h x) -> p h x", h=8), func=AF.Copy), writes=[uu, BK[c2], BK[c3]])
                yield None
                for d in range(2):
                    for hh in range(4):
                        hd = d * 4 + hh
                        op("pe", lambda E, d=d, hh=hh, hd=hd, tkv=tkv: E.matmul(psum[:, c1 * 512 + hd * 64:c1 * 512 + (hd + 1) * 64], lhsT=tkv[0:64, d, hh * 128:(hh + 1) * 128],
                                                                                rhs=De.ap[0:64, hd, :], start=True, stop=True), reads=[tk, De], writes=[BK[c1]])
                op("dve", lambda E: E.tensor_copy(out=qgT.ap, in_=pf(c1).rearrange("p (h x) -> p h x", h=8)), writes=[qgT, BK[c1]])
                yield "SCAN"
                for hd in range(8):
                    bk = c0 + hd // 4
                    op("pe", lambda E, hd=hd: E.matmul(psum[0:64, c0 * 512 + hd * 128:c0 * 512 + (hd + 1) * 128], lhsT=wT.ap[:, hd, :], rhs=Sb.ap[:, hd, :], start=True, stop=True),
                       reads=[wT, Sb], writes=[BK[bk]])
                yield None
                op("dve", lambda E: E.tensor_tensor(out=vnew.ap[0:64], in0=uu.ap[0:64], in1=psum[0:64, c0 * 512:(c0 + 2) * 512].rearrange("p (h x) -> p h x", h=8), op=ALU.subtract),
                   reads=[uu], writes=[vnew, BK[c0], BK[c1]])
                yield None
                for hd in range(8):
                    bk = c2 + hd // 4
                    op("pe", lambda E, hd=hd: E.matmul(psum[0:64, c2 * 512 + hd * 128:c2 * 512 + (hd + 1) * 128], lhsT=qgT.ap[:, hd, :], rhs=Sb.ap[:, hd, :], start=True, stop=False),
                       reads=[qgT, Sb], writes=[BK[bk]])
                    op("pe", lambda E, hd=hd: E.matmul(psum[0:64, c2 * 512 + hd * 128:c2 * 512 + (hd + 1) * 128], lhsT=qkT.ap[0:64, hd, :], rhs=vnew.ap[0:64, hd, :], start=False, stop=True),
                       reads=[qkT, vnew], writes=[BK[bk]])
                yield None
                op("act", lambda E, os_=os_: E.activation(out=os_[0:64, :], in_=psum[0:64, c2 * 512:(c2 + 2) * 512], func=AF.Copy), writes=[os_, BK[c2], BK[c3]])
                dma(OF[s][0, cf * 64:(cf + 1) * 64, :], os_[0:64, 0:512], reads=[os_], sb=os_, queue="pool")
                dma(OF[s][1, cb * 64:(cb + 1) * 64, :], os_[0:64, 512:1024], reads=[os_], sb=os_, queue="pool")
                yield None
                for hd in range(8):
                    bk = c0 + hd // 4
                    op("pe", lambda E, hd=hd: E.matmul(psum[:, c0 * 512 + hd * 128:c0 * 512 + (hd + 1) * 128], lhsT=kg.ap[0:64, hd, :], rhs=vnew.ap[0:64, hd, :], start=True, stop=True),
                       reads=[kg, vnew], writes=[BK[bk]])
                yield None
                op("pool", lambda E, m8=m8: E.tensor_tensor(out=S.ap, in0=S.ap, in1=m8.ap[:, 48:56].unsqueeze(2).to_broadcast([128, 8, 128]), op=ALU.mult),
                   reads=[m8, S], writes=[S])
                op("dve", lambda E: E.tensor_tensor(out=S.ap, in0=S.ap, in1=psum[:, c0 * 512:(c0 + 2) * 512].rearrange("p (h x) -> p h x", h=8), op=ALU.add),
                   reads=[S], writes=[S, BK[c0], BK[c1]])
                op("act", lambda E: E.activation(out=Sb.ap, in_=S.ap, func=AF.Copy), reads=[S], writes=[Sb])
            WIN = 2
            for s, L in SEQ:
                N = L // 64
                op("pool", lambda E: E.memset(S.ap, 0.0), writes=[S])
                op("pool", lambda E: E.memset(Sb.ap, 0.0), writes=[Sb])
                nxt = 0
                active = []
                while nxt < N or active:
                    while len(active) < WIN and nxt < N:
                        active.append([nxt, step(s, N, nxt), False])
                        nxt += 1
                    oldest = min(a_[0] for a_ in active)
                    for a_ in list(active):
                        if a_[2] and a_[0] != oldest:
                            continue
                        try:
                            r_ = next(a_[1])
                            a_[2] = (r_ == "SCAN")
                        except StopIteration:
                            active.remove(a_)
            phase_reset()

        if "C" in phases:
            T = ffn_bufs()
            load_wd(T, "ffn2_w_down")
            wst = alloc("wstc", 1024)
            wo = alloc("wo", 8 * 1024, BF16)
            wo.ap = wo.ap.rearrange("p (c f) -> p c f", c=8)
            for c in range(8):
                dma(wst[:], W["w_out"][c * 128:(c + 1) * 128, :], writes=[wst], sb=wst)
                sc = small[:, 5 + c:6 + c] if c < 4 else small[:, 9:10]
                op("dve", lambda E, c=c, sc=sc: E.tensor_scalar(out=wo.ap[:, c, :], in0=wst[:], scalar1=sc, scalar2=None, op0=ALU.mult),
                   reads=[wst, small], writes=[wo])
            ya = alloc("ya", 4 * 512)
            ya.ap = ya.ap.rearrange("p (j e) -> p j e", j=4)
            ofb = [alloc("ofb%d" % i, 4 * 512) for i in range(2)]
            for b in ofb:
                b.ap = b.ap.rearrange("p (j e) -> p j e", j=4)
            zz = alloc("zz", 4 * 512)
            zz.ap = zz.ap.rearrange("p (j e) -> p j e", j=4)
            mixb = [alloc("mixb%d" % i, 1024, BF16) for i in range(2)]
            sq = T["tmp"][1]
            ssc = alloc("ssc", 32)
            tilesC = [(s, L, ti) for s, L in SEQ for ti in range(L // 512)]
            ystg = Buf("ystg", arena[:, T["act_off"]:T["act_off"] + 4 * D].rearrange("p (j d) -> p j d", j=4))

            def load_h(ix):
                s_, L_, ti_ = tilesC[ix]
                dma(T["h"].ap, H1[s_][ti_ * 512:(ti_ + 1) * 512, :].rearrange("(j p) d -> p j d", p=128), writes=T["hb"], sb=T["h"])

            def load_front(ix):
                s_, L_, ti_ = tilesC[ix]
                rw = slice(ti_ * 512, (ti_ + 1) * 512)
                dma(ya.ap, YA[s_][rw, :].rearrange("(j p) e -> p j e", p=128), writes=[ya], sb=ya)
                dma(ofb[0].ap, OF[s_][0, rw, :].rearrange("(j p) e -> p j e", p=128), writes=[ofb[0]], sb=ofb[0])
                dma(ofb[1].ap, OF[s_][1, rw, :].rearrange("(j p) e -> p j e", p=128), writes=[ofb[1]], sb=ofb[1])
                dma(zz.ap, ZS[s_][rw, :].rearrange("(j p) e -> p j e", p=128), writes=[zz], sb=zz)
            load_h(0)
            load_front(0)
            for ix, (s, L, ti) in enumerate(tilesC):
                if True:
                    r0 = ti * 512
                    rows = slice(r0, r0 + 512)
                    h = T["h"]
                    o = ofb[0]
                    op("pool", lambda E: E.tensor_tensor(out=o.ap, in0=o.ap, in1=ofb[1].ap, op=ALU.add), reads=[ofb[1], o], writes=[o])
                    e2 = ofb[1]
                    op("act", lambda E: E.activation(out=e2.ap, in_=zz.ap, func=AF.Exp, scale=-1.0), reads=[zz], writes=[e2])
                    op("act", lambda E: E.activation(out=e2.ap, in_=e2.ap, func=AF.Ln, bias=1.0), reads=[e2], writes=[e2])
                    op("act", lambda E: E.activation(out=e2.ap, in_=e2.ap, func=AF.Exp, scale=-1.0), reads=[e2], writes=[e2])
                    op("pool", lambda E: E.tensor_tensor(out=zz.ap, in0=zz.ap, in1=e2.ap, op=ALU.mult), reads=[e2, zz], writes=[zz])
                    for j in range(4):
                        op("dve", lambda E, j=j: E.tensor_tensor(out=sq[:, 0:512], in0=o.ap[:, j, :], in1=o.ap[:, j, :], op=ALU.mult), reads=[o], writes=[sq])
                        op("dve", lambda E, j=j: E.tensor_reduce(out=ssc[:, j * 8:j * 8 + 4], in_=sq.ap[:, 0:512].rearrange("p (h d) -> p h d", h=4), axis=AX.X, op=ALU.add),
                           reads=[sq], writes=[ssc])
                        op("act", lambda E, j=j: E.activation(out=T["junk"][:, 0:512], in_=ya.ap[:, j, :], func=AF.Square, accum_out=ssc[:, j * 8 + 4:j * 8 + 5]),
                           reads=[ya], writes=[ssc])
                    sv = ssc.ap[:, 0:32].rearrange("p (j k) -> p j k", j=4)
                    op("dve", lambda E, sv=sv: E.tensor_scalar(out=sv[:, :, 0:4], in0=sv[:, :, 0:4], scalar1=1.0 / 128, scalar2=EPS, op0=ALU.mult, op1=ALU.add), reads=[ssc], writes=[ssc])
                    op("dve", lambda E, sv=sv: E.tensor_scalar(out=sv[:, :, 4:5], in0=sv[:, :, 4:5], scalar1=1.0 / 512, scalar2=EPS, op0=ALU.mult, op1=ALU.add), reads=[ssc], writes=[ssc])
                    op("act", lambda E, sv=sv: E.activation(out=sv[:, :, 0:5], in_=sv[:, :, 0:5], func=AF.Ln), reads=[ssc], writes=[ssc])
                    op("act", lambda E, sv=sv: E.activation(out=sv[:, :, 0:5], in_=sv[:, :, 0:5], func=AF.Exp, scale=-0.5), reads=[ssc], writes=[ssc])
                    xT = T["xT"]
                    for j in range(4):
                        mb = mixb[j % 2]
                        op("act", lambda E, j=j, mb=mb: E.activation(out=mb[:, 0:512], in_=ya.ap[:, j, :], func=AF.Copy, scale=ssc[:, j * 8 + 4:j * 8 + 5]),
                           reads=[ya, ssc], writes=[mb])
                        op("dve", lambda E, j=j: E.tensor_tensor(out=o.ap[:, j, :].rearrange("p (h d) -> p h d", h=4), in0=o.ap[:, j, :].rearrange("p (h d) -> p h d", h=4),
                                                                 in1=ssc.ap[:, j * 8:j * 8 + 4].unsqueeze(2).to_broadcast([128, 4, 128]), op=ALU.mult), reads=[ssc, o], writes=[o])
                        op("pool", lambda E, j=j, mb=mb: E.tensor_tensor(out=mb[:, 512:1024], in0=o.ap[:, j, :], in1=zz.ap[:, j, :], op=ALU.mult), reads=[o, zz, mb], writes=[mb])
                        bk = j % 2
                        for dc in range(8):
                            op("pe", lambda E, dc=dc, bk=bk, mb=mb: E.transpose(out=pb(bk, dc * 128, (dc + 1) * 128), in_=mb[:, dc * 128:(dc + 1) * 128], identity=identb[:]),
                               reads=[mb, identb], writes=[BK[bk]])
                        op("dve", lambda E, j=j, bk=bk: E.tensor_copy(out=xT.ap[:, :, j * 128:(j + 1) * 128], in_=pb(bk).rearrange("p (c t) -> p c t", c=8)),
                           writes=[xT, BK[bk]])
                    if ix + 1 < len(tilesC):
                        load_front(ix + 1)
                    ss2 = T["ss2"]
                    for j in range(4):
                        b0 = 2 if j % 2 == 0 else 4
                        for hf in range(2):
                            for c in range(8):
                                op("pe", lambda E, c=c, hf=hf, b0=b0, j=j: E.matmul(pf(b0 + hf), lhsT=xT.ap[:, c, j * 128:(j + 1) * 128], rhs=wo.ap[:, c, hf * 512:(hf + 1) * 512],
                                                                                     start=(c == 0), stop=(c == 7)), reads=[xT, wo], writes=[BK[b0 + hf]])
                        yps = psum[:, b0 * 512:(b0 + 2) * 512]
                        op("act", lambda E, j=j, yps=yps: E.activation(out=T["junk"][:], in_=yps, func=AF.Square, accum_out=ss2[:, j:j + 1]), writes=[ss2, BK[b0], BK[b0 + 1]])
                        op("dve", lambda E, j=j: E.tensor_scalar(out=ss2[:, 4 + j:5 + j], in0=ss2[:, j:j + 1], scalar1=1.0 / D, scalar2=EPS, op0=ALU.mult, op1=ALU.add), reads=[ss2], writes=[ss2])
                        op("act", lambda E, j=j: E.activation(out=ss2[:, 4 + j:5 + j], in_=ss2[:, 4 + j:5 + j], func=AF.Ln), reads=[ss2], writes=[ss2])
                        op("act", lambda E, j=j: E.activation(out=ss2[:, 4 + j:5 + j], in_=ss2[:, 4 + j:5 + j], func=AF.Exp, scale=-0.5), reads=[ss2], writes=[ss2])
                        tmp = T["tmp"][j % 2]
                        op("dve", lambda E, j=j, yps=yps, tmp=tmp: E.scalar_tensor_tensor(out=tmp[:], in0=yps, scalar=ss2[:, 4 + j:5 + j], in1=gB["mix_post_g"][:], op0=ALU.mult, op1=ALU.mult),
                           reads=[ss2, gB["mix_post_g"]], writes=[tmp, BK[b0], BK[b0 + 1]])
                        op("pool", lambda E, j=j, tmp=tmp: E.tensor_tensor(out=T["hs"][j], in0=T["hs"][j], in1=tmp[:], op=ALU.add), reads=[tmp], writes=[T["hb"][j]])
                    ffn(T["hb"], T["hs"], "ffn2_pre_g", WB["ffn2_w_gate"], WB["ffn2_w_up"], T["wd"], gB["ffn2_post_g"], T)
                    ss = T["ss"]
                    for j in range(4):
                        op("act", lambda E, j=j: E.activation(out=T["junk"][:], in_=T["hs"][j], func=AF.Square, accum_out=ss[:, j:j + 1]), reads=[T["hb"][j]], writes=[ss])
                    rstd_chain(ss, 4, D)
                    for j in range(4):
                        e1 = "dve"
                        op(e1, lambda E, j=j: E.scalar_tensor_tensor(out=ystg.ap[:, j, :], in0=T["hs"][j], scalar=ss[:, j:j + 1], in1=gB["final_norm_g"][:], op0=ALU.mult, op1=ALU.mult),
                           reads=[ss, gB["final_norm_g"], T["hb"][j]], writes=T["actc"][4 * j:4 * j + 4])
                    if ix + 1 < len(tilesC):
                        load_h(ix + 1)
                    dma(Y[s][rows, :].rearrange("(j p) d -> p j d", p=128), ystg.ap, reads=T["actc"][0:16], sb=ystg, queue="pool")
            phase_reset()
        P.barrier()
        P.emit()
        stats = dict(nins=P.nins, nwaits=P.nwaits, nsem=len(P.dmabufs) + 5)
    return nc, stats


def rope_table(L):
    inv = 10000.0 ** (-np.arange(0, 32, 2, dtype=np.float32) / 32)
    ang = np.arange(L, dtype=np.float32)[:, None] * inv[None, :].astype(np.float32)
    c, s = np.cos(ang).astype(np.float32), np.sin(ang).astype(np.float32)
    return np.ascontiguousarray(np.concatenate([c, c, -s, s], axis=1).astype(np.float32))


def gdn_masks():
    i = np.arange(64)
    m = np.zeros((64, 5, 8, 64), np.float32)
    for hd in range(8):
        fwd = hd < 4
        al = (i[:, None] > i[None, :]) if fwd else (i[:, None] < i[None, :])
        m[:, 0, hd, :] = np.where(al, 0.0, NEG)
        al = (i[None, :] > i[:, None]) if fwd else (i[None, :] < i[:, None])
        m[:, 1, hd, :] = np.where(al, 0.0, NEG)
        al = (i[None, :] >= i[:, None]) if fwd else (i[None, :] <= i[:, None])
        m[:, 2, hd, :] = np.where(al, 0.0, NEG)
    m[:, 3, 0, :] = (i[:, None] <= i[None, :]).astype(np.float32)
    m[:, 3, 1, :] = (i[:, None] >= i[None, :]).astype(np.float32)
    return m


_CACHE = {}


def kernel(**inputs):
    xp = np.asarray(inputs["x_prompt"], np.float32)
    xs = np.asarray(inputs["x_sample"], np.float32)
    B, Lp, _ = xp.shape
    Ls = xs.shape[1]
    assert B == 8 and xs.shape[0] == 8
    key = (Lp, Ls)
    if key not in _CACHE:
        _CACHE[key] = build(Lp, Ls)[0]
    nc = _CACHE[key]
    shared = {n: np.ascontiguousarray(np.asarray(inputs[n], np.float32).reshape(WSHAPES[n])) for n in WNAMES}
    shared["rope_tab"] = rope_table(max(Lp, Ls))
    shared["gdn_masks"] = gdn_masks()
    in_maps = []
    for c in range(8):
        m = dict(shared)
        m["x_p"] = np.ascontiguousarray(xp[c])
        m["x_s"] = np.ascontiguousarray(xs[c])
        in_maps.append(m)
    res = run_bass_kernel_spmd(nc, in_maps, core_ids=list(range(8)))
    yp = np.stack([np.asarray(r["y_p"], np.float32) for r in res.results], 0)
    ys = np.stack([np.asarray(r["y_s"], np.float32) for r in res.results], 0)
    return (yp, ys)
```

```python
from contextlib import ExitStack
import numpy as np
import concourse.bass as bass
import concourse.mybir as mybir
from concourse.bass_utils import run_bass_kernel_spmd

F32 = mybir.dt.float32
BF16 = mybir.dt.bfloat16
ALU = mybir.AluOpType
AF = mybir.ActivationFunctionType
AX = mybir.AxisListType

D = 1024
DFF = 2816
NF = DFF // 128
INC = 2736
EPS = 1e-6
NEG = -30000.0
ENG = ("pe", "act", "dve", "pool", "sp")


class Buf:
    __slots__ = ("name", "ap", "lw", "rd", "sem", "semcnt")

    def __init__(self, name, ap):
        self.name = name
        self.ap = ap
        self.lw = None
        self.rd = {}
        self.sem = None
        self.semcnt = 0

    def __getitem__(self, k):
        return self.ap[k]


class Prog:
    def __init__(self, nc, stack):
        self.nc = nc
        self.stack = stack
        self.q = {e: [] for e in ENG}
        self.cnt = {e: 0 for e in ENG}
        self.known = {e: {} for e in ENG}
        self.hist = {}
        self.esem = {e: stack.enter_context(nc.semaphore("s_" + e)) for e in ENG}
        self.semobj = {e: self.esem[e] for e in ENG}
        self.dmabufs = []
        self.nwaits = 0
        self.nins = 0
        self.E = {"pe": nc.tensor, "act": nc.scalar, "dve": nc.vector, "pool": nc.gpsimd, "sp": nc.sync}

    def _need(self, deps, tok):
        if tok is None:
            return
        k, v = tok
        if deps.get(k, 0) < v:
            deps[k] = v

    def _collect(self, reads, writes):
        deps = {}
        for b in reads:
            self._need(deps, b.lw)
        for b in writes:
            self._need(deps, b.lw)
            for k, v in b.rd.items():
                self._need(deps, (k, v))
        return deps

    def _emit_waits(self, eng, deps, defer=False):
        kn = self.known[eng]
        new = None
        pend = []
        for k, v in deps.items():
            if k == eng:
                if eng == "pe" or eng == "sp":
                    continue
                if self.cnt[eng] - v > 1:
                    continue
            cur = new if new is not None else kn
            if cur.get(k, 0) >= v:
                continue
            sem = self.semobj[k]
            pend.append((sem, v))
            self.nwaits += 1
            if new is None:
                new = dict(kn)
            new[k] = v
            h = self.hist.get((k, v))
            if h:
                for k2, v2 in h.items():
                    if k2 != eng and new.get(k2, 0) < v2:
                        new[k2] = v2
        if new is not None:
            self.known[eng] = new
        last = pend.pop() if (defer and pend) else None
        for sem, v in pend:
            self.E[eng].wait_ge(sem, v)
        return last

    def op(self, eng, fn, reads=(), writes=()):
        deps = self._collect(reads, writes)
        last = self._emit_waits(eng, deps, defer=True)
        sem = self.esem[eng]
        self.cnt[eng] += 1
        v = self.cnt[eng]
        ins = fn(self.E[eng])
        if last is not None:
            ins._wait_ge(last[0], last[1])
        ins.then_inc(sem, 1)
        self.nins += 1
        tok = (eng, v)
        self.hist[tok] = self.known[eng]
        for b in writes:
            b.lw = tok
            b.rd = {}
        for b in reads:
            if b.rd.get(eng, 0) < v:
                b.rd[eng] = v
        return tok

    def dma(self, out_ap, in_ap, reads=(), writes=(), sb=None, queue="sp", **kw):
        deps = self._collect(reads, writes)
        last = self._emit_waits(queue, deps, defer=True)
        if sb.sem is None:
            sb.sem = self.stack.enter_context(self.nc.semaphore("d%d_%s" % (len(self.dmabufs), sb.name)))
            self.semobj[id(sb)] = sb.sem
            self.dmabufs.append(sb)
        sb.semcnt += 16
        v = sb.semcnt
        sem = sb.sem
        ins = self.E[queue].dma_start(out=out_ap, in_=in_ap, **kw)
        if last is not None:
            ins._wait_ge(last[0], last[1])
        ins.then_inc(sem, 16)
        self.nins += 1
        tok = (id(sb), v)
        self.hist[tok] = self.known[queue]
        for b in writes:
            b.lw = tok
            b.rd = {}
        for b in reads:
            if b.rd.get(tok[0], 0) < v:
                b.rd[tok[0]] = v
        return tok

    def barrier(self):
        deps = {}
        for e in ("pe", "act", "dve", "pool"):
            if self.cnt[e]:
                deps[e] = self.cnt[e]
        for b in self.dmabufs:
            deps[id(b)] = b.semcnt
        for e in ENG:
            d = {k: v for k, v in deps.items() if k != e}
            self._emit_waits(e, d)

    def emit(self):
        pass


WNAMES = ["ffn1_pre_g", "ffn1_w_gate", "ffn1_w_up", "ffn1_w_down", "ffn1_post_g", "mix_pre_g", "w_in",
          "mla_q_norm_g", "mla_w_uq", "mla_kv_norm_g", "mla_w_ukv", "mla_out_norm_g", "gdn_conv_w",
          "gdn_a_log", "gdn_dt_bias", "gdn_out_norm_g", "w_out", "mix_post_g", "ffn2_pre_g",
          "ffn2_w_gate", "ffn2_w_up", "ffn2_w_down", "ffn2_post_g", "final_norm_g"]
WSHAPES = {"ffn1_pre_g": [1, D], "ffn1_w_gate": [D, DFF], "ffn1_w_up": [D, DFF], "ffn1_w_down": [DFF, D],
           "ffn1_post_g": [1, D], "mix_pre_g": [1, D], "w_in": [D, INC], "mla_q_norm_g": [1, 384],
           "mla_w_uq": [384, 768], "mla_kv_norm_g": [1, 256], "mla_w_ukv": [256, 1024],
           "mla_out_norm_g": [1, 512], "gdn_conv_w": [5, 1536], "gdn_a_log": [1, 8], "gdn_dt_bias": [1, 8],
           "gdn_out_norm_g": [1, 128], "w_out": [D, D], "mix_post_g": [1, D], "ffn2_pre_g": [1, D],
           "ffn2_w_gate": [D, DFF], "ffn2_w_up": [D, DFF], "ffn2_w_down": [DFF, D], "ffn2_post_g": [1, D],
           "final_norm_g": [1, D]}


def build(Lp, Ls, debug=False, phases="S A M B G H C"):
    phases = phases.split()
    nc = bass.Bass("TRN2", target_bir_lowering=False)
    Lmax = max(Lp, Ls)
    SEQ = (("p", Lp), ("s", Ls))

    def din(name, shape, dt=F32):
        return nc.dram_tensor(name, list(shape), dt, kind="ExternalInput").ap()

    def dscr(name, shape, dt=F32):
        return nc.dram_tensor(name, list(shape), dt, kind="ExternalOutput" if debug else "Internal").ap()

    X = {"p": din("x_p", [Lp, D]), "s": din("x_s", [Ls, D])}
    W = {n: din(n, WSHAPES[n]) for n in WNAMES}
    ROPE = din("rope_tab", [Lmax, 64])
    GMASK = din("gdn_masks", [64, 5, 8, 64])
    Y = {"p": nc.dram_tensor("y_p", [Lp, D], F32, kind="ExternalOutput").ap(),
         "s": nc.dram_tensor("y_s", [Ls, D], F32, kind="ExternalOutput").ap()}
    WB = {n: dscr("bf_" + n, WSHAPES[n], BF16) for n in
          ["ffn1_w_gate", "ffn1_w_up", "ffn1_w_down", "w_in", "ffn2_w_gate", "ffn2_w_up", "ffn2_w_down"]}
    H1 = {s: dscr("h1_" + s, [L, D]) for s, L in SEQ}
    LAT = {s: dscr("lat_" + s, [L, 672]) for s, L in SEQ}
    QKVT = {s: dscr("qkvT_" + s, [1536, L]) for s, L in SEQ}
    ZS = {s: dscr("z_" + s, [L, 512]) for s, L in SEQ}
    GBL = {s: dscr("gbl_" + s, [2, L, 12]) for s, L in SEQ}
    QT = {s: dscr("QT_" + s, [8, 96, L], BF16) for s, L in SEQ}
    KT = {s: dscr("KT_" + s, [8, 96, L], BF16) for s, L in SEQ}
    VV = {s: dscr("V_" + s, [8, 128, L // 128, 64], BF16) for s, L in SEQ}
    YA = {s: dscr("ya_" + s, [L, 512]) for s, L in SEQ}
    TOK = {s: dscr("tok_" + s, [L, 1536], BF16) for s, L in SEQ}
    OF = {s: dscr("of_" + s, [2, L, 512]) for s, L in SEQ}

    with ExitStack() as st:
        P = Prog(nc, st)
        ARENA = 52800
        arena = st.enter_context(nc.sbuf_tensor("arena", [128, ARENA], F32))
        psum = st.enter_context(nc.psum_tensor("psum", [128, 4096], F32))
        psum_bf = psum.bitcast(BF16)
        BK = [Buf("bank%d" % i, psum[:, i * 512:(i + 1) * 512]) for i in range(8)]

        def pf(i, a=0, b=512):
            return psum[:, i * 512 + a:i * 512 + b]

        def pb(i, a=0, b=1024):
            return psum_bf[:, i * 1024 + a:i * 1024 + b]

        off = [0]
        perm_end = [0]

        def alloc(name, ncols, dt=F32, parts=128):
            n32 = ncols if dt == F32 else (ncols + 1) // 2
            assert off[0] + n32 <= ARENA, ("SBUF arena overflow", name, off[0] + n32)
            a = arena[0:parts, off[0]:off[0] + n32]
            off[0] += n32
            if dt != F32:
                a = a.bitcast(dt)[:, 0:ncols]
            return Buf(name, a)

        def phase_reset():
            P.barrier()
            off[0] = perm_end[0]

        op = P.op
        dma = P.dma

        identf = alloc("identf", 128)
        identb = alloc("identb", 128, BF16)
        onesf = alloc("onesf", 128)
        gT = {n: alloc("gT_" + n, 8) for n in ("ffn1_pre_g", "mix_pre_g", "ffn2_pre_g")}
        gB = {n: alloc("gB_" + n, D) for n in ("ffn1_post_g", "mix_post_g", "ffn2_post_g", "final_norm_g")}
        small = alloc("small", 64)
        perm_end[0] = off[0]

        op("pool", lambda E: E.memset(identf[:], 0.0), writes=[identf])
        op("pool", lambda E: E.affine_select(out=identf[:], in_=identf[:], pattern=[[-1, 128]], compare_op=ALU.not_equal,
                                             fill=1.0, base=0, channel_multiplier=1), reads=[identf], writes=[identf])
        op("dve", lambda E: E.tensor_copy(out=identb[:], in_=identf[:]), reads=[identf], writes=[identb])
        op("pool", lambda E: E.memset(onesf[:], 1.0), writes=[onesf])
        for n, b in gT.items():
            dma(b[:], W[n].rearrange("o (c p) -> p (o c)", p=128), writes=[b], sb=b, allow_slow_non_contiguous=True)
        for n, b in gB.items():
            dma(b[:], W[n].broadcast_to([128, D]), writes=[b], sb=b)
        for n in ("ffn1_post_g", "ffn2_post_g"):
            b = gB[n]
            op("pool", lambda E, b=b: E.tensor_scalar(out=b[:], in0=b[:], scalar1=0.5, scalar2=None, op0=ALU.mult),
               reads=[b], writes=[b])
        sm_q = Buf("sm_q", small.ap)
        dma(small[:, 0:3], W["mla_q_norm_g"].rearrange("o (c p) -> p (o c)", p=128), writes=[small], sb=small,
            allow_slow_non_contiguous=True)
        dma(small[:, 3:5], W["mla_kv_norm_g"].rearrange("o (c p) -> p (o c)", p=128), reads=[small], sb=small,
            allow_slow_non_contiguous=True)
        dma(small[:, 5:9], W["mla_out_norm_g"].rearrange("o (c p) -> p (o c)", p=128), reads=[small], sb=small,
            allow_slow_non_contiguous=True)
        dma(small[:, 9:10], W["gdn_out_norm_g"].rearrange("o (c p) -> p (o c)", p=128), reads=[small], sb=small,
            allow_slow_non_contiguous=True)
        dma(small[:, 16:24], W["gdn_dt_bias"].broadcast_to([128, 8]), reads=[small], sb=small)
        dma(small[:, 24:32], W["gdn_a_log"].broadcast_to([128, 8]), reads=[small], sb=small)
        small.lw = (id(small), small.semcnt)
        op("act", lambda E: E.activation(out=small[:, 24:32], in_=small[:, 24:32], func=AF.Exp), reads=[small], writes=[small])
        op("dve", lambda E: E.tensor_scalar(out=small[:, 24:32], in0=small[:, 24:32], scalar1=-1.0, scalar2=None, op0=ALU.mult),
           reads=[small], writes=[small])

        def rstd_chain(ss, k, n, eps=EPS, post=None):
            op("dve", lambda E: E.tensor_scalar(out=ss[:, 0:k], in0=ss[:, 0:k], scalar1=1.0 / n, scalar2=eps, op0=ALU.mult, op1=ALU.add),
               reads=[ss], writes=[ss])
            op("act", lambda E: E.activation(out=ss[:, 0:k], in_=ss[:, 0:k], func=AF.Ln), reads=[ss], writes=[ss])
            op("act", lambda E: E.activation(out=ss[:, 0:k], in_=ss[:, 0:k], func=AF.Exp, scale=-0.5), reads=[ss], writes=[ss])

        cast_rr = [0]

        def cast(out_ap, in_ap, reads, writes):
            e = ("dve", "pool", "act")[cast_rr[0] % 3]
            cast_rr[0] += 1
            if e == "act":
                op("act", lambda E: E.activation(out=out_ap, in_=in_ap, func=AF.Copy), reads=reads, writes=writes)
            else:
                op(e, lambda E: E.tensor_copy(out=out_ap, in_=in_ap), reads=reads, writes=writes)

        if "S" in phases:
            stf = [alloc("stf%d" % i, DFF) for i in range(3)]
            stb = [alloc("stb%d" % i, DFF, BF16) for i in range(3)]
            it = 0
            for n in WB:
                K_, N_ = WSHAPES[n]
                for kc in range(K_ // 128):
                    f, b = stf[it % 3], stb[it % 3]
                    dma(f[:, 0:N_], W[n][kc * 128:(kc + 1) * 128, :], writes=[f], sb=f)
                    cast(b[:, 0:N_], f[:, 0:N_], [f], [b])
                    dma(WB[n][kc * 128:(kc + 1) * 128, :], b[:, 0:N_], reads=[b], sb=b, queue="pool")
                    it += 1
        phase_reset()

        def norm_T(srcs, src_bufs, gTb, xT, ss, xnb, junk):
            for j in range(4):
                op("act", lambda E, j=j: E.activation(out=junk[:], in_=srcs[j], func=AF.Square, accum_out=ss[:, j:j + 1]),
                   reads=[src_bufs[j]], writes=[ss])
            rstd_chain(ss, 4, D)
            for j in range(4):
                xb = xnb[j % 2]
                if j % 2 == 0:
                    op("dve", lambda E, j=j, xb=xb: E.tensor_scalar(out=xb[:], in0=srcs[j], scalar1=ss[:, j:j + 1], scalar2=None, op0=ALU.mult),
                       reads=[src_bufs[j], ss], writes=[xb])
                else:
                    op("act", lambda E, j=j, xb=xb: E.activation(out=xb[:], in_=srcs[j], func=AF.Copy, scale=ss[:, j:j + 1]),
                       reads=[src_bufs[j], ss], writes=[xb])
                bk = j % 2
                for dc in range(8):
                    op("pe", lambda E, dc=dc, bk=bk, xb=xb: E.transpose(out=pb(bk, dc * 128, (dc + 1) * 128), in_=xb[:, dc * 128:(dc + 1) * 128], identity=identb[:]),
                       reads=[xb, identb], writes=[BK[bk]])
                op("dve", lambda E, j=j, bk=bk: E.tensor_tensor(
                    out=xT.ap[:, :, j * 128:(j + 1) * 128], in0=pb(bk).rearrange("p (c t) -> p c t", c=8),
                    in1=gTb[:, 0:8].unsqueeze(2).to_broadcast([128, 8, 128]), op=ALU.mult),
                    reads=[gTb], writes=[xT, BK[bk]])

        PIECES = [(c0_, 256) for c0_ in range(0, DFF, 256)]

        def ffn(hbufs, hsrcs, pre_g, wg, wu, wd_res, post_bc, T):
            norm_T(hsrcs, hbufs, gT[pre_g], T["xT"], T["ss"], T["xnb"], T["junk"])
            xT, act = T["xT"], T["act"]
            for si, (c0, w) in enumerate(PIECES):
                sg = T["hsl"][T["hi"] % len(T["hsl"])]
                su = T["hsl"][(T["hi"] + 1) % len(T["hsl"])]
                T["hi"] += 2
                dma(sg.ap[:, :, 0:w], wg[:, c0:c0 + w].rearrange("(dc p) f -> p dc f", p=128), writes=[sg], sb=sg)
                dma(su.ap[:, :, 0:w], wu[:, c0:c0 + w].rearrange("(dc p) f -> p dc f", p=128), writes=[su], sb=su)
                for fl in range(w // 128):
                    f = c0 // 128 + fl
                    g_, u_ = 2 + f % 2, 4 + f % 2
                    for dc in range(8):
                        op("pe", lambda E, dc=dc, fl=fl, g_=g_, sg=sg: E.matmul(pf(g_), lhsT=sg.ap[:, dc, fl * 128:(fl + 1) * 128], rhs=xT.ap[:, dc, :],
                                                                                  start=(dc == 0), stop=(dc == 7)), reads=[sg, xT], writes=[BK[g_]])
                    for dc in range(8):
                        op("pe", lambda E, dc=dc, fl=fl, u_=u_, su=su: E.matmul(pf(u_), lhsT=su.ap[:, dc, fl * 128:(fl + 1) * 128], rhs=xT.ap[:, dc, :],
                                                                                  start=(dc == 0), stop=(dc == 7)), reads=[su, xT], writes=[BK[u_]])
                    et = T["etmp"][f % 2]
                    op("act", lambda E, et=et, g_=g_: E.activation(out=et[:], in_=pf(g_), func=AF.Exp, scale=-1.0), writes=[et, BK[g_]])
                    op("act", lambda E, et=et: E.activation(out=et[:], in_=et[:], func=AF.Ln, bias=1.0), reads=[et], writes=[et])
                    op("act", lambda E, et=et: E.activation(out=et[:], in_=et[:], func=AF.Exp, scale=-1.0), reads=[et], writes=[et])
                    op("dve", lambda E, et=et, g_=g_: E.tensor_tensor(out=et[:], in0=et[:], in1=pf(g_), op=ALU.mult), reads=[et], writes=[et, BK[g_]])
                    op("dve", lambda E, et=et, u_=u_, f=f: E.tensor_tensor(out=act.ap[:, f, :], in0=et[:], in1=pf(u_), op=ALU.mult),
                       reads=[et], writes=[T["actc"][f], BK[u_]])
            ss2 = T["ss2"]
            for j in range(4):
                b0 = 2 if j % 2 == 0 else 4
                for hf in range(2):
                    for f in range(NF):
                        op("pe", lambda E, f=f, hf=hf, b0=b0, j=j: E.matmul(pf(b0 + hf), lhsT=act.ap[:, f, j * 128:(j + 1) * 128],
                                                                             rhs=wd_res.ap[:, f, hf * 512:(hf + 1) * 512], start=(f == 0), stop=(f == NF - 1)),
                           reads=[T["actc"][f], wd_res], writes=[BK[b0 + hf]])
                yps = psum[:, b0 * 512:(b0 + 2) * 512]
                op("act", lambda E, j=j, yps=yps: E.activation(out=T["junk"][:], in_=yps, func=AF.Square, accum_out=ss2[:, j:j + 1]),
                   writes=[ss2, BK[b0], BK[b0 + 1]])
                op("dve", lambda E, j=j: E.tensor_scalar(out=ss2[:, 4 + j:5 + j], in0=ss2[:, j:j + 1], scalar1=1.0 / D, scalar2=EPS, op0=ALU.mult, op1=ALU.add),
                   reads=[ss2], writes=[ss2])
                op("act", lambda E, j=j: E.activation(out=ss2[:, 4 + j:5 + j], in_=ss2[:, 4 + j:5 + j], func=AF.Ln), reads=[ss2], writes=[ss2])
                op("act", lambda E, j=j: E.activation(out=ss2[:, 4 + j:5 + j], in_=ss2[:, 4 + j:5 + j], func=AF.Exp, scale=-0.5), reads=[ss2], writes=[ss2])
                tmp = T["tmp"][j % 2]
                op("dve", lambda E, j=j, yps=yps, tmp=tmp: E.scalar_tensor_tensor(out=tmp[:], in0=yps, scalar=ss2[:, 4 + j:5 + j], in1=post_bc[:],
                                                                                    op0=ALU.mult, op1=ALU.mult),
                   reads=[ss2, post_bc], writes=[tmp, BK[b0], BK[b0 + 1]])
                op("pool", lambda E, j=j, tmp=tmp: E.tensor_tensor(out=hsrcs[j], in0=hsrcs[j], in1=tmp[:], op=ALU.add),
                   reads=[tmp], writes=[hbufs[j]])

        def ffn_bufs(with_full=False):
            T = {}
            T["wd"] = alloc("wd_res", NF * D, BF16)
            T["wd"].ap = T["wd"].ap.rearrange("p (f d) -> p f d", f=NF)
            T["xT"] = alloc("xT", 8 * 512, BF16)
            T["xT"].ap = T["xT"].ap.rearrange("p (c t) -> p c t", c=8)
            T["act_off"] = off[0]
            T["act"] = alloc("act", NF * 512, BF16)
            T["act"].ap = T["act"].ap.rearrange("p (f t) -> p f t", f=NF)
            T["actc"] = [Buf("act%d" % f, T["act"].ap[:, f, :]) for f in range(NF)]
            T["slab"] = []
            for i in range(3 if with_full else 0):
                b = alloc("slab%d" % i, 8 * 512, BF16)
                b.ap = b.ap.rearrange("p (c f) -> p c f", c=8)
                T["slab"].append(b)
            T["slabi"] = 0
            T["hsl"] = []
            for i in range(8 if with_full else 6):
                b = alloc("hsl%d" % i, 8 * 256, BF16)
                b.ap = b.ap.rearrange("p (c f) -> p c f", c=8)
                T["hsl"].append(b)
            T["hi"] = 0
            T["etmp"] = [alloc("etmp%d" % i, 512) for i in range(2)]
            T["tmp"] = [alloc("tmp%d" % i, D) for i in range(2)]
            T["xnb"] = [alloc("xnb%d" % i, D, BF16) for i in range(2)]
            T["junk"] = alloc("junk", D, BF16)
            T["ss"] = alloc("ss", 8)
            T["ss2"] = alloc("ss2", 8)
            T["h"] = alloc("h", 4 * D)
            T["h"].ap = T["h"].ap.rearrange("p (j d) -> p j d", j=4)
            T["hb"] = [Buf("h%d" % j, T["h"].ap[:, j, :]) for j in range(4)]
            T["hs"] = [T["h"].ap[:, j, :] for j in range(4)]
            return T

        def load_wd(T, name):
            wd = T["wd"]
            for f0 in range(0, NF, 2):
                dma(wd.ap[:, f0:f0 + 2, :], WB[name][f0 * 128:(f0 + 2) * 128, :].rearrange("(f p) d -> p f d", p=128),
                    writes=[wd] if f0 == 0 else [], reads=[] if f0 == 0 else [], sb=wd)
            wd.lw = (id(wd), wd.semcnt)

        if "A" in phases:
            T = ffn_bufs(True)
            load_wd(T, "ffn1_w_down")
            pst = [alloc("pst%d" % i, 1200) for i in range(2)]
            qst = [alloc("qst%d" % i, 512) for i in range(3)]
            gst = [alloc("gst%d" % i, 64) for i in range(2)]
            pi = 0
            qi = 0
            tilesA = [(s, L, ti) for s, L in SEQ for ti in range(L // 512)]

            def load_x(ix):
                s_, L_, ti_ = tilesA[ix]
                dma(T["h"].ap, X[s_][ti_ * 512:(ti_ + 1) * 512, :].rearrange("(j p) d -> p j d", p=128), writes=T["hb"], sb=T["h"])
            load_x(0)
            for ix, (s, L, ti) in enumerate(tilesA):
                if True:
                    r0 = ti * 512
                    h = T["h"]
                    ffn(T["hb"], T["hs"], "ffn1_pre_g", WB["ffn1_w_gate"], WB["ffn1_w_up"], T["wd"], gB["ffn1_post_g"], T)
                    dma(H1[s][r0:r0 + 512, :].rearrange("(j p) d -> p j d", p=128), h.ap, reads=T["hb"], sb=h, queue="pool")
                    norm_T(T["hs"], T["hb"], gT["mix_pre_g"], T["xT"], T["ss"], T["xnb"], T["junk"])
                    if ix + 1 < len(tilesA):
                        load_x(ix + 1)
                    xT = T["xT"]
                    for q3 in range(3):
                        sl = T["slab"][T["slabi"] % 3]
                        T["slabi"] += 1
                        c0 = 672 + q3 * 512
                        dma(sl.ap[:, :, 0:512], WB["w_in"][:, c0:c0 + 512].rearrange("(dc p) f -> p dc f", p=128), writes=[sl], sb=sl)
                        for fl in range(4):
                            ch = q3 * 4 + fl
                            bk = 2 + ch % 2
                            for dc in range(8):
                                op("pe", lambda E, dc=dc, fl=fl, bk=bk, sl=sl: E.matmul(pf(bk), lhsT=sl.ap[:, dc, fl * 128:(fl + 1) * 128], rhs=xT.ap[:, dc, :],
                                                                                          start=(dc == 0), stop=(dc == 7)), reads=[sl, xT], writes=[BK[bk]])
                            qs = qst[qi % 3]
                            qi += 1
                            if ch % 2 == 0:
                                op("act", lambda E, qs=qs, bk=bk: E.activation(out=qs[:], in_=pf(bk), func=AF.Copy), writes=[qs, BK[bk]])
                            else:
                                op("dve", lambda E, qs=qs, bk=bk: E.tensor_copy(out=qs[:], in_=pf(bk)), writes=[qs, BK[bk]])
                            dma(QKVT[s][ch * 128:(ch + 1) * 128, r0:r0 + 512], qs[:], reads=[qs], sb=qs, queue="pool")
                    sA = T["slab"][T["slabi"] % 3]
                    sB = T["slab"][(T["slabi"] + 1) % 3]
                    sC = T["slab"][(T["slabi"] + 2) % 3]
                    T["slabi"] += 3
                    dma(sA.ap[:, :, 0:512], WB["w_in"][:, 0:512].rearrange("(dc p) f -> p dc f", p=128), writes=[sA], sb=sA)
                    dma(sB.ap[:, :, 0:512], WB["w_in"][:, 2208:2720].rearrange("(dc p) f -> p dc f", p=128), writes=[sB], sb=sB)
                    dma(sC.ap[:, :, 0:160], WB["w_in"][:, 512:672].rearrange("(dc p) f -> p dc f", p=128), writes=[sC], sb=sC)
                    dma(sC.ap[:, :, 160:176], WB["w_in"][:, 2720:2736].rearrange("(dc p) f -> p dc f", p=128), reads=[sC], sb=sC)
                    sC.lw = (id(sC), sC.semcnt)
                    for j in range(4):
                        for dc in range(8):
                            lhs = xT.ap[:, dc, j * 128:(j + 1) * 128]
                            op("pe", lambda E, dc=dc, lhs=lhs: E.matmul(pf(6), lhsT=lhs, rhs=sA.ap[:, dc, 0:512], start=(dc == 0), stop=(dc == 7)),
                               reads=[sA, xT], writes=[BK[6]])
                        for dc in range(8):
                            lhs = xT.ap[:, dc, j * 128:(j + 1) * 128]
                            op("pe", lambda E, dc=dc, lhs=lhs: E.matmul(pf(7), lhsT=lhs, rhs=sB.ap[:, dc, 0:512], start=(dc == 0), stop=(dc == 7)),
                               reads=[sB, xT], writes=[BK[7]])
                        for dc in range(8):
                            lhs = xT.ap[:, dc, j * 128:(j + 1) * 128]
                            op("pe", lambda E, dc=dc, lhs=lhs: E.matmul(pf(0, 0, 176), lhsT=lhs, rhs=sC.ap[:, dc, 0:176], start=(dc == 0), stop=(dc == 7)),
                               reads=[sC, xT], writes=[BK[0]])
                        ps_ = pst[pi % 2]
                        gs_ = gst[pi % 2]
                        pi += 1
                        op("act", lambda E, ps_=ps_: E.activation(out=ps_[:, 0:512], in_=pf(6), func=AF.Copy), writes=[ps_, BK[6]])
                        op("dve", lambda E, ps_=ps_: E.tensor_copy(out=ps_[:, 672:1184], in_=pf(7)), reads=[ps_], writes=[ps_, BK[7]])
                        op("dve", lambda E, ps_=ps_: E.tensor_copy(out=ps_[:, 512:672], in_=pf(0, 0, 160)), reads=[ps_], writes=[ps_, BK[0]])
                        op("dve", lambda E, gs_=gs_: E.tensor_tensor(out=gs_[:, 0:8], in0=pf(0, 160, 168), in1=small[:, 16:24], op=ALU.add),
                           reads=[small], writes=[gs_, BK[0]])
                        op("act", lambda E, gs_=gs_: E.activation(out=gs_[:, 0:8], in_=gs_[:, 0:8], func=AF.Exp), reads=[gs_], writes=[gs_])
                        op("act", lambda E, gs_=gs_: E.activation(out=gs_[:, 8:16], in_=pf(0, 168, 176), func=AF.Exp, scale=-1.0), reads=[gs_], writes=[gs_, BK[0]])
                        op("act", lambda E, gs_=gs_: E.activation(out=gs_[:, 0:16], in_=gs_[:, 0:16], func=AF.Ln, bias=1.0), reads=[gs_], writes=[gs_])
                        gv = gs_.ap[:, 32:56].rearrange("p (d k) -> p d k", d=2)
                        op("dve", lambda E, gs_=gs_, gv=gv: E.tensor_tensor(out=gv[:, :, 0:4], in0=gs_.ap[:, 0:8].rearrange("p (d k) -> p d k", d=2),
                                                                            in1=small.ap[:, 24:32].rearrange("p (d k) -> p d k", d=2), op=ALU.mult),
                           reads=[gs_, small], writes=[gs_])
                        op("dve", lambda E, gs_=gs_, gv=gv: E.tensor_scalar(out=gv[:, :, 4:8], in0=gs_.ap[:, 8:16].rearrange("p (d k) -> p d k", d=2),
                                                                            scalar1=-1.0, scalar2=None, op0=ALU.mult), reads=[gs_], writes=[gs_])
                        op("act", lambda E, gs_=gs_, gv=gv: E.activation(out=gv[:, :, 8:12], in_=gs_.ap[:, 8:16].rearrange("p (d k) -> p d k", d=2),
                                                                         func=AF.Exp, scale=-1.0), reads=[gs_], writes=[gs_])
                        rr = slice(r0 + j * 128, r0 + (j + 1) * 128)
                        dma(LAT[s][rr, :], ps_[:, 0:672], reads=[ps_], sb=ps_, queue="pool")
                        dma(ZS[s][rr, :], ps_[:, 672:1184], reads=[ps_], sb=ps_, queue="pool")
                        dma(GBL[s][:, rr, :].rearrange("d p k -> p d k"), gv, reads=[gs_], sb=gs_, queue="pool")
            phase_reset()

        if "M" in phases:
            wst = alloc("wst", 1024)
            wuq = alloc("wuq", 3 * 768, BF16)
            wuq.ap = wuq.ap.rearrange("p (c f) -> p c f", c=3)
            wk = alloc("wk", 2 * 512, BF16)
            wk.ap = wk.ap.rearrange("p (c f) -> p c f", c=2)
            wv = alloc("wv", 2 * 512, BF16)
            wv.ap = wv.ap.rearrange("p (c f) -> p c f", c=2)
            for c in range(3):
                dma(wst[:, 0:768], W["mla_w_uq"][c * 128:(c + 1) * 128, :], writes=[wst], sb=wst)
                op("dve", lambda E, c=c: E.tensor_scalar(out=wuq.ap[:, c, :], in0=wst[:, 0:768], scalar1=small[:, c:c + 1], scalar2=None, op0=ALU.mult),
                   reads=[wst, small], writes=[wuq])
            for c in range(2):
                dma(wst[:, 0:1024], W["mla_w_ukv"][c * 128:(c + 1) * 128, :], writes=[wst], sb=wst)
                wv4 = wst.ap[:, 0:1024].rearrange("p (h e) -> p h e", h=8)
                op("dve", lambda E, c=c, wv4=wv4: E.tensor_scalar(out=wk.ap[:, c, :].rearrange("p (h e) -> p h e", h=8), in0=wv4[:, :, 0:64],
                                                                  scalar1=small[:, 3 + c:4 + c], scalar2=None, op0=ALU.mult), reads=[wst, small], writes=[wk])
                op("dve", lambda E, c=c, wv4=wv4: E.tensor_scalar(out=wv.ap[:, c, :].rearrange("p (h e) -> p h e", h=8), in0=wv4[:, :, 64:128],
                                                                  scalar1=small[:, 3 + c:4 + c], scalar2=None, op0=ALU.mult), reads=[wst, small], writes=[wv])
            lat = [alloc("lat%d" % i, 672) for i in range(2)]
            rp = [alloc("rp%d" % i, 64) for i in range(2)]
            lnb = [alloc("lnb%d" % i, 736, BF16) for i in range(2)]
            for b in lnb:
                op("pool", lambda E, b=b: E.memset(b[:, 640:736], 0.0), writes=[b])
            ssm = [alloc("ssm%d" % i, 4) for i in range(2)]
            junkm = alloc("junkm", 384, BF16)
            cT = [alloc("cT%d" % i, 3 * 128, BF16) for i in range(2)]
            ckvT = alloc("ckvT", 2 * 512, BF16)
            ckvT.ap = ckvT.ap.rearrange("p (c t) -> p c t", c=2)
            ckvc = [Buf("ckv%d" % j, ckvT.ap[:, :, j * 128:(j + 1) * 128]) for j in range(4)]
            kpst = alloc("kpst", 512, BF16)
            kpc = [Buf("kp%d" % j, kpst.ap[:, j * 128:(j + 1) * 128]) for j in range(4)]
            qr = [alloc("qr%d" % i, 768, BF16) for i in range(2)]
            rtmp = [alloc("rtmp%d" % i, 512) for i in range(2)]
            qtst = alloc("qtst", 8 * 512, BF16)
            qtst.ap = qtst.ap.rearrange("p (h t) -> p h t", h=8)
            qtc = [Buf("qt%d" % j, qtst.ap[:, :, j * 128:(j + 1) * 128]) for j in range(4)]
            ktst = alloc("ktst", 4 * 512, BF16)
            ktst.ap = ktst.ap.rearrange("p (a t) -> p a t", a=4)
            vst = alloc("vst", 4 * 512, BF16)
            vst.ap = vst.ap.rearrange("p (j f) -> p j f", j=4)
            vsc = [Buf("vs%d" % j, vst.ap[:, j, :]) for j in range(4)]
            li = 0
            for s, L in SEQ:
                for ti in range(L // 512):
                    r0 = ti * 512
                    for j in range(4):
                        rr = slice(r0 + j * 128, r0 + (j + 1) * 128)
                        lt, rpt, lb, sm_, ct = lat[li % 2], rp[li % 2], lnb[li % 2], ssm[li % 2], cT[li % 2]
                        qrt, rt = qr[li % 2], rtmp[li % 2]
                        li += 1
                        dma(lt[:], LAT[s][rr, :], writes=[lt], sb=lt)
                        dma(rpt[:], ROPE[rr, :], writes=[rpt], sb=rpt)
                        op("act", lambda E, lt=lt, sm_=sm_: E.activation(out=junkm[:, 0:384], in_=lt[:, 0:384], func=AF.Square, accum_out=sm_[:, 0:1]),
                           reads=[lt], writes=[sm_])
                        op("act", lambda E, lt=lt, sm_=sm_: E.activation(out=junkm[:, 0:256], in_=lt[:, 384:640], func=AF.Square, accum_out=sm_[:, 1:2]),
                           reads=[lt], writes=[sm_])
                        op("dve", lambda E, sm_=sm_: E.tensor_scalar(out=sm_[:, 0:1], in0=sm_[:, 0:1], scalar1=1.0 / 384, scalar2=EPS, op0=ALU.mult, op1=ALU.add),
                           reads=[sm_], writes=[sm_])
                        op("dve", lambda E, sm_=sm_: E.tensor_scalar(out=sm_[:, 1:2], in0=sm_[:, 1:2], scalar1=1.0 / 256, scalar2=EPS, op0=ALU.mult, op1=ALU.add),
                           reads=[sm_], writes=[sm_])
                        op("act", lambda E, sm_=sm_: E.activation(out=sm_[:, 0:2], in_=sm_[:, 0:2], func=AF.Ln), reads=[sm_], writes=[sm_])
                        op("act", lambda E, sm_=sm_: E.activation(out=sm_[:, 0:2], in_=sm_[:, 0:2], func=AF.Exp, scale=-0.5), reads=[sm_], writes=[sm_])
                        op("dve", lambda E, lt=lt, lb=lb, sm_=sm_: E.tensor_scalar(out=lb[:, 0:384], in0=lt[:, 0:384], scalar1=sm_[:, 0:1], scalar2=None, op0=ALU.mult),
                           reads=[lt, sm_], writes=[lb])
                        op("act", lambda E, lt=lt, lb=lb, sm_=sm_: E.activation(out=lb[:, 384:640], in_=lt[:, 384:640], func=AF.Copy, scale=sm_[:, 1:2]),
                           reads=[lt, sm_, lb], writes=[lb])
                        op("pool", lambda E, lt=lt, rt=rt, rpt=rpt: E.tensor_tensor(out=rt[:, 0:32], in0=lt[:, 640:672], in1=rpt[:, 0:32], op=ALU.mult),
                           reads=[lt, rpt], writes=[rt])
                        op("pool", lambda E, lt=lt, rt=rt, rpt=rpt: E.tensor_tensor(out=rt[:, 32:48], in0=lt[:, 656:672], in1=rpt[:, 32:48], op=ALU.mult),
                           reads=[lt, rpt, rt], writes=[rt])
                        op("pool", lambda E, lt=lt, rt=rt, rpt=rpt: E.tensor_tensor(out=rt[:, 48:64], in0=lt[:, 640:656], in1=rpt[:, 48:64], op=ALU.mult),
                           reads=[lt, rpt, rt], writes=[rt])
                        op("pool", lambda E, rt=rt, lb=lb: E.tensor_tensor(out=lb[:, 704:736], in0=rt[:, 0:32], in1=rt[:, 32:64], op=ALU.add),
                           reads=[rt, lb], writes=[lb])
                        for c in range(5):
                            op("pe", lambda E, c=c, lb=lb: E.transpose(out=pb(0, c * 128, (c + 1) * 128), in_=lb[:, c * 128:(c + 1) * 128], identity=identb[:]),
                               reads=[lb, identb], writes=[BK[0]])
                        op("pe", lambda E, lb=lb: E.transpose(out=psum_bf[0:96, 640:768], in_=lb[:, 640:736], identity=identb[:]),
                           reads=[lb, identb], writes=[BK[0]])
                        op("dve", lambda E, ct=ct: E.tensor_copy(out=ct[:], in_=pb(0, 0, 384)), writes=[ct, BK[0]])
                        op("act", lambda E, j=j: E.activation(out=ckvT.ap[:, :, j * 128:(j + 1) * 128], in_=pb(0, 384, 640).rearrange("p (c t) -> p c t", c=2), func=AF.Copy),
                           writes=[ckvc[j], BK[0]])
                        op("dve", lambda E, j=j: E.tensor_copy(out=kpst.ap[64:96, j * 128:(j + 1) * 128], in_=psum_bf[64:96, 640:768]), writes=[kpc[j], BK[0]])
                        ctv = ct.ap.rearrange("p (c t) -> p c t", c=3)
                        for c in range(3):
                            op("pe", lambda E, c=c, ctv=ctv: E.matmul(pf(1, 0, 480), lhsT=ctv[:, c, :], rhs=wuq.ap[:, c, 0:480], start=(c == 0), stop=(c == 2)),
                               reads=[ct, wuq], writes=[BK[1]])
                        for c in range(3):
                            op("pe", lambda E, c=c, ctv=ctv: E.matmul(pf(2, 0, 288), lhsT=ctv[:, c, :], rhs=wuq.ap[:, c, 480:768], start=(c == 0), stop=(c == 2)),
                               reads=[ct, wuq], writes=[BK[2]])
                        for (bk, h0, nh) in ((1, 0, 5), (2, 5, 3)):
                            pv = pf(bk, 0, nh * 96).rearrange("p (h e) -> p h e", h=nh)
                            qv = qrt.ap[:, h0 * 96:(h0 + nh) * 96].rearrange("p (h e) -> p h e", h=nh)
                            tv = rt.ap[:, 64:64 + nh * 64].rearrange("p (h e) -> p h e", h=nh)
                            cs = rpt.ap[:, 0:32].unsqueeze(1).to_broadcast([128, nh, 32])
                            sn1 = rpt.ap[:, 32:48].unsqueeze(1).to_broadcast([128, nh, 16])
                            sn2 = rpt.ap[:, 48:64].unsqueeze(1).to_broadcast([128, nh, 16])
                            op("act", lambda E, pv=pv, qv=qv: E.activation(out=qv[:, :, 0:64], in_=pv[:, :, 0:64], func=AF.Copy), reads=[], writes=[qrt, BK[bk]])
                            op("dve", lambda E, pv=pv, tv=tv, cs=cs: E.tensor_tensor(out=tv[:, :, 0:32], in0=pv[:, :, 64:96], in1=cs, op=ALU.mult),
                               reads=[rpt], writes=[rt, BK[bk]])
                            op("dve", lambda E, pv=pv, tv=tv, sn1=sn1: E.tensor_tensor(out=tv[:, :, 32:48], in0=pv[:, :, 80:96], in1=sn1, op=ALU.mult),
                               reads=[rpt], writes=[rt, BK[bk]])
                            op("dve", lambda E, pv=pv, tv=tv, sn2=sn2: E.tensor_tensor(out=tv[:, :, 48:64], in0=pv[:, :, 64:80], in1=sn2, op=ALU.mult),
                               reads=[rpt], writes=[rt, BK[bk]])
                            op("pool", lambda E, tv=tv, qv=qv: E.tensor_tensor(out=qv[:, :, 64:96], in0=tv[:, :, 0:32], in1=tv[:, :, 32:64], op=ALU.add),
                               reads=[rt], writes=[qrt])
                        for hh in range(8):
                            op("pe", lambda E, hh=hh, qrt=qrt: E.transpose(out=psum_bf[0:96, 3 * 1024 + hh * 128:3 * 1024 + (hh + 1) * 128], in_=qrt[:, hh * 96:(hh + 1) * 96],
                                                                           identity=identb[:]), reads=[qrt, identb], writes=[BK[3]])
                        op("act", lambda E, j=j: E.activation(out=qtst.ap[0:96, :, j * 128:(j + 1) * 128],
                                                              in_=psum_bf[0:96, 3 * 1024:4 * 1024].rearrange("p (h t) -> p h t", h=8), func=AF.Copy),
                           writes=[qtc[j], BK[3]])
                        for c in range(2):
                            op("pe", lambda E, c=c, j=j: E.matmul(pf(4), lhsT=ckvT.ap[:, c, j * 128:(j + 1) * 128], rhs=wv.ap[:, c, :], start=(c == 0), stop=(c == 1)),
                               reads=[ckvc[j], wv], writes=[BK[4]])
                        op("dve", lambda E, j=j: E.tensor_copy(out=vst.ap[:, j, :], in_=pf(4)), writes=[vsc[j], BK[4]])
                    for pr in range(4):
                        bk = 5 + pr % 2
                        for c in range(2):
                            op("pe", lambda E, c=c, pr=pr, bk=bk: E.matmul(pf(bk), lhsT=wk.ap[:, c, pr * 128:(pr + 1) * 128], rhs=ckvT.ap[:, c, :], start=(c == 0), stop=(c == 1)),
                               reads=ckvc + [wk], writes=[BK[bk]])
                        if pr % 2 == 0:
                            op("act", lambda E, pr=pr, bk=bk: E.activation(out=ktst.ap[:, pr, :], in_=pf(bk), func=AF.Copy), writes=[ktst, BK[bk]])
                        else:
                            op("dve", lambda E, pr=pr, bk=bk: E.tensor_copy(out=ktst.ap[:, pr, :], in_=pf(bk)), reads=[ktst], writes=[ktst, BK[bk]])
                    cc = slice(r0, r0 + 512)
                    for hh in range(8):
                        pr, hi = hh // 2, hh % 2
                        dma(KT[s][hh, 0:64, cc], ktst.ap[hi * 64:(hi + 1) * 64, pr, :], reads=[ktst], sb=ktst, queue="pool")
                        dma(KT[s][hh, 64:96, cc], kpst.ap[64:96, :], reads=kpc, sb=kpst, queue="pool")
                    dma(QT[s][:, :, cc].rearrange("h r t -> r h t"), qtst.ap[0:96, :, :], reads=qtc, sb=qtst, queue="pool")
                    for j in range(4):
                        dma(VV[s][:, :, ti * 4 + j, :].rearrange("h p e -> p h e"),
                            vst.ap[:, j, :].rearrange("p (h e) -> p h e", h=8), reads=vsc, sb=vst, queue="pool")
            phase_reset()

        if "B" in phases:
            SC = 96 ** -0.5
            for s, L in SEQ:
                nkb = L // 128
                off_seq = off[0]
                ktb = []
                vtb = []
                for i in range(2):
                    b = alloc("ktb%d" % i, L, BF16)
                    ktb.append(b)
                    v = alloc("vtb%d" % i, nkb * 65, BF16)
                    v.ap = v.ap.rearrange("p (k e) -> p k e", e=65)
                    op("pool", lambda E, v=v: E.memset(v.ap[:, :, 64:65], 1.0), writes=[v])
                    vtb.append(v)
                qtb = [alloc("qtb%d" % i, 512, BF16) for i in range(3)]
                ptb = [alloc("ptb%d" % i, 512, BF16) for i in range(4)]
                osb = [alloc("osb%d" % i, 512) for i in range(2)]
                yst = [alloc("yst%d" % i, 4 * 64) for i in range(2)]
                rdn = [alloc("rdn%d" % i, 4) for i in range(2)]
                qi = 0
                pi = 0
                for hh in range(8):
                    kt, vt = ktb[hh % 2], vtb[hh % 2]
                    dma(kt[0:96, :], KT[s][hh, :, :], writes=[kt], sb=kt)
                    dma(vt.ap[:, :, 0:64], VV[s][hh, :, :, :], writes=[vt], sb=vt)
                    for qt in range(L // 512):
                        qb = qtb[qi % 3]
                        ob, ys, rd = osb[qi % 2], yst[qi % 2], rdn[qi % 2]
                        obk = 4 + qi % 2
                        qi += 1
                        dma(qb[0:96, :], QT[s][hh, :, qt * 512:(qt + 1) * 512], writes=[qb], sb=qb)

                        def smm(kb, qb=qb, kt=kt):
                            bk = kb % 4
                            op("pe", lambda E: E.matmul(pf(bk), lhsT=kt[0:96, kb * 128:(kb + 1) * 128], rhs=qb[0:96, :], start=True, stop=True),
                               reads=[kt, qb], writes=[BK[bk]])
                        smm(0)
                        if nkb > 1:
                            smm(1)
                        for kb in range(nkb):
                            if kb + 2 < nkb:
                                smm(kb + 2)
                            pt = ptb[pi % 4]
                            pi += 1
                            bk = kb % 4
                            op("act", lambda E, pt=pt, bk=bk: E.activation(out=pt[:], in_=pf(bk), func=AF.Exp, scale=SC), writes=[pt, BK[bk]])
                            op("pe", lambda E, pt=pt, kb=kb, vt=vt, obk=obk: E.matmul(psum[0:65, obk * 512:(obk + 1) * 512], lhsT=vt.ap[:, kb, 0:65], rhs=pt[:],
                                                                                         start=(kb == 0), stop=(kb == nkb - 1)), reads=[vt, pt], writes=[BK[obk]])
                        op("dve", lambda E, ob=ob, obk=obk: E.tensor_copy(out=ob[0:65, :], in_=psum[0:65, obk * 512:(obk + 1) * 512]), writes=[ob, BK[obk]])
                        for j in range(4):
                            op("pe", lambda E, j=j, ob=ob: E.matmul(pf(6, j * 65, (j + 1) * 65), lhsT=ob[0:65, j * 128:(j + 1) * 128], rhs=identf[0:65, 0:65], start=True, stop=True),
                               reads=[ob, identf], writes=[BK[6]])
                        p6 = pf(6, 0, 260).rearrange("p (j e) -> p j e", j=4)
                        op("dve", lambda E, rd=rd, p6=p6: E.reciprocal(out=rd.ap[:, 0:4].unsqueeze(2), in_=p6[:, :, 64:65]), writes=[rd, BK[6]])
                        op("dve", lambda E, rd=rd, ys=ys, p6=p6: E.tensor_tensor(out=ys.ap.rearrange("p (j e) -> p j e", j=4), in0=p6[:, :, 0:64],
                                                                                in1=rd.ap[:, 0:4].unsqueeze(2).to_broadcast([128, 4, 64]), op=ALU.mult),
                           reads=[rd], writes=[ys, BK[6]])
                        dma(YA[s][qt * 512:(qt + 1) * 512, hh * 64:(hh + 1) * 64].rearrange("(j p) e -> p j e", p=128),
                            ys.ap.rearrange("p (j e) -> p j e", j=4), reads=[ys], sb=ys, queue="pool")
                P.barrier()
                off[0] = off_seq
            phase_reset()

        def run_window(items, W):
            nxt = 0
            active = []
            while nxt < len(items) or active:
                while len(active) < W and nxt < len(items):
                    if items[nxt][1] and active:
                        break
                    active.append(items[nxt][0])
                    nxt += 1
                for g_ in list(active):
                    try:
                        next(g_)
                    except StopIteration:
                        active.remove(g_)

        if "G" in phases:
            cw = alloc("cw", 12 * 5)
            for c in range(12):
                dma(cw[:, c * 5:(c + 1) * 5], W["gdn_conv_w"][:, c * 128:(c + 1) * 128].rearrange("k p -> p k"), writes=[cw] if c == 0 else [],
                    reads=[] if c == 0 else [cw], sb=cw, allow_slow_non_contiguous=True)
            cw.lw = (id(cw), cw.semcnt)
            dg = alloc("dg", 60 * 128, BF16)
            dg.ap = dg.ap.rearrange("p (k f) -> p k f", k=60)
            for k in range(60):
                op("dve", lambda E, k=k: E.tensor_scalar(out=dg.ap[:, k, :], in0=identb[:], scalar1=cw[:, k:k + 1], scalar2=None, op0=ALU.mult),
                   reads=[identb, cw], writes=[dg] if k == 0 else [])
            dg.lw = ("dve", P.cnt["dve"])
            GW = 4
            xin = [alloc("gxin%d" % i, 516) for i in range(GW)]
            xbf = [alloc("gxbf%d" % i, 516, BF16) for i in range(GW)]
            ex = [alloc("gex%d" % i, 512) for i in range(GW)]
            sb_ = [alloc("gsb%d" % i, 512, BF16) for i in range(GW)]
            tokst2 = []
            tokc2 = []
            for i in range(2):
                t_ = alloc("tokst%d" % i, 4 * 1536, BF16)
                t_.ap = t_.ap.rearrange("p (j c) -> p j c", j=4)
                tokst2.append(t_)
                tokc2.append([Buf("tokc%d_%d" % (i, c), t_.ap[:, :, c * 128:(c + 1) * 128]) for c in range(12)])
            sq = alloc("gsq", 1024)
            ssg2 = [alloc("ssg%d" % i, 32) for i in range(2)]

            def gchunk(s, L, r0, c, k, tokst, tokc):
                xi, xb, e_, sbb = xin[k % GW], xbf[k % GW], ex[k % GW], sb_[k % GW]
                cb_, tb_ = 2 * (k % GW), 2 * (k % GW) + 1
                lo, hi = max(r0 - 2, 0), min(r0 + 514, L)
                if r0 == 0:
                    op("pool", lambda E: E.memset(xi[:, 0:2], 0.0), writes=[xi])
                if r0 + 512 == L:
                    op("pool", lambda E: E.memset(xi[:, 514:516], 0.0), writes=[xi])
                dma(xi[:, lo - (r0 - 2):hi - (r0 - 2)], QKVT[s][c * 128:(c + 1) * 128, lo:hi], writes=[xi], sb=xi)
                yield
                op("pool", lambda E: E.tensor_copy(out=xb[:], in_=xi[:]), reads=[xi], writes=[xb])
                yield
                for t5 in range(5):
                    op("pe", lambda E, t5=t5: E.matmul(pf(cb_), lhsT=dg.ap[:, c * 5 + t5, :], rhs=xb[:, t5:t5 + 512], start=(t5 == 0), stop=(t5 == 4)),
                       reads=[dg, xb], writes=[BK[cb_]])
                yield
                op("act", lambda E: E.activation(out=e_[:], in_=pf(cb_), func=AF.Exp, scale=-1.0), writes=[e_, BK[cb_]])
                op("act", lambda E: E.activation(out=e_[:], in_=e_[:], func=AF.Ln, bias=1.0), reads=[e_], writes=[e_])
                op("act", lambda E: E.activation(out=e_[:], in_=e_[:], func=AF.Exp, scale=-1.0), reads=[e_], writes=[e_])
                yield
                op("dve", lambda E: E.tensor_tensor(out=sbb[:], in0=e_[:], in1=pf(cb_), op=ALU.mult), reads=[e_], writes=[sbb, BK[cb_]])
                yield
                for j in range(4):
                    op("pe", lambda E, j=j: E.transpose(out=pb(tb_, j * 128, (j + 1) * 128), in_=sbb[:, j * 128:(j + 1) * 128], identity=identb[:]),
                       reads=[sbb, identb], writes=[BK[tb_]])
                yield
                if c % 2 == 0:
                    op("act", lambda E: E.activation(out=tokst.ap[:, :, c * 128:(c + 1) * 128], in_=pb(tb_, 0, 512).rearrange("p (j d) -> p j d", j=4), func=AF.Copy),
                       writes=[tokc[c], BK[tb_]])
                else:
                    op("dve", lambda E: E.tensor_copy(out=tokst.ap[:, :, c * 128:(c + 1) * 128], in_=pb(tb_, 0, 512).rearrange("p (j d) -> p j d", j=4)),
                       writes=[tokc[c], BK[tb_]])

            def gtail(s, r0, tokst, tokc, ssg):
                for j in range(4):
                    tv = tokst.ap[:, j, 0:1024]
                    op("dve", lambda E, tv=tv: E.tensor_tensor(out=sq[:], in0=tv, in1=tv, op=ALU.mult), reads=tokc[0:8], writes=[sq])
                    op("dve", lambda E, j=j: E.tensor_reduce(out=ssg[:, j * 8:(j + 1) * 8], in_=sq.ap.rearrange("p (h d) -> p h d", h=8), axis=AX.X, op=ALU.add),
                       reads=[sq], writes=[ssg])
                    yield
                op("dve", lambda E: E.tensor_scalar(out=ssg[:, 0:32], in0=ssg[:, 0:32], scalar1=EPS, scalar2=None, op0=ALU.add), reads=[ssg], writes=[ssg])
                op("act", lambda E: E.activation(out=ssg[:, 0:32], in_=ssg[:, 0:32], func=AF.Ln), reads=[ssg], writes=[ssg])
                op("act", lambda E: E.activation(out=ssg[:, 0:32], in_=ssg[:, 0:32], func=AF.Exp, scale=-0.5), reads=[ssg], writes=[ssg])
                sv = ssg.ap[:, 0:32].rearrange("p (j h) -> p j h", j=4)
                op("dve", lambda E: E.tensor_scalar(out=sv[:, :, 0:4], in0=sv[:, :, 0:4], scalar1=128 ** -0.5, scalar2=None, op0=ALU.mult), reads=[ssg], writes=[ssg])
                yield
                for j in range(4):
                    tv = tokst.ap[:, j, 0:1024].rearrange("p (h d) -> p h d", h=8)
                    e1 = "dve" if j % 2 == 0 else "pool"
                    op(e1, lambda E, tv=tv, j=j: E.tensor_tensor(out=tv, in0=tv, in1=ssg.ap[:, j * 8:(j + 1) * 8].unsqueeze(2).to_broadcast([128, 8, 128]), op=ALU.mult),
                       reads=[ssg] + tokc[0:8], writes=tokc[0:8])
                    yield
                dma(TOK[s][r0:r0 + 512, :].rearrange("(j p) c -> p j c", p=128), tokst.ap, reads=tokc, sb=tokst, queue="pool")

            items = []
            k = 0
            tix = 0
            for s, L in SEQ:
                for ti in range(L // 512):
                    r0 = ti * 512
                    tkst, tkc, ssg = tokst2[tix % 2], tokc2[tix % 2], ssg2[tix % 2]
                    tix += 1
                    for c in range(12):
                        items.append((gchunk(s, L, r0, c, k, tkst, tkc), False))
                        k += 1
                    items.append((gtail(s, r0, tkst, tkc, ssg), True))
            run_window(items, GW)
            phase_reset()

        if "H" in phases:
            gm = alloc("gm", 5 * 512)
            gm.ap = gm.ap.rearrange("p (m h j) -> p m h j", m=5, h=8)
            dma(gm.ap[0:64], GMASK[:, :, :, :], writes=[gm], sb=gm)
            NA, NAT, NQK = gm.ap[0:64, 0], gm.ap[0:64, 1], gm.ap[0:64, 2]
            trif, trib = gm.ap[0:64, 3, 0, :], gm.ap[0:64, 3, 1, :]
            idb8 = identb.ap[0:64, 0:64].unsqueeze(1).to_broadcast([64, 8, 64])
            idf8 = identf.ap[0:64, 0:64].unsqueeze(1).to_broadcast([64, 8, 64])

            def A3(name, n, dt=F32, parts=128):
                b = alloc(name, 8 * n, dt)
                b.ap = b.ap.rearrange("p (h x) -> p h x", h=8)
                return b
            NSET = 2
            tok = [alloc("tk%d" % i, 2 * 1536, BF16) for i in range(NSET)]
            gsel = [alloc("gsel%d" % i, 24) for i in range(NSET)]
            sm8 = [alloc("sm8_%d" % i, 64) for i in range(NSET)]
            ost = [alloc("ost%d" % i, 1024) for i in range(NSET)]
            SETS = []
            for i in range(NSET):
                d_ = {}
                d_["Dg"] = A3("Dg%d" % i, 64)
                d_["Dc"] = A3("Dc%d" % i, 64)
                d_["De"] = A3("De%d" % i, 64, BF16)
                kq_ = alloc("kqT%d" % i, 16 * 64, BF16)
                kq_.ap = kq_.ap.rearrange("p (h x) -> p h x", h=16)
                d_["kqT"] = kq_
                for nm in ("dA", "dAT", "dQK"):
                    d_[nm] = A3(nm + str(i), 64)
                d_["Xb"] = [A3("Xb%d_%d" % (i, k), 64, BF16) for k in range(2)]
                d_["Yb"] = [A3("Yb%d_%d" % (i, k), 64, BF16) for k in range(2)]
                d_["Zb"] = [A3("Zb%d_%d" % (i, k), 64, BF16) for k in range(2)]
                for nm, n_, dt_ in (("qkT", 64, BF16), ("kbg", 128, BF16), ("vb", 128, BF16), ("kg", 128, BF16), ("wT", 64, BF16),
                                   ("qgT", 64, BF16), ("uu", 128, F32), ("vnew", 128, BF16)):
                    d_[nm] = A3(nm + str(i), n_, dt_)
                SETS.append(d_)
            S = A3("S", 128)
            Sb = A3("Sb", 128, BF16)

            def step(s, N, n):
                st_ = SETS[n % NSET]
                c0 = 4 * (n % 2)
                c1, c2, c3 = c0 + 1, c0 + 2, c0 + 3
                Dg, Dc, De, kqT, dA, dAT, dQK = st_["Dg"], st_["Dc"], st_["De"], st_["kqT"], st_["dA"], st_["dAT"], st_["dQK"]
                Xb, Yb, Zb = st_["Xb"], st_["Yb"], st_["Zb"]
                qkT, kbg, vb, kg, wT, qgT, uu, vnew = (st_[k_] for k_ in ("qkT", "kbg", "vb", "kg", "wT", "qgT", "uu", "vnew"))
                cf, cb = n, N - 1 - n
                tk, gs, m8, os_ = tok[n % NSET], gsel[n % NSET], sm8[n % NSET], ost[n % NSET]
                tkv = tk.ap.rearrange("p (d c) -> p d c", d=2)
                for d, ch in ((0, cf), (1, cb)):
                    dma(tkv[0:64, d, :], TOK[s][ch * 64:(ch + 1) * 64, :], writes=[tk] if d == 0 else [], reads=[] if d == 0 else [tk], sb=tk)
                    dma(gs.ap[0:64, d * 12:(d + 1) * 12], GBL[s][d, ch * 64:(ch + 1) * 64, :], writes=[gs] if d == 0 else [], reads=[] if d == 0 else [gs], sb=gs)
                tk.lw = (id(tk), tk.semcnt)
                gs.lw = (id(gs), gs.semcnt)
                gv = gs.ap[0:64, :].rearrange("p (d k) -> p d k", d=2)
                g8, lnb8, beta8 = gv[:, :, 0:4], gv[:, :, 4:8], gv[:, :, 8:12]
                m = m8.ap[0:64, :]

                def v8(a, b):
                    return m8.ap[0:64, a:b].rearrange("p (d k) -> p d k", d=2)
                yield None
                op("pe", lambda E, gs=gs: E.matmul(psum[0:64, c0 * 512:c0 * 512 + 4], lhsT=trif, rhs=gs.ap[0:64, 0:4], start=True, stop=True), reads=[gm, gs], writes=[BK[c0]])
                op("pe", lambda E, gs=gs: E.matmul(psum[0:64, c0 * 512 + 4:c0 * 512 + 8], lhsT=trib, rhs=gs.ap[0:64, 12:16], start=True, stop=True), reads=[gm, gs], writes=[BK[c0]])
                op("pe", lambda E, g8=g8: E.matmul(psum[:, c0 * 512 + 8:c0 * 512 + 16].rearrange("p (d k) -> p d k", d=2), lhsT=onesf[0:64, :], rhs=g8, start=True, stop=True),
                   reads=[onesf, gs], writes=[BK[c0]])
                yield None
                op("dve", lambda E, m8=m8: E.tensor_copy(out=m8[0:64, 0:8], in_=psum[0:64, c0 * 512:c0 * 512 + 8]), writes=[m8, BK[c0]])
                op("dve", lambda E, m8=m8, lnb8=lnb8, v8=v8: E.tensor_tensor(out=v8(8, 16), in0=v8(0, 8), in1=lnb8, op=ALU.add), reads=[m8, gs], writes=[m8])
                op("act", lambda E, m8=m8: E.activation(out=m8[0:64, 16:24], in_=m8[0:64, 0:8], func=AF.Exp), reads=[m8], writes=[m8])
                op("dve", lambda E, m8=m8, beta8=beta8, v8=v8: E.tensor_tensor(out=v8(24, 32), in0=v8(16, 24), in1=beta8, op=ALU.mult), reads=[m8, gs], writes=[m8])
                op("dve", lambda E, m8=m8: E.tensor_tensor(out=m8[0:64, 40:48], in0=psum[0:64, c0 * 512 + 8:c0 * 512 + 16], in1=m8[0:64, 0:8], op=ALU.subtract), reads=[m8], writes=[m8, BK[c0]])
                op("act", lambda E, m8=m8: E.activation(out=m8[0:64, 32:40], in_=m8[0:64, 40:48], func=AF.Exp), reads=[m8], writes=[m8])
                op("act", lambda E, m8=m8: E.activation(out=m8[:, 48:56], in_=psum[:, c0 * 512 + 8:c0 * 512 + 16], func=AF.Exp), reads=[m8], writes=[m8, BK[c0]])
                yield None
                op("pool", lambda E, m8=m8: E.tensor_tensor(out=Dg.ap[0:64], in0=idf8, in1=m8.ap[0:64, 0:8].unsqueeze(2).to_broadcast([64, 8, 64]), op=ALU.mult),
                   reads=[identf, m8], writes=[Dg])
                op("pool", lambda E, m8=m8: E.tensor_tensor(out=Dc.ap[0:64], in0=idf8, in1=m8.ap[0:64, 8:16].unsqueeze(2).to_broadcast([64, 8, 64]), op=ALU.mult),
                   reads=[identf, m8], writes=[Dc])
                op("pool", lambda E, m8=m8: E.tensor_tensor(out=De.ap[0:64], in0=idf8, in1=m8.ap[0:64, 16:24].unsqueeze(2).to_broadcast([64, 8, 64]), op=ALU.mult),
                   reads=[identf, m8], writes=[De])
                yield None
                op("pe", lambda E: E.matmul(psum[0:64, c1 * 512:(c1 + 1) * 512], lhsT=onesf[0:64, 0:64], rhs=Dg.ap[0:64].rearrange("p h x -> p (h x)"), start=True, stop=True),
                   reads=[onesf, Dg], writes=[BK[c1]])
                op("pe", lambda E: E.matmul(psum[0:64, c2 * 512:(c2 + 1) * 512], lhsT=onesf[0:64, 0:64], rhs=Dc.ap[0:64].rearrange("p h x -> p (h x)"), start=True, stop=True),
                   reads=[onesf, Dc], writes=[BK[c2]])
                yield None
                for d in range(2):
                    for hh in range(4):
                        hd = d * 4 + hh
                        op("pe", lambda E, d=d, hh=hh, hd=hd, tkv=tkv: E.transpose(out=psum_bf[:, c3 * 1024 + hd * 64:c3 * 1024 + (hd + 1) * 64],
                                                                                 in_=tkv[0:64, d, 512 + hh * 128:512 + (hh + 1) * 128], identity=identb[0:64, 0:64]),
                           reads=[tk, identb], writes=[BK[c3]])
                        op("pe", lambda E, d=d, hh=hh, hd=hd, tkv=tkv: E.transpose(out=psum_bf[:, c3 * 1024 + 512 + hd * 64:c3 * 1024 + 512 + (hd + 1) * 64],
                                                                                 in_=tkv[0:64, d, hh * 128:(hh + 1) * 128], identity=identb[0:64, 0:64]),
                           reads=[tk, identb], writes=[BK[c3]])
                yield None
                op("act", lambda E: E.activation(out=kqT.ap, in_=pb(c3).rearrange("p (h x) -> p h x", h=16), func=AF.Copy), writes=[kqT, BK[c3]])
                yield None
                for hd in range(8):
                    op("pe", lambda E, hd=hd: E.matmul(psum[0:64, c0 * 512 + hd * 64:c0 * 512 + (hd + 1) * 64], lhsT=kqT.ap[:, hd, :], rhs=kqT.ap[:, hd, :], start=True, stop=True),
                       reads=[kqT], writes=[BK[c0]])
                for hd in range(8):
                    op("pe", lambda E, hd=hd: E.matmul(psum[0:64, c3 * 512 + hd * 64:c3 * 512 + (hd + 1) * 64], lhsT=kqT.ap[:, hd, :], rhs=kqT.ap[:, 8 + hd, :], start=True, stop=True),
                       reads=[kqT], writes=[BK[c3]])
                P1 = psum[0:64, c1 * 512:(c1 + 1) * 512].rearrange("p (h x) -> p h x", h=8)
                P2 = psum[0:64, c2 * 512:(c2 + 1) * 512].rearrange("p (h x) -> p h x", h=8)
                PG = psum[0:64, c0 * 512:(c0 + 1) * 512].rearrange("p (h x) -> p h x", h=8)
                PQ = psum[0:64, c3 * 512:(c3 + 1) * 512].rearrange("p (h x) -> p h x", h=8)

                def bc8(a):
                    return m8.ap[0:64, a:a + 8].unsqueeze(2).to_broadcast([64, 8, 64])
                yield None
                op("dve", lambda E: E.scalar_tensor_tensor(out=dA.ap[0:64], in0=P1, scalar=-1.0, in1=NA, op0=ALU.mult, op1=ALU.add), reads=[gm], writes=[dA, BK[c1]])
                op("pool", lambda E, bc8=bc8: E.tensor_tensor(out=dA.ap[0:64], in0=dA.ap[0:64], in1=bc8(8), op=ALU.add), reads=[m8, dA], writes=[dA])
                op("act", lambda E: E.activation(out=dA.ap[0:64], in_=dA.ap[0:64], func=AF.Exp), reads=[dA], writes=[dA])
                yield None
                op("dve", lambda E: E.tensor_tensor(out=dAT.ap[0:64], in0=P2, in1=NAT, op=ALU.add), reads=[gm], writes=[dAT, BK[c2]])
                op("pool", lambda E, bc8=bc8: E.tensor_tensor(out=dAT.ap[0:64], in0=dAT.ap[0:64], in1=bc8(0), op=ALU.subtract), reads=[m8, dAT], writes=[dAT])
                op("act", lambda E: E.activation(out=dAT.ap[0:64], in_=dAT.ap[0:64], func=AF.Exp), reads=[dAT], writes=[dAT])
                yield None
                op("dve", lambda E: E.tensor_tensor(out=dQK.ap[0:64], in0=P1, in1=NQK, op=ALU.add), reads=[gm], writes=[dQK, BK[c1]])
                op("pool", lambda E, bc8=bc8: E.tensor_tensor(out=dQK.ap[0:64], in0=dQK.ap[0:64], in1=bc8(0), op=ALU.subtract), reads=[m8, dQK], writes=[dQK])
                op("act", lambda E: E.activation(out=dQK.ap[0:64], in_=dQK.ap[0:64], func=AF.Exp), reads=[dQK], writes=[dQK])
                yield None
                X0, Y0, Z0 = Xb[0], Yb[0], Zb[0]
                op("dve", lambda E, X0=X0: E.scalar_tensor_tensor(out=X0.ap[0:64], in0=dA.ap[0:64], scalar=-1.0, in1=PG, op0=ALU.mult, op1=ALU.mult), reads=[dA], writes=[X0, BK[c0]])
                op("dve", lambda E, Y0=Y0: E.scalar_tensor_tensor(out=Y0.ap[0:64], in0=dAT.ap[0:64], scalar=-1.0, in1=PG, op0=ALU.mult, op1=ALU.mult), reads=[dAT], writes=[Y0, BK[c0]])
                op("dve", lambda E: E.tensor_tensor(out=qkT.ap[0:64], in0=dQK.ap[0:64], in1=PQ, op=ALU.mult), reads=[dQK], writes=[qkT, BK[c3]])
                op("pool", lambda E, Y0=Y0, Z0=Z0: E.tensor_tensor(out=Z0.ap[0:64], in0=Y0.ap[0:64], in1=idb8, op=ALU.add), reads=[Y0, identb], writes=[Z0])
                yield None
                for k in range(1, 6):
                    Xp, Yp, Zp = Xb[(k - 1) % 2], Yb[(k - 1) % 2], Zb[(k - 1) % 2]
                    Xn, Yn, Zn = Xb[k % 2], Yb[k % 2], Zb[k % 2]
                    for hd in range(8):
                        op("pe", lambda E, hd=hd, Xp=Xp, Yp=Yp: E.matmul(psum[0:64, c1 * 512 + hd * 64:c1 * 512 + (hd + 1) * 64], lhsT=Yp.ap[0:64, hd, :], rhs=Xp.ap[0:64, hd, :],
                                                                         start=True, stop=True), reads=[Xp, Yp], writes=[BK[c1]])
                    if k < 5:
                        for hd in range(8):
                            op("pe", lambda E, hd=hd, Xp=Xp, Yp=Yp: E.matmul(psum[0:64, c2 * 512 + hd * 64:c2 * 512 + (hd + 1) * 64], lhsT=Xp.ap[0:64, hd, :], rhs=Yp.ap[0:64, hd, :],
                                                                             start=True, stop=True), reads=[Xp, Yp], writes=[BK[c2]])
                    yield None
                    op("act", lambda E, Xn=Xn: E.activation(out=Xn.ap[0:64], in_=P1, func=AF.Copy), writes=[Xn, BK[c1]])
                    if k < 5:
                        op("dve", lambda E, Yn=Yn: E.tensor_copy(out=Yn.ap[0:64], in_=P2), writes=[Yn, BK[c2]])
                    yield None
                    for hd in range(8):
                        op("pe", lambda E, hd=hd, Xn=Xn, Zp=Zp: E.matmul(psum[0:64, c3 * 512 + hd * 64:c3 * 512 + (hd + 1) * 64], lhsT=Xn.ap[0:64, hd, :], rhs=Zp.ap[0:64, hd, :],
                                                                         start=True, stop=True), reads=[Xn, Zp], writes=[BK[c3]])
                    op("dve", lambda E, Zn=Zn, Zp=Zp: E.tensor_tensor(out=Zn.ap[0:64], in0=psum[0:64, c3 * 512:(c3 + 1) * 512].rearrange("p (h x) -> p h x", h=8), in1=Zp.ap[0:64], op=ALU.add),
                       reads=[Zp], writes=[Zn, BK[c3]])
                    yield None
                Z = Zb[5 % 2]
                yield None
                kv4 = tkv[0:64, :, 512:1024].rearrange("p d (h x) -> p d h x", h=4)
                vv4 = tkv[0:64, :, 1024:1536].rearrange("p d (h x) -> p d h x", h=4)

                def b4(a):
                    return m8.ap[0:64, a:a + 8].rearrange("p (d h) -> p d h", d=2).unsqueeze(3).to_broadcast([64, 2, 4, 128])
                op("pool", lambda E, kv4=kv4, b4=b4: E.tensor_tensor(out=kbg.ap[0:64].rearrange("p (d h) x -> p d h x", d=2), in0=kv4, in1=b4(24), op=ALU.mult),
                   reads=[tk, m8], writes=[kbg])
                op("dve", lambda E, vv4=vv4, gs=gs: E.tensor_tensor(out=vb.ap[0:64].rearrange("p (d h) x -> p d h x", d=2), in0=vv4,
                                                                    in1=gs.ap[0:64, :].rearrange("p (d k) -> p d k", d=2)[:, :, 8:12].unsqueeze(3).to_broadcast([64, 2, 4, 128]), op=ALU.mult),
                   reads=[tk, gs], writes=[vb])
                op("pool", lambda E, kv4=kv4, b4=b4: E.tensor_tensor(out=kg.ap[0:64].rearrange("p (d h) x -> p d h x", d=2), in0=kv4, in1=b4(32), op=ALU.mult),
                   reads=[tk, m8], writes=[kg])
                yield None
                for hd in range(8):
                    op("pe", lambda E, hd=hd, Z=Z: E.matmul(psum[:, c0 * 512 + hd * 64:c0 * 512 + (hd + 1) * 64], lhsT=kbg.ap[0:64, hd, :], rhs=Z.ap[0:64, hd, :], start=True, stop=True),
                       reads=[kbg, Z], writes=[BK[c0]])
                op("act", lambda E: E.activation(out=wT.ap, in_=pf(c0).rearrange("p (h x) -> p h x", h=8), func=AF.Copy), writes=[wT, BK[c0]])
                yield None
                for hd in range(8):
                    bk = c2 + hd // 4
                    op("pe", lambda E, hd=hd, Z=Z: E.matmul(psum[0:64, c2 * 512 + hd * 128:c2 * 512 + (hd + 1) * 128], lhsT=Z.ap[0:64, hd, :], rhs=vb.ap[0:64, hd, :], start=True, stop=True),
                       reads=[vb, Z], writes=[BK[bk]])
                op("act", lambda E: E.activation(out=uu.ap[0:64], in_=psum[0:64, c2 * 512:(c2 + 2) * 512].rearrange("p (h x) -> p h x", h=8), func=AF.Copy), writes=[uu, BK[c2], BK[c3]])
                yield None
                for d in range(2):
                    for hh in range(4):
                        hd = d * 4 + hh
                        op("pe", lambda E, d=d, hh=hh, hd=hd, tkv=tkv: E.matmul(psum[:, c1 * 512 + hd * 64:c1 * 512 + (hd + 1) * 64], lhsT=tkv[0:64, d, hh * 128:(hh + 1) * 128],
                                                                                rhs=De.ap[0:64, hd, :], start=True, stop=True), reads=[tk, De], writes=[BK[c1]])
                op("dve", lambda E: E.tensor_copy(out=qgT.ap, in_=pf(c1).rearrange("p (h x) -> p h x", h=8)), writes=[qgT, BK[c1]])
                yield "SCAN"
                for hd in range(8):
                    bk = c0 + hd // 4
                    op("pe", lambda E, hd=hd: E.matmul(psum[0:64, c0 * 512 + hd * 128:c0 * 512 + (hd + 1) * 128], lhsT=wT.ap[:, hd, :], rhs=Sb.ap[:, hd, :], start=True, stop=True),
                       reads=[wT, Sb], writes=[BK[bk]])
                yield None
                op("dve", lambda E: E.tensor_tensor(out=vnew.ap[0:64], in0=uu.ap[0:64], in1=psum[0:64, c0 * 512:(c0 + 2) * 512].rearrange("p (h x) -> p h x", h=8), op=ALU.subtract),
                   reads=[uu], writes=[vnew, BK[c0], BK[c1]])
                yield None
                for hd in range(8):
                    bk = c2 + hd // 4
                    op("pe", lambda E, hd=hd: E.matmul(psum[0:64, c2 * 512 + hd * 128:c2 * 512 + (hd + 1) * 128], lhsT=qgT.ap[:, hd, :], rhs=Sb.ap[:, hd, :], start=True, stop=False),
                       reads=[qgT, Sb], writes=[BK[bk]])
                    op("pe", lambda E, hd=hd: E.matmul(psum[0:64, c2 * 512 + hd * 128:c2 * 512 + (hd + 1) * 128], lhsT=qkT.ap[0:64, hd, :], rhs=vnew.ap[0:64, hd, :], start=False, stop=True),
                       reads=[qkT, vnew], writes=[BK[bk]])
                yield None
                op("act", lambda E, os_=os_: E.activation(out=os_[0:64, :], in_=psum[0:64, c2 * 512:(c2 + 2) * 512], func=AF.Copy), writes=[os_, BK[c2], BK[c3]])
                dma(OF[s][0, cf * 64:(cf + 1) * 64, :], os_[0:64, 0:512], reads=[os_], sb=os_, queue="pool")
                dma(OF[s][1, cb * 64:(cb + 1) * 64, :], os_[0:64, 512:1024], reads=[os_], sb=os_, queue="pool")
                yield None
                for hd in range(8):
                    bk = c0 + hd // 4
                    op("pe", lambda E, hd=hd: E.matmul(psum[:, c0 * 512 + hd * 128:c0 * 512 + (hd + 1) * 128], lhsT=kg.ap[0:64, hd, :], rhs=vnew.ap[0:64, hd, :], start=True, stop=True),
                       reads=[kg, vnew], writes=[BK[bk]])
                yield None
                op("pool", lambda E, m8=m8: E.tensor_tensor(out=S.ap, in0=S.ap, in1=m8.ap[:, 48:56].unsqueeze(2).to_broadcast([128, 8, 128]), op=ALU.mult),
                   reads=[m8, S], writes=[S])
                op("dve", lambda E: E.tensor_tensor(out=S.ap, in0=S.ap, in1=psum[:, c0 * 512:(c0 + 2) * 512].rearrange("p (h x) -> p h x", h=8), op=ALU.add),
                   reads=[S], writes=[S, BK[c0], BK[c1]])
                op("act", lambda E: E.activation(out=Sb.ap, in_=S.ap, func=AF.Copy), reads=[S], writes=[Sb])
            WIN = 2
            for s, L in SEQ:
                N = L // 64
                op("pool", lambda E: E.memset(S.ap, 0.0), writes=[S])
                op("pool", lambda E: E.memset(Sb.ap, 0.0), writes=[Sb])
                nxt = 0
                active = []
                while nxt < N or active:
                    while len(active) < WIN and nxt < N:
                        active.append([nxt, step(s, N, nxt), False])
                        nxt += 1
                    oldest = min(a_[0] for a_ in active)
                    for a_ in list(active):
                        if a_[2] and a_[0] != oldest:
                            continue
                        try:
                            r_ = next(a_[1])
                            a_[2] = (r_ == "SCAN")
                        except StopIteration:
                            active.remove(a_)
            phase_reset()

        if "C" in phases:
            T = ffn_bufs()
            load_wd(T, "ffn2_w_down")
            wst = alloc("wstc", 1024)
            wo = alloc("wo", 8 * 1024, BF16)
            wo.ap = wo.ap.rearrange("p (c f) -> p c f", c=8)
            for c in range(8):
                dma(wst[:], W["w_out"][c * 128:(c + 1) * 128, :], writes=[wst], sb=wst)
                sc = small[:, 5 + c:6 + c] if c < 4 else small[:, 9:10]
                op("dve", lambda E, c=c, sc=sc: E.tensor_scalar(out=wo.ap[:, c, :], in0=wst[:], scalar1=sc, scalar2=None, op0=ALU.mult),
                   reads=[wst, small], writes=[wo])
            ya = alloc("ya", 4 * 512)
            ya.ap = ya.ap.rearrange("p (j e) -> p j e", j=4)
            ofb = [alloc("ofb%d" % i, 4 * 512) for i in range(2)]
            for b in ofb:
                b.ap = b.ap.rearrange("p (j e) -> p j e", j=4)
            zz = alloc("zz", 4 * 512)
            zz.ap = zz.ap.rearrange("p (j e) -> p j e", j=4)
            mixb = [alloc("mixb%d" % i, 1024, BF16) for i in range(2)]
            sq = T["tmp"][1]
            ssc = alloc("ssc", 32)
            tilesC = [(s, L, ti) for s, L in SEQ for ti in range(L // 512)]
            ystg = Buf("ystg", arena[:, T["act_off"]:T["act_off"] + 4 * D].rearrange("p (j d) -> p j d", j=4))

            def load_h(ix):
                s_, L_, ti_ = tilesC[ix]
                dma(T["h"].ap, H1[s_][ti_ * 512:(ti_ + 1) * 512, :].rearrange("(j p) d -> p j d", p=128), writes=T["hb"], sb=T["h"])

            def load_front(ix):
                s_, L_, ti_ = tilesC[ix]
                rw = slice(ti_ * 512, (ti_ + 1) * 512)
                dma(ya.ap, YA[s_][rw, :].rearrange("(j p) e -> p j e", p=128), writes=[ya], sb=ya)
                dma(ofb[0].ap, OF[s_][0, rw, :].rearrange("(j p) e -> p j e", p=128), writes=[ofb[0]], sb=ofb[0])
                dma(ofb[1].ap, OF[s_][1, rw, :].rearrange("(j p) e -> p j e", p=128), writes=[ofb[1]], sb=ofb[1])
                dma(zz.ap, ZS[s_][rw, :].rearrange("(j p) e -> p j e", p=128), writes=[zz], sb=zz)
            load_h(0)
            load_front(0)
            for ix, (s, L, ti) in enumerate(tilesC):
                if True:
                    r0 = ti * 512
                    rows = slice(r0, r0 + 512)
                    h = T["h"]
                    o = ofb[0]
                    op("pool", lambda E: E.tensor_tensor(out=o.ap, in0=o.ap, in1=ofb[1].ap, op=ALU.add), reads=[ofb[1], o], writes=[o])
                    e2 = ofb[1]
                    op("act", lambda E: E.activation(out=e2.ap, in_=zz.ap, func=AF.Exp, scale=-1.0), reads=[zz], writes=[e2])
                    op("act", lambda E: E.activation(out=e2.ap, in_=e2.ap, func=AF.Ln, bias=1.0), reads=[e2], writes=[e2])
                    op("act", lambda E: E.activation(out=e2.ap, in_=e2.ap, func=AF.Exp, scale=-1.0), reads=[e2], writes=[e2])
                    op("pool", lambda E: E.tensor_tensor(out=zz.ap, in0=zz.ap, in1=e2.ap, op=ALU.mult), reads=[e2, zz], writes=[zz])
                    for j in range(4):
                        op("dve", lambda E, j=j: E.tensor_tensor(out=sq[:, 0:512], in0=o.ap[:, j, :], in1=o.ap[:, j, :], op=ALU.mult), reads=[o], writes=[sq])
                        op("dve", lambda E, j=j: E.tensor_reduce(out=ssc[:, j * 8:j * 8 + 4], in_=sq.ap[:, 0:512].rearrange("p (h d) -> p h d", h=4), axis=AX.X, op=ALU.add),
                           reads=[sq], writes=[ssc])
                        op("act", lambda E, j=j: E.activation(out=T["junk"][:, 0:512], in_=ya.ap[:, j, :], func=AF.Square, accum_out=ssc[:, j * 8 + 4:j * 8 + 5]),
                           reads=[ya], writes=[ssc])
                    sv = ssc.ap[:, 0:32].rearrange("p (j k) -> p j k", j=4)
                    op("dve", lambda E, sv=sv: E.tensor_scalar(out=sv[:, :, 0:4], in0=sv[:, :, 0:4], scalar1=1.0 / 128, scalar2=EPS, op0=ALU.mult, op1=ALU.add), reads=[ssc], writes=[ssc])
                    op("dve", lambda E, sv=sv: E.tensor_scalar(out=sv[:, :, 4:5], in0=sv[:, :, 4:5], scalar1=1.0 / 512, scalar2=EPS, op0=ALU.mult, op1=ALU.add), reads=[ssc], writes=[ssc])
                    op("act", lambda E, sv=sv: E.activation(out=sv[:, :, 0:5], in_=sv[:, :, 0:5], func=AF.Ln), reads=[ssc], writes=[ssc])
                    op("act", lambda E, sv=sv: E.activation(out=sv[:, :, 0:5], in_=sv[:, :, 0:5], func=AF.Exp, scale=-0.5), reads=[ssc], writes=[ssc])
                    xT = T["xT"]
                    for j in range(4):
                        mb = mixb[j % 2]
                        op("act", lambda E, j=j, mb=mb: E.activation(out=mb[:, 0:512], in_=ya.ap[:, j, :], func=AF.Copy, scale=ssc[:, j * 8 + 4:j * 8 + 5]),
                           reads=[ya, ssc], writes=[mb])
                        op("dve", lambda E, j=j: E.tensor_tensor(out=o.ap[:, j, :].rearrange("p (h d) -> p h d", h=4), in0=o.ap[:, j, :].rearrange("p (h d) -> p h d", h=4),
                                                                 in1=ssc.ap[:, j * 8:j * 8 + 4].unsqueeze(2).to_broadcast([128, 4, 128]), op=ALU.mult), reads=[ssc, o], writes=[o])
                        op("pool", lambda E, j=j, mb=mb: E.tensor_tensor(out=mb[:, 512:1024], in0=o.ap[:, j, :], in1=zz.ap[:, j, :], op=ALU.mult), reads=[o, zz, mb], writes=[mb])
                        bk = j % 2
                        for dc in range(8):
                            op("pe", lambda E, dc=dc, bk=bk, mb=mb: E.transpose(out=pb(bk, dc * 128, (dc + 1) * 128), in_=mb[:, dc * 128:(dc + 1) * 128], identity=identb[:]),
                               reads=[mb, identb], writes=[BK[bk]])
                        op("dve", lambda E, j=j, bk=bk: E.tensor_copy(out=xT.ap[:, :, j * 128:(j + 1) * 128], in_=pb(bk).rearrange("p (c t) -> p c t", c=8)),
                           writes=[xT, BK[bk]])
                    if ix + 1 < len(tilesC):
                        load_front(ix + 1)
                    ss2 = T["ss2"]
                    for j in range(4):
                        b0 = 2 if j % 2 == 0 else 4
                        for hf in range(2):
                            for c in range(8):
                                op("pe", lambda E, c=c, hf=hf, b0=b0, j=j: E.matmul(pf(b0 + hf), lhsT=xT.ap[:, c, j * 128:(j + 1) * 128], rhs=wo.ap[:, c, hf * 512:(hf + 1) * 512],
                                                                                     start=(c == 0), stop=(c == 7)), reads=[xT, wo], writes=[BK[b0 + hf]])
                        yps = psum[:, b0 * 512:(b0 + 2) * 512]
                        op("act", lambda E, j=j, yps=yps: E.activation(out=T["junk"][:], in_=yps, func=AF.Square, accum_out=ss2[:, j:j + 1]), writes=[ss2, BK[b0], BK[b0 + 1]])
                        op("dve", lambda E, j=j: E.tensor_scalar(out=ss2[:, 4 + j:5 + j], in0=ss2[:, j:j + 1], scalar1=1.0 / D, scalar2=EPS, op0=ALU.mult, op1=ALU.add), reads=[ss2], writes=[ss2])
                        op("act", lambda E, j=j: E.activation(out=ss2[:, 4 + j:5 + j], in_=ss2[:, 4 + j:5 + j], func=AF.Ln), reads=[ss2], writes=[ss2])
                        op("act", lambda E, j=j: E.activation(out=ss2[:, 4 + j:5 + j], in_=ss2[:, 4 + j:5 + j], func=AF.Exp, scale=-0.5), reads=[ss2], writes=[ss2])
                        tmp = T["tmp"][j % 2]
                        op("dve", lambda E, j=j, yps=yps, tmp=tmp: E.scalar_tensor_tensor(out=tmp[:], in0=yps, scalar=ss2[:, 4 + j:5 + j], in1=gB["mix_post_g"][:], op0=ALU.mult, op1=ALU.mult),
                           reads=[ss2, gB["mix_post_g"]], writes=[tmp, BK[b0], BK[b0 + 1]])
                        op("pool", lambda E, j=j, tmp=tmp: E.tensor_tensor(out=T["hs"][j], in0=T["hs"][j], in1=tmp[:], op=ALU.add), reads=[tmp], writes=[T["hb"][j]])
                    ffn(T["hb"], T["hs"], "ffn2_pre_g", WB["ffn2_w_gate"], WB["ffn2_w_up"], T["wd"], gB["ffn2_post_g"], T)
                    ss = T["ss"]
                    for j in range(4):
                        op("act", lambda E, j=j: E.activation(out=T["junk"][:], in_=T["hs"][j], func=AF.Square, accum_out=ss[:, j:j + 1]), reads=[T["hb"][j]], writes=[ss])
                    rstd_chain(ss, 4, D)
                    for j in range(4):
                        e1 = "dve"
                        op(e1, lambda E, j=j: E.scalar_tensor_tensor(out=ystg.ap[:, j, :], in0=T["hs"][j], scalar=ss[:, j:j + 1], in1=gB["final_norm_g"][:], op0=ALU.mult, op1=ALU.mult),
                           reads=[ss, gB["final_norm_g"], T["hb"][j]], writes=T["actc"][4 * j:4 * j + 4])
                    if ix + 1 < len(tilesC):
                        load_h(ix + 1)
                    dma(Y[s][rows, :].rearrange("(j p) d -> p j d", p=128), ystg.ap, reads=T["actc"][0:16], sb=ystg, queue="pool")
            phase_reset()
        P.barrier()
        P.emit()
        stats = dict(nins=P.nins, nwaits=P.nwaits, nsem=len(P.dmabufs) + 5)
    return nc, stats


def rope_table(L):
    inv = 10000.0 ** (-np.arange(0, 32, 2, dtype=np.float32) / 32)
    ang = np.arange(L, dtype=np.float32)[:, None] * inv[None, :].astype(np.float32)
    c, s = np.cos(ang).astype(np.float32), np.sin(ang).astype(np.float32)
    return np.ascontiguousarray(np.concatenate([c, c, -s, s], axis=1).astype(np.float32))


def gdn_masks():
    i = np.arange(64)
    m = np.zeros((64, 5, 8, 64), np.float32)
    for hd in range(8):
        fwd = hd < 4
        al = (i[:, None] > i[None, :]) if fwd else (i[:, None] < i[None, :])
        m[:, 0, hd, :] = np.where(al, 0.0, NEG)
        al = (i[None, :] > i[:, None]) if fwd else (i[None, :] < i[:, None])
        m[:, 1, hd, :] = np.where(al, 0.0, NEG)
        al = (i[None, :] >= i[:, None]) if fwd else (i[None, :] <= i[:, None])
        m[:, 2, hd, :] = np.where(al, 0.0, NEG)
    m[:, 3, 0, :] = (i[:, None] <= i[None, :]).astype(np.float32)
    m[:, 3, 1, :] = (i[:, None] >= i[None, :]).astype(np.float32)
    return m


_CACHE = {}


def kernel(**inputs):
    xp = np.asarray(inputs["x_prompt"], np.float32)
    xs = np.asarray(inputs["x_sample"], np.float32)
    B, Lp, _ = xp.shape
    Ls = xs.shape[1]
    assert B == 8 and xs.shape[0] == 8
    key = (Lp, Ls)
    if key not in _CACHE:
        _CACHE[key] = build(Lp, Ls)[0]
    nc = _CACHE[key]
    shared = {n: np.ascontiguousarray(np.asarray(inputs[n], np.float32).reshape(WSHAPES[n])) for n in WNAMES}
    shared["rope_tab"] = rope_table(max(Lp, Ls))
    shared["gdn_masks"] = gdn_masks()
    in_maps = []
    for c in range(8):
        m = dict(shared)
        m["x_p"] = np.ascontiguousarray(xp[c])
        m["x_s"] = np.ascontiguousarray(xs[c])
        in_maps.append(m)
    res = run_bass_kernel_spmd(nc, in_maps, core_ids=list(range(8)))
    yp = np.stack([np.asarray(r["y_p"], np.float32) for r in res.results], 0)
    ys = np.stack([np.asarray(r["y_s"], np.float32) for r in res.results], 0)
    return (yp, ys)
```
